# Optimizing a Trainium2 kernel written in Bass

```python
import math
import jax, jax.numpy as jnp
from jax import lax
import numpy as np

D_MODEL = 2048
BATCH = 16
SEQ = 2048
DEPTH = 2

D_FF = 256 * ((8 * D_MODEL // 3 + 255) // 256)
FFN_RES = 0.5
N_MOD = 9
EPS = 1e-6

GDN_HEADS = 8
GDN_HEAD_DIM = 128
GDN_WIDTH = GDN_HEADS * GDN_HEAD_DIM
GDN_CONV = 4
GDN_CHUNK = 64

S5_GROUP = 16
S5_WIDTH = 768
S5_GROUPS = S5_WIDTH // S5_GROUP
S5_STATE = 64
S5_MAX_RE = -1e-4

DIL_PAIRS = ((128, 1), (512, 4), (2048, 16))
DIL_HEADS_PER_GROUP = 4
DIL_HEAD_DIM = 64
DIL_SUBHEADS = len(DIL_PAIRS) * DIL_HEADS_PER_GROUP
DIL_WIDTH = DIL_SUBHEADS * DIL_HEAD_DIM
DIL_OUT = DIL_HEADS_PER_GROUP * DIL_HEAD_DIM
ALIBI_MAX = 8.0

N_BRANCH = 3
IN_SPLITS = (3 * GDN_WIDTH, GDN_WIDTH, GDN_HEADS, GDN_HEADS, S5_WIDTH, 3 * DIL_WIDTH, N_BRANCH * D_MODEL)
IN_COLS = sum(IN_SPLITS)

kernel_name = 'hybrid_gdn_s5_dilated_macaron_block'


def rms_norm(x, g):
    xf = x.astype(jnp.float32)
    y = xf * lax.rsqrt(jnp.mean(xf * xf, axis=-1, keepdims=True) + EPS)
    return (y * g.astype(jnp.float32)).astype(x.dtype)


def l2_normalize(x):
    return x * lax.rsqrt(jnp.sum(x * x, axis=-1, keepdims=True) + EPS)


def modulate(x, shift, scale):
    return x * (1.0 + scale[:, None, :]) + shift[:, None, :]


def swiglu(x, w1, w3, w2):
    return (jax.nn.silu(x @ w1) * (x @ w3)) @ w2


def causal_depthwise_conv(x, w):
    K, C = w.shape
    return lax.conv_general_dilated(x, w[:, None, :].astype(x.dtype), window_strides=(1,),
                                    padding=[(K - 1, 0)], dimension_numbers=('NWC', 'WIO', 'NWC'),
                                    feature_group_count=C)


def gated_delta_rule_chunked(q, k, v, g, beta):
    Bsz, S_, H, Dk = q.shape
    Dv = v.shape[-1]
    C = GDN_CHUNK
    N = S_ // C

    def chunks(t):
        t = t.reshape((Bsz, N, C) + t.shape[2:])
        return jnp.moveaxis(t, 3, 1)

    q, k, v, g, beta = chunks(q), chunks(k), chunks(v), chunks(g), chunks(beta)
    g = jnp.cumsum(g, axis=-1)
    kb = k * beta[..., None]
    vb = v * beta[..., None]
    tril = jnp.tril(jnp.ones((C, C), bool))
    tril_strict = jnp.tril(jnp.ones((C, C), bool), -1)
    decay = jnp.exp(jnp.where(tril, g[..., :, None] - g[..., None, :], -jnp.inf))
    lower = jnp.where(tril_strict, jnp.einsum('bhnid,bhnjd->bhnij', kb, k) * decay, 0.0)
    system = jnp.eye(C, dtype=q.dtype) + lower
    u = lax.linalg.triangular_solve(system, vb, left_side=True, lower=True, unit_diagonal=True)
    w = lax.linalg.triangular_solve(system, kb * jnp.exp(g)[..., None], left_side=True,
                                    lower=True, unit_diagonal=True)

    def step(h, inp):
        qc, kc, uc, wc, gc, dc = inp
        attn = jnp.einsum('bhid,bhjd->bhij', qc, kc) * dc
        v_new = uc - jnp.einsum('bhcd,bhde->bhce', wc, h)
        o = (jnp.einsum('bhcd,bhde->bhce', qc * jnp.exp(gc)[..., None], h)
             + jnp.einsum('bhij,bhje->bhie', attn, v_new))
        g_last = gc[..., -1]
        h = (h * jnp.exp(g_last)[..., None, None]
             + jnp.einsum('bhcd,bhce->bhde', kc * jnp.exp(g_last[..., None] - gc)[..., None], v_new))
        return h, o

    xs = tuple(jnp.moveaxis(t, 2, 0) for t in (q, k, u, w, g, decay))
    h0 = jnp.zeros((Bsz, H, Dk, Dv), q.dtype)
    _, o = lax.scan(step, h0, xs)
    return o.transpose(1, 0, 3, 2, 4).reshape(Bsz, S_, H, Dv)


def gdn_branch(qkv, z, beta_logit, alpha_logit, conv_w, a_log, dt_bias, out_norm):
    Bsz, S_, _ = qkv.shape
    f32 = jnp.float32
    qkv = jax.nn.silu(causal_depthwise_conv(qkv, conv_w)).astype(f32)
    q, k, v = [t.reshape(Bsz, S_, GDN_HEADS, GDN_HEAD_DIM) for t in jnp.split(qkv, 3, axis=-1)]
    q = l2_normalize(q) * GDN_HEAD_DIM ** -0.5
    k = l2_normalize(k)
    beta = jax.nn.sigmoid(beta_logit.astype(f32))
    g = -jnp.exp(a_log.astype(f32)) * jax.nn.softplus(alpha_logit.astype(f32) + dt_bias.astype(f32))
    o = gated_delta_rule_chunked(q, k, v, g, beta)
    o = rms_norm(o, out_norm) * jax.nn.silu(z.astype(f32).reshape(Bsz, S_, GDN_HEADS, GDN_HEAD_DIM))
    return o.reshape(Bsz, S_, GDN_WIDTH).astype(z.dtype)


def s5_branch(u, a_re, a_im, b_re, b_im, c_re, c_im, d_skip, log_step, glu_w, glu_b):
    Bsz, S_, _ = u.shape
    f32 = jnp.float32
    uf = u.astype(f32).reshape(Bsz, S_, S5_GROUPS, S5_GROUP)
    lam = lax.complex(jnp.minimum(a_re.astype(f32), S5_MAX_RE), a_im.astype(f32))
    step = jnp.exp(log_step.astype(f32))[:, None]
    lam_bar = jnp.exp(lam * step)
    b = lax.complex(b_re.astype(f32), b_im.astype(f32))
    b_bar = ((lam_bar - 1.0) / lam)[..., None] * b
    bu = jnp.einsum('gpi,bsgi->bsgp', b_bar, uf)
    a = jnp.broadcast_to(lam_bar, (1, S_) + lam_bar.shape)

    def combine(left, right):
        a_l, b_l = left
        a_r, b_r = right
        return a_r * a_l, a_r * b_l + b_r

    _, states = lax.associative_scan(combine, (a, bu), axis=1)
    cmat = lax.complex(c_re.astype(f32), c_im.astype(f32))
    y = jnp.real(jnp.einsum('gip,bsgp->bsgi', cmat, states)) + d_skip.astype(f32).reshape(S5_GROUPS, S5_GROUP) * uf
    y = jax.nn.gelu(y.reshape(Bsz, S_, S5_WIDTH))
    val, gate = jnp.split(y @ glu_w.astype(f32) + glu_b.astype(f32), 2, axis=-1)
    return (val * jax.nn.sigmoid(gate)).astype(u.dtype)


def dilated_window_attention(q, k, v, slopes, window, dilation):
    Bsz, S_, H, E = q.shape
    span = window // dilation
    sd = S_ // dilation
    blk = min(span, sd)
    nb = -(-sd // blk)
    sp = nb * blk

    def residues(t):
        t = t.reshape(Bsz, sd, dilation, H, E).transpose(0, 2, 1, 3, 4)
        return jnp.pad(t, ((0, 0), (0, 0), (0, sp - sd), (0, 0), (0, 0)))

    def key_blocks(t):
        t = jnp.pad(t, ((0, 0), (0, 0), (blk, 0), (0, 0), (0, 0))).reshape(Bsz, dilation, nb + 1, blk, H, E)
        return jnp.concatenate([t[:, :, :-1], t[:, :, 1:]], axis=3)

    qb = residues(q).reshape(Bsz, dilation, nb, blk, H, E)
    kb = key_blocks(residues(k))
    vb = key_blocks(residues(v))
    s = jnp.einsum('brnqhe,brnkhe->brnhqk', qb, kb)
    qi = jnp.arange(blk)[:, None]
    kj = jnp.arange(2 * blk)[None, :]
    steps = qi - kj + blk
    key_pos = jnp.arange(nb)[:, None, None] * blk + kj[None] - blk
    valid = (steps >= 0) & (steps <= span) & (key_pos >= 0)
    bias = -slopes[:, None, None] * (steps * dilation).astype(jnp.float32)[None]
    s = jnp.where(valid[None, None, :, None], s + bias[None, None, None], -jnp.inf)
    m = jnp.max(s, axis=-1, keepdims=True)
    p = jnp.exp(s - m)
    l = jnp.sum(p, axis=-1, keepdims=True)
    o = jnp.einsum('brnhqk,brnkhe->brnqhe', p / l, vb)
    lse = (m + jnp.log(l))[..., 0].transpose(0, 1, 2, 4, 3)

    def back(t):
        t = t.reshape((Bsz, dilation, sp) + t.shape[4:])[:, :, :sd]
        t = jnp.moveaxis(t, 1, 2)
        return t.reshape((Bsz, S_) + t.shape[3:])

    return back(o), back(lse)


def dilated_branch(qkv, q_norm, k_norm):
    Bsz, S_, _ = qkv.shape
    f32 = jnp.float32
    shape = (Bsz, S_, DIL_SUBHEADS, DIL_HEAD_DIM)
    q, k, v = [t.astype(f32).reshape(shape) for t in jnp.split(qkv, 3, axis=-1)]
    q = rms_norm(q, q_norm) * DIL_HEAD_DIM ** -0.5
    k = rms_norm(k, k_norm)
    slopes = jnp.power(2.0, -ALIBI_MAX * jnp.arange(1, DIL_SUBHEADS + 1, dtype=f32) / DIL_SUBHEADS)
    outs, lses = [], []
    for gi, (window, dilation) in enumerate(DIL_PAIRS):
        hs = slice(gi * DIL_HEADS_PER_GROUP, (gi + 1) * DIL_HEADS_PER_GROUP)
        o, lse = dilated_window_attention(q[:, :, hs], k[:, :, hs], v[:, :, hs], slopes[hs], window, dilation)
        outs.append(o)
        lses.append(lse)
    weights = jax.nn.softmax(jnp.stack(lses, 0), axis=0)
    o = jnp.sum(weights[..., None] * jnp.stack(outs, 0), axis=0)
    return o.reshape(Bsz, S_, DIL_OUT).astype(qkv.dtype)


def hybrid_mixer(u, w_in, gdn_conv, gdn_a_log, gdn_dt_bias, gdn_out_norm,
                 s5_a_re, s5_a_im, s5_b_re, s5_b_im, s5_c_re, s5_c_im, s5_d, s5_log_step, s5_glu_w, s5_glu_b,
                 dil_q_norm, dil_k_norm, w_branch_a, w_branch_b, w_branch_c, w_out):
    Bsz, S_, D = u.shape
    proj = u @ w_in
    offsets = np.cumsum(IN_SPLITS)[:-1].tolist()
    a_qkv, a_z, a_beta, a_alpha, b_u, c_qkv, gate_logits = jnp.split(proj, offsets, axis=-1)
    y_a = gdn_branch(a_qkv, a_z, a_beta, a_alpha, gdn_conv, gdn_a_log, gdn_dt_bias, gdn_out_norm)
    y_b = s5_branch(b_u, s5_a_re, s5_a_im, s5_b_re, s5_b_im, s5_c_re, s5_c_im, s5_d, s5_log_step, s5_glu_w, s5_glu_b)
    y_c = dilated_branch(c_qkv, dil_q_norm, dil_k_norm)
    gates = jax.nn.sigmoid(gate_logits.astype(jnp.float32)).astype(u.dtype).reshape(Bsz, S_, N_BRANCH, D)
    merged = (gates[:, :, 0] * (y_a @ w_branch_a)
              + gates[:, :, 1] * (y_b @ w_branch_b)
              + gates[:, :, 2] * (y_c @ w_branch_c))
    return merged @ w_out


def setup_inputs(seed: int = 0) -> dict:
    key = jax.random.key(seed)
    keys = list(jax.random.split(key, 48))
    f32 = jnp.float32
    L, D = DEPTH, D_MODEL
    G, P, I = S5_GROUPS, S5_STATE, S5_GROUP

    def nrm(shape, scale):
        return jax.random.normal(keys.pop(), shape, f32) * scale

    def gain(shape):
        return 1.0 + nrm(shape, 0.05)

    def unif(shape, lo, hi):
        return jax.random.uniform(keys.pop(), shape, f32, lo, hi)

    dt = jnp.exp(unif((L, GDN_HEADS), math.log(1e-3), math.log(1e-1)))
    return {
        'x': nrm((BATCH, SEQ, D), 1.0),
        'c': nrm((BATCH, D), 1.0),
        'ada_w': nrm((L, D, N_MOD * D), 0.5 * D ** -0.5),
        'ada_b': nrm((L, N_MOD * D), 0.01),
        'norm_ffn1': gain((L, D)),
        'ffn1_w1': nrm((L, D, D_FF), D ** -0.5),
        'ffn1_w3': nrm((L, D, D_FF), D ** -0.5),
        'ffn1_w2': nrm((L, D_FF, D), D_FF ** -0.5),
        'norm_mix': gain((L, D)),
        'w_in': nrm((L, D, IN_COLS), D ** -0.5),
        'gdn_conv': nrm((L, GDN_CONV, 3 * GDN_WIDTH), GDN_CONV ** -0.5),
        'gdn_a_log': jnp.log(unif((L, GDN_HEADS), 1.0, 16.0)),
        'gdn_dt_bias': dt + jnp.log(-jnp.expm1(-dt)),
        'gdn_out_norm': gain((L, GDN_HEAD_DIM)),
        's5_a_re': -0.5 + nrm((L, G, P), 0.01),
        's5_a_im': math.pi * jnp.arange(P, dtype=f32) + nrm((L, G, P), 0.01),
        's5_b_re': nrm((L, G, P, I), (2 * I) ** -0.5),
        's5_b_im': nrm((L, G, P, I), (2 * I) ** -0.5),
        's5_c_re': nrm((L, G, I, P), (2 * P) ** -0.5),
        's5_c_im': nrm((L, G, I, P), (2 * P) ** -0.5),
        's5_d': nrm((L, S5_WIDTH), 1.0),
        's5_log_step': unif((L, G), math.log(1e-3), math.log(1e-1)),
        's5_glu_w': nrm((L, S5_WIDTH, 2 * S5_WIDTH), S5_WIDTH ** -0.5),
        's5_glu_b': nrm((L, 2 * S5_WIDTH), 0.01),
        'dil_q_norm': gain((L, DIL_HEAD_DIM)),
        'dil_k_norm': gain((L, DIL_HEAD_DIM)),
        'w_branch_a': nrm((L, GDN_WIDTH, D), GDN_WIDTH ** -0.5),
        'w_branch_b': nrm((L, S5_WIDTH, D), S5_WIDTH ** -0.5),
        'w_branch_c': nrm((L, DIL_OUT, D), DIL_OUT ** -0.5),
        'w_out': nrm((L, D, D), D ** -0.5),
        'norm_ffn2': gain((L, D)),
        'ffn2_w1': nrm((L, D, D_FF), D ** -0.5),
        'ffn2_w3': nrm((L, D, D_FF), D ** -0.5),
        'ffn2_w2': nrm((L, D_FF, D), D_FF ** -0.5),
    }


def reference(x, c, ada_w, ada_b, norm_ffn1, ffn1_w1, ffn1_w3, ffn1_w2, norm_mix, w_in,
              gdn_conv, gdn_a_log, gdn_dt_bias, gdn_out_norm,
              s5_a_re, s5_a_im, s5_b_re, s5_b_im, s5_c_re, s5_c_im, s5_d, s5_log_step, s5_glu_w, s5_glu_b,
              dil_q_norm, dil_k_norm, w_branch_a, w_branch_b, w_branch_c, w_out,
              norm_ffn2, ffn2_w1, ffn2_w3, ffn2_w2):
    for l in range(DEPTH):
        mod = jax.nn.silu(c) @ ada_w[l] + ada_b[l]
        sh1, sc1, g1, sh2, sc2, g2, sh3, sc3, g3 = jnp.split(mod, N_MOD, axis=-1)
        h = modulate(rms_norm(x, norm_ffn1[l]), sh1, sc1)
        x = x + FFN_RES * g1[:, None, :] * swiglu(h, ffn1_w1[l], ffn1_w3[l], ffn1_w2[l])
        h = modulate(rms_norm(x, norm_mix[l]), sh2, sc2)
        x = x + g2[:, None, :] * hybrid_mixer(
            h, w_in[l], gdn_conv[l], gdn_a_log[l], gdn_dt_bias[l], gdn_out_norm[l],
            s5_a_re[l], s5_a_im[l], s5_b_re[l], s5_b_im[l], s5_c_re[l], s5_c_im[l], s5_d[l], s5_log_step[l],
            s5_glu_w[l], s5_glu_b[l], dil_q_norm[l], dil_k_norm[l],
            w_branch_a[l], w_branch_b[l], w_branch_c[l], w_out[l])
        h = modulate(rms_norm(x, norm_ffn2[l]), sh3, sc3)
        x = x + FFN_RES * g3[:, None, :] * swiglu(h, ffn2_w1[l], ffn2_w3[l], ffn2_w2[l])
    return x
```

```python
import math
from contextlib import ExitStack

import numpy as np
import concourse.bass as bass
import concourse.mybir as mybir
from concourse.bass_utils import run_bass_kernel_spmd

F32 = mybir.dt.float32
BF16 = mybir.dt.bfloat16
AF = mybir.ActivationFunctionType
ALU = mybir.AluOpType

D = 2048
S = 2048
DFF = 5632
NKC = D // 128
NFC = DFF // 128
TT = 512
EPS = 1e-6
DEPTH = 2
NCORES = 8
IN_COLS = 13328
OFF_Z = 3072
OFF_BA = 4096
OFF_BU = 4112
OFF_CQ = 4880
OFF_GATE = 7184
GW = 256


class Buf:
    def __init__(self, name, n=1, t=None):
        self.name = name
        self.n = n
        self.t = t
        self.lastw = [None] * n
        self.readers = [[] for _ in range(n)]

    def __getitem__(self, idx):
        return self.t[idx]


class Op:
    __slots__ = ("eng", "dma", "sig")

    def __init__(self, eng, dma):
        self.eng = eng
        self.dma = dma
        self.sig = None


def _parts(acc):
    if isinstance(acc, Buf):
        return acc, range(acc.n)
    b, p = acc
    if p is None:
        return b, range(b.n)
    if isinstance(p, int):
        return b, (p,)
    return b, p


class Prog:
    ENGS = ("pe", "dve", "act", "pool", "sp")
    NS = 8

    def __init__(self, nc, stack):
        self.nc = nc
        self.e = {"pe": nc.tensor, "dve": nc.vector, "act": nc.scalar, "pool": nc.gpsimd, "sp": nc.sync}
        self.sem = {k: stack.enter_context(nc.semaphore("s_" + k)) for k in self.ENGS}
        self.cnt = {k: 0 for k in self.ENGS}
        self.dsem = {k: [stack.enter_context(nc.semaphore("d_%s%d" % (k, i))) for i in range(self.NS)]
                     for k in ("sp", "pool", "act")}
        self.dcnt = {k: 0 for k in self.dsem}
        self.dlast = {k: [None] * self.NS for k in self.dsem}
        self.waited = {k: {} for k in self.ENGS}
        self.last = {k: None for k in self.ENGS}
        self.nops = 0

    def _wait(self, eng, sig):
        sem, val = sig
        w = self.waited[eng]
        key = id(sem)
        if w.get(key, 0) < val:
            self.e[eng].wait_ge(sem, val)
            w[key] = val

    def op(self, eng, fn, reads=(), writes=(), dma=False):
        o = Op(eng, dma)
        deps = []
        for acc in reads:
            b, ps = _parts(acc)
            for p in ps:
                lw = b.lastw[p]
                if lw is not None:
                    deps.append((lw, True))
                b.readers[p].append(o)
        for acc in writes:
            b, ps = _parts(acc)
            for p in ps:
                lw = b.lastw[p]
                if lw is not None:
                    deps.append((lw, False))
                for r in b.readers[p]:
                    if r is not o:
                        deps.append((r, False))
                b.lastw[p] = o
                b.readers[p] = []
        for d, raw in deps:
            if d is o:
                continue
            if (not d.dma) and (not dma) and d.eng == eng:
                if not raw or eng == "pe":
                    continue
            self._wait(eng, d.sig)
        if dma:
            k = self.dcnt[eng]
            slot = k % self.NS
            prev = self.dlast[eng][slot]
            if prev is not None:
                self._wait(eng, prev.sig)
            o.sig = (self.dsem[eng][slot], 16 * (k // self.NS + 1))
            self.dcnt[eng] = k + 1
            self.dlast[eng][slot] = o
            fn(self.e[eng]).then_inc(self.dsem[eng][slot], 16)
        else:
            self.cnt[eng] += 1
            o.sig = (self.sem[eng], self.cnt[eng])
            fn(self.e[eng]).then_inc(self.sem[eng], 1)
            self.last[eng] = o
        self.nops += 1
        return o

    def barrier(self):
        sigs = [self.last[k].sig for k in self.ENGS if self.last[k] is not None]
        for q in self.dsem:
            for o in self.dlast[q]:
                if o is not None:
                    sigs.append(o.sig)
        for eng in self.ENGS:
            for s in sigs:
                self._wait(eng, s)

    def finish(self):
        sigs = [self.last[k].sig for k in self.ENGS if self.last[k] is not None]
        for q in self.dsem:
            for o in self.dlast[q]:
                if o is not None:
                    sigs.append(o.sig)
        for s in sigs:
            self._wait("sp", s)


class Ctx:
    def __init__(self, nc, stack):
        self.nc = nc
        self.P = Prog(nc, stack)
        self.gstack = stack
        self.uid = 0
        self._fill = {}

    def fill(self, val):
        if val not in self._fill:
            self._fill[val] = self.nc.gpsimd.to_reg(float(val))
        return self._fill[val]

    def sb(self, stack, shape, dt, name, n=1):
        self.uid += 1
        t = stack.enter_context(self.nc.sbuf_tensor("%s_%d" % (name, self.uid), list(shape), dt))
        return Buf(name, n, t)

    def ps(self, stack, shape, dt, name, n=1):
        self.uid += 1
        t = stack.enter_context(self.nc.psum_tensor("%s_%d" % (name, self.uid), list(shape), dt))
        return Buf(name, n, t)


class Stream:
    def __init__(self, bufs, loads, depth=None):
        self.bufs = bufs
        self.loads = loads
        self.depth = len(bufs) if depth is None else depth
        self.issued = 0

    def get(self, k):
        while self.issued < min(len(self.loads), k + self.depth):
            self.loads[self.issued](self.bufs[self.issued % len(self.bufs)])
            self.issued += 1
        return self.bufs[k % len(self.bufs)]


def mm(C, ps, lhsT, rhs, start, stop, reads, wr):
    C.P.op("pe", lambda e: e.matmul(ps, lhsT, rhs, start=start, stop=stop), reads=reads, writes=[wr])


def load_vec_fm(C, stack, dst, dst_cols, src_rows_ap, nrows, tmp, pst, ident):
    P = C.P
    P.op("sp", lambda e: e.dma_start(out=tmp[0:nrows, :], in_=src_rows_ap), writes=[tmp], dma=True)
    P.op("pe", lambda e: e.transpose(pst[:, 0:nrows], tmp[0:nrows, :], ident[0:nrows, 0:nrows]),
         reads=[tmp, ident], writes=[pst])
    P.op("dve", lambda e: e.tensor_copy(out=dst_cols, in_=pst[:, 0:nrows]), reads=[pst], writes=[dst])


def build_program(nseq=2, depth=DEPTH, debug=False, phases=None):
    nc = bass.Bass("TRN2", target_bir_lowering=False)
    ntok = nseq * S
    ntile = ntok // TT
    dr = {}

    def din(name, shape, dt=F32):
        dr[name] = nc.dram_tensor(name, list(shape), dt, kind="ExternalInput").ap()
        return dr[name]

    def dscratch(name, shape, dt=F32):
        kind = "ExternalOutput" if debug else "Internal"
        dr[name] = nc.dram_tensor(name, list(shape), dt, kind=kind).ap()
        return dr[name]

    xT_in = din("xT", [D, ntok])
    c_in = din("c", [nseq, D])
    L = depth
    din("ada_w", [L, 72, 128, NKC, GW])
    din("ada_b", [L, 144, 128])
    din("norms", [L, 3, NKC, 128])
    for nm in ("ffn1", "ffn2"):
        din(nm + "_w1", [L, DFF // GW, 128, NKC, GW])
        din(nm + "_w3", [L, DFF // GW, 128, NKC, GW])
        din(nm + "_w2", [L, NKC, 128, NFC, 128])
    din("w_inT", [L, 52, 128, NKC, GW])
    din("w_ba", [L, 128, NKC, 16])
    din("w_br", [L, 8, 128, NKC, GW])
    din("w_outT", [L, 8, 128, NKC, GW])
    din("dil_g", [L, 2, 128])
    din("gdn_conv", [L, 128, 24, 4])
    din("gdn_ad", [L, 128, 16])
    din("gdn_on", [L, 128, 1])
    din("gdn_masks", [128, 7, 2, 128])
    din("s5_vec", [L, 3, 24, 128])
    din("s5_dg", [L, 18, 128])
    din("s5_Bm", [L, 128, 6, 2, 512])
    din("s5_Cm", [L, 128, 24, 2, 128])
    din("s5_gluw", [L, 6, 128, 6, GW])
    out_ap = nc.dram_tensor("outT", [D, ntok], F32, kind="ExternalOutput").ap()
    scr = {}
    scr["HT"] = dscratch("HT", [nseq, D, S], BF16)
    scr["PA"] = dscratch("PA", [nseq, 4096, S])
    scr["PB"] = dscratch("PB", [nseq, 768, S])
    scr["PC"] = dscratch("PC", [nseq, 2304, S])
    scr["BA"] = dscratch("BA", [nseq, S, 16])
    scr["YA"] = dscratch("YA", [nseq, 1024, S], BF16)
    scr["YB"] = dscratch("YB", [nseq, 768, S], BF16)
    scr["YC"] = dscratch("YC", [nseq, 256, S], BF16)
    for k in ("HT", "PA", "PB", "PC", "BA", "YA", "YB", "YC"):
        scr[k + "b"] = Buf(k, ntile)
    xT = dscratch("xres", [D, ntok])

    with ExitStack() as gs:
        C = Ctx(nc, gs)
        P = C.P
        ones_bf = C.sb(gs, [128, 128], BF16, "ones_bf")
        ident_f = C.sb(gs, [128, 128], F32, "ident_f")
        ident_bf = C.sb(gs, [128, 128], BF16, "ident_bf")
        neghalf = C.sb(gs, [128, TT], F32, "neghalf")
        modv = C.sb(gs, [128, nseq, 9, NKC], F32, "modv")
        gains = C.sb(gs, [128, 3, NKC], F32, "gains")
        P.op("pool", lambda e: e.memset(ones_bf[:], 1.0), writes=[ones_bf])
        P.op("pool", lambda e: e.memset(neghalf[:], -0.5), writes=[neghalf])
        epsc = C.sb(gs, [128, 1], F32, "epsc")
        P.op("pool", lambda e: e.memset(epsc[:], EPS), writes=[epsc])
        C.epsc = epsc
        P.op("pool", lambda e: e.memset(ident_f[:], 1.0), writes=[ident_f])
        P.op("pool", lambda e: e.affine_select(out=ident_f[:], in_=ident_f[:], pattern=[[1, 128]],
                                               compare_op=ALU.is_equal, fill=C.fill(0.0), base=0,
                                               channel_multiplier=-1),
             reads=[ident_f], writes=[ident_f])
        P.op("dve", lambda e: e.tensor_copy(out=ident_bf[:], in_=ident_f[:]), reads=[ident_f], writes=[ident_bf])
        C.ones_bf, C.ident_f, C.ident_bf, C.neghalf = ones_bf, ident_f, ident_bf, neghalf

        xres = Buf("xres", ntile)
        xin = Buf("xin", 1)

        def xview(ap, i):
            return ap.rearrange("(c p) t -> p c t", p=128)[:, :, i * TT:(i + 1) * TT]

        for l in range(depth):
            src = xT_in if l == 0 else xT
            ada_phase(C, dr, l, nseq, modv, gains)
            ffn_phase(C, dr, l, "ffn1", 0, nseq, ntile, modv, gains, src, xT, xres, xview)
            if phases is None or "proj" in phases:
                proj_phase(C, dr, l, nseq, ntile, modv, gains, xT, xres, xview, scr)
            if phases is None or "s5" in phases:
                s5_phase(C, dr, l, nseq, scr)
            if phases is None or "dil" in phases:
                dil_phase(C, dr, l, nseq, scr)
            if phases is None or "gdn" in phases:
                gdn_phase(C, dr, l, nseq, scr)
            if phases is None or "merge" in phases:
                merge_phase(C, dr, l, nseq, ntile, modv, gains, xT, xres, xview, scr)
            ffn_phase(C, dr, l, "ffn2", 2, nseq, ntile, modv, gains, xT, xT if l < depth - 1 else out_ap,
                      xres, xview)
        P.finish()
    return nc


def mod_vectors(C, AB, modv, gains, sub, nseq, gate_scale):
    P = C.P
    for b in range(nseq):
        P.op("dve", lambda e, b=b: e.scalar_tensor_tensor(
            out=AB[:, b, 0, :], in0=modv[:, b, 3 * sub + 1, :], scalar=1.0, in1=gains[:, sub, :],
            op0=ALU.add, op1=ALU.mult), reads=[modv, gains], writes=[AB])
        P.op("dve", lambda e, b=b: e.tensor_copy(out=AB[:, b, 1, :], in_=modv[:, b, 3 * sub, :]),
             reads=[modv], writes=[AB])
        P.op("dve", lambda e, b=b: e.tensor_scalar(out=AB[:, b, 2, :], in0=modv[:, b, 3 * sub + 2, :],
                                                   scalar1=gate_scale, scalar2=None, op0=ALU.mult),
             reads=[modv], writes=[AB])


def norm_mod(C, xs, hT, AB, b, sq, tmpf, ms, rstd, ps_stat):
    P = C.P
    for c in range(NKC):
        q = sq[c % len(sq)]
        P.op("act", lambda e, c=c, q=q: e.activation(out=q[:], in_=xs[:, c, :], func=AF.Square),
             reads=[xs], writes=[q])
        mm(C, ps_stat[:], C.ones_bf[:], q[:], c == 0, c == NKC - 1, [q, C.ones_bf], ps_stat)
    P.op("act", lambda e: e.activation(out=ms[:], in_=ps_stat[:], func=AF.Sqrt, bias=C.epsc[:, 0:1], scale=1.0 / D),
         reads=[ps_stat, C.epsc], writes=[ms])
    P.op("dve", lambda e: e.reciprocal(out=rstd[:], in_=ms[:]), reads=[ms], writes=[rstd])
    for c in range(NKC):
        t = tmpf[c % len(tmpf)]
        P.op("dve", lambda e, c=c, t=t: e.scalar_tensor_tensor(
            out=t[:], in0=xs[:, c, :], scalar=AB[:, b, 0, c:c + 1], in1=rstd[:],
            op0=ALU.mult, op1=ALU.mult), reads=[xs, AB, rstd], writes=[t])
        P.op("act", lambda e, c=c, t=t: e.activation(
            out=hT[:, c, :], in_=t[:], func=AF.Identity, bias=AB[:, b, 1, c:c + 1], scale=1.0),
            reads=[t, AB], writes=[(hT, c)])


def ada_phase(C, dr, l, nseq, modv, gains):
    P = C.P
    with ExitStack() as st:
        tmp = C.sb(st, [128, 128], F32, "ada_tmp")
        pst = C.ps(st, [128, 512], F32, "ada_pst")
        psm = C.ps(st, [128, 512], F32, "ada_psm")
        cT = C.sb(st, [128, NKC, nseq], F32, "cT")
        bias = C.sb(st, [128, 144], F32, "ada_bias")
        wb = [C.sb(st, [128, NKC, GW], F32, "ada_w%d" % i) for i in range(2)]
        for c in range(NKC):
            P.op("sp", lambda e, c=c: e.dma_start(out=tmp[0:nseq, :], in_=dr["c"][:, c * 128:(c + 1) * 128]),
                 writes=[tmp], dma=True)
            P.op("pe", lambda e: e.transpose(pst[:, 0:nseq], tmp[0:nseq, :], C.ident_f[0:nseq, 0:nseq]),
                 reads=[tmp, C.ident_f], writes=[pst])
            P.op("act", lambda e, c=c: e.activation(out=cT[:, c, :], in_=pst[:, 0:nseq], func=AF.Silu),
                 reads=[pst], writes=[cT])
        load_vec_fm(C, st, bias, bias[:, 0:128], dr["ada_b"][l, 0:128, :], 128, tmp, pst, C.ident_f)
        load_vec_fm(C, st, bias, bias[:, 128:144], dr["ada_b"][l, 128:144, :], 16, tmp, pst, C.ident_f)
        for j in range(3):
            load_vec_fm(C, st, gains, gains[:, j, :], dr["norms"][l, j, :, :], NKC, tmp, pst, C.ident_f)
        loads = []
        for g in range(72):
            loads.append(lambda b, g=g: P.op("sp", lambda e: e.dma_start(out=b[:], in_=dr["ada_w"][l, g]),
                                             writes=[b], dma=True))
        strm = Stream(wb, loads)
        for g in range(72):
            w = strm.get(g)
            for fi in range(2):
                ch = 2 * g + fi
                for c in range(NKC):
                    mm(C, psm[:, ch * nseq:(ch + 1) * nseq], w[:, c, fi * 128:(fi + 1) * 128], cT[:, c, :],
                       c == 0, c == NKC - 1, [w, cT], psm)
        for b in range(nseq):
            P.op("dve", lambda e, b=b: e.tensor_tensor(
                out=modv[:, b, :, :].rearrange("p j c -> p (j c)"),
                in0=psm[:, 0:144 * nseq].rearrange("p (k b) -> p k b", b=nseq)[:, :, b],
                in1=bias[:], op=ALU.add), reads=[psm, bias], writes=[modv])
    P.barrier()


def ffn_phase(C, dr, l, nm, sub, nseq, ntile, modv, gains, src, dst, xres, xview):
    P = C.P
    w1d, w3d, w2d = dr[nm + "_w1"], dr[nm + "_w3"], dr[nm + "_w2"]
    NG = DFF // GW
    with ExitStack() as st:
        xs = C.sb(st, [128, NKC, TT], F32, "xs")
        hT = C.sb(st, [128, NKC, TT], BF16, "hT", n=NKC)
        actT = C.sb(st, [128, NFC, TT], BF16, "actT", n=NFC)
        w1b = [C.sb(st, [128, NKC, GW], BF16, "w1b%d" % i) for i in range(2)]
        w3b = [C.sb(st, [128, NKC, GW], BF16, "w3b%d" % i) for i in range(2)]
        w2b = [C.sb(st, [128, NFC, 128], BF16, "w2b%d" % i) for i in range(2)]
        sq = [C.sb(st, [128, TT], BF16, "sq%d" % i) for i in range(3)]
        tmpf = [C.sb(st, [128, TT], F32, "tmpf%d" % i) for i in range(3)]
        sg = [C.sb(st, [128, TT], F32, "sg%d" % i) for i in range(3)]
        xr = [C.sb(st, [128, TT], F32, "xr%d" % i) for i in range(4)]
        ms = C.sb(st, [128, TT], F32, "ms")
        rstd = C.sb(st, [128, TT], F32, "rstd")
        AB = C.sb(st, [128, nseq, 3, NKC], F32, "AB")
        ps_stat = C.ps(st, [128, TT], F32, "ps_stat")
        ps_g = [C.ps(st, [128, TT], F32, "ps_g%d" % i) for i in range(2)]
        ps_u = [C.ps(st, [128, TT], F32, "ps_u%d" % i) for i in range(2)]
        ps_o = [C.ps(st, [128, TT], F32, "ps_o%d" % i) for i in range(2)]

        mod_vectors(C, AB, modv, gains, sub, nseq, 0.5)

        def mk(wd, g):
            return lambda b: P.op("pool", lambda e: e.dma_start(out=b[:], in_=wd[l, g]), writes=[b], dma=True)
        l1 = [mk(w1d, g) for _ in range(ntile) for g in range(NG)]
        l3 = [mk(w3d, g) for _ in range(ntile) for g in range(NG)]
        l2 = [mk(w2d, m) for _ in range(ntile) for m in range(NKC)]
        s1, s3, s2 = Stream(w1b, l1), Stream(w3b, l3), Stream(w2b, l2)

        for i in range(ntile):
            b = i // (S // TT)
            P.op("sp", lambda e, i=i: e.dma_start(out=xs[:], in_=xview(src, i)),
                 reads=[(xres, i)], writes=[xs], dma=True)
            norm_mod(C, xs, hT, AB, b, sq, tmpf, ms, rstd, ps_stat)
            for g in range(NG):
                k = i * NG + g
                wa, wb_ = s1.get(k), s3.get(k)
                for fi in range(GW // 128):
                    f = g * (GW // 128) + fi
                    pg, pu = ps_g[f % 2], ps_u[f % 2]
                    for c in range(NKC):
                        mm(C, pg[:], wa[:, c, fi * 128:(fi + 1) * 128], hT[:, c, :], c == 0, c == NKC - 1,
                           [wa, (hT, c)], pg)
                    for c in range(NKC):
                        mm(C, pu[:], wb_[:, c, fi * 128:(fi + 1) * 128], hT[:, c, :], c == 0, c == NKC - 1,
                           [wb_, (hT, c)], pu)
                    s_ = sg[f % 3]
                    P.op("act", lambda e, s_=s_, pg=pg: e.activation(out=s_[:], in_=pg[:], func=AF.Silu),
                         reads=[pg], writes=[s_])
                    P.op("dve", lambda e, s_=s_, pu=pu, f=f: e.tensor_tensor(
                        out=actT[:, f, :], in0=pu[:], in1=s_[:], op=ALU.mult),
                        reads=[pu, s_], writes=[(actT, f)])
            for m in range(NKC):
                k = i * NKC + m
                w2 = s2.get(k)
                po = ps_o[m % 2]
                r = xr[m % 4]
                P.op("sp", lambda e, i=i, m=m, r=r: e.dma_start(out=r[:], in_=xview(src, i)[:, m, :]),
                     reads=[(xres, i)], writes=[r], dma=True)
                for f in range(NFC):
                    mm(C, po[:], w2[:, f, :], actT[:, f, :], f == 0, f == NFC - 1, [w2, (actT, f)], po)
                P.op("dve", lambda e, m=m, r=r, po=po, b=b: e.scalar_tensor_tensor(
                    out=r[:], in0=po[:], scalar=AB[:, b, 2, m:m + 1], in1=r[:], op0=ALU.mult, op1=ALU.add),
                    reads=[po, AB, r], writes=[r])
                P.op("sp", lambda e, i=i, m=m, r=r: e.dma_start(out=xview(dst, i)[:, m, :], in_=r[:]),
                     reads=[r], writes=[(xres, i)], dma=True)
    P.barrier()


def proj_phase(C, dr, l, nseq, ntile, modv, gains, xT, xres, xview, scr):
    P = C.P
    wd = dr["w_inT"]
    NGP = 28
    with ExitStack() as st:
        xs = C.sb(st, [128, NKC, TT], F32, "xs")
        hT = C.sb(st, [128, NKC, TT], BF16, "hT", n=NKC)
        wb = [C.sb(st, [128, NKC, GW], BF16, "pw%d" % i) for i in range(3)]
        wba = C.sb(st, [128, NKC, 16], BF16, "wba")
        sq = [C.sb(st, [128, TT], BF16, "sq%d" % i) for i in range(3)]
        tmpf = [C.sb(st, [128, TT], F32, "tmpf%d" % i) for i in range(3)]
        stg = [C.sb(st, [128, TT], F32, "stg%d" % i) for i in range(4)]
        bas = [C.sb(st, [128, 16], F32, "bas%d" % i) for i in range(2)]
        ms = C.sb(st, [128, TT], F32, "ms")
        rstd = C.sb(st, [128, TT], F32, "rstd")
        AB = C.sb(st, [128, nseq, 3, NKC], F32, "AB")
        ps_stat = C.ps(st, [128, TT], F32, "ps_stat")
        ps_p = [C.ps(st, [128, TT], F32, "ps_p%d" % i) for i in range(3)]
        ps_ba = C.ps(st, [128, TT], F32, "ps_ba")
        mod_vectors(C, AB, modv, gains, 1, nseq, 1.0)
        P.op("pool", lambda e: e.dma_start(out=wba[:], in_=dr["w_ba"][l]), writes=[wba], dma=True)
        loads = [(lambda b, g=g: P.op("pool", lambda e: e.dma_start(out=b[:], in_=wd[l, g]), writes=[b], dma=True))
                 for _ in range(ntile) for g in range(NGP)]
        strm = Stream(wb, loads)
        nev = 0
        for i in range(ntile):
            s, ti = divmod(i, S // TT)
            tsl = slice(ti * TT, (ti + 1) * TT)
            P.op("sp", lambda e, i=i: e.dma_start(out=xs[:], in_=xview(xT, i)),
                 reads=[(xres, i)], writes=[xs], dma=True)
            norm_mod(C, xs, hT, AB, s, sq, tmpf, ms, rstd, ps_stat)
            P.op("sp", lambda e, s=s, tsl=tsl: e.dma_start(
                out=scr["HT"][s].rearrange("(c p) t -> p c t", p=128)[:, :, tsl], in_=hT[:]),
                reads=[hT], writes=[(scr["HTb"], i)], dma=True)
            for ts in range(TT // 128):
                for c in range(NKC):
                    mm(C, ps_ba[:, ts * 16:(ts + 1) * 16], hT[:, c, ts * 128:(ts + 1) * 128], wba[:, c, :],
                       c == 0, c == NKC - 1, [(hT, c), wba], ps_ba)
            bb = bas[i % 2]
            for ts in range(TT // 128):
                pass
            P.op("dve", lambda e, bb=bb: e.tensor_copy(out=stg[3][:, 0:64], in_=ps_ba[:, 0:64]),
                 reads=[ps_ba], writes=[stg[3]])
            P.op("sp", lambda e, s=s, ti=ti: e.dma_start(
                out=scr["BA"][s, ti * TT:(ti + 1) * TT, :].rearrange("(a p) k -> p a k", p=128),
                in_=stg[3][:, 0:64].rearrange("p (a k) -> p a k", k=16)),
                reads=[stg[3]], writes=[(scr["BAb"], i)], dma=True)
            for g in range(NGP):
                w = strm.get(i * NGP + g)
                for fi in range(2):
                    ch = 2 * g + fi
                    pp = ps_p[ch % 3]
                    for c in range(NKC):
                        mm(C, pp[:], w[:, c, fi * 128:(fi + 1) * 128], hT[:, c, :], c == 0, c == NKC - 1,
                           [w, (hT, c)], pp)
                    sg_ = stg[nev % 3]
                    if nev % 2 == 0:
                        P.op("act", lambda e, sg_=sg_, pp=pp: e.activation(out=sg_[:], in_=pp[:], func=AF.Copy),
                             reads=[pp], writes=[sg_])
                    else:
                        P.op("dve", lambda e, sg_=sg_, pp=pp: e.tensor_copy(out=sg_[:], in_=pp[:]),
                             reads=[pp], writes=[sg_])
                    nev += 1
                    if ch < 32:
                        dst, row, key = scr["PA"], ch, "PAb"
                    elif ch < 38:
                        dst, row, key = scr["PB"], ch - 32, "PBb"
                    else:
                        dst, row, key = scr["PC"], ch - 38, "PCb"
                    P.op("sp", lambda e, dst=dst, row=row, s=s, tsl=tsl, sg_=sg_: e.dma_start(
                        out=dst[s, row * 128:(row + 1) * 128, tsl], in_=sg_[:]),
                        reads=[sg_], writes=[(scr[key], i)], dma=True)
    P.barrier()


def s5_phase(C, dr, l, nseq, scr):
    P = C.P
    NSC = 24
    TWO_PI = 2.0 * math.pi
    with ExitStack() as st:
        tmp = C.sb(st, [128, 128], F32, "s5tmp")
        pst = C.ps(st, [128, 512], F32, "s5pst")
        vec = C.sb(st, [128, 3, NSC], F32, "s5vec")
        dg = C.sb(st, [128, 18], F32, "s5dg")
        sm = C.sb(st, [128, 24, NSC], F32, "s5sm")
        halfpi = C.sb(st, [128, 1], F32, "halfpi")
        Bre = C.sb(st, [128, NSC, 128], F32, "Bre")
        Bim = C.sb(st, [128, NSC, 128], F32, "Bim")
        Atre = C.sb(st, [128, NSC * 128], F32, "Atre")
        Atim = C.sb(st, [128, NSC * 128], F32, "Atim")
        Bm = C.sb(st, [128, 6, 2, 512], BF16, "Bm")
        Cm = C.sb(st, [128, NSC, 2, 128], BF16, "Cm")
        gw = C.sb(st, [128, 6, 6, GW], BF16, "gluw")
        tri = C.sb(st, [128, 128], BF16, "tri")
        for j in range(3):
            load_vec_fm(C, st, vec, vec[:, j, :], dr["s5_vec"][l, j], NSC, tmp, pst, C.ident_f)
        load_vec_fm(C, st, dg, dg[:, :], dr["s5_dg"][l], 18, tmp, pst, C.ident_f)
        P.op("pool", lambda e: e.dma_start(out=Bm[:], in_=dr["s5_Bm"][l]), writes=[Bm], dma=True)
        P.op("pool", lambda e: e.dma_start(out=Cm[:], in_=dr["s5_Cm"][l]), writes=[Cm], dma=True)
        for g in range(6):
            P.op("pool", lambda e, g=g: e.dma_start(out=gw[:, g], in_=dr["s5_gluw"][l, g]), writes=[gw], dma=True)
        P.op("pool", lambda e: e.memset(halfpi[:], math.pi / 2), writes=[halfpi])
        P.op("pool", lambda e: e.memset(tri[:], 1.0), writes=[tri])
        P.op("pool", lambda e: e.affine_select(out=tri[:], in_=tri[:], pattern=[[1, 128]], compare_op=ALU.is_ge,
                                               fill=C.fill(0.0), base=0, channel_multiplier=-1),
             reads=[tri], writes=[tri])

        V = lambda k: sm[:, k, :]

        def tt(o, a, b, op, eng="dve"):
            P.op(eng, lambda e: e.tensor_tensor(out=o, in0=a, in1=b, op=op), reads=[sm, vec], writes=[sm])

        def ts(o, a, s1, op0, s2=None, op1=None):
            if op1 is None:
                P.op("dve", lambda e: e.tensor_scalar(out=o, in0=a, scalar1=s1, scalar2=None, op0=op0),
                     reads=[sm, vec], writes=[sm])
            else:
                P.op("dve", lambda e: e.tensor_scalar(out=o, in0=a, scalar1=s1, scalar2=s2, op0=op0, op1=op1),
                     reads=[sm, vec], writes=[sm])

        def act(o, a, f, scale=1.0, bias=None):
            if bias is None:
                P.op("act", lambda e: e.activation(out=o, in_=a, func=f, scale=scale), reads=[sm, vec], writes=[sm])
            else:
                P.op("act", lambda e: e.activation(out=o, in_=a, func=f, scale=scale, bias=bias),
                     reads=[sm, vec, halfpi], writes=[sm])
        STEP, ARE, RS, TH, THR, T0, RHO, SIN, COS, LR, LI, MR, MI, CR, CI, DEN, NR, T1, T2, PWR, PWI, T3 = range(22)
        act(V(STEP), vec[:, 2, :], AF.Exp)
        ts(V(ARE), vec[:, 0, :], -1e-4, ALU.min)
        tt(V(RS), V(ARE), V(STEP), ALU.mult)
        tt(V(TH), vec[:, 1, :], V(STEP), ALU.mult)
        ts(V(THR), V(TH), 1.0, ALU.mult)
        for m in range(1, 6):
            ts(V(T0), V(TH), (2 * m - 1) * math.pi, ALU.is_gt, TWO_PI, ALU.mult)
            tt(V(THR), V(THR), V(T0), ALU.subtract)
        act(V(RHO), V(RS), AF.Exp)
        act(V(SIN), V(THR), AF.Sin)
        ts(V(T0), V(THR), -1.0, ALU.mult)
        tt(V(T0), V(T0), V(THR), ALU.max)
        act(V(COS), V(T0), AF.Sin, scale=-1.0, bias=halfpi[:])
        tt(V(LR), V(RHO), V(COS), ALU.mult)
        tt(V(LI), V(RHO), V(SIN), ALU.mult)
        P.op("dve", lambda e: e.reciprocal(out=V(T1), in_=V(RHO)), reads=[sm], writes=[sm])
        tt(V(MR), V(COS), V(T1), ALU.mult)
        tt(V(MI), V(SIN), V(T1), ALU.mult)
        ts(V(MI), V(MI), -1.0, ALU.mult)
        ts(V(NR), V(LR), -1.0, ALU.add)
        tt(V(T1), V(ARE), V(ARE), ALU.mult)
        tt(V(T2), vec[:, 1, :], vec[:, 1, :], ALU.mult)
        tt(V(DEN), V(T1), V(T2), ALU.add)
        P.op("dve", lambda e: e.reciprocal(out=V(DEN), in_=V(DEN)), reads=[sm], writes=[sm])
        tt(V(T1), V(NR), V(ARE), ALU.mult)
        tt(V(T2), V(LI), vec[:, 1, :], ALU.mult)
        tt(V(T1), V(T1), V(T2), ALU.add)
        tt(V(CR), V(T1), V(DEN), ALU.mult)
        tt(V(T1), V(LI), V(ARE), ALU.mult)
        tt(V(T2), V(NR), vec[:, 1, :], ALU.mult)
        tt(V(T1), V(T1), V(T2), ALU.subtract)
        tt(V(CI), V(T1), V(DEN), ALU.mult)

        def pow_table(Tr, Ti, init, base, s2):
            ta = C.sb(s2, [128, NSC, 64], F32, "pt_a")
            tb = C.sb(s2, [128, NSC, 64], F32, "pt_b")
            if init is None:
                P.op("pool", lambda e: e.memset(Tr[:, :, 0:1], 1.0), writes=[Tr])
                P.op("pool", lambda e: e.memset(Ti[:, :, 0:1], 0.0), writes=[Ti])
            else:
                P.op("dve", lambda e: e.tensor_copy(out=Tr[:, :, 0], in_=V(init[0])), reads=[sm], writes=[Tr])
                P.op("dve", lambda e: e.tensor_copy(out=Ti[:, :, 0], in_=V(init[1])), reads=[sm], writes=[Ti])
            ts(V(PWR), V(base[0]), 1.0, ALU.mult)
            ts(V(PWI), V(base[1]), 1.0, ALU.mult)
            for k in range(7):
                w = 1 << k
                pr = V(PWR).unsqueeze(2).to_broadcast([128, NSC, w])
                pi = V(PWI).unsqueeze(2).to_broadcast([128, NSC, w])
                lo = slice(0, w)
                hi = slice(w, 2 * w)
                P.op("dve", lambda e, pr=pr, lo=lo, w=w: e.tensor_tensor(out=ta[:, :, 0:w], in0=Tr[:, :, lo], in1=pr, op=ALU.mult),
                     reads=[Tr, sm], writes=[ta])
                P.op("dve", lambda e, pi=pi, lo=lo, w=w: e.tensor_tensor(out=tb[:, :, 0:w], in0=Ti[:, :, lo], in1=pi, op=ALU.mult),
                     reads=[Ti, sm], writes=[tb])
                P.op("dve", lambda e, hi=hi, w=w: e.tensor_tensor(out=Tr[:, :, hi], in0=ta[:, :, 0:w], in1=tb[:, :, 0:w], op=ALU.subtract),
                     reads=[ta, tb], writes=[Tr])
                P.op("dve", lambda e, pi=pi, lo=lo, w=w: e.tensor_tensor(out=ta[:, :, 0:w], in0=Tr[:, :, lo], in1=pi, op=ALU.mult),
                     reads=[Tr, sm], writes=[ta])
                P.op("dve", lambda e, pr=pr, lo=lo, w=w: e.tensor_tensor(out=tb[:, :, 0:w], in0=Ti[:, :, lo], in1=pr, op=ALU.mult),
                     reads=[Ti, sm], writes=[tb])
                P.op("dve", lambda e, hi=hi, w=w: e.tensor_tensor(out=Ti[:, :, hi], in0=ta[:, :, 0:w], in1=tb[:, :, 0:w], op=ALU.add),
                     reads=[ta, tb], writes=[Ti])
                tt(V(T1), V(PWR), V(PWR), ALU.mult)
                tt(V(T2), V(PWI), V(PWI), ALU.mult)
                tt(V(T3), V(PWR), V(PWI), ALU.mult)
                tt(V(PWR), V(T1), V(T2), ALU.subtract)
                ts(V(PWI), V(T3), 2.0, ALU.mult)

        with ExitStack() as s2:
            pow_table(Bre, Bim, None, (LR, LI), s2)
        with ExitStack() as s2:
            Asr = C.sb(s2, [128, NSC, 128], F32, "Asr")
            Asi = C.sb(s2, [128, NSC, 128], F32, "Asi")
            pow_table(Asr, Asi, (CR, CI), (MR, MI), s2)
            for sc in range(NSC):
                for (Tsm, Tt) in ((Asr, Atre), (Asi, Atim)):
                    P.op("pe", lambda e, Tsm=Tsm, sc=sc: e.transpose(pst[:, 0:128], Tsm[:, sc, :], C.ident_f[:]),
                         reads=[Tsm, C.ident_f], writes=[pst])
                    P.op("dve", lambda e, Tt=Tt, sc=sc: e.tensor_copy(out=Tt[:, sc * 128:(sc + 1) * 128], in_=pst[:, 0:128]),
                         reads=[pst], writes=[Tt])
            P.barrier()

        with ExitStack() as s3:
            uT = C.sb(s3, [128, 6, S], F32, "uT")
            uTb = C.sb(s3, [128, 6, S], BF16, "uTb")
            hcr = C.sb(s3, [128, NSC], F32, "hcr")
            hci = C.sb(s3, [128, NSC], F32, "hci")
            zre = [C.sb(s3, [128, 512], BF16, "zre%d" % i) for i in range(2)]
            zim = [C.sb(s3, [128, 512], BF16, "zim%d" % i) for i in range(2)]
            tf = [C.sb(s3, [128, 512], F32, "s5t%d" % i) for i in range(8)]
            srb = C.sb(s3, [128, 4, 128], BF16, "srb")
            sib = C.sb(s3, [128, 4, 128], BF16, "sib")
            hs = C.sb(s3, [128, 6, 4], F32, "hs")
            ystg = [C.sb(s3, [128, 512], BF16, "ystg%d" % i) for i in range(2)]
            ps_br = C.ps(s3, [128, 512], F32, "ps_br")
            ps_bi = C.ps(s3, [128, 512], F32, "ps_bi")
            ps_cr = C.ps(s3, [128, 512], F32, "ps_cr")
            ps_ci = C.ps(s3, [128, 512], F32, "ps_ci")
            ps_y = C.ps(s3, [128, 512], F32, "ps_y")
            ps_v = C.ps(s3, [128, 512], F32, "ps_v")
            ps_g = C.ps(s3, [128, 512], F32, "ps_g")
            v3 = lambda t: t[:].rearrange("p (a j) -> p a j", j=128)
            for s in range(nseq):
                ub = scr["PB"][s].rearrange("(c p) t -> p c t", p=128)
                P.op("sp", lambda e, ub=ub: e.dma_start(out=uT[:], in_=ub), reads=[scr["PBb"]], writes=[uT], dma=True)
                P.op("pool", lambda e, ub=ub: e.dma_start(out=uTb[:], in_=ub), reads=[scr["PBb"]], writes=[uTb], dma=True)
                P.op("pool", lambda e: e.memset(hcr[:], 0.0), writes=[hcr])
                P.op("pool", lambda e: e.memset(hci[:], 0.0), writes=[hci])
                nz = 0
                for tc in range(S // 128):
                    tsl = slice(tc * 128, (tc + 1) * 128)
                    for kc in range(6):
                        sc0 = 4 * kc
                        csl = slice(sc0 * 128, (sc0 + 4) * 128)
                        mm(C, ps_br[:], uTb[:, kc, tsl], Bm[:, kc, 0, :], True, True, [uTb, Bm], ps_br)
                        mm(C, ps_bi[:], uTb[:, kc, tsl], Bm[:, kc, 1, :], True, True, [uTb, Bm], ps_bi)
                        zr, zi = zre[nz % 2], zim[nz % 2]
                        nz += 1
                        P.op("dve", lambda e, csl=csl: e.tensor_tensor(out=tf[0][:], in0=ps_br[:], in1=Atre[:, csl], op=ALU.mult),
                             reads=[ps_br, Atre], writes=[tf[0]])
                        P.op("dve", lambda e, csl=csl: e.tensor_tensor(out=tf[1][:], in0=ps_bi[:], in1=Atim[:, csl], op=ALU.mult),
                             reads=[ps_bi, Atim], writes=[tf[1]])
                        P.op("pool", lambda e, zr=zr: e.tensor_tensor(out=zr[:], in0=tf[0][:], in1=tf[1][:], op=ALU.subtract),
                             reads=[tf[0], tf[1]], writes=[zr])
                        P.op("dve", lambda e, csl=csl: e.tensor_tensor(out=tf[2][:], in0=ps_bi[:], in1=Atre[:, csl], op=ALU.mult),
                             reads=[ps_bi, Atre], writes=[tf[2]])
                        P.op("dve", lambda e, csl=csl: e.tensor_tensor(out=tf[3][:], in0=ps_br[:], in1=Atim[:, csl], op=ALU.mult),
                             reads=[ps_br, Atim], writes=[tf[3]])
                        P.op("pool", lambda e, zi=zi: e.tensor_tensor(out=zi[:], in0=tf[2][:], in1=tf[3][:], op=ALU.add),
                             reads=[tf[2], tf[3]], writes=[zi])
                        for scl in range(4):
                            mm(C, ps_cr[:, scl * 128:(scl + 1) * 128], zr[:, scl * 128:(scl + 1) * 128], tri[:],
                               True, True, [zr, tri], ps_cr)
                        for scl in range(4):
                            mm(C, ps_ci[:, scl * 128:(scl + 1) * 128], zi[:, scl * 128:(scl + 1) * 128], tri[:],
                               True, True, [zi, tri], ps_ci)
                        hr_b = hcr[:, sc0:sc0 + 4].unsqueeze(2).to_broadcast([128, 4, 128])
                        hi_b = hci[:, sc0:sc0 + 4].unsqueeze(2).to_broadcast([128, 4, 128])
                        P.op("dve", lambda e, hr_b=hr_b: e.tensor_tensor(out=v3(tf[4]), in0=v3(ps_cr), in1=hr_b, op=ALU.add),
                             reads=[ps_cr, hcr], writes=[tf[4]])
                        P.op("dve", lambda e, hi_b=hi_b: e.tensor_tensor(out=v3(tf[5]), in0=v3(ps_ci), in1=hi_b, op=ALU.add),
                             reads=[ps_ci, hci], writes=[tf[5]])
                        bsl = slice(sc0, sc0 + 4)
                        P.op("dve", lambda e, bsl=bsl: e.tensor_tensor(out=v3(tf[0]), in0=v3(tf[4]), in1=Bre[:, bsl, :], op=ALU.mult),
                             reads=[tf[4], Bre], writes=[tf[0]])
                        P.op("pool", lambda e, bsl=bsl: e.tensor_tensor(out=v3(tf[1]), in0=v3(tf[5]), in1=Bim[:, bsl, :], op=ALU.mult),
                             reads=[tf[5], Bim], writes=[tf[1]])
                        P.op("dve", lambda e: e.tensor_tensor(out=tf[6][:], in0=tf[0][:], in1=tf[1][:], op=ALU.subtract),
                             reads=[tf[0], tf[1]], writes=[tf[6]])
                        P.op("pool", lambda e, bsl=bsl: e.tensor_tensor(out=v3(tf[2]), in0=v3(tf[5]), in1=Bre[:, bsl, :], op=ALU.mult),
                             reads=[tf[5], Bre], writes=[tf[2]])
                        P.op("dve", lambda e, bsl=bsl: e.tensor_tensor(out=v3(tf[3]), in0=v3(tf[4]), in1=Bim[:, bsl, :], op=ALU.mult),
                             reads=[tf[4], Bim], writes=[tf[3]])
                        P.op("pool", lambda e: e.tensor_tensor(out=tf[7][:], in0=tf[2][:], in1=tf[3][:], op=ALU.add),
                             reads=[tf[2], tf[3]], writes=[tf[7]])
                        P.op("act", lambda e: e.activation(out=srb[:], in_=v3(tf[6]), func=AF.Copy), reads=[tf[6]], writes=[srb])
                        P.op("act", lambda e: e.activation(out=sib[:], in_=v3(tf[7]), func=AF.Copy, scale=-1.0),
                             reads=[tf[7]], writes=[sib])
                        sl_r = v3(tf[6])[:, :, 127]
                        sl_i = v3(tf[7])[:, :, 127]
                        P.op("dve", lambda e, kc=kc, bsl=bsl, sl_r=sl_r: e.tensor_tensor(out=hs[:, 0, :], in0=sm[:, LR, bsl], in1=sl_r, op=ALU.mult),
                             reads=[tf[6], sm], writes=[hs])
                        P.op("dve", lambda e, bsl=bsl, sl_i=sl_i: e.tensor_tensor(out=hs[:, 1, :], in0=sm[:, LI, bsl], in1=sl_i, op=ALU.mult),
                             reads=[tf[7], sm], writes=[hs])
                        P.op("dve", lambda e, bsl=bsl: e.tensor_tensor(out=hcr[:, bsl], in0=hs[:, 0, :], in1=hs[:, 1, :], op=ALU.subtract),
                             reads=[hs], writes=[hcr])
                        P.op("dve", lambda e, bsl=bsl, sl_i=sl_i: e.tensor_tensor(out=hs[:, 2, :], in0=sm[:, LR, bsl], in1=sl_i, op=ALU.mult),
                             reads=[tf[7], sm], writes=[hs])
                        P.op("dve", lambda e, bsl=bsl, sl_r=sl_r: e.tensor_tensor(out=hs[:, 3, :], in0=sm[:, LI, bsl], in1=sl_r, op=ALU.mult),
                             reads=[tf[6], sm], writes=[hs])
                        P.op("dve", lambda e, bsl=bsl: e.tensor_tensor(out=hci[:, bsl], in0=hs[:, 2, :], in1=hs[:, 3, :], op=ALU.add),
                             reads=[hs], writes=[hci])
                        for scl in range(4):
                            mm(C, ps_y[:, 0:128], Cm[:, sc0 + scl, 0, :], srb[:, scl, :], scl == 0, False, [Cm, srb], ps_y)
                        for scl in range(4):
                            mm(C, ps_y[:, 0:128], Cm[:, sc0 + scl, 1, :], sib[:, scl, :], False, scl == 3, [Cm, sib], ps_y)
                        P.op("dve", lambda e, kc=kc, tsl=tsl: e.scalar_tensor_tensor(
                            out=uT[:, kc, tsl], in0=uT[:, kc, tsl], scalar=dg[:, kc:kc + 1], in1=ps_y[:, 0:128],
                            op0=ALU.mult, op1=ALU.add), reads=[uT, dg, ps_y], writes=[uT])
                for kc in range(6):
                    for q in range(S // 512):
                        qs = slice(q * 512, (q + 1) * 512)
                        yv = uT[:, kc, qs]
                        P.op("pool", lambda e, yv=yv: e.tensor_tensor(out=tf[0][:], in0=yv, in1=yv, op=ALU.mult),
                             reads=[uT], writes=[tf[0]])
                        P.op("dve", lambda e: e.tensor_scalar(out=tf[1][:], in0=tf[0][:], scalar1=0.044715, scalar2=1.0,
                                                              op0=ALU.mult, op1=ALU.add), reads=[tf[0]], writes=[tf[1]])
                        P.op("pool", lambda e, yv=yv: e.tensor_tensor(out=tf[2][:], in0=tf[1][:], in1=yv, op=ALU.mult),
                             reads=[tf[1], uT], writes=[tf[2]])
                        P.op("act", lambda e: e.activation(out=tf[3][:], in_=tf[2][:], func=AF.Sigmoid, scale=1.5957691216057308),
                             reads=[tf[2]], writes=[tf[3]])
                        P.op("dve", lambda e, yv=yv, kc=kc, qs=qs: e.tensor_tensor(out=uTb[:, kc, qs], in0=yv, in1=tf[3][:], op=ALU.mult),
                             reads=[uT, tf[3]], writes=[uTb])
                ne = 0
                for q in range(S // 512):
                    qs = slice(q * 512, (q + 1) * 512)
                    for oc in range(6):
                        for kc in range(6):
                            mm(C, ps_v[:], gw[:, oc // 2, kc, (oc % 2) * 128:(oc % 2) * 128 + 128], uTb[:, kc, qs],
                               kc == 0, kc == 5, [gw, uTb], ps_v)
                        for kc in range(6):
                            mm(C, ps_g[:], gw[:, 3 + oc // 2, kc, (oc % 2) * 128:(oc % 2) * 128 + 128], uTb[:, kc, qs],
                               kc == 0, kc == 5, [gw, uTb], ps_g)
                        P.op("act", lambda e, oc=oc: e.activation(out=tf[4][:], in_=ps_g[:], func=AF.Sigmoid,
                                                                  bias=dg[:, 12 + oc:13 + oc], scale=1.0),
                             reads=[ps_g, dg], writes=[tf[4]])
                        ys = ystg[ne % 2]
                        ne += 1
                        P.op("dve", lambda e, oc=oc, ys=ys: e.scalar_tensor_tensor(
                            out=ys[:], in0=ps_v[:], scalar=dg[:, 6 + oc:7 + oc], in1=tf[4][:], op0=ALU.add, op1=ALU.mult),
                            reads=[ps_v, dg, tf[4]], writes=[ys])
                        P.op("sp", lambda e, oc=oc, ys=ys, s=s, qs=qs: e.dma_start(
                            out=scr["YB"][s, oc * 128:(oc + 1) * 128, qs], in_=ys[:]),
                            reads=[ys], writes=[scr["YBb"]], dma=True)
    P.barrier()


def merge_phase(C, dr, l, nseq, ntile, modv, gains, xT, xres, xview, scr):
    P = C.P
    wd = dr["w_inT"]
    with ExitStack() as st:
        hT = [C.sb(st, [128, NKC, TT], BF16, "mhT%d" % i) for i in range(2)]
        yy = [C.sb(st, [128, NKC, TT], BF16, "myy%d" % i) for i in range(2)]
        mg_ = C.sb(st, [128, NKC, TT], BF16, "merged", n=NKC)
        gwb = [C.sb(st, [128, NKC, GW], BF16, "mgw%d" % i) for i in range(6)]
        bwb = [C.sb(st, [128, NKC, GW], BF16, "mbw%d" % i) for i in range(2)]
        owb = [C.sb(st, [128, NKC, GW], BF16, "mow%d" % i) for i in range(2)]
        sgb = [C.sb(st, [128, TT], F32, "msg%d" % i) for i in range(3)]
        tb = [C.sb(st, [128, TT], F32, "mtb%d" % i) for i in range(3)]
        acc = [C.sb(st, [128, TT], F32, "macc%d" % i) for i in range(2)]
        xr = [C.sb(st, [128, TT], F32, "mxr%d" % i) for i in range(4)]
        AB = C.sb(st, [128, nseq, 3, NKC], F32, "mAB")
        ps_gt = [C.ps(st, [128, TT], F32, "ps_gt%d" % i) for i in range(2)]
        ps_b = [C.ps(st, [128, TT], F32, "ps_b%d" % i) for i in range(2)]
        ps_o = [C.ps(st, [128, TT], F32, "ps_mo%d" % i) for i in range(2)]
        mod_vectors(C, AB, modv, gains, 1, nseq, 1.0)
        lg = [(lambda b, g=28 + j * 8 + m: P.op("pool", lambda e: e.dma_start(out=b[:], in_=wd[l, g]), writes=[b], dma=True))
              for _ in range(ntile) for m in range(8) for j in range(3)]
        lb = [(lambda b, m=m: P.op("pool", lambda e: e.dma_start(out=b[:], in_=dr["w_br"][l, m]), writes=[b], dma=True))
              for _ in range(ntile) for m in range(8)]
        lo = [(lambda b, m=m: P.op("pool", lambda e: e.dma_start(out=b[:], in_=dr["w_outT"][l, m]), writes=[b], dma=True))
              for _ in range(ntile) for m in range(8)]
        sg_, sb_, so_ = Stream(gwb, lg, depth=4), Stream(bwb, lb), Stream(owb, lo)
        KOFF = (0, 8, 14)
        KN = (8, 6, 2)

        def load_acts(i):
            s, ti = divmod(i, S // TT)
            tsl = slice(ti * TT, (ti + 1) * TT)
            h, y = hT[i % 2], yy[i % 2]
            P.op("sp", lambda e: e.dma_start(out=h[:], in_=scr["HT"][s].rearrange("(c p) t -> p c t", p=128)[:, :, tsl]),
                 reads=[(scr["HTb"], i)], writes=[h], dma=True)
            P.op("sp", lambda e: e.dma_start(out=y[:, 0:8, :], in_=scr["YA"][s].rearrange("(c p) t -> p c t", p=128)[:, :, tsl]),
                 reads=[scr["YAb"]], writes=[y], dma=True)
            P.op("sp", lambda e: e.dma_start(out=y[:, 8:14, :], in_=scr["YB"][s].rearrange("(c p) t -> p c t", p=128)[:, :, tsl]),
                 reads=[scr["YBb"]], writes=[y], dma=True)
            P.op("sp", lambda e: e.dma_start(out=y[:, 14:16, :], in_=scr["YC"][s].rearrange("(c p) t -> p c t", p=128)[:, :, tsl]),
                 reads=[scr["YCb"]], writes=[y], dma=True)

        load_acts(0)
        for i in range(ntile):
            b = i // (S // TT)
            if i + 1 < ntile:
                load_acts(i + 1)
            h, y = hT[i % 2], yy[i % 2]
            for m8 in range(8):
                gws = [sg_.get((i * 8 + m8) * 3 + j) for j in range(3)]
                bw = sb_.get(i * 8 + m8)
                for mi in range(2):
                    m = 2 * m8 + mi
                    csl = slice(mi * 128, (mi + 1) * 128)
                    for j in range(3):
                        pg, pb = ps_gt[(3 * m + j) % 2], ps_b[(3 * m + j) % 2]
                        for c in range(NKC):
                            mm(C, pg[:], gws[j][:, c, csl], h[:, c, :], c == 0, c == NKC - 1, [gws[j], h], pg)
                        for kc in range(KN[j]):
                            mm(C, pb[:], bw[:, KOFF[j] + kc, csl], y[:, KOFF[j] + kc, :], kc == 0, kc == KN[j] - 1,
                               [bw, y], pb)
                        sgt, tj = sgb[j], tb[j]
                        P.op("act", lambda e, sgt=sgt, pg=pg: e.activation(out=sgt[:], in_=pg[:], func=AF.Sigmoid),
                             reads=[pg], writes=[sgt])
                        P.op("dve", lambda e, sgt=sgt, tj=tj, pb=pb: e.tensor_tensor(out=tj[:], in0=pb[:], in1=sgt[:], op=ALU.mult),
                             reads=[pb, sgt], writes=[tj])
                    a_ = acc[m % 2]
                    P.op("pool", lambda e, a_=a_: e.tensor_tensor(out=a_[:], in0=tb[0][:], in1=tb[1][:], op=ALU.add),
                         reads=[tb[0], tb[1]], writes=[a_])
                    P.op("pool", lambda e, a_=a_, m=m: e.tensor_tensor(out=mg_[:, m, :], in0=a_[:], in1=tb[2][:], op=ALU.add),
                         reads=[a_, tb[2]], writes=[(mg_, m)])
            for m8 in range(8):
                ow = so_.get(i * 8 + m8)
                for mi in range(2):
                    m = 2 * m8 + mi
                    po = ps_o[m % 2]
                    r = xr[m % 4]
                    P.op("sp", lambda e, i=i, m=m, r=r: e.dma_start(out=r[:], in_=xview(xT, i)[:, m, :]),
                         reads=[(xres, i)], writes=[r], dma=True)
                    for c in range(NKC):
                        mm(C, po[:], ow[:, c, mi * 128:(mi + 1) * 128], mg_[:, c, :], c == 0, c == NKC - 1,
                           [ow, (mg_, c)], po)
                    P.op("dve", lambda e, m=m, r=r, po=po, b=b: e.scalar_tensor_tensor(
                        out=r[:], in0=po[:], scalar=AB[:, b, 2, m:m + 1], in1=r[:], op0=ALU.mult, op1=ALU.add),
                        reads=[po, AB, r], writes=[r])
                    P.op("sp", lambda e, i=i, m=m, r=r: e.dma_start(out=xview(xT, i)[:, m, :], in_=r[:]),
                         reads=[r], writes=[(xres, i)], dma=True)
    P.barrier()


def dil_phase(C, dr, l, nseq, scr):
    P = C.P
    DILS = (1, 4, 16)
    NEG = -30000.0
    with ExitStack() as st:
        tmp = C.sb(st, [128, 128], F32, "dtmp")
        pst = C.ps(st, [128, 512], F32, "dpst")
        gv = C.sb(st, [128, 2], F32, "dgv")
        bones = C.sb(st, [128, 128], BF16, "bones")
        ones64 = C.sb(st, [128, 64], BF16, "ones64")
        d0i = C.sb(st, [128, 256], mybir.dt.int32, "d0i")
        d0 = C.sb(st, [128, 256], F32, "d0")
        bias = C.sb(st, [128, 12, 256], F32, "dbias")
        qraw = C.sb(st, [128, 2, S], F32, "qraw")
        kraw = C.sb(st, [128, 2, S], F32, "kraw")
        qn = C.sb(st, [128, 2, S], BF16, "qn")
        kn = C.sb(st, [128, 2, S], BF16, "kn")
        vb = C.sb(st, [128, 2, S], BF16, "vb")
        vtok = C.sb(st, [128, 2, 16, 128], BF16, "vtok")
        acc = C.sb(st, [64, 2, 4, S], F32, "dacc")
        sq = [C.sb(st, [128, 512], BF16, "dsq%d" % i) for i in range(2)]
        msb = [C.sb(st, [128, 512], F32, "dms%d" % i) for i in range(2)]
        rsb = [C.sb(st, [128, 512], F32, "drs%d" % i) for i in range(2)]
        stmp = [C.sb(st, [128, 256], F32, "dst%d" % i) for i in range(2)]
        pT = [C.sb(st, [128, 256], BF16, "dpT%d" % i) for i in range(3)]
        rec = C.sb(st, [64, S], F32, "drec")
        ob = C.sb(st, [64, S], BF16, "dob")
        ps_n = C.ps(st, [128, 512], F32, "ps_dn")
        ps_s = [C.ps(st, [128, 512], F32, "ps_ds%d" % i) for i in range(2)]
        ps_nd = [C.ps(st, [128, 512], F32, "ps_dnd%d" % i) for i in range(2)]
        ps_vt = C.ps(st, [128, 1024], BF16, "ps_dvt")

        load_vec_fm(C, st, gv, gv[:, :], dr["dil_g"][l], 2, tmp, pst, C.ident_f)
        P.op("dve", lambda e: e.tensor_scalar(out=gv[:, 0:1], in0=gv[:, 0:1], scalar1=0.125, scalar2=None, op0=ALU.mult),
             reads=[gv], writes=[gv])
        P.op("pool", lambda e: e.memset(bones[:], 0.0), writes=[bones])
        P.op("pool", lambda e: e.memset(bones[0:64, 0:64], 1.0), writes=[bones])
        P.op("pool", lambda e: e.memset(bones[64:128, 64:128], 1.0), writes=[bones])
        P.op("pool", lambda e: e.memset(ones64[:], 1.0), writes=[ones64])
        P.op("pool", lambda e: e.iota(d0i[:], pattern=[[1, 256]], base=0, channel_multiplier=-1), writes=[d0i])
        P.op("dve", lambda e: e.tensor_copy(out=d0[:], in_=d0i[:]), reads=[d0i], writes=[d0])
        for hg in range(12):
            slope = 2.0 ** (-8.0 * (hg + 1) / 12.0)
            dil = DILS[hg // 4]
            P.op("dve", lambda e, hg=hg, v=-slope * dil: e.tensor_scalar(out=bias[:, hg, :], in0=d0[:], scalar1=v, scalar2=None,
                                                                         op0=ALU.mult), reads=[d0], writes=[bias])
            P.op("pool", lambda e, hg=hg: e.affine_select(out=bias[:, hg, :], in_=bias[:, hg, :], pattern=[[1, 256]],
                                                          compare_op=ALU.is_ge, fill=C.fill(NEG), base=0, channel_multiplier=-1),
                 reads=[bias], writes=[bias])
            P.op("pool", lambda e, hg=hg: e.affine_select(out=bias[:, hg, :], in_=bias[:, hg, :], pattern=[[-1, 256]],
                                                          compare_op=ALU.is_ge, fill=C.fill(NEG), base=128, channel_multiplier=1),
                 reads=[bias], writes=[bias])

        nsc = 0
        for s in range(nseq):
            pc = scr["PC"][s]
            for gi in range(3):
                dil = DILS[gi]
                nb = S // dil // 128
                for (dst, r0, eng) in ((qraw, 0, "sp"), (kraw, 768, "sp")):
                    P.op(eng, lambda e, dst=dst, r0=r0: e.dma_start(
                        out=dst[:], in_=pc[r0 + gi * 256:r0 + gi * 256 + 256, :].rearrange("(c p) t -> p c t", p=128)),
                        reads=[scr["PCb"]], writes=[dst], dma=True)
                P.op("pool", lambda e: e.dma_start(
                    out=vb[:], in_=pc[1536 + gi * 256:1536 + gi * 256 + 256, :].rearrange("(c p) t -> p c t", p=128)),
                    reads=[scr["PCb"]], writes=[vb], dma=True)
                k2 = 0
                for (raw, nrm, gcol) in ((qraw, qn, 0), (kraw, kn, 1)):
                    for c2 in range(2):
                        for q4 in range(S // 512):
                            qs = slice(q4 * 512, (q4 + 1) * 512)
                            sq_, ms_, rs_ = sq[k2 % 2], msb[k2 % 2], rsb[k2 % 2]
                            k2 += 1
                            P.op("act", lambda e, raw=raw, c2=c2, qs=qs, sq_=sq_: e.activation(out=sq_[:], in_=raw[:, c2, qs], func=AF.Square),
                                 reads=[raw], writes=[sq_])
                            mm(C, ps_n[:], bones[:], sq_[:], True, True, [bones, sq_], ps_n)
                            P.op("act", lambda e, ms_=ms_: e.activation(out=ms_[:], in_=ps_n[:], func=AF.Sqrt, bias=C.epsc[:, 0:1],
                                                                        scale=1.0 / 64), reads=[ps_n, C.epsc], writes=[ms_])
                            P.op("dve", lambda e, ms_=ms_, rs_=rs_: e.reciprocal(out=rs_[:], in_=ms_[:]), reads=[ms_], writes=[rs_])
                            P.op("dve", lambda e, raw=raw, nrm=nrm, c2=c2, qs=qs, rs_=rs_, gcol=gcol: e.scalar_tensor_tensor(
                                out=nrm[:, c2, qs], in0=raw[:, c2, qs], scalar=gv[:, gcol:gcol + 1], in1=rs_[:],
                                op0=ALU.mult, op1=ALU.mult), reads=[raw, gv, rs_], writes=[nrm])
                for c2 in range(2):
                    for r in range(dil):
                        for n in range(nb):
                            bi = r * nb + n
                            ks = slice(r + dil * 128 * n, r + dil * 128 * n + dil * 127 + 1, dil)
                            P.op("pe", lambda e, c2=c2, ks=ks, bi=bi: e.transpose(ps_vt[:, (bi % 8) * 128:(bi % 8 + 1) * 128], vb[:, c2, ks], C.ident_bf[:]),
                                 reads=[vb, C.ident_bf], writes=[ps_vt])
                            P.op("act", lambda e, c2=c2, bi=bi: e.activation(out=vtok[:, c2, bi, :], in_=ps_vt[:, (bi % 8) * 128:(bi % 8 + 1) * 128], func=AF.Copy),
                                 reads=[ps_vt], writes=[vtok])
                for hh in range(4):
                    c2, pb = hh // 2, (hh % 2) * 64
                    hg = gi * 4 + hh
                    for r in range(dil):
                        prev = None
                        for n in range(nb):
                            bi = r * nb + n
                            nq = 256 if n + 1 < nb else 128
                            t0 = r + dil * 128 * n
                            ks = slice(t0, t0 + dil * 127 + 1, dil)
                            qsl = slice(t0, t0 + dil * (nq - 1) + 1, dil)
                            pss = ps_s[nsc % 2]
                            stp = stmp[nsc % 2]
                            cur = pT[nsc % 3]
                            psnd = ps_nd[nsc % 2]
                            nsc += 1
                            mm(C, pss[:, 0:nq], kn[pb:pb + 64, c2, ks], qn[pb:pb + 64, c2, qsl], True, True, [kn, qn], pss)
                            P.op("dve", lambda e, pss=pss, stp=stp, nq=nq, hg=hg: e.tensor_tensor(
                                out=stp[:, 0:nq], in0=pss[:, 0:nq], in1=bias[:, hg, 0:nq], op=ALU.add),
                                reads=[pss, bias], writes=[stp])
                            P.op("act", lambda e, stp=stp, cur=cur, nq=nq: e.activation(out=cur[:, 0:nq], in_=stp[:, 0:nq], func=AF.Exp),
                                 reads=[stp], writes=[cur])
                            for di, lhs_of in enumerate((lambda b_: vtok[:, c2, b_, pb:pb + 64], lambda b_: ones64[:])):
                                o_ap = psnd[0:64, di * 128:(di + 1) * 128]
                                if prev is not None:
                                    mm(C, o_ap, lhs_of(bi - 1), prev[:, 128:256], True, False, [vtok, ones64, prev], psnd)
                                mm(C, o_ap, lhs_of(bi), cur[:, 0:128], prev is None, True, [vtok, ones64, cur], psnd)
                            a_view = acc[:, :, hh, ks]
                            p_view = psnd[0:64, 0:256].rearrange("p (a j) -> p a j", j=128)
                            if gi == 0:
                                P.op("dve", lambda e, a_view=a_view, p_view=p_view: e.tensor_copy(out=a_view, in_=p_view),
                                     reads=[psnd], writes=[acc])
                            else:
                                P.op("dve", lambda e, a_view=a_view, p_view=p_view: e.tensor_tensor(
                                    out=a_view, in0=p_view, in1=a_view, op=ALU.add), reads=[psnd, acc], writes=[acc])
                            prev = cur
            for hh in range(4):
                P.op("dve", lambda e, hh=hh: e.reciprocal(out=rec[:], in_=acc[:, 1, hh, :]), reads=[acc], writes=[rec])
                P.op("dve", lambda e, hh=hh: e.tensor_tensor(out=ob[:], in0=acc[:, 0, hh, :], in1=rec[:], op=ALU.mult),
                     reads=[acc, rec], writes=[ob])
                P.op("sp", lambda e, hh=hh, s=s: e.dma_start(out=scr["YC"][s, hh * 64:(hh + 1) * 64, :], in_=ob[:]),
                     reads=[ob], writes=[scr["YCb"]], dma=True)
    P.barrier()


def gdn_phase(C, dr, l, nseq, scr):
    P = C.P
    NT = S // 128
    NEG = -30000.0
    with ExitStack() as st:
        cw = C.sb(st, [128, 24, 4], F32, "cw")
        ad = C.sb(st, [128, 16], F32, "gad")
        onm = C.sb(st, [128, 1], F32, "gon")
        one1 = C.sb(st, [128, 1], F32, "one1")
        triF = C.sb(st, [128, 128], F32, "triF")
        onesF = C.sb(st, [128, 128], F32, "onesF")
        mneg = C.sb(st, [128, 128], F32, "mneg")
        smask = C.sb(st, [128, 128], F32, "smask")
        ba = C.sb(st, [128, NT, 16], F32, "gba")
        sc = C.sb(st, [128, 10, NT, 8], F32, "gsc")
        BETA, G, GC, GL, EGC, NGC, EGL, KDEC, NEGC, NBETA = range(10)
        raw = [C.sb(st, [128, S + 3], F32, "graw%d" % i) for i in range(2)]
        cacc = [C.sb(st, [128, S], F32, "gcacc%d" % i) for i in range(2)]
        qT = C.sb(st, [128, S], BF16, "gqT")
        kT = C.sb(st, [128, S], BF16, "gkT")
        vTb = C.sb(st, [128, S], BF16, "gvTb")
        kd = C.sb(st, [128, NT, 128], BF16, "gkd", n=NT)
        vtok = C.sb(st, [128, NT, 128], BF16, "gvtok", n=NT)
        siluz = C.sb(st, [128, S], F32, "gsz")
        DT = C.sb(st, [128, NT, 128], F32, "gDT", n=NT)
        DTs = C.sb(st, [128, NT, 128], F32, "gDTs", n=NT)
        Gb = [C.sb(st, [128, 128], F32, "gGb%d" % i) for i in range(2)]
        Pp = [C.sb(st, [128, NT, 256], BF16, "gP%d" % i, n=NT) for i in range(2)]
        RRT = C.sb(st, [128, NT, 256], BF16, "gRRT", n=NT)
        xob = [C.sb(st, [128, 256], BF16, "gxo%d" % i) for i in range(2)]
        msk = C.sb(st, [128, 7, 2, 128], BF16, "gmsk")
        attnT = C.sb(st, [128, NT, 128], BF16, "gattn", n=NT)
        hS = C.sb(st, [128, 128], F32, "ghS")
        hSb = C.sb(st, [128, 128], BF16, "ghSb")
        rb = [C.sb(st, [128, 128], BF16, "grb%d" % i) for i in range(2)]
        vn = [C.sb(st, [128, 128], BF16, "gvn%d" % i) for i in range(2)]
        o1 = [C.sb(st, [128, 128], F32, "go1%d" % i) for i in range(2)]
        of = [C.sb(st, [128, 128], F32, "gof%d" % i) for i in range(2)]
        osq = [C.sb(st, [128, 128], F32, "gosq%d" % i) for i in range(2)]
        onb = [C.sb(st, [128, 128], BF16, "gonb%d" % i) for i in range(2)]
        ssq = [C.sb(st, [128, 4], F32, "gssq%d" % i) for i in range(2)]
        yst = C.sb(st, [128, S], BF16, "gyst")
        sqb = [C.sb(st, [128, 512], BF16, "gsq%d" % i) for i in range(2)]
        rnb = [C.sb(st, [128, 512], F32, "grn%d" % i) for i in range(2)]
        tmp = C.sb(st, [128, 128], F32, "gtmp")
        psA = C.ps(st, [128, 512], F32, "gpsA")
        psB = C.ps(st, [128, 512], F32, "gpsB")
        psC = C.ps(st, [128, 512], F32, "gpsC")
        psD = C.ps(st, [128, 512], F32, "gpsD")
        psE = C.ps(st, [128, 512], F32, "gpsE")
        psF = C.ps(st, [128, 512], F32, "gpsF")
        psTs = [C.ps(st, [128, 1024], BF16, "gpsT%d" % i) for i in range(2)]
        c4 = lambda k: slice(k * 128, (k + 1) * 128)

        P.op("sp", lambda e: e.dma_start(out=cw[:], in_=dr["gdn_conv"][l]), writes=[cw], dma=True)
        P.op("sp", lambda e: e.dma_start(out=ad[:], in_=dr["gdn_ad"][l]), writes=[ad], dma=True)
        P.op("sp", lambda e: e.dma_start(out=onm[:], in_=dr["gdn_on"][l]), writes=[onm], dma=True)
        P.op("pool", lambda e: e.dma_start(out=msk[:], in_=dr["gdn_masks"]), writes=[msk], dma=True)
        P.op("pool", lambda e: e.memset(one1[:], 1.0), writes=[one1])
        P.op("pool", lambda e: e.memset(onesF[:], 1.0), writes=[onesF])
        P.op("pool", lambda e: e.memset(triF[:], 1.0), writes=[triF])
        P.op("pool", lambda e: e.affine_select(out=triF[:], in_=triF[:], pattern=[[1, 128]], compare_op=ALU.is_ge,
                                               fill=C.fill(0.0), base=0, channel_multiplier=-1), reads=[triF], writes=[triF])
        P.op("pool", lambda e: e.memset(mneg[:], 0.0), writes=[mneg])
        P.op("pool", lambda e: e.affine_select(out=mneg[:], in_=mneg[:], pattern=[[1, 128]], compare_op=ALU.is_ge,
                                               fill=C.fill(NEG), base=0, channel_multiplier=-1), reads=[mneg], writes=[mneg])
        P.op("pool", lambda e: e.memset(smask[:], 1.0), writes=[smask])
        P.op("pool", lambda e: e.affine_select(out=smask[:], in_=smask[:], pattern=[[1, 128]], compare_op=ALU.is_ge,
                                               fill=C.fill(0.0), base=-1, channel_multiplier=-1), reads=[smask], writes=[smask])
        for rw in raw:
            P.op("pool", lambda e, rw=rw: e.memset(rw[:, 0:3], 0.0), writes=[rw])
        P.op("act", lambda e: e.activation(out=ad[:, 0:8], in_=ad[:, 0:8], func=AF.Exp), reads=[ad], writes=[ad])

        for s in range(nseq):
            pa = scr["PA"][s]
            P.op("sp", lambda e, s=s: e.dma_start(out=ba[:], in_=scr["BA"][s].rearrange("(a p) k -> p a k", p=128)),
                 reads=[scr["BAb"]], writes=[ba], dma=True)
            S_ = lambda k: sc[:, k, :, :]
            P.op("act", lambda e: e.activation(out=S_(BETA), in_=ba[:, :, 0:8], func=AF.Sigmoid), reads=[ba], writes=[sc])
            P.op("dve", lambda e: e.tensor_scalar(out=S_(NBETA), in0=S_(BETA), scalar1=-1.0, scalar2=None, op0=ALU.mult),
                 reads=[sc], writes=[sc])
            P.op("dve", lambda e: e.tensor_tensor(out=S_(G), in0=ba[:, :, 8:16],
                                                  in1=ad[:, 8:16].unsqueeze(1).to_broadcast([128, NT, 8]), op=ALU.add),
                 reads=[ba, ad], writes=[sc])
            P.op("act", lambda e: e.activation(out=S_(G), in_=S_(G), func=AF.Exp), reads=[sc], writes=[sc])
            P.op("act", lambda e: e.activation(out=S_(G), in_=S_(G), func=AF.Ln, bias=one1[:], scale=1.0),
                 reads=[sc, one1], writes=[sc])
            P.op("dve", lambda e: e.tensor_tensor(out=S_(G), in0=S_(G),
                                                  in1=ad[:, 0:8].unsqueeze(1).to_broadcast([128, NT, 8]), op=ALU.mult),
                 reads=[sc, ad], writes=[sc])
            P.op("dve", lambda e: e.tensor_scalar(out=S_(G), in0=S_(G), scalar1=-1.0, scalar2=None, op0=ALU.mult),
                 reads=[sc], writes=[sc])
            for tc in range(NT):
                mm(C, psF[:, tc * 8:(tc + 1) * 8], triF[:], sc[:, G, tc, :], True, True, [triF, sc], psF)
                mm(C, psF[:, 128 + tc * 8:128 + (tc + 1) * 8], onesF[:], sc[:, G, tc, :], True, True, [onesF, sc], psF)
            P.op("dve", lambda e: e.tensor_copy(out=S_(GC), in_=psF[:, 0:128].rearrange("p (a k) -> p a k", k=8)),
                 writes=[psF, sc])
            P.op("dve", lambda e: e.tensor_copy(out=S_(GL), in_=psF[:, 128:256].rearrange("p (a k) -> p a k", k=8)),
                 writes=[psF, sc])
            P.op("act", lambda e: e.activation(out=S_(EGC), in_=S_(GC), func=AF.Exp), reads=[sc], writes=[sc])
            P.op("act", lambda e: e.activation(out=S_(EGL), in_=S_(GL), func=AF.Exp), reads=[sc], writes=[sc])
            P.op("dve", lambda e: e.tensor_scalar(out=S_(NGC), in0=S_(GC), scalar1=-1.0, scalar2=None, op0=ALU.mult),
                 reads=[sc], writes=[sc])
            P.op("dve", lambda e: e.tensor_scalar(out=S_(NEGC), in0=S_(EGC), scalar1=-1.0, scalar2=None, op0=ALU.mult),
                 reads=[sc], writes=[sc])
            P.op("dve", lambda e: e.tensor_tensor(out=S_(KDEC), in0=S_(GL), in1=S_(GC), op=ALU.subtract),
                 reads=[sc], writes=[sc])
            P.op("act", lambda e: e.activation(out=S_(KDEC), in_=S_(KDEC), func=AF.Exp), reads=[sc], writes=[sc])

            for h in range(8):
                k2 = 0
                for wi, (row0, kind) in enumerate(((h * 128, "q"), (1024 + h * 128, "k"), (2048 + h * 128, "v"), (3072 + h * 128, "z"))):
                    rw = raw[wi % 2]
                    ca = cacc[wi % 2]
                    P.op("sp", lambda e, rw=rw, row0=row0: e.dma_start(out=rw[:, 3:], in_=pa[row0:row0 + 128, :]),
                         reads=[scr["PAb"]], writes=[rw], dma=True)
                    if kind == "z":
                        P.op("act", lambda e, rw=rw: e.activation(out=siluz[:], in_=rw[:, 3:], func=AF.Silu),
                             reads=[rw], writes=[siluz])
                        continue
                    ch = row0 // 128
                    P.op("dve", lambda e, rw=rw, ca=ca, ch=ch: e.tensor_scalar(out=ca[:], in0=rw[:, 0:S], scalar1=cw[:, ch, 0:1],
                                                                             scalar2=None, op0=ALU.mult), reads=[rw, cw], writes=[ca])
                    for k in range(1, 4):
                        P.op("dve", lambda e, rw=rw, ca=ca, ch=ch, k=k: e.scalar_tensor_tensor(
                            out=ca[:], in0=rw[:, k:k + S], scalar=cw[:, ch, k:k + 1], in1=ca[:], op0=ALU.mult, op1=ALU.add),
                            reads=[rw, cw, ca], writes=[ca])
                    if kind == "v":
                        P.op("act", lambda e, ca=ca: e.activation(out=vTb[:], in_=ca[:], func=AF.Silu), reads=[ca], writes=[vTb])
                        continue
                    P.op("act", lambda e, ca=ca: e.activation(out=ca[:], in_=ca[:], func=AF.Silu), reads=[ca], writes=[ca])
                    dstT = qT if kind == "q" else kT
                    scl = 128.0 ** -0.5 if kind == "q" else 1.0
                    for q4 in range(S // 512):
                        qs = slice(q4 * 512, (q4 + 1) * 512)
                        sq_, rn_ = sqb[k2 % 2], rnb[k2 % 2]
                        k2 += 1
                        P.op("act", lambda e, ca=ca, qs=qs, sq_=sq_: e.activation(out=sq_[:], in_=ca[:, qs], func=AF.Square),
                             reads=[ca], writes=[sq_])
                        mm(C, psF[:], C.ones_bf[:], sq_[:], True, True, [C.ones_bf, sq_], psF)
                        P.op("act", lambda e, rn_=rn_: e.activation(out=rn_[:], in_=psF[:], func=AF.Sqrt, bias=C.epsc[:, 0:1], scale=1.0),
                             reads=[C.epsc], writes=[psF, rn_])
                        P.op("dve", lambda e, rn_=rn_: e.reciprocal(out=rn_[:], in_=rn_[:]), reads=[rn_], writes=[rn_])
                        P.op("dve", lambda e, ca=ca, qs=qs, rn_=rn_, dstT=dstT, scl=scl: e.scalar_tensor_tensor(
                            out=dstT[:, qs], in0=ca[:, qs], scalar=scl, in1=rn_[:], op0=ALU.mult, op1=ALU.mult),
                            reads=[ca, rn_], writes=[dstT])
                for tc in range(NT):
                    pt_ = psTs[tc % 2]
                    P.op("pe", lambda e, tc=tc, pt_=pt_: e.transpose(pt_[:, 0:128], vTb[:, c4(tc)], C.ident_bf[:]),
                         reads=[vTb, C.ident_bf], writes=[pt_])
                    P.op("act", lambda e, tc=tc, pt_=pt_: e.activation(out=vtok[:, tc, :], in_=pt_[:, 0:128], func=AF.Copy),
                         writes=[pt_, (vtok, tc)])
                for tc in range(NT):
                    pt_ = psTs[tc % 2]
                    P.op("pe", lambda e, tc=tc, pt_=pt_: e.transpose(pt_[:, 0:128], kT[:, c4(tc)], C.ident_bf[:]),
                         reads=[kT, C.ident_bf], writes=[pt_])
                    P.op("act", lambda e, tc=tc, h=h, pt_=pt_: e.activation(out=kd[:, tc, :], in_=pt_[:, 0:128], func=AF.Identity,
                                                                   scale=sc[:, KDEC, tc, h:h + 1]),
                         reads=[sc], writes=[pt_, (kd, tc)])
                for tc in range(NT):
                    gb = Gb[tc % 2]
                    P.op("dve", lambda e, gb=gb, tc=tc, h=h: e.tensor_scalar(out=gb[:], in0=onesF[:], scalar1=sc[:, G, tc, h:h + 1],
                                                                           scalar2=None, op0=ALU.mult), reads=[onesF, sc], writes=[gb])
                    pd = psA if tc % 2 == 0 else psD
                    mm(C, pd[:, 0:128], gb[:], triF[:], True, False, [gb, triF], pd)
                    mm(C, pd[:, 0:128], C.ident_f[:], mneg[:], False, True, [C.ident_f, mneg], pd)
                    P.op("act", lambda e, tc=tc, h=h, pd=pd: e.activation(out=DT[:, tc, :], in_=pd[:, 0:128], func=AF.Exp,
                                                                   bias=sc[:, NGC, tc, h:h + 1], scale=1.0),
                         reads=[sc], writes=[pd, (DT, tc)])
                    P.op("pool", lambda e, tc=tc: e.tensor_tensor(out=DTs[:, tc, :], in0=DT[:, tc, :], in1=smask[:], op=ALU.mult),
                         reads=[(DT, tc), smask], writes=[(DTs, tc)])
                    pk = psB if tc % 2 == 0 else psC
                    pq = pk
                    mm(C, pk[:, 0:128], kT[:, c4(tc)], kT[:, c4(tc)], True, True, [kT], pk)
                    mm(C, pk[:, 128:256], kT[:, c4(tc)], qT[:, c4(tc)], True, True, [kT, qT], pq)
                    P.op("dve", lambda e, tc=tc, h=h: e.scalar_tensor_tensor(
                        out=Pp[0][:, tc, 0:128], in0=pk[:, 0:128], scalar=sc[:, NBETA, tc, h:h + 1], in1=DTs[:, tc, :],
                        op0=ALU.mult, op1=ALU.mult), reads=[sc, (DTs, tc)], writes=[pk, (Pp[0], tc)])
                    P.op("dve", lambda e, tc=tc, pk=pk: e.tensor_tensor(out=attnT[:, tc, :], in0=pk[:, 128:256], in1=DT[:, tc, :],
                                                                 op=ALU.mult), reads=[(DT, tc)], writes=[pq, (attnT, tc)])
                for tc in range(NT):
                    pt_ = psTs[tc % 2]
                    P.op("pe", lambda e, tc=tc, pt_=pt_: e.transpose(pt_[:, 0:128], Pp[0][:, tc, 0:128], C.ident_bf[:]),
                         reads=[(Pp[0], tc), C.ident_bf], writes=[pt_])
                    P.op("act", lambda e, tc=tc, pt_=pt_: e.activation(out=Pp[0][:, tc, 128:256], in_=pt_[:, 0:128], func=AF.Copy),
                         writes=[pt_, (Pp[0], tc)])
                    P.op("pool", lambda e, tc=tc: e.tensor_copy(out=RRT[:, tc, 0:128], in_=C.ident_bf[:]),
                         reads=[C.ident_bf], writes=[(RRT, tc)])
                    P.op("pool", lambda e, tc=tc: e.tensor_copy(out=RRT[:, tc, 128:256], in_=C.ident_bf[:]),
                         reads=[C.ident_bf], writes=[(RRT, tc)])
                for lvl in range(7):
                    for tc in range(NT):
                        xo = xob[tc % 2]
                        P.op("pool", lambda e, tc=tc, xo=xo, lvl=lvl: e.tensor_tensor(
                            out=xo[:], in0=Pp[0][:, tc, :], in1=msk[:, lvl, :, :].rearrange("p a j -> p (a j)"), op=ALU.mult),
                            reads=[(Pp[0], tc), msk], writes=[xo])
                        pc_ = psC if tc % 2 == 0 else psB
                        mm(C, pc_[:, 0:128], xo[:, 128:256], RRT[:, tc, 0:128], True, True, [xo, (RRT, tc)], pc_)
                        mm(C, pc_[:, 128:256], xo[:, 0:128], RRT[:, tc, 128:256], True, True, [xo, (RRT, tc)], pc_)
                        P.op("act", lambda e, tc=tc, pc_=pc_: e.activation(out=Pp[1][:, tc, :], in_=pc_[:, 0:256], func=AF.Copy),
                             writes=[pc_, (Pp[1], tc)])
                        pdd = psD if tc % 2 == 0 else psA
                        mm(C, pdd[:, 0:128], RRT[:, tc, 128:256], Pp[1][:, tc, 0:128], True, True, [(RRT, tc), (Pp[1], tc)], pdd)
                        mm(C, pdd[:, 128:256], RRT[:, tc, 0:128], Pp[1][:, tc, 128:256], True, True, [(RRT, tc), (Pp[1], tc)], pdd)
                        P.op("dve", lambda e, tc=tc, pdd=pdd: e.tensor_tensor(out=RRT[:, tc, :], in0=pdd[:, 0:256], in1=RRT[:, tc, :], op=ALU.add),
                             writes=[pdd, (RRT, tc)])
                P.op("pool", lambda e: e.memset(hS[:], 0.0), writes=[hS])
                P.op("pool", lambda e: e.memset(hSb[:], 0.0), writes=[hSb])
                for tc in range(NT):
                    r_, vn_, o1_, of_, osq_, onb_, ssq_ = rb[tc % 2], vn[tc % 2], o1[tc % 2], of[tc % 2], osq[tc % 2], onb[tc % 2], ssq[tc % 2]
                    p1, p2, p3, p4 = psA, psB, psC, psD
                    mm(C, p1[:, 0:128], kT[:, c4(tc)], hSb[:], True, True, [kT, hSb], p1)
                    mm(C, p3[:, 0:128], qT[:, c4(tc)], hSb[:], True, True, [qT, hSb], p3)
                    P.op("dve", lambda e, tc=tc, h=h, r_=r_: e.scalar_tensor_tensor(
                        out=r_[:], in0=psA[:, 0:128], scalar=sc[:, NEGC, tc, h:h + 1], in1=vtok[:, tc, :], op0=ALU.mult, op1=ALU.add),
                        reads=[sc, (vtok, tc)], writes=[p1, r_])
                    mm(C, p2[:, 0:128], RRT[:, tc, 0:128], r_[:], True, True, [(RRT, tc), r_], p2)
                    P.op("act", lambda e, tc=tc, h=h, vn_=vn_: e.activation(out=vn_[:], in_=psB[:, 0:128], func=AF.Identity,
                                                                            scale=sc[:, BETA, tc, h:h + 1]),
                         reads=[sc], writes=[p2, vn_])
                    P.op("act", lambda e, tc=tc, h=h, o1_=o1_: e.activation(out=o1_[:], in_=psC[:, 0:128], func=AF.Identity,
                                                                            scale=sc[:, EGC, tc, h:h + 1]),
                         reads=[sc], writes=[p3, o1_])
                    mm(C, p4[:, 0:128], attnT[:, tc, :], vn_[:], True, True, [(attnT, tc), vn_], p4)
                    pu = psE
                    mm(C, psE[:, 0:128], kd[:, tc, :], vn_[:], True, True, [(kd, tc), vn_], pu)
                    P.op("dve", lambda e, tc=tc, h=h: e.scalar_tensor_tensor(
                        out=hS[:], in0=hS[:], scalar=sc[:, EGL, tc, h:h + 1], in1=psE[:, 0:128], op0=ALU.mult, op1=ALU.add),
                        reads=[hS, sc], writes=[pu, hS])
                    P.op("act", lambda e: e.activation(out=hSb[:], in_=hS[:], func=AF.Copy), reads=[hS], writes=[hSb])
                    P.op("dve", lambda e, o1_=o1_, of_=of_: e.tensor_tensor(out=of_[:], in0=psD[:, 0:128], in1=o1_[:], op=ALU.add),
                         reads=[o1_], writes=[p4, of_])
                    P.op("dve", lambda e, of_=of_, osq_=osq_, ssq_=ssq_: e.tensor_tensor(out=osq_[:], in0=of_[:], in1=of_[:], op=ALU.mult),
                         reads=[of_], writes=[osq_])
                    P.op("dve", lambda e, osq_=osq_, ssq_=ssq_: e.tensor_reduce(out=ssq_[:, 0:1], in_=osq_[:], axis=mybir.AxisListType.X, op=ALU.add),
                         reads=[osq_], writes=[ssq_])
                    P.op("act", lambda e, ssq_=ssq_: e.activation(out=ssq_[:, 1:2], in_=ssq_[:, 0:1], func=AF.Sqrt, bias=C.epsc[:, 0:1],
                                                                  scale=1.0 / 128), reads=[ssq_, C.epsc], writes=[ssq_])
                    P.op("dve", lambda e, ssq_=ssq_: e.reciprocal(out=ssq_[:, 2:3], in_=ssq_[:, 1:2]), reads=[ssq_], writes=[ssq_])
                    P.op("dve", lambda e, of_=of_, onb_=onb_, ssq_=ssq_: e.tensor_scalar(out=onb_[:], in0=of_[:], scalar1=ssq_[:, 2:3],
                                                                                      scalar2=None, op0=ALU.mult),
                         reads=[of_, ssq_], writes=[onb_])
                    pt_ = psTs[tc % 2]
                    P.op("pe", lambda e, tc=tc, onb_=onb_, pt_=pt_: e.transpose(pt_[:, 0:128], onb_[:], C.ident_bf[:]),
                         reads=[onb_, C.ident_bf], writes=[pt_])
                    P.op("dve", lambda e, tc=tc, pt_=pt_: e.scalar_tensor_tensor(
                        out=yst[:, c4(tc)], in0=pt_[:, 0:128], scalar=onm[:, 0:1], in1=siluz[:, c4(tc)], op0=ALU.mult, op1=ALU.mult),
                        reads=[onm, siluz], writes=[pt_, yst])
                P.op("sp", lambda e, h=h, s=s: e.dma_start(out=scr["YA"][s, h * 128:(h + 1) * 128, :], in_=yst[:]),
                     reads=[yst], writes=[scr["YAb"]], dma=True)
    P.barrier()

def tile_cols(W, width, kc=None):
    K, N = W.shape
    return np.ascontiguousarray(W.reshape(K // 128, 128, N // width, width).transpose(2, 1, 0, 3))


def prep_weights(inp, depth=DEPTH):
    f = lambda a: np.asarray(a, dtype=np.float32)
    out = {}
    out["ada_w"] = np.stack([tile_cols(f(inp["ada_w"][l]), GW) for l in range(depth)])
    out["ada_b"] = np.ascontiguousarray(f(inp["ada_b"])[:depth].reshape(depth, 144, 128))
    out["norms"] = np.ascontiguousarray(np.stack(
        [f(inp["norm_ffn1"])[:depth], f(inp["norm_mix"])[:depth], f(inp["norm_ffn2"])[:depth]], axis=1
    ).reshape(depth, 3, NKC, 128))
    for nm in ("ffn1", "ffn2"):
        out[nm + "_w1"] = np.stack([tile_cols(f(inp[nm + "_w1"][l]), GW) for l in range(depth)])
        out[nm + "_w3"] = np.stack([tile_cols(f(inp[nm + "_w3"][l]), GW) for l in range(depth)])
        out[nm + "_w2"] = np.stack([tile_cols(f(inp[nm + "_w2"][l]), 128) for l in range(depth)])
    w_in = f(inp["w_in"])[:depth]
    segs = []
    for l in range(depth):
        W = w_in[l]
        segs.append(np.concatenate([tile_cols(W[:, 0:4096], GW), tile_cols(W[:, OFF_BU:OFF_CQ], GW),
                                    tile_cols(W[:, OFF_CQ:OFF_GATE], GW), tile_cols(W[:, OFF_GATE:], GW)], axis=0))
    out["w_inT"] = np.stack(segs)
    out["w_ba"] = np.stack([tile_cols(w_in[l][:, OFF_BA:OFF_BU], 16)[0] for l in range(depth)])
    out["w_br"] = np.stack([np.concatenate([tile_cols(f(inp["w_branch_a"][l]), GW), tile_cols(f(inp["w_branch_b"][l]), GW),
                                            tile_cols(f(inp["w_branch_c"][l]), GW)], axis=2) for l in range(depth)])
    out["w_outT"] = np.stack([tile_cols(f(inp["w_out"][l]), GW) for l in range(depth)])
    out["dil_g"] = np.ascontiguousarray(np.stack([np.tile(f(inp["dil_q_norm"])[:depth], (1, 2)),
                                                  np.tile(f(inp["dil_k_norm"])[:depth], (1, 2))], axis=1))
    out["gdn_conv"] = np.ascontiguousarray(f(inp["gdn_conv"])[:depth].reshape(depth, 4, 24, 128).transpose(0, 3, 2, 1))
    ad = np.concatenate([f(inp["gdn_a_log"])[:depth], f(inp["gdn_dt_bias"])[:depth]], axis=1)
    out["gdn_ad"] = np.ascontiguousarray(np.broadcast_to(ad[:, None, :], (depth, 128, 16)))
    out["gdn_on"] = np.ascontiguousarray(f(inp["gdn_out_norm"])[:depth].reshape(depth, 128, 1))
    jj, ii = np.meshgrid(np.arange(128), np.arange(128), indexing="ij")
    msk = np.zeros((128, 7, 2, 128), np.float32)
    for lv in range(7):
        bsz = 1 << lv
        m_ = (((jj // bsz) % 2 == 0) & (ii // bsz == jj // bsz + 1)).astype(np.float32)
        msk[:, lv, 0, :] = m_
        msk[:, lv, 1, :] = m_.T
    out["gdn_masks"] = msk
    G_, P_, I_ = 48, 64, 16
    out["s5_vec"] = np.ascontiguousarray(np.stack(
        [f(inp["s5_a_re"])[:depth].reshape(depth, 24, 128), f(inp["s5_a_im"])[:depth].reshape(depth, 24, 128),
         np.repeat(f(inp["s5_log_step"])[:depth], P_, axis=1).reshape(depth, 24, 128)], axis=1))
    out["s5_dg"] = np.ascontiguousarray(np.concatenate(
        [f(inp["s5_d"])[:depth].reshape(depth, 6, 128), f(inp["s5_glu_b"])[:depth].reshape(depth, 12, 128)], axis=1))
    Bm = np.zeros((depth, 128, 6, 2, 512), np.float32)
    Cm = np.zeros((depth, 128, 24, 2, 128), np.float32)
    for ri, (bn, cn) in enumerate((("s5_b_re", "s5_c_re"), ("s5_b_im", "s5_c_im"))):
        b = f(inp[bn])[:depth]
        cc = f(inp[cn])[:depth]
        for g in range(G_):
            kc, gl8 = divmod(g, 8)
            Bm[:, gl8 * 16:(gl8 + 1) * 16, kc, ri, gl8 * 64:(gl8 + 1) * 64] = b[:, g].transpose(0, 2, 1)
            sc, gl = divmod(g, 2)
            Cm[:, gl * 64:(gl + 1) * 64, sc, ri, gl8 * 16:(gl8 + 1) * 16] = cc[:, g].transpose(0, 2, 1)
    out["s5_Bm"] = Bm
    out["s5_Cm"] = Cm
    out["s5_gluw"] = np.stack([tile_cols(f(inp["s5_glu_w"][l]), GW) for l in range(depth)])
    return out


def kernel(**inputs):
    x = np.asarray(inputs["x"], dtype=np.float32)
    c = np.asarray(inputs["c"], dtype=np.float32)
    B = x.shape[0]
    nseq = B // NCORES
    wts = prep_weights(inputs)
    nc = build_program(nseq=nseq)
    in_maps = []
    for i in range(NCORES):
        xs = x[i * nseq:(i + 1) * nseq].reshape(nseq * S, D)
        m = dict(wts)
        m["xT"] = np.ascontiguousarray(xs.T)
        m["c"] = np.ascontiguousarray(c[i * nseq:(i + 1) * nseq])
        in_maps.append(m)
    res = run_bass_kernel_spmd(nc, in_maps, core_ids=list(range(NCORES)))
    outs = [np.ascontiguousarray(r["outT"].T).reshape(nseq, S, D) for r in res.results]
    return np.concatenate(outs, axis=0).astype(np.float32)
```

```python
import math
from contextlib import ExitStack

import numpy as np
import concourse.bass as bass
import concourse.mybir as mybir
from concourse.bass_utils import run_bass_kernel_spmd

F32 = mybir.dt.float32
BF16 = mybir.dt.bfloat16
AF = mybir.ActivationFunctionType
ALU = mybir.AluOpType

D = 2048
S = 2048
DFF = 5632
NKC = D // 128
NFC = DFF // 128
TT = 512
EPS = 1e-6
DEPTH = 2
NCORES = 8
IN_COLS = 13328
OFF_Z = 3072
OFF_BA = 4096
OFF_BU = 4112
OFF_CQ = 4880
OFF_GATE = 7184
GW = 256


class Buf:
    def __init__(self, name, n=1, t=None):
        self.name = name
        self.n = n
        self.t = t
        self.lastw = [None] * n
        self.readers = [[] for _ in range(n)]

    def __getitem__(self, idx):
        return self.t[idx]


class Op:
    __slots__ = ("eng", "dma", "sig")

    def __init__(self, eng, dma):
        self.eng = eng
        self.dma = dma
        self.sig = None


def _parts(acc):
    if isinstance(acc, Buf):
        return acc, range(acc.n)
    b, p = acc
    if p is None:
        return b, range(b.n)
    if isinstance(p, int):
        return b, (p,)
    return b, p


class Prog:
    ENGS = ("pe", "dve", "act", "pool", "sp")
    NS = 8

    def __init__(self, nc, stack):
        self.nc = nc
        self.e = {"pe": nc.tensor, "dve": nc.vector, "act": nc.scalar, "pool": nc.gpsimd, "sp": nc.sync}
        self.sem = {k: stack.enter_context(nc.semaphore("s_" + k)) for k in self.ENGS}
        self.cnt = {k: 0 for k in self.ENGS}
        self.dsem = {k: [stack.enter_context(nc.semaphore("d_%s%d" % (k, i))) for i in range(self.NS)]
                     for k in ("sp", "pool", "act")}
        self.dcnt = {k: 0 for k in self.dsem}
        self.dlast = {k: [None] * self.NS for k in self.dsem}
        self.waited = {k: {} for k in self.ENGS}
        self.last = {k: None for k in self.ENGS}
        self.nops = 0

    def _wait(self, eng, sig):
        sem, val = sig
        w = self.waited[eng]
        key = id(sem)
        if w.get(key, 0) < val:
            self.e[eng].wait_ge(sem, val)
            w[key] = val

    def op(self, eng, fn, reads=(), writes=(), dma=False):
        o = Op(eng, dma)
        deps = []
        for acc in reads:
            b, ps = _parts(acc)
            for p in ps:
                lw = b.lastw[p]
                if lw is not None:
                    deps.append((lw, True))
                b.readers[p].append(o)
        for acc in writes:
            b, ps = _parts(acc)
            for p in ps:
                lw = b.lastw[p]
                if lw is not None:
                    deps.append((lw, False))
                for r in b.readers[p]:
                    if r is not o:
                        deps.append((r, False))
                b.lastw[p] = o
                b.readers[p] = []
        for d, raw in deps:
            if d is o:
                continue
            if (not d.dma) and (not dma) and d.eng == eng:
                if not raw or eng == "pe":
                    continue
            self._wait(eng, d.sig)
        if dma:
            k = self.dcnt[eng]
            slot = k % self.NS
            prev = self.dlast[eng][slot]
            if prev is not None:
                self._wait(eng, prev.sig)
            o.sig = (self.dsem[eng][slot], 16 * (k // self.NS + 1))
            self.dcnt[eng] = k + 1
            self.dlast[eng][slot] = o
            fn(self.e[eng]).then_inc(self.dsem[eng][slot], 16)
        else:
            self.cnt[eng] += 1
            o.sig = (self.sem[eng], self.cnt[eng])
            fn(self.e[eng]).then_inc(self.sem[eng], 1)
            self.last[eng] = o
        self.nops += 1
        return o

    def barrier(self):
        sigs = [self.last[k].sig for k in self.ENGS if self.last[k] is not None]
        for q in self.dsem:
            for o in self.dlast[q]:
                if o is not None:
                    sigs.append(o.sig)
        for eng in self.ENGS:
            for s in sigs:
                self._wait(eng, s)

    def finish(self):
        sigs = [self.last[k].sig for k in self.ENGS if self.last[k] is not None]
        for q in self.dsem:
            for o in self.dlast[q]:
                if o is not None:
                    sigs.append(o.sig)
        for s in sigs:
            self._wait("sp", s)


class Ctx:
    def __init__(self, nc, stack):
        self.nc = nc
        self.P = Prog(nc, stack)
        self.gstack = stack
        self.uid = 0
        self._fill = {}

    def fill(self, val):
        if val not in self._fill:
            self._fill[val] = self.nc.gpsimd.to_reg(float(val))
        return self._fill[val]

    def sb(self, stack, shape, dt, name, n=1):
        self.uid += 1
        t = stack.enter_context(self.nc.sbuf_tensor("%s_%d" % (name, self.uid), list(shape), dt))
        return Buf(name, n, t)

    def ps(self, stack, shape, dt, name, n=1):
        self.uid += 1
        t = stack.enter_context(self.nc.psum_tensor("%s_%d" % (name, self.uid), list(shape), dt))
        return Buf(name, n, t)


class Stream:
    def __init__(self, bufs, loads, depth=None):
        self.bufs = bufs
        self.loads = loads
        self.depth = len(bufs) if depth is None else depth
        self.issued = 0

    def get(self, k):
        while self.issued < min(len(self.loads), k + self.depth):
            self.loads[self.issued](self.bufs[self.issued % len(self.bufs)])
            self.issued += 1
        return self.bufs[k % len(self.bufs)]


def mm(C, ps, lhsT, rhs, start, stop, reads, wr):
    C.P.op("pe", lambda e: e.matmul(ps, lhsT, rhs, start=start, stop=stop), reads=reads, writes=[wr])


def load_vec_fm(C, stack, dst, dst_cols, src_rows_ap, nrows, tmp, pst, ident):
    P = C.P
    P.op("sp", lambda e: e.dma_start(out=tmp[0:nrows, :], in_=src_rows_ap), writes=[tmp], dma=True)
    P.op("pe", lambda e: e.transpose(pst[:, 0:nrows], tmp[0:nrows, :], ident[0:nrows, 0:nrows]),
         reads=[tmp, ident], writes=[pst])
    P.op("dve", lambda e: e.tensor_copy(out=dst_cols, in_=pst[:, 0:nrows]), reads=[pst], writes=[dst])


def build_program(nseq=2, depth=DEPTH, debug=False, phases=None):
    nc = bass.Bass("TRN2", target_bir_lowering=False)
    ntok = nseq * S
    ntile = ntok // TT
    dr = {}

    def din(name, shape, dt=F32):
        dr[name] = nc.dram_tensor(name, list(shape), dt, kind="ExternalInput").ap()
        return dr[name]

    def dscratch(name, shape, dt=F32):
        kind = "ExternalOutput" if debug else "Internal"
        dr[name] = nc.dram_tensor(name, list(shape), dt, kind=kind).ap()
        return dr[name]

    xT_in = din("xT", [D, ntok])
    c_in = din("c", [nseq, D])
    L = depth
    din("ada_w", [L, 72, 128, NKC, GW])
    din("ada_b", [L, 144, 128])
    din("norms", [L, 3, NKC, 128])
    for nm in ("ffn1", "ffn2"):
        din(nm + "_w1", [L, DFF // GW, 128, NKC, GW])
        din(nm + "_w3", [L, DFF // GW, 128, NKC, GW])
        din(nm + "_w2", [L, NKC, 128, NFC, 128])
    din("w_inT", [L, 52, 128, NKC, GW])
    din("w_ba", [L, 128, NKC, 16])
    din("w_br", [L, 8, 128, NKC, GW])
    din("w_outT", [L, 8, 128, NKC, GW])
    din("dil_g", [L, 2, 128])
    din("gdn_conv", [L, 128, 24, 4])
    din("gdn_ad", [L, 128, 16])
    din("gdn_on", [L, 128, 1])
    din("gdn_masks", [128, 7, 2, 128])
    din("s5_vec", [L, 3, 24, 128])
    din("s5_dg", [L, 18, 128])
    din("s5_Bm", [L, 128, 6, 2, 512])
    din("s5_Cm", [L, 128, 24, 2, 128])
    din("s5_gluw", [L, 6, 128, 6, GW])
    out_ap = nc.dram_tensor("outT", [D, ntok], F32, kind="ExternalOutput").ap()
    scr = {}
    scr["HT"] = dscratch("HT", [nseq, D, S], BF16)
    scr["PA"] = dscratch("PA", [nseq, 4096, S])
    scr["PB"] = dscratch("PB", [nseq, 768, S])
    scr["PC"] = dscratch("PC", [nseq, 2304, S])
    scr["BA"] = dscratch("BA", [nseq, S, 16])
    scr["YA"] = dscratch("YA", [nseq, 1024, S], BF16)
    scr["YB"] = dscratch("YB", [nseq, 768, S], BF16)
    scr["YC"] = dscratch("YC", [nseq, 256, S], BF16)
    for k in ("HT", "PA", "PB", "PC", "BA", "YA", "YB", "YC"):
        scr[k + "b"] = Buf(k, ntile)
    xT = dscratch("xres", [D, ntok])

    with ExitStack() as gs:
        C = Ctx(nc, gs)
        P = C.P
        ones_bf = C.sb(gs, [128, 128], BF16, "ones_bf")
        ident_f = C.sb(gs, [128, 128], F32, "ident_f")
        ident_bf = C.sb(gs, [128, 128], BF16, "ident_bf")
        neghalf = C.sb(gs, [128, TT], F32, "neghalf")
        modv = C.sb(gs, [128, nseq, 9, NKC], F32, "modv")
        gains = C.sb(gs, [128, 3, NKC], F32, "gains")
        P.op("pool", lambda e: e.memset(ones_bf[:], 1.0), writes=[ones_bf])
        P.op("pool", lambda e: e.memset(neghalf[:], -0.5), writes=[neghalf])
        epsc = C.sb(gs, [128, 1], F32, "epsc")
        P.op("pool", lambda e: e.memset(epsc[:], EPS), writes=[epsc])
        C.epsc = epsc
        P.op("pool", lambda e: e.memset(ident_f[:], 1.0), writes=[ident_f])
        P.op("pool", lambda e: e.affine_select(out=ident_f[:], in_=ident_f[:], pattern=[[1, 128]],
                                               compare_op=ALU.is_equal, fill=C.fill(0.0), base=0,
                                               channel_multiplier=-1),
             reads=[ident_f], writes=[ident_f])
        P.op("dve", lambda e: e.tensor_copy(out=ident_bf[:], in_=ident_f[:]), reads=[ident_f], writes=[ident_bf])
        C.ones_bf, C.ident_f, C.ident_bf, C.neghalf = ones_bf, ident_f, ident_bf, neghalf

        xres = Buf("xres", ntile)
        xin = Buf("xin", 1)

        def xview(ap, i):
            return ap.rearrange("(c p) t -> p c t", p=128)[:, :, i * TT:(i + 1) * TT]

        for l in range(depth):
            src = xT_in if l == 0 else xT
            ada_phase(C, dr, l, nseq, modv, gains)
            ffn_phase(C, dr, l, "ffn1", 0, nseq, ntile, modv, gains, src, xT, xres, xview)
            if phases is None or "proj" in phases:
                proj_phase(C, dr, l, nseq, ntile, modv, gains, xT, xres, xview, scr)
            if phases is None or "s5" in phases:
                s5_phase(C, dr, l, nseq, scr)
            if phases is None or "dil" in phases:
                dil_phase(C, dr, l, nseq, scr)
            if phases is None or "gdn" in phases:
                gdn_phase(C, dr, l, nseq, scr)
            if phases is None or "merge" in phases:
                merge_phase(C, dr, l, nseq, ntile, modv, gains, xT, xres, xview, scr)
            ffn_phase(C, dr, l, "ffn2", 2, nseq, ntile, modv, gains, xT, xT if l < depth - 1 else out_ap,
                      xres, xview)
        P.finish()
    return nc


def mod_vectors(C, AB, modv, gains, sub, nseq, gate_scale):
    P = C.P
    for b in range(nseq):
        P.op("dve", lambda e, b=b: e.scalar_tensor_tensor(
            out=AB[:, b, 0, :], in0=modv[:, b, 3 * sub + 1, :], scalar=1.0, in1=gains[:, sub, :],
            op0=ALU.add, op1=ALU.mult), reads=[modv, gains], writes=[AB])
        P.op("dve", lambda e, b=b: e.tensor_copy(out=AB[:, b, 1, :], in_=modv[:, b, 3 * sub, :]),
             reads=[modv], writes=[AB])
        P.op("dve", lambda e, b=b: e.tensor_scalar(out=AB[:, b, 2, :], in0=modv[:, b, 3 * sub + 2, :],
                                                   scalar1=gate_scale, scalar2=None, op0=ALU.mult),
             reads=[modv], writes=[AB])


def norm_mod(C, xs, hT, AB, b, sq, tmpf, ms, rstd, ps_stat):
    P = C.P
    for c in range(NKC):
        q = sq[c % len(sq)]
        P.op("act", lambda e, c=c, q=q: e.activation(out=q[:], in_=xs[:, c, :], func=AF.Square),
             reads=[xs], writes=[q])
        mm(C, ps_stat[:], C.ones_bf[:], q[:], c == 0, c == NKC - 1, [q, C.ones_bf], ps_stat)
    P.op("act", lambda e: e.activation(out=ms[:], in_=ps_stat[:], func=AF.Sqrt, bias=C.epsc[:, 0:1], scale=1.0 / D),
         reads=[ps_stat, C.epsc], writes=[ms])
    P.op("dve", lambda e: e.reciprocal(out=rstd[:], in_=ms[:]), reads=[ms], writes=[rstd])
    for c in range(NKC):
        t = tmpf[c % len(tmpf)]
        P.op("dve", lambda e, c=c, t=t: e.scalar_tensor_tensor(
            out=t[:], in0=xs[:, c, :], scalar=AB[:, b, 0, c:c + 1], in1=rstd[:],
            op0=ALU.mult, op1=ALU.mult), reads=[xs, AB, rstd], writes=[t])
        P.op("act", lambda e, c=c, t=t: e.activation(
            out=hT[:, c, :], in_=t[:], func=AF.Identity, bias=AB[:, b, 1, c:c + 1], scale=1.0),
            reads=[t, AB], writes=[(hT, c)])


def ada_phase(C, dr, l, nseq, modv, gains):
    P = C.P
    with ExitStack() as st:
        tmp = C.sb(st, [128, 128], F32, "ada_tmp")
        pst = C.ps(st, [128, 512], F32, "ada_pst")
        psm = C.ps(st, [128, 512], F32, "ada_psm")
        cT = C.sb(st, [128, NKC, nseq], F32, "cT")
        bias = C.sb(st, [128, 144], F32, "ada_bias")
        wb = [C.sb(st, [128, NKC, GW], F32, "ada_w%d" % i) for i in range(2)]
        for c in range(NKC):
            P.op("sp", lambda e, c=c: e.dma_start(out=tmp[0:nseq, :], in_=dr["c"][:, c * 128:(c + 1) * 128]),
                 writes=[tmp], dma=True)
            P.op("pe", lambda e: e.transpose(pst[:, 0:nseq], tmp[0:nseq, :], C.ident_f[0:nseq, 0:nseq]),
                 reads=[tmp, C.ident_f], writes=[pst])
            P.op("act", lambda e, c=c: e.activation(out=cT[:, c, :], in_=pst[:, 0:nseq], func=AF.Silu),
                 reads=[pst], writes=[cT])
        load_vec_fm(C, st, bias, bias[:, 0:128], dr["ada_b"][l, 0:128, :], 128, tmp, pst, C.ident_f)
        load_vec_fm(C, st, bias, bias[:, 128:144], dr["ada_b"][l, 128:144, :], 16, tmp, pst, C.ident_f)
        for j in range(3):
            load_vec_fm(C, st, gains, gains[:, j, :], dr["norms"][l, j, :, :], NKC, tmp, pst, C.ident_f)
        loads = []
        for g in range(72):
            loads.append(lambda b, g=g: P.op("sp", lambda e: e.dma_start(out=b[:], in_=dr["ada_w"][l, g]),
                                             writes=[b], dma=True))
        strm = Stream(wb, loads)
        for g in range(72):
            w = strm.get(g)
            for fi in range(2):
                ch = 2 * g + fi
                for c in range(NKC):
                    mm(C, psm[:, ch * nseq:(ch + 1) * nseq], w[:, c, fi * 128:(fi + 1) * 128], cT[:, c, :],
                       c == 0, c == NKC - 1, [w, cT], psm)
        for b in range(nseq):
            P.op("dve", lambda e, b=b: e.tensor_tensor(
                out=modv[:, b, :, :].rearrange("p j c -> p (j c)"),
                in0=psm[:, 0:144 * nseq].rearrange("p (k b) -> p k b", b=nseq)[:, :, b],
                in1=bias[:], op=ALU.add), reads=[psm, bias], writes=[modv])
    P.barrier()


def ffn_phase(C, dr, l, nm, sub, nseq, ntile, modv, gains, src, dst, xres, xview):
    P = C.P
    w1d, w3d, w2d = dr[nm + "_w1"], dr[nm + "_w3"], dr[nm + "_w2"]
    NG = DFF // GW
    with ExitStack() as st:
        xs = C.sb(st, [128, NKC, TT], F32, "xs")
        hT = C.sb(st, [128, NKC, TT], BF16, "hT", n=NKC)
        actT = C.sb(st, [128, NFC, TT], BF16, "actT", n=NFC)
        w1b = [C.sb(st, [128, NKC, GW], BF16, "w1b%d" % i) for i in range(2)]
        w3b = [C.sb(st, [128, NKC, GW], BF16, "w3b%d" % i) for i in range(2)]
        w2b = [C.sb(st, [128, NFC, 128], BF16, "w2b%d" % i) for i in range(2)]
        sq = [C.sb(st, [128, TT], BF16, "sq%d" % i) for i in range(3)]
        tmpf = [C.sb(st, [128, TT], F32, "tmpf%d" % i) for i in range(3)]
        sg = [C.sb(st, [128, TT], F32, "sg%d" % i) for i in range(3)]
        xr = [C.sb(st, [128, TT], F32, "xr%d" % i) for i in range(4)]
        ms = C.sb(st, [128, TT], F32, "ms")
        rstd = C.sb(st, [128, TT], F32, "rstd")
        AB = C.sb(st, [128, nseq, 3, NKC], F32, "AB")
        ps_stat = C.ps(st, [128, TT], F32, "ps_stat")
        ps_g = [C.ps(st, [128, TT], F32, "ps_g%d" % i) for i in range(2)]
        ps_u = [C.ps(st, [128, TT], F32, "ps_u%d" % i) for i in range(2)]
        ps_o = [C.ps(st, [128, TT], F32, "ps_o%d" % i) for i in range(2)]

        mod_vectors(C, AB, modv, gains, sub, nseq, 0.5)

        def mk(wd, g):
            return lambda b: P.op("pool", lambda e: e.dma_start(out=b[:], in_=wd[l, g]), writes=[b], dma=True)
        l1 = [mk(w1d, g) for _ in range(ntile) for g in range(NG)]
        l3 = [mk(w3d, g) for _ in range(ntile) for g in range(NG)]
        l2 = [mk(w2d, m) for _ in range(ntile) for m in range(NKC)]
        s1, s3, s2 = Stream(w1b, l1), Stream(w3b, l3), Stream(w2b, l2)

        for i in range(ntile):
            b = i // (S // TT)
            P.op("sp", lambda e, i=i: e.dma_start(out=xs[:], in_=xview(src, i)),
                 reads=[(xres, i)], writes=[xs], dma=True)
            norm_mod(C, xs, hT, AB, b, sq, tmpf, ms, rstd, ps_stat)
            for g in range(NG):
                k = i * NG + g
                wa, wb_ = s1.get(k), s3.get(k)
                for fi in range(GW // 128):
                    f = g * (GW // 128) + fi
                    pg, pu = ps_g[f % 2], ps_u[f % 2]
                    for c in range(NKC):
                        mm(C, pg[:], wa[:, c, fi * 128:(fi + 1) * 128], hT[:, c, :], c == 0, c == NKC - 1,
                           [wa, (hT, c)], pg)
                    for c in range(NKC):
                        mm(C, pu[:], wb_[:, c, fi * 128:(fi + 1) * 128], hT[:, c, :], c == 0, c == NKC - 1,
                           [wb_, (hT, c)], pu)
                    s_ = sg[f % 3]
                    P.op("act", lambda e, s_=s_, pg=pg: e.activation(out=s_[:], in_=pg[:], func=AF.Silu),
                         reads=[pg], writes=[s_])
                    P.op("dve", lambda e, s_=s_, pu=pu, f=f: e.tensor_tensor(
                        out=actT[:, f, :], in0=pu[:], in1=s_[:], op=ALU.mult),
                        reads=[pu, s_], writes=[(actT, f)])
            for m in range(NKC):
                k = i * NKC + m
                w2 = s2.get(k)
                po = ps_o[m % 2]
                r = xr[m % 4]
                P.op("sp", lambda e, i=i, m=m, r=r: e.dma_start(out=r[:], in_=xview(src, i)[:, m, :]),
                     reads=[(xres, i)], writes=[r], dma=True)
                for f in range(NFC):
                    mm(C, po[:], w2[:, f, :], actT[:, f, :], f == 0, f == NFC - 1, [w2, (actT, f)], po)
                P.op("dve", lambda e, m=m, r=r, po=po, b=b: e.scalar_tensor_tensor(
                    out=r[:], in0=po[:], scalar=AB[:, b, 2, m:m + 1], in1=r[:], op0=ALU.mult, op1=ALU.add),
                    reads=[po, AB, r], writes=[r])
                P.op("sp", lambda e, i=i, m=m, r=r: e.dma_start(out=xview(dst, i)[:, m, :], in_=r[:]),
                     reads=[r], writes=[(xres, i)], dma=True)
    P.barrier()


def proj_phase(C, dr, l, nseq, ntile, modv, gains, xT, xres, xview, scr):
    P = C.P
    wd = dr["w_inT"]
    NGP = 28
    with ExitStack() as st:
        xs = C.sb(st, [128, NKC, TT], F32, "xs")
        hT = C.sb(st, [128, NKC, TT], BF16, "hT", n=NKC)
        wb = [C.sb(st, [128, NKC, GW], BF16, "pw%d" % i) for i in range(3)]
        wba = C.sb(st, [128, NKC, 16], BF16, "wba")
        sq = [C.sb(st, [128, TT], BF16, "sq%d" % i) for i in range(3)]
        tmpf = [C.sb(st, [128, TT], F32, "tmpf%d" % i) for i in range(3)]
        stg = [C.sb(st, [128, TT], F32, "stg%d" % i) for i in range(4)]
        bas = [C.sb(st, [128, 16], F32, "bas%d" % i) for i in range(2)]
        ms = C.sb(st, [128, TT], F32, "ms")
        rstd = C.sb(st, [128, TT], F32, "rstd")
        AB = C.sb(st, [128, nseq, 3, NKC], F32, "AB")
        ps_stat = C.ps(st, [128, TT], F32, "ps_stat")
        ps_p = [C.ps(st, [128, TT], F32, "ps_p%d" % i) for i in range(3)]
        ps_ba = C.ps(st, [128, TT], F32, "ps_ba")
        mod_vectors(C, AB, modv, gains, 1, nseq, 1.0)
        P.op("pool", lambda e: e.dma_start(out=wba[:], in_=dr["w_ba"][l]), writes=[wba], dma=True)
        loads = [(lambda b, g=g: P.op("pool", lambda e: e.dma_start(out=b[:], in_=wd[l, g]), writes=[b], dma=True))
                 for _ in range(ntile) for g in range(NGP)]
        strm = Stream(wb, loads)
        nev = 0
        for i in range(ntile):
            s, ti = divmod(i, S // TT)
            tsl = slice(ti * TT, (ti + 1) * TT)
            P.op("sp", lambda e, i=i: e.dma_start(out=xs[:], in_=xview(xT, i)),
                 reads=[(xres, i)], writes=[xs], dma=True)
            norm_mod(C, xs, hT, AB, s, sq, tmpf, ms, rstd, ps_stat)
            P.op("sp", lambda e, s=s, tsl=tsl: e.dma_start(
                out=scr["HT"][s].rearrange("(c p) t -> p c t", p=128)[:, :, tsl], in_=hT[:]),
                reads=[hT], writes=[(scr["HTb"], i)], dma=True)
            for ts in range(TT // 128):
                for c in range(NKC):
                    mm(C, ps_ba[:, ts * 16:(ts + 1) * 16], hT[:, c, ts * 128:(ts + 1) * 128], wba[:, c, :],
                       c == 0, c == NKC - 1, [(hT, c), wba], ps_ba)
            bb = bas[i % 2]
            for ts in range(TT // 128):
                pass
            P.op("dve", lambda e, bb=bb: e.tensor_copy(out=stg[3][:, 0:64], in_=ps_ba[:, 0:64]),
                 reads=[ps_ba], writes=[stg[3]])
            P.op("sp", lambda e, s=s, ti=ti: e.dma_start(
                out=scr["BA"][s, ti * TT:(ti + 1) * TT, :].rearrange("(a p) k -> p a k", p=128),
                in_=stg[3][:, 0:64].rearrange("p (a k) -> p a k", k=16)),
                reads=[stg[3]], writes=[(scr["BAb"], i)], dma=True)
            for g in range(NGP):
                w = strm.get(i * NGP + g)
                for fi in range(2):
                    ch = 2 * g + fi
                    pp = ps_p[ch % 3]
                    for c in range(NKC):
                        mm(C, pp[:], w[:, c, fi * 128:(fi + 1) * 128], hT[:, c, :], c == 0, c == NKC - 1,
                           [w, (hT, c)], pp)
                    sg_ = stg[nev % 3]
                    if nev % 2 == 0:
                        P.op("act", lambda e, sg_=sg_, pp=pp: e.activation(out=sg_[:], in_=pp[:], func=AF.Copy),
                             reads=[pp], writes=[sg_])
                    else:
                        P.op("dve", lambda e, sg_=sg_, pp=pp: e.tensor_copy(out=sg_[:], in_=pp[:]),
                             reads=[pp], writes=[sg_])
                    nev += 1
                    if ch < 32:
                        dst, row, key = scr["PA"], ch, "PAb"
                    elif ch < 38:
                        dst, row, key = scr["PB"], ch - 32, "PBb"
                    else:
                        dst, row, key = scr["PC"], ch - 38, "PCb"
                    P.op("sp", lambda e, dst=dst, row=row, s=s, tsl=tsl, sg_=sg_: e.dma_start(
                        out=dst[s, row * 128:(row + 1) * 128, tsl], in_=sg_[:]),
                        reads=[sg_], writes=[(scr[key], i)], dma=True)
    P.barrier()


def s5_phase(C, dr, l, nseq, scr):
    P = C.P
    NSC = 24
    TWO_PI = 2.0 * math.pi
    with ExitStack() as st:
        tmp = C.sb(st, [128, 128], F32, "s5tmp")
        pst = C.ps(st, [128, 512], F32, "s5pst")
        vec = C.sb(st, [128, 3, NSC], F32, "s5vec")
        dg = C.sb(st, [128, 18], F32, "s5dg")
        sm = C.sb(st, [128, 24, NSC], F32, "s5sm")
        halfpi = C.sb(st, [128, 1], F32, "halfpi")
        Bre = C.sb(st, [128, NSC, 128], F32, "Bre")
        Bim = C.sb(st, [128, NSC, 128], F32, "Bim")
        Atre = C.sb(st, [128, NSC * 128], F32, "Atre")
        Atim = C.sb(st, [128, NSC * 128], F32, "Atim")
        Bm = C.sb(st, [128, 6, 2, 512], BF16, "Bm")
        Cm = C.sb(st, [128, NSC, 2, 128], BF16, "Cm")
        gw = C.sb(st, [128, 6, 6, GW], BF16, "gluw")
        tri = C.sb(st, [128, 128], BF16, "tri")
        for j in range(3):
            load_vec_fm(C, st, vec, vec[:, j, :], dr["s5_vec"][l, j], NSC, tmp, pst, C.ident_f)
        load_vec_fm(C, st, dg, dg[:, :], dr["s5_dg"][l], 18, tmp, pst, C.ident_f)
        P.op("pool", lambda e: e.dma_start(out=Bm[:], in_=dr["s5_Bm"][l]), writes=[Bm], dma=True)
        P.op("pool", lambda e: e.dma_start(out=Cm[:], in_=dr["s5_Cm"][l]), writes=[Cm], dma=True)
        for g in range(6):
            P.op("pool", lambda e, g=g: e.dma_start(out=gw[:, g], in_=dr["s5_gluw"][l, g]), writes=[gw], dma=True)
        P.op("pool", lambda e: e.memset(halfpi[:], math.pi / 2), writes=[halfpi])
        P.op("pool", lambda e: e.memset(tri[:], 1.0), writes=[tri])
        P.op("pool", lambda e: e.affine_select(out=tri[:], in_=tri[:], pattern=[[1, 128]], compare_op=ALU.is_ge,
                                               fill=C.fill(0.0), base=0, channel_multiplier=-1),
             reads=[tri], writes=[tri])

        V = lambda k: sm[:, k, :]

        def tt(o, a, b, op, eng="dve"):
            P.op(eng, lambda e: e.tensor_tensor(out=o, in0=a, in1=b, op=op), reads=[sm, vec], writes=[sm])

        def ts(o, a, s1, op0, s2=None, op1=None):
            if op1 is None:
                P.op("dve", lambda e: e.tensor_scalar(out=o, in0=a, scalar1=s1, scalar2=None, op0=op0),
                     reads=[sm, vec], writes=[sm])
            else:
                P.op("dve", lambda e: e.tensor_scalar(out=o, in0=a, scalar1=s1, scalar2=s2, op0=op0, op1=op1),
                     reads=[sm, vec], writes=[sm])

        def act(o, a, f, scale=1.0, bias=None):
            if bias is None:
                P.op("act", lambda e: e.activation(out=o, in_=a, func=f, scale=scale), reads=[sm, vec], writes=[sm])
            else:
                P.op("act", lambda e: e.activation(out=o, in_=a, func=f, scale=scale, bias=bias),
                     reads=[sm, vec, halfpi], writes=[sm])
        STEP, ARE, RS, TH, THR, T0, RHO, SIN, COS, LR, LI, MR, MI, CR, CI, DEN, NR, T1, T2, PWR, PWI, T3 = range(22)
        act(V(STEP), vec[:, 2, :], AF.Exp)
        ts(V(ARE), vec[:, 0, :], -1e-4, ALU.min)
        tt(V(RS), V(ARE), V(STEP), ALU.mult)
        tt(V(TH), vec[:, 1, :], V(STEP), ALU.mult)
        ts(V(THR), V(TH), 1.0, ALU.mult)
        for m in range(1, 6):
            ts(V(T0), V(TH), (2 * m - 1) * math.pi, ALU.is_gt, TWO_PI, ALU.mult)
            tt(V(THR), V(THR), V(T0), ALU.subtract)
        act(V(RHO), V(RS), AF.Exp)
        act(V(SIN), V(THR), AF.Sin)
        ts(V(T0), V(THR), -1.0, ALU.mult)
        tt(V(T0), V(T0), V(THR), ALU.max)
        act(V(COS), V(T0), AF.Sin, scale=-1.0, bias=halfpi[:])
        tt(V(LR), V(RHO), V(COS), ALU.mult)
        tt(V(LI), V(RHO), V(SIN), ALU.mult)
        P.op("dve", lambda e: e.reciprocal(out=V(T1), in_=V(RHO)), reads=[sm], writes=[sm])
        tt(V(MR), V(COS), V(T1), ALU.mult)
        tt(V(MI), V(SIN), V(T1), ALU.mult)
        ts(V(MI), V(MI), -1.0, ALU.mult)
        ts(V(NR), V(LR), -1.0, ALU.add)
        tt(V(T1), V(ARE), V(ARE), ALU.mult)
        tt(V(T2), vec[:, 1, :], vec[:, 1, :], ALU.mult)
        tt(V(DEN), V(T1), V(T2), ALU.add)
        P.op("dve", lambda e: e.reciprocal(out=V(DEN), in_=V(DEN)), reads=[sm], writes=[sm])
        tt(V(T1), V(NR), V(ARE), ALU.mult)
        tt(V(T2), V(LI), vec[:, 1, :], ALU.mult)
        tt(V(T1), V(T1), V(T2), ALU.add)
        tt(V(CR), V(T1), V(DEN), ALU.mult)
        tt(V(T1), V(LI), V(ARE), ALU.mult)
        tt(V(T2), V(NR), vec[:, 1, :], ALU.mult)
        tt(V(T1), V(T1), V(T2), ALU.subtract)
        tt(V(CI), V(T1), V(DEN), ALU.mult)

        def pow_table(Tr, Ti, init, base, s2):
            ta = C.sb(s2, [128, NSC, 64], F32, "pt_a")
            tb = C.sb(s2, [128, NSC, 64], F32, "pt_b")
            if init is None:
                P.op("pool", lambda e: e.memset(Tr[:, :, 0:1], 1.0), writes=[Tr])
                P.op("pool", lambda e: e.memset(Ti[:, :, 0:1], 0.0), writes=[Ti])
            else:
                P.op("dve", lambda e: e.tensor_copy(out=Tr[:, :, 0], in_=V(init[0])), reads=[sm], writes=[Tr])
                P.op("dve", lambda e: e.tensor_copy(out=Ti[:, :, 0], in_=V(init[1])), reads=[sm], writes=[Ti])
            ts(V(PWR), V(base[0]), 1.0, ALU.mult)
            ts(V(PWI), V(base[1]), 1.0, ALU.mult)
            for k in range(7):
                w = 1 << k
                pr = V(PWR).unsqueeze(2).to_broadcast([128, NSC, w])
                pi = V(PWI).unsqueeze(2).to_broadcast([128, NSC, w])
                lo = slice(0, w)
                hi = slice(w, 2 * w)
                P.op("dve", lambda e, pr=pr, lo=lo, w=w: e.tensor_tensor(out=ta[:, :, 0:w], in0=Tr[:, :, lo], in1=pr, op=ALU.mult),
                     reads=[Tr, sm], writes=[ta])
                P.op("dve", lambda e, pi=pi, lo=lo, w=w: e.tensor_tensor(out=tb[:, :, 0:w], in0=Ti[:, :, lo], in1=pi, op=ALU.mult),
                     reads=[Ti, sm], writes=[tb])
                P.op("dve", lambda e, hi=hi, w=w: e.tensor_tensor(out=Tr[:, :, hi], in0=ta[:, :, 0:w], in1=tb[:, :, 0:w], op=ALU.subtract),
                     reads=[ta, tb], writes=[Tr])
                P.op("dve", lambda e, pi=pi, lo=lo, w=w: e.tensor_tensor(out=ta[:, :, 0:w], in0=Tr[:, :, lo], in1=pi, op=ALU.mult),
                     reads=[Tr, sm], writes=[ta])
                P.op("dve", lambda e, pr=pr, lo=lo, w=w: e.tensor_tensor(out=tb[:, :, 0:w], in0=Ti[:, :, lo], in1=pr, op=ALU.mult),
                     reads=[Ti, sm], writes=[tb])
                P.op("dve", lambda e, hi=hi, w=w: e.tensor_tensor(out=Ti[:, :, hi], in0=ta[:, :, 0:w], in1=tb[:, :, 0:w], op=ALU.add),
                     reads=[ta, tb], writes=[Ti])
                tt(V(T1), V(PWR), V(PWR), ALU.mult)
                tt(V(T2), V(PWI), V(PWI), ALU.mult)
                tt(V(T3), V(PWR), V(PWI), ALU.mult)
                tt(V(PWR), V(T1), V(T2), ALU.subtract)
                ts(V(PWI), V(T3), 2.0, ALU.mult)

        with ExitStack() as s2:
            pow_table(Bre, Bim, None, (LR, LI), s2)
        with ExitStack() as s2:
            Asr = C.sb(s2, [128, NSC, 128], F32, "Asr")
            Asi = C.sb(s2, [128, NSC, 128], F32, "Asi")
            pow_table(Asr, Asi, (CR, CI), (MR, MI), s2)
            for sc in range(NSC):
                for (Tsm, Tt) in ((Asr, Atre), (Asi, Atim)):
                    P.op("pe", lambda e, Tsm=Tsm, sc=sc: e.transpose(pst[:, 0:128], Tsm[:, sc, :], C.ident_f[:]),
                         reads=[Tsm, C.ident_f], writes=[pst])
                    P.op("dve", lambda e, Tt=Tt, sc=sc: e.tensor_copy(out=Tt[:, sc * 128:(sc + 1) * 128], in_=pst[:, 0:128]),
                         reads=[pst], writes=[Tt])
            P.barrier()

        with ExitStack() as s3:
            uT = C.sb(s3, [128, 6, S], F32, "uT")
            uTb = C.sb(s3, [128, 6, S], BF16, "uTb")
            hcr = C.sb(s3, [128, NSC], F32, "hcr")
            hci = C.sb(s3, [128, NSC], F32, "hci")
            zre = [C.sb(s3, [128, 512], BF16, "zre%d" % i) for i in range(2)]
            zim = [C.sb(s3, [128, 512], BF16, "zim%d" % i) for i in range(2)]
            tf = [C.sb(s3, [128, 512], F32, "s5t%d" % i) for i in range(8)]
            srb = C.sb(s3, [128, 4, 128], BF16, "srb")
            sib = C.sb(s3, [128, 4, 128], BF16, "sib")
            hs = C.sb(s3, [128, 6, 4], F32, "hs")
            ystg = [C.sb(s3, [128, 512], BF16, "ystg%d" % i) for i in range(2)]
            ps_br = C.ps(s3, [128, 512], F32, "ps_br")
            ps_bi = C.ps(s3, [128, 512], F32, "ps_bi")
            ps_cr = C.ps(s3, [128, 512], F32, "ps_cr")
            ps_ci = C.ps(s3, [128, 512], F32, "ps_ci")
            ps_y = C.ps(s3, [128, 512], F32, "ps_y")
            ps_v = C.ps(s3, [128, 512], F32, "ps_v")
            ps_g = C.ps(s3, [128, 512], F32, "ps_g")
            v3 = lambda t: t[:].rearrange("p (a j) -> p a j", j=128)
            for s in range(nseq):
                ub = scr["PB"][s].rearrange("(c p) t -> p c t", p=128)
                P.op("sp", lambda e, ub=ub: e.dma_start(out=uT[:], in_=ub), reads=[scr["PBb"]], writes=[uT], dma=True)
                P.op("pool", lambda e, ub=ub: e.dma_start(out=uTb[:], in_=ub), reads=[scr["PBb"]], writes=[uTb], dma=True)
                P.op("pool", lambda e: e.memset(hcr[:], 0.0), writes=[hcr])
                P.op("pool", lambda e: e.memset(hci[:], 0.0), writes=[hci])
                nz = 0
                for tc in range(S // 128):
                    tsl = slice(tc * 128, (tc + 1) * 128)
                    for kc in range(6):
                        sc0 = 4 * kc
                        csl = slice(sc0 * 128, (sc0 + 4) * 128)
                        mm(C, ps_br[:], uTb[:, kc, tsl], Bm[:, kc, 0, :], True, True, [uTb, Bm], ps_br)
                        mm(C, ps_bi[:], uTb[:, kc, tsl], Bm[:, kc, 1, :], True, True, [uTb, Bm], ps_bi)
                        zr, zi = zre[nz % 2], zim[nz % 2]
                        nz += 1
                        P.op("dve", lambda e, csl=csl: e.tensor_tensor(out=tf[0][:], in0=ps_br[:], in1=Atre[:, csl], op=ALU.mult),
                             reads=[ps_br, Atre], writes=[tf[0]])
                        P.op("dve", lambda e, csl=csl: e.tensor_tensor(out=tf[1][:], in0=ps_bi[:], in1=Atim[:, csl], op=ALU.mult),
                             reads=[ps_bi, Atim], writes=[tf[1]])
                        P.op("pool", lambda e, zr=zr: e.tensor_tensor(out=zr[:], in0=tf[0][:], in1=tf[1][:], op=ALU.subtract),
                             reads=[tf[0], tf[1]], writes=[zr])
                        P.op("dve", lambda e, csl=csl: e.tensor_tensor(out=tf[2][:], in0=ps_bi[:], in1=Atre[:, csl], op=ALU.mult),
                             reads=[ps_bi, Atre], writes=[tf[2]])
                        P.op("dve", lambda e, csl=csl: e.tensor_tensor(out=tf[3][:], in0=ps_br[:], in1=Atim[:, csl], op=ALU.mult),
                             reads=[ps_br, Atim], writes=[tf[3]])
                        P.op("pool", lambda e, zi=zi: e.tensor_tensor(out=zi[:], in0=tf[2][:], in1=tf[3][:], op=ALU.add),
                             reads=[tf[2], tf[3]], writes=[zi])
                        for scl in range(4):
                            mm(C, ps_cr[:, scl * 128:(scl + 1) * 128], zr[:, scl * 128:(scl + 1) * 128], tri[:],
                               True, True, [zr, tri], ps_cr)
                        for scl in range(4):
                            mm(C, ps_ci[:, scl * 128:(scl + 1) * 128], zi[:, scl * 128:(scl + 1) * 128], tri[:],
                               True, True, [zi, tri], ps_ci)
                        hr_b = hcr[:, sc0:sc0 + 4].unsqueeze(2).to_broadcast([128, 4, 128])
                        hi_b = hci[:, sc0:sc0 + 4].unsqueeze(2).to_broadcast([128, 4, 128])
                        P.op("dve", lambda e, hr_b=hr_b: e.tensor_tensor(out=v3(tf[4]), in0=v3(ps_cr), in1=hr_b, op=ALU.add),
                             reads=[ps_cr, hcr], writes=[tf[4]])
                        P.op("dve", lambda e, hi_b=hi_b: e.tensor_tensor(out=v3(tf[5]), in0=v3(ps_ci), in1=hi_b, op=ALU.add),
                             reads=[ps_ci, hci], writes=[tf[5]])
                        bsl = slice(sc0, sc0 + 4)
                        P.op("dve", lambda e, bsl=bsl: e.tensor_tensor(out=v3(tf[0]), in0=v3(tf[4]), in1=Bre[:, bsl, :], op=ALU.mult),
                             reads=[tf[4], Bre], writes=[tf[0]])
                        P.op("pool", lambda e, bsl=bsl: e.tensor_tensor(out=v3(tf[1]), in0=v3(tf[5]), in1=Bim[:, bsl, :], op=ALU.mult),
                             reads=[tf[5], Bim], writes=[tf[1]])
                        P.op("dve", lambda e: e.tensor_tensor(out=tf[6][:], in0=tf[0][:], in1=tf[1][:], op=ALU.subtract),
                             reads=[tf[0], tf[1]], writes=[tf[6]])
                        P.op("pool", lambda e, bsl=bsl: e.tensor_tensor(out=v3(tf[2]), in0=v3(tf[5]), in1=Bre[:, bsl, :], op=ALU.mult),
                             reads=[tf[5], Bre], writes=[tf[2]])
                        P.op("dve", lambda e, bsl=bsl: e.tensor_tensor(out=v3(tf[3]), in0=v3(tf[4]), in1=Bim[:, bsl, :], op=ALU.mult),
                             reads=[tf[4], Bim], writes=[tf[3]])
                        P.op("pool", lambda e: e.tensor_tensor(out=tf[7][:], in0=tf[2][:], in1=tf[3][:], op=ALU.add),
                             reads=[tf[2], tf[3]], writes=[tf[7]])
                        P.op("act", lambda e: e.activation(out=srb[:], in_=v3(tf[6]), func=AF.Copy), reads=[tf[6]], writes=[srb])
                        P.op("act", lambda e: e.activation(out=sib[:], in_=v3(tf[7]), func=AF.Copy, scale=-1.0),
                             reads=[tf[7]], writes=[sib])
                        sl_r = v3(tf[6])[:, :, 127]
                        sl_i = v3(tf[7])[:, :, 127]
                        P.op("dve", lambda e, kc=kc, bsl=bsl, sl_r=sl_r: e.tensor_tensor(out=hs[:, 0, :], in0=sm[:, LR, bsl], in1=sl_r, op=ALU.mult),
                             reads=[tf[6], sm], writes=[hs])
                        P.op("dve", lambda e, bsl=bsl, sl_i=sl_i: e.tensor_tensor(out=hs[:, 1, :], in0=sm[:, LI, bsl], in1=sl_i, op=ALU.mult),
                             reads=[tf[7], sm], writes=[hs])
                        P.op("dve", lambda e, bsl=bsl: e.tensor_tensor(out=hcr[:, bsl], in0=hs[:, 0, :], in1=hs[:, 1, :], op=ALU.subtract),
                             reads=[hs], writes=[hcr])
                        P.op("dve", lambda e, bsl=bsl, sl_i=sl_i: e.tensor_tensor(out=hs[:, 2, :], in0=sm[:, LR, bsl], in1=sl_i, op=ALU.mult),
                             reads=[tf[7], sm], writes=[hs])
                        P.op("dve", lambda e, bsl=bsl, sl_r=sl_r: e.tensor_tensor(out=hs[:, 3, :], in0=sm[:, LI, bsl], in1=sl_r, op=ALU.mult),
                             reads=[tf[6], sm], writes=[hs])
                        P.op("dve", lambda e, bsl=bsl: e.tensor_tensor(out=hci[:, bsl], in0=hs[:, 2, :], in1=hs[:, 3, :], op=ALU.add),
                             reads=[hs], writes=[hci])
                        for scl in range(4):
                            mm(C, ps_y[:, 0:128], Cm[:, sc0 + scl, 0, :], srb[:, scl, :], scl == 0, False, [Cm, srb], ps_y)
                        for scl in range(4):
                            mm(C, ps_y[:, 0:128], Cm[:, sc0 + scl, 1, :], sib[:, scl, :], False, scl == 3, [Cm, sib], ps_y)
                        P.op("dve", lambda e, kc=kc, tsl=tsl: e.scalar_tensor_tensor(
                            out=uT[:, kc, tsl], in0=uT[:, kc, tsl], scalar=dg[:, kc:kc + 1], in1=ps_y[:, 0:128],
                            op0=ALU.mult, op1=ALU.add), reads=[uT, dg, ps_y], writes=[uT])
                for kc in range(6):
                    for q in range(S // 512):
                        qs = slice(q * 512, (q + 1) * 512)
                        yv = uT[:, kc, qs]
                        P.op("pool", lambda e, yv=yv: e.tensor_tensor(out=tf[0][:], in0=yv, in1=yv, op=ALU.mult),
                             reads=[uT], writes=[tf[0]])
                        P.op("dve", lambda e: e.tensor_scalar(out=tf[1][:], in0=tf[0][:], scalar1=0.044715, scalar2=1.0,
                                                              op0=ALU.mult, op1=ALU.add), reads=[tf[0]], writes=[tf[1]])
                        P.op("pool", lambda e, yv=yv: e.tensor_tensor(out=tf[2][:], in0=tf[1][:], in1=yv, op=ALU.mult),
                             reads=[tf[1], uT], writes=[tf[2]])
                        P.op("act", lambda e: e.activation(out=tf[3][:], in_=tf[2][:], func=AF.Sigmoid, scale=1.5957691216057308),
                             reads=[tf[2]], writes=[tf[3]])
                        P.op("dve", lambda e, yv=yv, kc=kc, qs=qs: e.tensor_tensor(out=uTb[:, kc, qs], in0=yv, in1=tf[3][:], op=ALU.mult),
                             reads=[uT, tf[3]], writes=[uTb])
                ne = 0
                for q in range(S // 512):
                    qs = slice(q * 512, (q + 1) * 512)
                    for oc in range(6):
                        for kc in range(6):
                            mm(C, ps_v[:], gw[:, oc // 2, kc, (oc % 2) * 128:(oc % 2) * 128 + 128], uTb[:, kc, qs],
                               kc == 0, kc == 5, [gw, uTb], ps_v)
                        for kc in range(6):
                            mm(C, ps_g[:], gw[:, 3 + oc // 2, kc, (oc % 2) * 128:(oc % 2) * 128 + 128], uTb[:, kc, qs],
                               kc == 0, kc == 5, [gw, uTb], ps_g)
                        P.op("act", lambda e, oc=oc: e.activation(out=tf[4][:], in_=ps_g[:], func=AF.Sigmoid,
                                                                  bias=dg[:, 12 + oc:13 + oc], scale=1.0),
                             reads=[ps_g, dg], writes=[tf[4]])
                        ys = ystg[ne % 2]
                        ne += 1
                        P.op("dve", lambda e, oc=oc, ys=ys: e.scalar_tensor_tensor(
                            out=ys[:], in0=ps_v[:], scalar=dg[:, 6 + oc:7 + oc], in1=tf[4][:], op0=ALU.add, op1=ALU.mult),
                            reads=[ps_v, dg, tf[4]], writes=[ys])
                        P.op("sp", lambda e, oc=oc, ys=ys, s=s, qs=qs: e.dma_start(
                            out=scr["YB"][s, oc * 128:(oc + 1) * 128, qs], in_=ys[:]),
                            reads=[ys], writes=[scr["YBb"]], dma=True)
    P.barrier()


def merge_phase(C, dr, l, nseq, ntile, modv, gains, xT, xres, xview, scr):
    P = C.P
    wd = dr["w_inT"]
    with ExitStack() as st:
        hT = [C.sb(st, [128, NKC, TT], BF16, "mhT%d" % i) for i in range(2)]
        yy = [C.sb(st, [128, NKC, TT], BF16, "myy%d" % i) for i in range(2)]
        mg_ = C.sb(st, [128, NKC, TT], BF16, "merged", n=NKC)
        gwb = [C.sb(st, [128, NKC, GW], BF16, "mgw%d" % i) for i in range(6)]
        bwb = [C.sb(st, [128, NKC, GW], BF16, "mbw%d" % i) for i in range(2)]
        owb = [C.sb(st, [128, NKC, GW], BF16, "mow%d" % i) for i in range(2)]
        sgb = [C.sb(st, [128, TT], F32, "msg%d" % i) for i in range(3)]
        tb = [C.sb(st, [128, TT], F32, "mtb%d" % i) for i in range(3)]
        acc = [C.sb(st, [128, TT], F32, "macc%d" % i) for i in range(2)]
        xr = [C.sb(st, [128, TT], F32, "mxr%d" % i) for i in range(4)]
        AB = C.sb(st, [128, nseq, 3, NKC], F32, "mAB")
        ps_gt = [C.ps(st, [128, TT], F32, "ps_gt%d" % i) for i in range(2)]
        ps_b = [C.ps(st, [128, TT], F32, "ps_b%d" % i) for i in range(2)]
        ps_o = [C.ps(st, [128, TT], F32, "ps_mo%d" % i) for i in range(2)]
        mod_vectors(C, AB, modv, gains, 1, nseq, 1.0)
        lg = [(lambda b, g=28 + j * 8 + m: P.op("pool", lambda e: e.dma_start(out=b[:], in_=wd[l, g]), writes=[b], dma=True))
              for _ in range(ntile) for m in range(8) for j in range(3)]
        lb = [(lambda b, m=m: P.op("pool", lambda e: e.dma_start(out=b[:], in_=dr["w_br"][l, m]), writes=[b], dma=True))
              for _ in range(ntile) for m in range(8)]
        lo = [(lambda b, m=m: P.op("pool", lambda e: e.dma_start(out=b[:], in_=dr["w_outT"][l, m]), writes=[b], dma=True))
              for _ in range(ntile) for m in range(8)]
        sg_, sb_, so_ = Stream(gwb, lg, depth=4), Stream(bwb, lb), Stream(owb, lo)
        KOFF = (0, 8, 14)
        KN = (8, 6, 2)

        def load_acts(i):
            s, ti = divmod(i, S // TT)
            tsl = slice(ti * TT, (ti + 1) * TT)
            h, y = hT[i % 2], yy[i % 2]
            P.op("sp", lambda e: e.dma_start(out=h[:], in_=scr["HT"][s].rearrange("(c p) t -> p c t", p=128)[:, :, tsl]),
                 reads=[(scr["HTb"], i)], writes=[h], dma=True)
            P.op("sp", lambda e: e.dma_start(out=y[:, 0:8, :], in_=scr["YA"][s].rearrange("(c p) t -> p c t", p=128)[:, :, tsl]),
                 reads=[scr["YAb"]], writes=[y], dma=True)
            P.op("sp", lambda e: e.dma_start(out=y[:, 8:14, :], in_=scr["YB"][s].rearrange("(c p) t -> p c t", p=128)[:, :, tsl]),
                 reads=[scr["YBb"]], writes=[y], dma=True)
            P.op("sp", lambda e: e.dma_start(out=y[:, 14:16, :], in_=scr["YC"][s].rearrange("(c p) t -> p c t", p=128)[:, :, tsl]),
                 reads=[scr["YCb"]], writes=[y], dma=True)

        load_acts(0)
        for i in range(ntile):
            b = i // (S // TT)
            if i + 1 < ntile:
                load_acts(i + 1)
            h, y = hT[i % 2], yy[i % 2]
            for m8 in range(8):
                gws = [sg_.get((i * 8 + m8) * 3 + j) for j in range(3)]
                bw = sb_.get(i * 8 + m8)
                for mi in range(2):
                    m = 2 * m8 + mi
                    csl = slice(mi * 128, (mi + 1) * 128)
                    for j in range(3):
                        pg, pb = ps_gt[(3 * m + j) % 2], ps_b[(3 * m + j) % 2]
                        for c in range(NKC):
                            mm(C, pg[:], gws[j][:, c, csl], h[:, c, :], c == 0, c == NKC - 1, [gws[j], h], pg)
                        for kc in range(KN[j]):
                            mm(C, pb[:], bw[:, KOFF[j] + kc, csl], y[:, KOFF[j] + kc, :], kc == 0, kc == KN[j] - 1,
                               [bw, y], pb)
                        sgt, tj = sgb[j], tb[j]
                        P.op("act", lambda e, sgt=sgt, pg=pg: e.activation(out=sgt[:], in_=pg[:], func=AF.Sigmoid),
                             reads=[pg], writes=[sgt])
                        P.op("dve", lambda e, sgt=sgt, tj=tj, pb=pb: e.tensor_tensor(out=tj[:], in0=pb[:], in1=sgt[:], op=ALU.mult),
                             reads=[pb, sgt], writes=[tj])
                    a_ = acc[m % 2]
                    P.op("pool", lambda e, a_=a_: e.tensor_tensor(out=a_[:], in0=tb[0][:], in1=tb[1][:], op=ALU.add),
                         reads=[tb[0], tb[1]], writes=[a_])
                    P.op("pool", lambda e, a_=a_, m=m: e.tensor_tensor(out=mg_[:, m, :], in0=a_[:], in1=tb[2][:], op=ALU.add),
                         reads=[a_, tb[2]], writes=[(mg_, m)])
            for m8 in range(8):
                ow = so_.get(i * 8 + m8)
                for mi in range(2):
                    m = 2 * m8 + mi
                    po = ps_o[m % 2]
                    r = xr[m % 4]
                    P.op("sp", lambda e, i=i, m=m, r=r: e.dma_start(out=r[:], in_=xview(xT, i)[:, m, :]),
                         reads=[(xres, i)], writes=[r], dma=True)
                    for c in range(NKC):
                        mm(C, po[:], ow[:, c, mi * 128:(mi + 1) * 128], mg_[:, c, :], c == 0, c == NKC - 1,
                           [ow, (mg_, c)], po)
                    P.op("dve", lambda e, m=m, r=r, po=po, b=b: e.scalar_tensor_tensor(
                        out=r[:], in0=po[:], scalar=AB[:, b, 2, m:m + 1], in1=r[:], op0=ALU.mult, op1=ALU.add),
                        reads=[po, AB, r], writes=[r])
                    P.op("sp", lambda e, i=i, m=m, r=r: e.dma_start(out=xview(xT, i)[:, m, :], in_=r[:]),
                         reads=[r], writes=[(xres, i)], dma=True)
    P.barrier()


def dil_phase(C, dr, l, nseq, scr):
    P = C.P
    DILS = (1, 4, 16)
    NEG = -30000.0
    with ExitStack() as st:
        tmp = C.sb(st, [128, 128], F32, "dtmp")
        pst = C.ps(st, [128, 512], F32, "dpst")
        gv = C.sb(st, [128, 2], F32, "dgv")
        bones = C.sb(st, [128, 128], BF16, "bones")
        ones64 = C.sb(st, [128, 64], BF16, "ones64")
        d0i = C.sb(st, [128, 256], mybir.dt.int32, "d0i")
        d0 = C.sb(st, [128, 256], F32, "d0")
        bias = C.sb(st, [128, 12, 256], F32, "dbias")
        qraw = C.sb(st, [128, 2, S], F32, "qraw")
        kraw = C.sb(st, [128, 2, S], F32, "kraw")
        qn = C.sb(st, [128, 2, S], BF16, "qn")
        kn = C.sb(st, [128, 2, S], BF16, "kn")
        vb = C.sb(st, [128, 2, S], BF16, "vb")
        vtok = C.sb(st, [128, 2, 16, 128], BF16, "vtok")
        acc = C.sb(st, [64, 2, 4, S], F32, "dacc")
        sq = [C.sb(st, [128, 512], BF16, "dsq%d" % i) for i in range(2)]
        msb = [C.sb(st, [128, 512], F32, "dms%d" % i) for i in range(2)]
        rsb = [C.sb(st, [128, 512], F32, "drs%d" % i) for i in range(2)]
        stmp = [C.sb(st, [128, 256], F32, "dst%d" % i) for i in range(2)]
        pT = [C.sb(st, [128, 256], BF16, "dpT%d" % i) for i in range(3)]
        rec = C.sb(st, [64, S], F32, "drec")
        ob = C.sb(st, [64, S], BF16, "dob")
        ps_n = C.ps(st, [128, 512], F32, "ps_dn")
        ps_s = [C.ps(st, [128, 512], F32, "ps_ds%d" % i) for i in range(2)]
        ps_nd = [C.ps(st, [128, 512], F32, "ps_dnd%d" % i) for i in range(2)]
        ps_vt = C.ps(st, [128, 1024], BF16, "ps_dvt")

        load_vec_fm(C, st, gv, gv[:, :], dr["dil_g"][l], 2, tmp, pst, C.ident_f)
        P.op("dve", lambda e: e.tensor_scalar(out=gv[:, 0:1], in0=gv[:, 0:1], scalar1=0.125, scalar2=None, op0=ALU.mult),
             reads=[gv], writes=[gv])
        P.op("pool", lambda e: e.memset(bones[:], 0.0), writes=[bones])
        P.op("pool", lambda e: e.memset(bones[0:64, 0:64], 1.0), writes=[bones])
        P.op("pool", lambda e: e.memset(bones[64:128, 64:128], 1.0), writes=[bones])
        P.op("pool", lambda e: e.memset(ones64[:], 1.0), writes=[ones64])
        P.op("pool", lambda e: e.iota(d0i[:], pattern=[[1, 256]], base=0, channel_multiplier=-1), writes=[d0i])
        P.op("dve", lambda e: e.tensor_copy(out=d0[:], in_=d0i[:]), reads=[d0i], writes=[d0])
        for hg in range(12):
            slope = 2.0 ** (-8.0 * (hg + 1) / 12.0)
            dil = DILS[hg // 4]
            P.op("dve", lambda e, hg=hg, v=-slope * dil: e.tensor_scalar(out=bias[:, hg, :], in0=d0[:], scalar1=v, scalar2=None,
                                                                         op0=ALU.mult), reads=[d0], writes=[bias])
            P.op("pool", lambda e, hg=hg: e.affine_select(out=bias[:, hg, :], in_=bias[:, hg, :], pattern=[[1, 256]],
                                                          compare_op=ALU.is_ge, fill=C.fill(NEG), base=0, channel_multiplier=-1),
                 reads=[bias], writes=[bias])
            P.op("pool", lambda e, hg=hg: e.affine_select(out=bias[:, hg, :], in_=bias[:, hg, :], pattern=[[-1, 256]],
                                                          compare_op=ALU.is_ge, fill=C.fill(NEG), base=128, channel_multiplier=1),
                 reads=[bias], writes=[bias])

        nsc = 0
        for s in range(nseq):
            pc = scr["PC"][s]
            for gi in range(3):
                dil = DILS[gi]
                nb = S // dil // 128
                for (dst, r0, eng) in ((qraw, 0, "sp"), (kraw, 768, "sp")):
                    P.op(eng, lambda e, dst=dst, r0=r0: e.dma_start(
                        out=dst[:], in_=pc[r0 + gi * 256:r0 + gi * 256 + 256, :].rearrange("(c p) t -> p c t", p=128)),
                        reads=[scr["PCb"]], writes=[dst], dma=True)
                P.op("pool", lambda e: e.dma_start(
                    out=vb[:], in_=pc[1536 + gi * 256:1536 + gi * 256 + 256, :].rearrange("(c p) t -> p c t", p=128)),
                    reads=[scr["PCb"]], writes=[vb], dma=True)
                k2 = 0
                for (raw, nrm, gcol) in ((qraw, qn, 0), (kraw, kn, 1)):
                    for c2 in range(2):
                        for q4 in range(S // 512):
                            qs = slice(q4 * 512, (q4 + 1) * 512)
                            sq_, ms_, rs_ = sq[k2 % 2], msb[k2 % 2], rsb[k2 % 2]
                            k2 += 1
                            P.op("act", lambda e, raw=raw, c2=c2, qs=qs, sq_=sq_: e.activation(out=sq_[:], in_=raw[:, c2, qs], func=AF.Square),
                                 reads=[raw], writes=[sq_])
                            mm(C, ps_n[:], bones[:], sq_[:], True, True, [bones, sq_], ps_n)
                            P.op("act", lambda e, ms_=ms_: e.activation(out=ms_[:], in_=ps_n[:], func=AF.Sqrt, bias=C.epsc[:, 0:1],
                                                                        scale=1.0 / 64), reads=[ps_n, C.epsc], writes=[ms_])
                            P.op("dve", lambda e, ms_=ms_, rs_=rs_: e.reciprocal(out=rs_[:], in_=ms_[:]), reads=[ms_], writes=[rs_])
                            P.op("dve", lambda e, raw=raw, nrm=nrm, c2=c2, qs=qs, rs_=rs_, gcol=gcol: e.scalar_tensor_tensor(
                                out=nrm[:, c2, qs], in0=raw[:, c2, qs], scalar=gv[:, gcol:gcol + 1], in1=rs_[:],
                                op0=ALU.mult, op1=ALU.mult), reads=[raw, gv, rs_], writes=[nrm])
                for c2 in range(2):
                    for r in range(dil):
                        for n in range(nb):
                            bi = r * nb + n
                            ks = slice(r + dil * 128 * n, r + dil * 128 * n + dil * 127 + 1, dil)
                            P.op("pe", lambda e, c2=c2, ks=ks, bi=bi: e.transpose(ps_vt[:, (bi % 8) * 128:(bi % 8 + 1) * 128], vb[:, c2, ks], C.ident_bf[:]),
                                 reads=[vb, C.ident_bf], writes=[ps_vt])
                            P.op("act", lambda e, c2=c2, bi=bi: e.activation(out=vtok[:, c2, bi, :], in_=ps_vt[:, (bi % 8) * 128:(bi % 8 + 1) * 128], func=AF.Copy),
                                 reads=[ps_vt], writes=[vtok])
                for hh in range(4):
                    c2, pb = hh // 2, (hh % 2) * 64
                    hg = gi * 4 + hh
                    for r in range(dil):
                        prev = None
                        for n in range(nb):
                            bi = r * nb + n
                            nq = 256 if n + 1 < nb else 128
                            t0 = r + dil * 128 * n
                            ks = slice(t0, t0 + dil * 127 + 1, dil)
                            qsl = slice(t0, t0 + dil * (nq - 1) + 1, dil)
                            pss = ps_s[nsc % 2]
                            stp = stmp[nsc % 2]
                            cur = pT[nsc % 3]
                            psnd = ps_nd[nsc % 2]
                            nsc += 1
                            mm(C, pss[:, 0:nq], kn[pb:pb + 64, c2, ks], qn[pb:pb + 64, c2, qsl], True, True, [kn, qn], pss)
                            P.op("dve", lambda e, pss=pss, stp=stp, nq=nq, hg=hg: e.tensor_tensor(
                                out=stp[:, 0:nq], in0=pss[:, 0:nq], in1=bias[:, hg, 0:nq], op=ALU.add),
                                reads=[pss, bias], writes=[stp])
                            P.op("act", lambda e, stp=stp, cur=cur, nq=nq: e.activation(out=cur[:, 0:nq], in_=stp[:, 0:nq], func=AF.Exp),
                                 reads=[stp], writes=[cur])
                            for di, lhs_of in enumerate((lambda b_: vtok[:, c2, b_, pb:pb + 64], lambda b_: ones64[:])):
                                o_ap = psnd[0:64, di * 128:(di + 1) * 128]
                                if prev is not None:
                                    mm(C, o_ap, lhs_of(bi - 1), prev[:, 128:256], True, False, [vtok, ones64, prev], psnd)
                                mm(C, o_ap, lhs_of(bi), cur[:, 0:128], prev is None, True, [vtok, ones64, cur], psnd)
                            a_view = acc[:, :, hh, ks]
                            p_view = psnd[0:64, 0:256].rearrange("p (a j) -> p a j", j=128)
                            if gi == 0:
                                P.op("dve", lambda e, a_view=a_view, p_view=p_view: e.tensor_copy(out=a_view, in_=p_view),
                                     reads=[psnd], writes=[acc])
                            else:
                                P.op("dve", lambda e, a_view=a_view, p_view=p_view: e.tensor_tensor(
                                    out=a_view, in0=p_view, in1=a_view, op=ALU.add), reads=[psnd, acc], writes=[acc])
                            prev = cur
            for hh in range(4):
                P.op("dve", lambda e, hh=hh: e.reciprocal(out=rec[:], in_=acc[:, 1, hh, :]), reads=[acc], writes=[rec])
                P.op("dve", lambda e, hh=hh: e.tensor_tensor(out=ob[:], in0=acc[:, 0, hh, :], in1=rec[:], op=ALU.mult),
                     reads=[acc, rec], writes=[ob])
                P.op("sp", lambda e, hh=hh, s=s: e.dma_start(out=scr["YC"][s, hh * 64:(hh + 1) * 64, :], in_=ob[:]),
                     reads=[ob], writes=[scr["YCb"]], dma=True)
    P.barrier()


def gdn_phase(C, dr, l, nseq, scr):
    P = C.P
    NT = S // 128
    NEG = -30000.0
    with ExitStack() as st:
        cw = C.sb(st, [128, 24, 4], F32, "cw")
        ad = C.sb(st, [128, 16], F32, "gad")
        onm = C.sb(st, [128, 1], F32, "gon")
        one1 = C.sb(st, [128, 1], F32, "one1")
        triF = C.sb(st, [128, 128], F32, "triF")
        onesF = C.sb(st, [128, 128], F32, "onesF")
        mneg = C.sb(st, [128, 128], F32, "mneg")
        smask = C.sb(st, [128, 128], F32, "smask")
        ba = C.sb(st, [128, NT, 16], F32, "gba")
        sc = C.sb(st, [128, 10, NT, 8], F32, "gsc")
        BETA, G, GC, GL, EGC, NGC, EGL, KDEC, NEGC, NBETA = range(10)
        raw = [C.sb(st, [128, S + 3], F32, "graw%d" % i) for i in range(2)]
        cacc = [C.sb(st, [128, S], F32, "gcacc%d" % i) for i in range(2)]
        vTb = C.sb(st, [128, S], BF16, "gvTb")
        DT = C.sb(st, [128, NT, 128], F32, "gDT", n=NT)
        Gb = C.sb(st, [128, NT, 128], F32, "gGb", n=NT)
        Pp = [C.sb(st, [128, NT, 256], BF16, "gP%d" % i, n=NT) for i in range(2)]
        xob = [C.sb(st, [128, 4, 256], BF16, "gxo%d" % i) for i in range(2)]
        msk = C.sb(st, [128, 7, 2, 128], BF16, "gmsk")
        sqb = [C.sb(st, [128, 512], BF16, "gsq%d" % i) for i in range(2)]
        rnb = [C.sb(st, [128, 512], F32, "grn%d" % i) for i in range(2)]
        qT = [C.sb(st, [128, S], BF16, "gqT%d" % i) for i in range(2)]
        kT = [C.sb(st, [128, S], BF16, "gkT%d" % i) for i in range(2)]
        kd = [C.sb(st, [128, NT, 128], BF16, "gkd%d" % i, n=NT) for i in range(2)]
        vtok = [C.sb(st, [128, NT, 128], BF16, "gvtok%d" % i, n=NT) for i in range(2)]
        siluz = [C.sb(st, [128, S], F32, "gsz%d" % i) for i in range(2)]
        RRT = [C.sb(st, [128, NT, 256], BF16, "gRRT%d" % i, n=NT) for i in range(2)]
        attnT = [C.sb(st, [128, NT, 128], BF16, "gattn%d" % i, n=NT) for i in range(2)]
        hS = [C.sb(st, [128, 128], F32, "ghS%d" % i) for i in range(2)]
        hSb = [C.sb(st, [128, 128], BF16, "ghSb%d" % i) for i in range(2)]
        yst = [C.sb(st, [128, S], BF16, "gyst%d" % i) for i in range(2)]
        rb = [[C.sb(st, [128, 128], BF16, "grb%d%d" % (i, j)) for j in range(2)] for i in range(2)]
        vn = [[C.sb(st, [128, 128], BF16, "gvn%d%d" % (i, j)) for j in range(2)] for i in range(2)]
        o1 = [[C.sb(st, [128, 128], F32, "go1%d%d" % (i, j)) for j in range(2)] for i in range(2)]
        of = [[C.sb(st, [128, 128], F32, "gof%d%d" % (i, j)) for j in range(2)] for i in range(2)]
        osq = [[C.sb(st, [128, 128], F32, "gosq%d%d" % (i, j)) for j in range(2)] for i in range(2)]
        onb = [[C.sb(st, [128, 128], BF16, "gonb%d%d" % (i, j)) for j in range(2)] for i in range(2)]
        ssq = [[C.sb(st, [128, 4], F32, "gssq%d%d" % (i, j)) for j in range(2)] for i in range(2)]
        psA = C.ps(st, [128, 512], F32, "gpsA")
        psB = C.ps(st, [128, 512], F32, "gpsB")
        psC = C.ps(st, [128, 512], F32, "gpsC")
        psD = C.ps(st, [128, 512], F32, "gpsD")
        psE = C.ps(st, [128, 512], F32, "gpsE")
        psF = C.ps(st, [128, 512], F32, "gpsF")
        psTs = [C.ps(st, [128, 1024], BF16, "gpsT%d" % i) for i in range(2)]
        c4 = lambda k: slice(k * 128, (k + 1) * 128)

        P.op("sp", lambda e: e.dma_start(out=cw[:], in_=dr["gdn_conv"][l]), writes=[cw], dma=True)
        P.op("sp", lambda e: e.dma_start(out=ad[:], in_=dr["gdn_ad"][l]), writes=[ad], dma=True)
        P.op("sp", lambda e: e.dma_start(out=onm[:], in_=dr["gdn_on"][l]), writes=[onm], dma=True)
        P.op("pool", lambda e: e.dma_start(out=msk[:], in_=dr["gdn_masks"]), writes=[msk], dma=True)
        P.op("pool", lambda e: e.memset(one1[:], 1.0), writes=[one1])
        P.op("pool", lambda e: e.memset(onesF[:], 1.0), writes=[onesF])
        P.op("pool", lambda e: e.memset(triF[:], 1.0), writes=[triF])
        P.op("pool", lambda e: e.affine_select(out=triF[:], in_=triF[:], pattern=[[1, 128]], compare_op=ALU.is_ge,
                                               fill=C.fill(0.0), base=0, channel_multiplier=-1), reads=[triF], writes=[triF])
        P.op("pool", lambda e: e.memset(mneg[:], 0.0), writes=[mneg])
        P.op("pool", lambda e: e.affine_select(out=mneg[:], in_=mneg[:], pattern=[[1, 128]], compare_op=ALU.is_ge,
                                               fill=C.fill(NEG), base=0, channel_multiplier=-1), reads=[mneg], writes=[mneg])
        P.op("pool", lambda e: e.memset(smask[:], 1.0), writes=[smask])
        P.op("pool", lambda e: e.affine_select(out=smask[:], in_=smask[:], pattern=[[1, 128]], compare_op=ALU.is_ge,
                                               fill=C.fill(0.0), base=-1, channel_multiplier=-1), reads=[smask], writes=[smask])
        for rw in raw:
            P.op("pool", lambda e, rw=rw: e.memset(rw[:, 0:3], 0.0), writes=[rw])
        P.op("act", lambda e: e.activation(out=ad[:, 0:8], in_=ad[:, 0:8], func=AF.Exp), reads=[ad], writes=[ad])

        def g1(s, h, sl):
            pa = scr["PA"][s]
            k2 = 0
            for wi, (row0, kind) in enumerate(((h * 128, "q"), (1024 + h * 128, "k"), (2048 + h * 128, "v"), (3072 + h * 128, "z"))):
                rw = raw[wi % 2]
                ca = cacc[wi % 2]
                P.op("sp", lambda e, rw=rw, row0=row0: e.dma_start(out=rw[:, 3:], in_=pa[row0:row0 + 128, :]),
                     reads=[scr["PAb"]], writes=[rw], dma=True)
                if kind == "z":
                    P.op("act", lambda e, rw=rw: e.activation(out=siluz[sl][:], in_=rw[:, 3:], func=AF.Silu),
                         reads=[rw], writes=[siluz[sl]])
                    continue
                ch = row0 // 128
                P.op("dve", lambda e, rw=rw, ca=ca, ch=ch: e.tensor_scalar(out=ca[:], in0=rw[:, 0:S], scalar1=cw[:, ch, 0:1],
                                                                         scalar2=None, op0=ALU.mult), reads=[rw, cw], writes=[ca])
                for k in range(1, 4):
                    P.op("dve", lambda e, rw=rw, ca=ca, ch=ch, k=k: e.scalar_tensor_tensor(
                        out=ca[:], in0=rw[:, k:k + S], scalar=cw[:, ch, k:k + 1], in1=ca[:], op0=ALU.mult, op1=ALU.add),
                        reads=[rw, cw, ca], writes=[ca])
                if kind == "v":
                    P.op("act", lambda e, ca=ca: e.activation(out=vTb[:], in_=ca[:], func=AF.Silu), reads=[ca], writes=[vTb])
                    continue
                P.op("act", lambda e, ca=ca: e.activation(out=ca[:], in_=ca[:], func=AF.Silu), reads=[ca], writes=[ca])
                dstT = qT[sl] if kind == "q" else kT[sl]
                scl = 128.0 ** -0.5 if kind == "q" else 1.0
                for q4 in range(S // 512):
                    qs = slice(q4 * 512, (q4 + 1) * 512)
                    sq_, rn_ = sqb[k2 % 2], rnb[k2 % 2]
                    k2 += 1
                    P.op("act", lambda e, ca=ca, qs=qs, sq_=sq_: e.activation(out=sq_[:], in_=ca[:, qs], func=AF.Square),
                         reads=[ca], writes=[sq_])
                    mm(C, psF[:], C.ones_bf[:], sq_[:], True, True, [C.ones_bf, sq_], psF)
                    P.op("act", lambda e, rn_=rn_: e.activation(out=rn_[:], in_=psF[:], func=AF.Sqrt, bias=C.epsc[:, 0:1], scale=1.0),
                         reads=[C.epsc], writes=[psF, rn_])
                    P.op("dve", lambda e, rn_=rn_: e.reciprocal(out=rn_[:], in_=rn_[:]), reads=[rn_], writes=[rn_])
                    P.op("dve", lambda e, ca=ca, qs=qs, rn_=rn_, dstT=dstT, scl=scl: e.scalar_tensor_tensor(
                        out=dstT[:, qs], in0=ca[:, qs], scalar=scl, in1=rn_[:], op0=ALU.mult, op1=ALU.mult),
                        reads=[ca, rn_], writes=[dstT])
            for tc in range(NT):
                pt_ = psTs[tc % 2]
                P.op("pe", lambda e, tc=tc, pt_=pt_: e.transpose(pt_[:, 0:128], vTb[:, c4(tc)], C.ident_bf[:]),
                     reads=[vTb, C.ident_bf], writes=[pt_])
                P.op("act", lambda e, tc=tc, pt_=pt_: e.activation(out=vtok[sl][:, tc, :], in_=pt_[:, 0:128], func=AF.Copy),
                     writes=[pt_, (vtok[sl], tc)])
            for tc in range(NT):
                pt_ = psTs[tc % 2]
                P.op("pe", lambda e, tc=tc, pt_=pt_: e.transpose(pt_[:, 0:128], kT[sl][:, c4(tc)], C.ident_bf[:]),
                     reads=[kT[sl], C.ident_bf], writes=[pt_])
                P.op("act", lambda e, tc=tc, pt_=pt_: e.activation(out=kd[sl][:, tc, :], in_=pt_[:, 0:128], func=AF.Identity,
                                                                  scale=sc[:, KDEC, tc, h:h + 1]),
                     reads=[sc], writes=[pt_, (kd[sl], tc)])
            for tc in range(NT + 1):
                if tc < NT:
                    P.op("dve", lambda e, tc=tc: e.tensor_scalar(out=Gb[:, tc, :], in0=onesF[:], scalar1=sc[:, G, tc, h:h + 1],
                                                                 scalar2=None, op0=ALU.mult), reads=[onesF, sc], writes=[(Gb, tc)])
                    pd = psA if tc % 2 == 0 else psD
                    mm(C, pd[:, 0:128], Gb[:, tc, :], triF[:], True, False, [(Gb, tc), triF], pd)
                    mm(C, pd[:, 0:128], C.ident_f[:], mneg[:], False, True, [C.ident_f, mneg], pd)
                    P.op("act", lambda e, tc=tc, pd=pd: e.activation(out=DT[:, tc, :], in_=pd[:, 0:128], func=AF.Exp,
                                                                    bias=sc[:, NGC, tc, h:h + 1], scale=1.0),
                         reads=[sc], writes=[pd, (DT, tc)])
                    pk = psB if tc % 2 == 0 else psC
                    mm(C, pk[:, 0:128], kT[sl][:, c4(tc)], kT[sl][:, c4(tc)], True, True, [kT[sl]], pk)
                    mm(C, pk[:, 128:256], kT[sl][:, c4(tc)], qT[sl][:, c4(tc)], True, True, [kT[sl], qT[sl]], pk)
                if tc >= 1:
                    t = tc - 1
                    pk = psB if t % 2 == 0 else psC
                    P.op("dve", lambda e, t=t, pk=pk: e.scalar_tensor_tensor(
                        out=Pp[0][:, t, 0:128], in0=pk[:, 0:128], scalar=sc[:, NBETA, t, h:h + 1], in1=DT[:, t, :],
                        op0=ALU.mult, op1=ALU.mult), reads=[sc, (DT, t)], writes=[pk, (Pp[0], t)])
                    P.op("dve", lambda e, t=t, pk=pk: e.tensor_tensor(out=attnT[sl][:, t, :], in0=pk[:, 128:256], in1=DT[:, t, :],
                                                                      op=ALU.mult), reads=[(DT, t)], writes=[pk, (attnT[sl], t)])
            for tc in range(NT):
                pt_ = psTs[tc % 2]
                P.op("pe", lambda e, tc=tc, pt_=pt_: e.transpose(pt_[:, 0:128], Pp[0][:, tc, 0:128], C.ident_bf[:]),
                     reads=[(Pp[0], tc), C.ident_bf], writes=[pt_])
                P.op("act", lambda e, tc=tc, pt_=pt_: e.activation(out=Pp[0][:, tc, 128:256], in_=pt_[:, 0:128], func=AF.Copy),
                     writes=[pt_, (Pp[0], tc)])
                P.op("pool", lambda e, tc=tc: e.tensor_copy(out=RRT[sl][:, tc, 0:128], in_=C.ident_bf[:]),
                     reads=[C.ident_bf], writes=[(RRT[sl], tc)])
                P.op("pool", lambda e, tc=tc: e.tensor_copy(out=RRT[sl][:, tc, 128:256], in_=C.ident_bf[:]),
                     reads=[C.ident_bf], writes=[(RRT[sl], tc)])
            items = [(lvl, grp) for lvl in range(7) for grp in range(4)]
            R_ = RRT[sl]

            def my(idx):
                lvl, grp = items[idx]
                xo = xob[idx % 2]
                t0 = 4 * grp
                tr = range(t0, t0 + 4)
                mk = msk[:, lvl, :, :].rearrange("p a j -> p (a j)").unsqueeze(1).to_broadcast([128, 4, 256])
                P.op("pool", lambda e: e.tensor_tensor(out=xo[:], in0=Pp[0][:, t0:t0 + 4, :], in1=mk, op=ALU.mult),
                     reads=[(Pp[0], tr), msk], writes=[xo])
                yb = (psA, psB) if idx % 2 == 0 else (psC, psD)
                for k in range(4):
                    bk = yb[k // 2]
                    o_ = (k % 2) * 256
                    mm(C, bk[:, o_:o_ + 128], xo[:, k, 128:256], R_[:, t0 + k, 0:128], True, True, [xo, (R_, t0 + k)], bk)
                    mm(C, bk[:, o_ + 128:o_ + 256], xo[:, k, 0:128], R_[:, t0 + k, 128:256], True, True, [xo, (R_, t0 + k)], bk)
                for b2 in range(2):
                    bk = yb[b2]
                    t1 = t0 + 2 * b2
                    P.op("act", lambda e, bk=bk, t1=t1: e.activation(
                        out=Pp[1][:, t1:t1 + 2, :].rearrange("p a j -> p (a j)"), in_=bk[:, 0:512], func=AF.Copy),
                        writes=[bk, (Pp[1], (t1, t1 + 1))])

            def za(idx):
                lvl, grp = items[idx]
                t0 = 4 * grp
                zb = (psE, psF)
                for k in range(4):
                    bk = zb[k // 2]
                    o_ = (k % 2) * 256
                    mm(C, bk[:, o_:o_ + 128], R_[:, t0 + k, 128:256], Pp[1][:, t0 + k, 0:128], True, True,
                       [(R_, t0 + k), (Pp[1], t0 + k)], bk)
                    mm(C, bk[:, o_ + 128:o_ + 256], R_[:, t0 + k, 0:128], Pp[1][:, t0 + k, 128:256], True, True,
                       [(R_, t0 + k), (Pp[1], t0 + k)], bk)
                for b2 in range(2):
                    bk = zb[b2]
                    t1 = t0 + 2 * b2
                    rv = R_[:, t1:t1 + 2, :].rearrange("p a j -> p (a j)")
                    P.op("dve", lambda e, bk=bk, rv=rv: e.tensor_tensor(out=rv, in0=bk[:, 0:512], in1=rv, op=ALU.add),
                         writes=[bk, (R_, (t1, t1 + 1))])

            for idx in range(len(items)):
                my(idx)
                if idx >= 1:
                    za(idx - 1)
            za(len(items) - 1)

        def g2(s, heads):
            banks = ((psA, psB, psC), (psD, psE, psF))
            sls = range(len(heads))
            for sl in sls:
                P.op("pool", lambda e, sl=sl: e.memset(hS[sl][:], 0.0), writes=[hS[sl]])
                P.op("pool", lambda e, sl=sl: e.memset(hSb[sl][:], 0.0), writes=[hSb[sl]])
            for tc in range(NT):
                j2 = tc % 2
                for sl in sls:
                    bx = banks[sl][0]
                    mm(C, bx[:, 0:128], kT[sl][:, c4(tc)], hSb[sl][:], True, True, [kT[sl], hSb[sl]], bx)
                    mm(C, bx[:, 128:256], qT[sl][:, c4(tc)], hSb[sl][:], True, True, [qT[sl], hSb[sl]], bx)
                for sl in sls:
                    h = heads[sl]
                    bx = banks[sl][0]
                    r_, o1_ = rb[sl][j2], o1[sl][j2]
                    P.op("dve", lambda e, tc=tc, h=h, r_=r_, bx=bx, sl=sl: e.scalar_tensor_tensor(
                        out=r_[:], in0=bx[:, 0:128], scalar=sc[:, NEGC, tc, h:h + 1], in1=vtok[sl][:, tc, :], op0=ALU.mult, op1=ALU.add),
                        reads=[sc, (vtok[sl], tc)], writes=[bx, r_])
                    P.op("dve", lambda e, tc=tc, h=h, o1_=o1_, bx=bx: e.tensor_scalar(
                        out=o1_[:], in0=bx[:, 128:256], scalar1=sc[:, EGC, tc, h:h + 1], scalar2=None, op0=ALU.mult),
                        reads=[sc], writes=[bx, o1_])
                for sl in sls:
                    by = banks[sl][1]
                    mm(C, by[:, 0:128], RRT[sl][:, tc, 0:128], rb[sl][j2][:], True, True, [(RRT[sl], tc), rb[sl][j2]], by)
                for sl in sls:
                    h = heads[sl]
                    by = banks[sl][1]
                    vn_ = vn[sl][j2]
                    P.op("act", lambda e, tc=tc, h=h, vn_=vn_, by=by: e.activation(out=vn_[:], in_=by[:, 0:128], func=AF.Identity,
                                                                                   scale=sc[:, BETA, tc, h:h + 1]),
                         reads=[sc], writes=[by, vn_])
                for sl in sls:
                    bz = banks[sl][2]
                    vn_ = vn[sl][j2]
                    mm(C, bz[:, 0:128], attnT[sl][:, tc, :], vn_[:], True, True, [(attnT[sl], tc), vn_], bz)
                    mm(C, bz[:, 128:256], kd[sl][:, tc, :], vn_[:], True, True, [(kd[sl], tc), vn_], bz)
                for sl in sls:
                    h = heads[sl]
                    bz = banks[sl][2]
                    P.op("dve", lambda e, tc=tc, h=h, bz=bz, sl=sl: e.scalar_tensor_tensor(
                        out=hS[sl][:], in0=hS[sl][:], scalar=sc[:, EGL, tc, h:h + 1], in1=bz[:, 128:256], op0=ALU.mult, op1=ALU.add),
                        reads=[hS[sl], sc], writes=[bz, hS[sl]])
                    P.op("act", lambda e, sl=sl: e.activation(out=hSb[sl][:], in_=hS[sl][:], func=AF.Copy),
                         reads=[hS[sl]], writes=[hSb[sl]])
                    of_, o1_ = of[sl][j2], o1[sl][j2]
                    P.op("dve", lambda e, o1_=o1_, of_=of_, bz=bz: e.tensor_tensor(out=of_[:], in0=bz[:, 0:128], in1=o1_[:], op=ALU.add),
                         reads=[o1_], writes=[bz, of_])
                for sl in sls:
                    of_, osq_, ssq_, onb_ = of[sl][j2], osq[sl][j2], ssq[sl][j2], onb[sl][j2]
                    P.op("pool", lambda e, of_=of_, osq_=osq_: e.tensor_tensor(out=osq_[:], in0=of_[:], in1=of_[:], op=ALU.mult),
                         reads=[of_], writes=[osq_])
                    P.op("dve", lambda e, osq_=osq_, ssq_=ssq_: e.tensor_reduce(out=ssq_[:, 0:1], in_=osq_[:], axis=mybir.AxisListType.X, op=ALU.add),
                         reads=[osq_], writes=[ssq_])
                    P.op("act", lambda e, ssq_=ssq_: e.activation(out=ssq_[:, 1:2], in_=ssq_[:, 0:1], func=AF.Sqrt, bias=C.epsc[:, 0:1],
                                                                  scale=1.0 / 128), reads=[ssq_, C.epsc], writes=[ssq_])
                    P.op("dve", lambda e, ssq_=ssq_: e.reciprocal(out=ssq_[:, 2:3], in_=ssq_[:, 1:2]), reads=[ssq_], writes=[ssq_])
                    P.op("dve", lambda e, of_=of_, onb_=onb_, ssq_=ssq_: e.tensor_scalar(out=onb_[:], in0=of_[:], scalar1=ssq_[:, 2:3],
                                                                                      scalar2=None, op0=ALU.mult),
                         reads=[of_, ssq_], writes=[onb_])
                for sl in sls:
                    onb_ = onb[sl][j2]
                    pt_ = psTs[sl]
                    P.op("pe", lambda e, onb_=onb_, pt_=pt_: e.transpose(pt_[:, 0:128], onb_[:], C.ident_bf[:]),
                         reads=[onb_, C.ident_bf], writes=[pt_])
                    P.op("dve", lambda e, tc=tc, pt_=pt_, sl=sl: e.scalar_tensor_tensor(
                        out=yst[sl][:, c4(tc)], in0=pt_[:, 0:128], scalar=onm[:, 0:1], in1=siluz[sl][:, c4(tc)], op0=ALU.mult, op1=ALU.mult),
                        reads=[onm, siluz[sl]], writes=[pt_, yst[sl]])
            for sl in sls:
                h = heads[sl]
                P.op("sp", lambda e, h=h, sl=sl: e.dma_start(out=scr["YA"][s, h * 128:(h + 1) * 128, :], in_=yst[sl][:]),
                     reads=[yst[sl]], writes=[scr["YAb"]], dma=True)

        for s in range(nseq):
            P.op("sp", lambda e, s=s: e.dma_start(out=ba[:], in_=scr["BA"][s].rearrange("(a p) k -> p a k", p=128)),
                 reads=[scr["BAb"]], writes=[ba], dma=True)
            S_ = lambda k: sc[:, k, :, :]
            P.op("act", lambda e: e.activation(out=S_(BETA), in_=ba[:, :, 0:8], func=AF.Sigmoid), reads=[ba], writes=[sc])
            P.op("dve", lambda e: e.tensor_scalar(out=S_(NBETA), in0=S_(BETA), scalar1=-1.0, scalar2=None, op0=ALU.mult),
                 reads=[sc], writes=[sc])
            P.op("dve", lambda e: e.tensor_tensor(out=S_(G), in0=ba[:, :, 8:16],
                                                  in1=ad[:, 8:16].unsqueeze(1).to_broadcast([128, NT, 8]), op=ALU.add),
                 reads=[ba, ad], writes=[sc])
            P.op("act", lambda e: e.activation(out=S_(G), in_=S_(G), func=AF.Exp), reads=[sc], writes=[sc])
            P.op("act", lambda e: e.activation(out=S_(G), in_=S_(G), func=AF.Ln, bias=one1[:], scale=1.0),
                 reads=[sc, one1], writes=[sc])
            P.op("dve", lambda e: e.tensor_tensor(out=S_(G), in0=S_(G),
                                                  in1=ad[:, 0:8].unsqueeze(1).to_broadcast([128, NT, 8]), op=ALU.mult),
                 reads=[sc, ad], writes=[sc])
            P.op("dve", lambda e: e.tensor_scalar(out=S_(G), in0=S_(G), scalar1=-1.0, scalar2=None, op0=ALU.mult),
                 reads=[sc], writes=[sc])
            for tc in range(NT):
                mm(C, psF[:, tc * 8:(tc + 1) * 8], triF[:], sc[:, G, tc, :], True, True, [triF, sc], psF)
                mm(C, psF[:, 128 + tc * 8:128 + (tc + 1) * 8], onesF[:], sc[:, G, tc, :], True, True, [onesF, sc], psF)
            P.op("dve", lambda e: e.tensor_copy(out=S_(GC), in_=psF[:, 0:128].rearrange("p (a k) -> p a k", k=8)),
                 writes=[psF, sc])
            P.op("dve", lambda e: e.tensor_copy(out=S_(GL), in_=psF[:, 128:256].rearrange("p (a k) -> p a k", k=8)),
                 writes=[psF, sc])
            P.op("act", lambda e: e.activation(out=S_(EGC), in_=S_(GC), func=AF.Exp), reads=[sc], writes=[sc])
            P.op("act", lambda e: e.activation(out=S_(EGL), in_=S_(GL), func=AF.Exp), reads=[sc], writes=[sc])
            P.op("dve", lambda e: e.tensor_scalar(out=S_(NGC), in0=S_(GC), scalar1=-1.0, scalar2=None, op0=ALU.mult),
                 reads=[sc], writes=[sc])
            P.op("dve", lambda e: e.tensor_scalar(out=S_(NEGC), in0=S_(EGC), scalar1=-1.0, scalar2=None, op0=ALU.mult),
                 reads=[sc], writes=[sc])
            P.op("dve", lambda e: e.tensor_tensor(out=S_(KDEC), in0=S_(GL), in1=S_(GC), op=ALU.subtract),
                 reads=[sc], writes=[sc])
            P.op("act", lambda e: e.activation(out=S_(KDEC), in_=S_(KDEC), func=AF.Exp), reads=[sc], writes=[sc])
            for hp in range(4):
                heads = (2 * hp, 2 * hp + 1)
                for sl, h in enumerate(heads):
                    g1(s, h, sl)
                g2(s, heads)
    P.barrier()


def tile_cols(W, width, kc=None):
    K, N = W.shape
    return np.ascontiguousarray(W.reshape(K // 128, 128, N // width, width).transpose(2, 1, 0, 3))


def prep_weights(inp, depth=DEPTH):
    f = lambda a: np.asarray(a, dtype=np.float32)
    out = {}
    out["ada_w"] = np.stack([tile_cols(f(inp["ada_w"][l]), GW) for l in range(depth)])
    out["ada_b"] = np.ascontiguousarray(f(inp["ada_b"])[:depth].reshape(depth, 144, 128))
    out["norms"] = np.ascontiguousarray(np.stack(
        [f(inp["norm_ffn1"])[:depth], f(inp["norm_mix"])[:depth], f(inp["norm_ffn2"])[:depth]], axis=1
    ).reshape(depth, 3, NKC, 128))
    for nm in ("ffn1", "ffn2"):
        out[nm + "_w1"] = np.stack([tile_cols(f(inp[nm + "_w1"][l]), GW) for l in range(depth)])
        out[nm + "_w3"] = np.stack([tile_cols(f(inp[nm + "_w3"][l]), GW) for l in range(depth)])
        out[nm + "_w2"] = np.stack([tile_cols(f(inp[nm + "_w2"][l]), 128) for l in range(depth)])
    w_in = f(inp["w_in"])[:depth]
    segs = []
    for l in range(depth):
        W = w_in[l]
        segs.append(np.concatenate([tile_cols(W[:, 0:4096], GW), tile_cols(W[:, OFF_BU:OFF_CQ], GW),
                                    tile_cols(W[:, OFF_CQ:OFF_GATE], GW), tile_cols(W[:, OFF_GATE:], GW)], axis=0))
    out["w_inT"] = np.stack(segs)
    out["w_ba"] = np.stack([tile_cols(w_in[l][:, OFF_BA:OFF_BU], 16)[0] for l in range(depth)])
    out["w_br"] = np.stack([np.concatenate([tile_cols(f(inp["w_branch_a"][l]), GW), tile_cols(f(inp["w_branch_b"][l]), GW),
                                            tile_cols(f(inp["w_branch_c"][l]), GW)], axis=2) for l in range(depth)])
    out["w_outT"] = np.stack([tile_cols(f(inp["w_out"][l]), GW) for l in range(depth)])
    out["dil_g"] = np.ascontiguousarray(np.stack([np.tile(f(inp["dil_q_norm"])[:depth], (1, 2)),
                                                  np.tile(f(inp["dil_k_norm"])[:depth], (1, 2))], axis=1))
    out["gdn_conv"] = np.ascontiguousarray(f(inp["gdn_conv"])[:depth].reshape(depth, 4, 24, 128).transpose(0, 3, 2, 1))
    ad = np.concatenate([f(inp["gdn_a_log"])[:depth], f(inp["gdn_dt_bias"])[:depth]], axis=1)
    out["gdn_ad"] = np.ascontiguousarray(np.broadcast_to(ad[:, None, :], (depth, 128, 16)))
    out["gdn_on"] = np.ascontiguousarray(f(inp["gdn_out_norm"])[:depth].reshape(depth, 128, 1))
    jj, ii = np.meshgrid(np.arange(128), np.arange(128), indexing="ij")
    msk = np.zeros((128, 7, 2, 128), np.float32)
    for lv in range(7):
        bsz = 1 << lv
        m_ = (((jj // bsz) % 2 == 0) & (ii // bsz == jj // bsz + 1)).astype(np.float32)
        msk[:, lv, 0, :] = m_
        msk[:, lv, 1, :] = m_.T
    out["gdn_masks"] = msk
    G_, P_, I_ = 48, 64, 16
    out["s5_vec"] = np.ascontiguousarray(np.stack(
        [f(inp["s5_a_re"])[:depth].reshape(depth, 24, 128), f(inp["s5_a_im"])[:depth].reshape(depth, 24, 128),
         np.repeat(f(inp["s5_log_step"])[:depth], P_, axis=1).reshape(depth, 24, 128)], axis=1))
    out["s5_dg"] = np.ascontiguousarray(np.concatenate(
        [f(inp["s5_d"])[:depth].reshape(depth, 6, 128), f(inp["s5_glu_b"])[:depth].reshape(depth, 12, 128)], axis=1))
    Bm = np.zeros((depth, 128, 6, 2, 512), np.float32)
    Cm = np.zeros((depth, 128, 24, 2, 128), np.float32)
    for ri, (bn, cn) in enumerate((("s5_b_re", "s5_c_re"), ("s5_b_im", "s5_c_im"))):
        b = f(inp[bn])[:depth]
        cc = f(inp[cn])[:depth]
        for g in range(G_):
            kc, gl8 = divmod(g, 8)
            Bm[:, gl8 * 16:(gl8 + 1) * 16, kc, ri, gl8 * 64:(gl8 + 1) * 64] = b[:, g].transpose(0, 2, 1)
            sc, gl = divmod(g, 2)
            Cm[:, gl * 64:(gl + 1) * 64, sc, ri, gl8 * 16:(gl8 + 1) * 16] = cc[:, g].transpose(0, 2, 1)
    out["s5_Bm"] = Bm
    out["s5_Cm"] = Cm
    out["s5_gluw"] = np.stack([tile_cols(f(inp["s5_glu_w"][l]), GW) for l in range(depth)])
    return out


def kernel(**inputs):
    x = np.asarray(inputs["x"], dtype=np.float32)
    c = np.asarray(inputs["c"], dtype=np.float32)
    B = x.shape[0]
    nseq = B // NCORES
    wts = prep_weights(inputs)
    nc = build_program(nseq=nseq)
    in_maps = []
    for i in range(NCORES):
        xs = x[i * nseq:(i + 1) * nseq].reshape(nseq * S, D)
        m = dict(wts)
        m["xT"] = np.ascontiguousarray(xs.T)
        m["c"] = np.ascontiguousarray(c[i * nseq:(i + 1) * nseq])
        in_maps.append(m)
    res = run_bass_kernel_spmd(nc, in_maps, core_ids=list(range(NCORES)))
    outs = [np.ascontiguousarray(r["outT"].T).reshape(nseq, S, D) for r in res.results]
    return np.concatenate(outs, axis=0).astype(np.float32)
```

```python
import math
from contextlib import ExitStack

import numpy as np
import concourse.bass as bass
import concourse.mybir as mybir
from concourse.bass_utils import run_bass_kernel_spmd

F32 = mybir.dt.float32
BF16 = mybir.dt.bfloat16
AF = mybir.ActivationFunctionType
ALU = mybir.AluOpType

D = 2048
S = 2048
DFF = 5632
NKC = D // 128
NFC = DFF // 128
TT = 512
EPS = 1e-6
DEPTH = 2
NCORES = 8
IN_COLS = 13328
OFF_Z = 3072
OFF_BA = 4096
OFF_BU = 4112
OFF_CQ = 4880
OFF_GATE = 7184
GW = 256


class Buf:
    def __init__(self, name, n=1, t=None):
        self.name = name
        self.n = n
        self.t = t
        self.lastw = [None] * n
        self.readers = [[] for _ in range(n)]

    def __getitem__(self, idx):
        return self.t[idx]


class Op:
    __slots__ = ("eng", "dma", "sig")

    def __init__(self, eng, dma):
        self.eng = eng
        self.dma = dma
        self.sig = None


def _parts(acc):
    if isinstance(acc, Buf):
        return acc, range(acc.n)
    b, p = acc
    if p is None:
        return b, range(b.n)
    if isinstance(p, int):
        return b, (p,)
    return b, p


class Prog:
    ENGS = ("pe", "dve", "act", "pool", "sp")
    NS = 8

    def __init__(self, nc, stack):
        self.nc = nc
        self.e = {"pe": nc.tensor, "dve": nc.vector, "act": nc.scalar, "pool": nc.gpsimd, "sp": nc.sync}
        self.sem = {k: stack.enter_context(nc.semaphore("s_" + k)) for k in self.ENGS}
        self.cnt = {k: 0 for k in self.ENGS}
        self.dsem = {k: [stack.enter_context(nc.semaphore("d_%s%d" % (k, i))) for i in range(self.NS)]
                     for k in ("sp", "pool", "act")}
        self.dcnt = {k: 0 for k in self.dsem}
        self.dlast = {k: [None] * self.NS for k in self.dsem}
        self.waited = {k: {} for k in self.ENGS}
        self.last = {k: None for k in self.ENGS}
        self.nops = 0

    def _wait(self, eng, sig):
        sem, val = sig
        w = self.waited[eng]
        key = id(sem)
        if w.get(key, 0) < val:
            self.e[eng].wait_ge(sem, val)
            w[key] = val

    def op(self, eng, fn, reads=(), writes=(), dma=False, sig=True):
        o = Op(eng, dma)
        deps = []
        for acc in reads:
            b, ps = _parts(acc)
            for p in ps:
                lw = b.lastw[p]
                if lw is not None:
                    deps.append((lw, True))
                b.readers[p].append(o)
        for acc in writes:
            b, ps = _parts(acc)
            for p in ps:
                lw = b.lastw[p]
                if lw is not None:
                    deps.append((lw, False))
                for r in b.readers[p]:
                    if r is not o:
                        deps.append((r, False))
                b.lastw[p] = o
                b.readers[p] = []
        for d, raw in deps:
            if d is o:
                continue
            if (not d.dma) and (not dma) and d.eng == eng:
                if not raw or eng == "pe":
                    continue
            self._wait(eng, d.sig)
        if dma:
            k = self.dcnt[eng]
            slot = k % self.NS
            prev = self.dlast[eng][slot]
            if prev is not None:
                self._wait(eng, prev.sig)
            o.sig = (self.dsem[eng][slot], 16 * (k // self.NS + 1))
            self.dcnt[eng] = k + 1
            self.dlast[eng][slot] = o
            fn(self.e[eng]).then_inc(self.dsem[eng][slot], 16)
        elif not sig:
            o.sig = (self.sem[eng], self.cnt[eng] + 1)
            fn(self.e[eng])
        else:
            self.cnt[eng] += 1
            o.sig = (self.sem[eng], self.cnt[eng])
            fn(self.e[eng]).then_inc(self.sem[eng], 1)
            self.last[eng] = o
        self.nops += 1
        return o

    def barrier(self):
        sigs = [self.last[k].sig for k in self.ENGS if self.last[k] is not None]
        for q in self.dsem:
            for o in self.dlast[q]:
                if o is not None:
                    sigs.append(o.sig)
        for eng in self.ENGS:
            for s in sigs:
                self._wait(eng, s)

    def finish(self):
        sigs = [self.last[k].sig for k in self.ENGS if self.last[k] is not None]
        for q in self.dsem:
            for o in self.dlast[q]:
                if o is not None:
                    sigs.append(o.sig)
        for s in sigs:
            self._wait("sp", s)


class Ctx:
    def __init__(self, nc, stack):
        self.nc = nc
        self.P = Prog(nc, stack)
        self.gstack = stack
        self.uid = 0
        self._fill = {}

    def fill(self, val):
        if val not in self._fill:
            self._fill[val] = self.nc.gpsimd.to_reg(float(val))
        return self._fill[val]

    def sb(self, stack, shape, dt, name, n=1):
        self.uid += 1
        t = stack.enter_context(self.nc.sbuf_tensor("%s_%d" % (name, self.uid), list(shape), dt))
        return Buf(name, n, t)

    def ps(self, stack, shape, dt, name, n=1):
        self.uid += 1
        t = stack.enter_context(self.nc.psum_tensor("%s_%d" % (name, self.uid), list(shape), dt))
        return Buf(name, n, t)


class Stream:
    def __init__(self, bufs, loads, depth=None):
        self.bufs = bufs
        self.loads = loads
        self.depth = len(bufs) if depth is None else depth
        self.issued = 0

    def get(self, k):
        while self.issued < min(len(self.loads), k + self.depth):
            self.loads[self.issued](self.bufs[self.issued % len(self.bufs)])
            self.issued += 1
        return self.bufs[k % len(self.bufs)]


def mm(C, ps, lhsT, rhs, start, stop, reads, wr, lazy=False):
    C.P.op("pe", lambda e: e.matmul(ps, lhsT, rhs, start=start, stop=stop), reads=reads, writes=[wr],
           sig=bool(stop) or not lazy)


def load_vec_fm(C, stack, dst, dst_cols, src_rows_ap, nrows, tmp, pst, ident):
    P = C.P
    P.op("sp", lambda e: e.dma_start(out=tmp[0:nrows, :], in_=src_rows_ap), writes=[tmp], dma=True)
    P.op("pe", lambda e: e.transpose(pst[:, 0:nrows], tmp[0:nrows, :], ident[0:nrows, 0:nrows]),
         reads=[tmp, ident], writes=[pst])
    P.op("dve", lambda e: e.tensor_copy(out=dst_cols, in_=pst[:, 0:nrows]), reads=[pst], writes=[dst])


def build_program(nseq=2, depth=DEPTH, debug=False, phases=None):
    nc = bass.Bass("TRN2", target_bir_lowering=False)
    ntok = nseq * S
    ntile = ntok // TT
    dr = {}

    def din(name, shape, dt=F32):
        dr[name] = nc.dram_tensor(name, list(shape), dt, kind="ExternalInput").ap()
        return dr[name]

    def dscratch(name, shape, dt=F32):
        kind = "ExternalOutput" if debug else "Internal"
        dr[name] = nc.dram_tensor(name, list(shape), dt, kind=kind).ap()
        return dr[name]

    xT_in = din("xT", [D, ntok])
    c_in = din("c", [nseq, D])
    L = depth
    din("ada_w", [L, 72, 128, NKC, GW])
    din("ada_b", [L, 144, 128])
    din("norms", [L, 3, NKC, 128])
    for nm in ("ffn1", "ffn2"):
        din(nm + "_w1", [L, DFF // GW, 128, NKC, GW])
        din(nm + "_w3", [L, DFF // GW, 128, NKC, GW])
        din(nm + "_w2", [L, NKC, 128, NFC, 128])
    din("w_inT", [L, 52, 128, NKC, GW])
    din("w_ba", [L, 128, NKC, 16])
    din("w_br", [L, 8, 128, NKC, GW])
    din("w_outT", [L, 8, 128, NKC, GW])
    din("dil_g", [L, 2, 128])
    din("gdn_conv", [L, 128, 24, 4])
    din("gdn_ad", [L, 128, 16])
    din("gdn_on", [L, 128, 1])
    din("gdn_masks", [128, 7, 2, 128])
    din("s5_vec", [L, 3, 24, 128])
    din("s5_dg", [L, 18, 128])
    din("s5_Bm", [L, 128, 6, 2, 512])
    din("s5_Cm", [L, 128, 24, 2, 128])
    din("s5_gluw", [L, 6, 128, 6, GW])
    out_ap = nc.dram_tensor("outT", [D, ntok], F32, kind="ExternalOutput").ap()
    scr = {}
    scr["HT"] = dscratch("HT", [nseq, D, S], BF16)
    scr["PA"] = dscratch("PA", [nseq, 4096, S])
    scr["PB"] = dscratch("PB", [nseq, 768, S])
    scr["PC"] = dscratch("PC", [nseq, 2304, S])
    scr["BA"] = dscratch("BA", [nseq, S, 16])
    scr["YA"] = dscratch("YA", [nseq, 1024, S], BF16)
    scr["YB"] = dscratch("YB", [nseq, 768, S], BF16)
    scr["YC"] = dscratch("YC", [nseq, 256, S], BF16)
    for k in ("HT", "PA", "PB", "PC", "BA", "YA", "YB", "YC"):
        scr[k + "b"] = Buf(k, ntile)
    xT = dscratch("xres", [D, ntok])

    with ExitStack() as gs:
        C = Ctx(nc, gs)
        P = C.P
        ones_bf = C.sb(gs, [128, 128], BF16, "ones_bf")
        ident_f = C.sb(gs, [128, 128], F32, "ident_f")
        ident_bf = C.sb(gs, [128, 128], BF16, "ident_bf")
        neghalf = C.sb(gs, [128, TT], F32, "neghalf")
        modv = C.sb(gs, [128, nseq, 9, NKC], F32, "modv")
        gains = C.sb(gs, [128, 3, NKC], F32, "gains")
        P.op("pool", lambda e: e.memset(ones_bf[:], 1.0), writes=[ones_bf])
        P.op("pool", lambda e: e.memset(neghalf[:], -0.5), writes=[neghalf])
        epsc = C.sb(gs, [128, 1], F32, "epsc")
        P.op("pool", lambda e: e.memset(epsc[:], EPS), writes=[epsc])
        C.epsc = epsc
        P.op("pool", lambda e: e.memset(ident_f[:], 1.0), writes=[ident_f])
        P.op("pool", lambda e: e.affine_select(out=ident_f[:], in_=ident_f[:], pattern=[[1, 128]],
                                               compare_op=ALU.is_equal, fill=C.fill(0.0), base=0,
                                               channel_multiplier=-1),
             reads=[ident_f], writes=[ident_f])
        P.op("dve", lambda e: e.tensor_copy(out=ident_bf[:], in_=ident_f[:]), reads=[ident_f], writes=[ident_bf])
        C.ones_bf, C.ident_f, C.ident_bf, C.neghalf = ones_bf, ident_f, ident_bf, neghalf

        xres = Buf("xres", ntile)
        xin = Buf("xin", 1)

        def xview(ap, i):
            return ap.rearrange("(c p) t -> p c t", p=128)[:, :, i * TT:(i + 1) * TT]

        for l in range(depth):
            src = xT_in if l == 0 else xT
            ada_phase(C, dr, l, nseq, modv, gains)
            ffn_phase(C, dr, l, "ffn1", 0, nseq, ntile, modv, gains, src, xT, xres, xview)
            if phases is None or "proj" in phases:
                proj_phase(C, dr, l, nseq, ntile, modv, gains, xT, xres, xview, scr)
            if phases is None or "s5" in phases:
                s5_phase(C, dr, l, nseq, scr)
            if phases is None or "dil" in phases:
                dil_phase(C, dr, l, nseq, scr)
            if phases is None or "gdn" in phases:
                gdn_phase(C, dr, l, nseq, scr)
            if phases is None or "merge" in phases:
                merge_phase(C, dr, l, nseq, ntile, modv, gains, xT, xres, xview, scr)
            ffn_phase(C, dr, l, "ffn2", 2, nseq, ntile, modv, gains, xT, xT if l < depth - 1 else out_ap,
                      xres, xview)
        P.finish()
    return nc


def mod_vectors(C, AB, modv, gains, sub, nseq, gate_scale):
    P = C.P
    for b in range(nseq):
        P.op("dve", lambda e, b=b: e.scalar_tensor_tensor(
            out=AB[:, b, 0, :], in0=modv[:, b, 3 * sub + 1, :], scalar=1.0, in1=gains[:, sub, :],
            op0=ALU.add, op1=ALU.mult), reads=[modv, gains], writes=[AB])
        P.op("dve", lambda e, b=b: e.tensor_copy(out=AB[:, b, 1, :], in_=modv[:, b, 3 * sub, :]),
             reads=[modv], writes=[AB])
        P.op("dve", lambda e, b=b: e.tensor_scalar(out=AB[:, b, 2, :], in0=modv[:, b, 3 * sub + 2, :],
                                                   scalar1=gate_scale, scalar2=None, op0=ALU.mult),
             reads=[modv], writes=[AB])


def norm_mod(C, xs, hT, AB, b, sq, tmpf, ms, rstd, ps_stat):
    P = C.P
    for c in range(NKC):
        q = sq[c % len(sq)]
        P.op("act", lambda e, c=c, q=q: e.activation(out=q[:], in_=xs[:, c, :], func=AF.Square),
             reads=[xs], writes=[q])
        mm(C, ps_stat[:], C.ones_bf[:], q[:], c == 0, c == NKC - 1, [q, C.ones_bf], ps_stat)
    P.op("act", lambda e: e.activation(out=ms[:], in_=ps_stat[:], func=AF.Sqrt, bias=C.epsc[:, 0:1], scale=1.0 / D),
         reads=[ps_stat, C.epsc], writes=[ms])
    P.op("dve", lambda e: e.reciprocal(out=rstd[:], in_=ms[:]), reads=[ms], writes=[rstd])
    for c in range(NKC):
        t = tmpf[c % len(tmpf)]
        P.op("dve", lambda e, c=c, t=t: e.scalar_tensor_tensor(
            out=t[:], in0=xs[:, c, :], scalar=AB[:, b, 0, c:c + 1], in1=rstd[:],
            op0=ALU.mult, op1=ALU.mult), reads=[xs, AB, rstd], writes=[t])
        P.op("act", lambda e, c=c, t=t: e.activation(
            out=hT[:, c, :], in_=t[:], func=AF.Identity, bias=AB[:, b, 1, c:c + 1], scale=1.0),
            reads=[t, AB], writes=[(hT, c)])


def nm_squares(C, xs, sq16):
    for c in range(NKC):
        C.P.op("act", lambda e, c=c: e.activation(out=sq16[:, c, :], in_=xs[:, c, :], func=AF.Square),
               reads=[xs], writes=[(sq16, c)])


def nm_stats(C, sq16, ps_stat):
    for c in range(NKC):
        mm(C, ps_stat[:], C.ones_bf[:], sq16[:, c, :], c == 0, c == NKC - 1, [(sq16, c), C.ones_bf], ps_stat)


def nm_finish(C, xs, hT, AB, b, tmpf, ms, rstd, ps_stat):
    P = C.P
    P.op("act", lambda e: e.activation(out=ms[:], in_=ps_stat[:], func=AF.Sqrt, bias=C.epsc[:, 0:1], scale=1.0 / D),
         reads=[ps_stat, C.epsc], writes=[ms])
    P.op("dve", lambda e: e.reciprocal(out=rstd[:], in_=ms[:]), reads=[ms], writes=[rstd])
    for c in range(NKC):
        t = tmpf[c % len(tmpf)]
        P.op("dve", lambda e, c=c, t=t: e.scalar_tensor_tensor(
            out=t[:], in0=xs[:, c, :], scalar=AB[:, b, 0, c:c + 1], in1=rstd[:],
            op0=ALU.mult, op1=ALU.mult), reads=[xs, AB, rstd], writes=[t])
        P.op("act", lambda e, c=c, t=t: e.activation(
            out=hT[:, c, :], in_=t[:], func=AF.Identity, bias=AB[:, b, 1, c:c + 1], scale=1.0),
            reads=[t, AB], writes=[(hT, c)])


def ada_phase(C, dr, l, nseq, modv, gains):
    P = C.P
    with ExitStack() as st:
        tmp = C.sb(st, [128, 128], F32, "ada_tmp")
        pst = C.ps(st, [128, 512], F32, "ada_pst")
        psm = C.ps(st, [128, 512], F32, "ada_psm")
        cT = C.sb(st, [128, NKC, nseq], F32, "cT")
        bias = C.sb(st, [128, 144], F32, "ada_bias")
        wb = [C.sb(st, [128, NKC, GW], F32, "ada_w%d" % i) for i in range(2)]
        for c in range(NKC):
            P.op("sp", lambda e, c=c: e.dma_start(out=tmp[0:nseq, :], in_=dr["c"][:, c * 128:(c + 1) * 128]),
                 writes=[tmp], dma=True)
            P.op("pe", lambda e: e.transpose(pst[:, 0:nseq], tmp[0:nseq, :], C.ident_f[0:nseq, 0:nseq]),
                 reads=[tmp, C.ident_f], writes=[pst])
            P.op("act", lambda e, c=c: e.activation(out=cT[:, c, :], in_=pst[:, 0:nseq], func=AF.Silu),
                 reads=[pst], writes=[cT])
        load_vec_fm(C, st, bias, bias[:, 0:128], dr["ada_b"][l, 0:128, :], 128, tmp, pst, C.ident_f)
        load_vec_fm(C, st, bias, bias[:, 128:144], dr["ada_b"][l, 128:144, :], 16, tmp, pst, C.ident_f)
        for j in range(3):
            load_vec_fm(C, st, gains, gains[:, j, :], dr["norms"][l, j, :, :], NKC, tmp, pst, C.ident_f)
        loads = []
        for g in range(72):
            loads.append(lambda b, g=g: P.op("sp", lambda e: e.dma_start(out=b[:], in_=dr["ada_w"][l, g]),
                                             writes=[b], dma=True))
        strm = Stream(wb, loads)
        for g in range(72):
            w = strm.get(g)
            for fi in range(2):
                ch = 2 * g + fi
                for c in range(NKC):
                    mm(C, psm[:, ch * nseq:(ch + 1) * nseq], w[:, c, fi * 128:(fi + 1) * 128], cT[:, c, :],
                       c == 0, c == NKC - 1, [w, cT], psm)
        for b in range(nseq):
            P.op("dve", lambda e, b=b: e.tensor_tensor(
                out=modv[:, b, :, :].rearrange("p j c -> p (j c)"),
                in0=psm[:, 0:144 * nseq].rearrange("p (k b) -> p k b", b=nseq)[:, :, b],
                in1=bias[:], op=ALU.add), reads=[psm, bias], writes=[modv])
    P.barrier()


def ffn_phase(C, dr, l, nm, sub, nseq, ntile, modv, gains, src, dst, xres, xview):
    P = C.P
    w1d, w3d, w2d = dr[nm + "_w1"], dr[nm + "_w3"], dr[nm + "_w2"]
    NG = DFF // GW
    with ExitStack() as st:
        xs = C.sb(st, [128, NKC, TT], F32, "xs")
        hT = C.sb(st, [128, NKC, TT], BF16, "hT", n=NKC)
        actT = C.sb(st, [128, NFC, TT], BF16, "actT", n=NFC)
        w1b = [C.sb(st, [128, NKC, GW], BF16, "w1b%d" % i) for i in range(2)]
        w3b = [C.sb(st, [128, NKC, GW], BF16, "w3b%d" % i) for i in range(2)]
        w2b = [C.sb(st, [128, NFC, 128], BF16, "w2b%d" % i) for i in range(2)]
        sq16 = C.sb(st, [128, NKC, TT], BF16, "sq16", n=NKC)
        tmpf = [C.sb(st, [128, TT], F32, "tmpf%d" % i) for i in range(2)]
        sg = [C.sb(st, [128, TT], F32, "sg%d" % i) for i in range(2)]
        xr = [C.sb(st, [128, TT], F32, "xr%d" % i) for i in range(3)]
        ms = C.sb(st, [128, TT], F32, "ms")
        rstd = C.sb(st, [128, TT], F32, "rstd")
        AB = C.sb(st, [128, nseq, 3, NKC], F32, "AB")
        ps_stat = C.ps(st, [128, TT], F32, "ps_stat")
        ps_g = [C.ps(st, [128, TT], F32, "ps_g%d" % i) for i in range(2)]
        ps_u = [C.ps(st, [128, TT], F32, "ps_u%d" % i) for i in range(2)]
        ps_o = [C.ps(st, [128, TT], F32, "ps_o%d" % i) for i in range(2)]

        mod_vectors(C, AB, modv, gains, sub, nseq, 0.5)

        def mk(wd, g):
            return lambda b: P.op("pool", lambda e: e.dma_start(out=b[:], in_=wd[l, g]), writes=[b], dma=True)
        l1 = [mk(w1d, g) for _ in range(ntile) for g in range(NG)]
        l3 = [mk(w3d, g) for _ in range(ntile) for g in range(NG)]
        l2 = [mk(w2d, m) for _ in range(ntile) for m in range(NKC)]
        s1, s3, s2 = Stream(w1b, l1), Stream(w3b, l3), Stream(w2b, l2)

        def prep1(i):
            P.op("sp", lambda e, i=i: e.dma_start(out=xs[:], in_=xview(src, i)),
                 reads=[(xres, i)], writes=[xs], dma=True)
            nm_squares(C, xs, sq16)

        def prep23(i):
            nm_stats(C, sq16, ps_stat)
            nm_finish(C, xs, hT, AB, i // (S // TT), tmpf, ms, rstd, ps_stat)

        prep1(0)
        prep23(0)
        for i in range(ntile):
            b = i // (S // TT)
            for g in range(NG):
                k = i * NG + g
                wa, wb_ = s1.get(k), s3.get(k)
                if g == NG - 4:
                    s2.get(i * NKC)
                for fi in range(GW // 128):
                    f = g * (GW // 128) + fi
                    pg, pu = ps_g[f % 2], ps_u[f % 2]
                    for c in range(NKC):
                        mm(C, pg[:], wa[:, c, fi * 128:(fi + 1) * 128], hT[:, c, :], c == 0, c == NKC - 1,
                           [wa, (hT, c)], pg, lazy=True)
                    for c in range(NKC):
                        mm(C, pu[:], wb_[:, c, fi * 128:(fi + 1) * 128], hT[:, c, :], c == 0, c == NKC - 1,
                           [wb_, (hT, c)], pu, lazy=True)
                    s_ = sg[f % 2]
                    P.op("act", lambda e, s_=s_, pg=pg: e.activation(out=s_[:], in_=pg[:], func=AF.Silu),
                         reads=[pg], writes=[s_])
                    P.op("dve", lambda e, s_=s_, pu=pu, f=f: e.tensor_tensor(
                        out=actT[:, f, :], in0=pu[:], in1=s_[:], op=ALU.mult),
                        reads=[pu, s_], writes=[(actT, f)])
            if i + 1 < ntile:
                prep1(i + 1)
            for m in range(NKC):
                if m == NKC // 2 and i + 1 < ntile:
                    prep23(i + 1)
                k = i * NKC + m
                w2 = s2.get(k)
                po = ps_o[m % 2]
                r = xr[m % 3]
                P.op("sp", lambda e, i=i, m=m, r=r: e.dma_start(out=r[:], in_=xview(src, i)[:, m, :]),
                     reads=[(xres, i)], writes=[r], dma=True)
                for f in range(NFC):
                    mm(C, po[:], w2[:, f, :], actT[:, f, :], f == 0, f == NFC - 1, [w2, (actT, f)], po, lazy=True)
                P.op("dve", lambda e, m=m, r=r, po=po, b=b: e.scalar_tensor_tensor(
                    out=r[:], in0=po[:], scalar=AB[:, b, 2, m:m + 1], in1=r[:], op0=ALU.mult, op1=ALU.add),
                    reads=[po, AB, r], writes=[r])
                P.op("sp", lambda e, i=i, m=m, r=r: e.dma_start(out=xview(dst, i)[:, m, :], in_=r[:]),
                     reads=[r], writes=[(xres, i)], dma=True)
    P.barrier()


def proj_phase(C, dr, l, nseq, ntile, modv, gains, xT, xres, xview, scr):
    P = C.P
    wd = dr["w_inT"]
    NGP = 28
    with ExitStack() as st:
        xs = C.sb(st, [128, NKC, TT], F32, "xs")
        hT = C.sb(st, [128, NKC, TT], BF16, "hT", n=NKC)
        wb = [C.sb(st, [128, NKC, GW], BF16, "pw%d" % i) for i in range(3)]
        wba = C.sb(st, [128, NKC, 16], BF16, "wba")
        sq = [C.sb(st, [128, TT], BF16, "sq%d" % i) for i in range(3)]
        tmpf = [C.sb(st, [128, TT], F32, "tmpf%d" % i) for i in range(3)]
        stg = [C.sb(st, [128, TT], F32, "stg%d" % i) for i in range(4)]
        bas = [C.sb(st, [128, 16], F32, "bas%d" % i) for i in range(2)]
        ms = C.sb(st, [128, TT], F32, "ms")
        rstd = C.sb(st, [128, TT], F32, "rstd")
        AB = C.sb(st, [128, nseq, 3, NKC], F32, "AB")
        ps_stat = C.ps(st, [128, TT], F32, "ps_stat")
        ps_p = [C.ps(st, [128, TT], F32, "ps_p%d" % i) for i in range(3)]
        ps_ba = C.ps(st, [128, TT], F32, "ps_ba")
        mod_vectors(C, AB, modv, gains, 1, nseq, 1.0)
        P.op("pool", lambda e: e.dma_start(out=wba[:], in_=dr["w_ba"][l]), writes=[wba], dma=True)
        loads = [(lambda b, g=g: P.op("pool", lambda e: e.dma_start(out=b[:], in_=wd[l, g]), writes=[b], dma=True))
                 for _ in range(ntile) for g in range(NGP)]
        strm = Stream(wb, loads)
        nev = 0
        for i in range(ntile):
            s, ti = divmod(i, S // TT)
            tsl = slice(ti * TT, (ti + 1) * TT)
            P.op("sp", lambda e, i=i: e.dma_start(out=xs[:], in_=xview(xT, i)),
                 reads=[(xres, i)], writes=[xs], dma=True)
            norm_mod(C, xs, hT, AB, s, sq, tmpf, ms, rstd, ps_stat)
            P.op("sp", lambda e, s=s, tsl=tsl: e.dma_start(
                out=scr["HT"][s].rearrange("(c p) t -> p c t", p=128)[:, :, tsl], in_=hT[:]),
                reads=[hT], writes=[(scr["HTb"], i)], dma=True)
            for ts in range(TT // 128):
                for c in range(NKC):
                    mm(C, ps_ba[:, ts * 16:(ts + 1) * 16], hT[:, c, ts * 128:(ts + 1) * 128], wba[:, c, :],
                       c == 0, c == NKC - 1, [(hT, c), wba], ps_ba)
            bb = bas[i % 2]
            for ts in range(TT // 128):
                pass
            P.op("dve", lambda e, bb=bb: e.tensor_copy(out=stg[3][:, 0:64], in_=ps_ba[:, 0:64]),
                 reads=[ps_ba], writes=[stg[3]])
            P.op("sp", lambda e, s=s, ti=ti: e.dma_start(
                out=scr["BA"][s, ti * TT:(ti + 1) * TT, :].rearrange("(a p) k -> p a k", p=128),
                in_=stg[3][:, 0:64].rearrange("p (a k) -> p a k", k=16)),
                reads=[stg[3]], writes=[(scr["BAb"], i)], dma=True)
            for g in range(NGP):
                w = strm.get(i * NGP + g)
                for fi in range(2):
                    ch = 2 * g + fi
                    pp = ps_p[ch % 3]
                    for c in range(NKC):
                        mm(C, pp[:], w[:, c, fi * 128:(fi + 1) * 128], hT[:, c, :], c == 0, c == NKC - 1,
                           [w, (hT, c)], pp, lazy=True)
                    sg_ = stg[nev % 3]
                    if nev % 2 == 0:
                        P.op("act", lambda e, sg_=sg_, pp=pp: e.activation(out=sg_[:], in_=pp[:], func=AF.Copy),
                             reads=[pp], writes=[sg_])
                    else:
                        P.op("dve", lambda e, sg_=sg_, pp=pp: e.tensor_copy(out=sg_[:], in_=pp[:]),
                             reads=[pp], writes=[sg_])
                    nev += 1
                    if ch < 32:
                        dst, row, key = scr["PA"], ch, "PAb"
                    elif ch < 38:
                        dst, row, key = scr["PB"], ch - 32, "PBb"
                    else:
                        dst, row, key = scr["PC"], ch - 38, "PCb"
                    P.op("sp", lambda e, dst=dst, row=row, s=s, tsl=tsl, sg_=sg_: e.dma_start(
                        out=dst[s, row * 128:(row + 1) * 128, tsl], in_=sg_[:]),
                        reads=[sg_], writes=[(scr[key], i)], dma=True)
    P.barrier()


def s5_phase(C, dr, l, nseq, scr):
    P = C.P
    NSC = 24
    TWO_PI = 2.0 * math.pi
    with ExitStack() as st:
        tmp = C.sb(st, [128, 128], F32, "s5tmp")
        pst = C.ps(st, [128, 512], F32, "s5pst")
        vec = C.sb(st, [128, 3, NSC], F32, "s5vec")
        dg = C.sb(st, [128, 18], F32, "s5dg")
        sm = C.sb(st, [128, 24, NSC], F32, "s5sm")
        halfpi = C.sb(st, [128, 1], F32, "halfpi")
        Bre = C.sb(st, [128, NSC, 128], F32, "Bre")
        Bim = C.sb(st, [128, NSC, 128], F32, "Bim")
        Atre = C.sb(st, [128, NSC * 128], BF16, "Atre")
        Atim = C.sb(st, [128, NSC * 128], BF16, "Atim")
        Bm = C.sb(st, [128, 6, 2, 512], BF16, "Bm")
        Cm = C.sb(st, [128, NSC, 2, 128], BF16, "Cm")
        gw = C.sb(st, [128, 6, 6, GW], BF16, "gluw")
        tri = C.sb(st, [128, 128], BF16, "tri")
        for j in range(3):
            load_vec_fm(C, st, vec, vec[:, j, :], dr["s5_vec"][l, j], NSC, tmp, pst, C.ident_f)
        load_vec_fm(C, st, dg, dg[:, :], dr["s5_dg"][l], 18, tmp, pst, C.ident_f)
        P.op("pool", lambda e: e.dma_start(out=Bm[:], in_=dr["s5_Bm"][l]), writes=[Bm], dma=True)
        P.op("pool", lambda e: e.dma_start(out=Cm[:], in_=dr["s5_Cm"][l]), writes=[Cm], dma=True)
        for g in range(6):
            P.op("pool", lambda e, g=g: e.dma_start(out=gw[:, g], in_=dr["s5_gluw"][l, g]), writes=[gw], dma=True)
        P.op("pool", lambda e: e.memset(halfpi[:], math.pi / 2), writes=[halfpi])
        P.op("pool", lambda e: e.memset(tri[:], 1.0), writes=[tri])
        P.op("pool", lambda e: e.affine_select(out=tri[:], in_=tri[:], pattern=[[1, 128]], compare_op=ALU.is_ge,
                                               fill=C.fill(0.0), base=0, channel_multiplier=-1),
             reads=[tri], writes=[tri])

        V = lambda k: sm[:, k, :]

        def tt(o, a, b, op, eng="dve"):
            P.op(eng, lambda e: e.tensor_tensor(out=o, in0=a, in1=b, op=op), reads=[sm, vec], writes=[sm])

        def ts(o, a, s1, op0, s2=None, op1=None):
            if op1 is None:
                P.op("dve", lambda e: e.tensor_scalar(out=o, in0=a, scalar1=s1, scalar2=None, op0=op0),
                     reads=[sm, vec], writes=[sm])
            else:
                P.op("dve", lambda e: e.tensor_scalar(out=o, in0=a, scalar1=s1, scalar2=s2, op0=op0, op1=op1),
                     reads=[sm, vec], writes=[sm])

        def act(o, a, f, scale=1.0, bias=None):
            if bias is None:
                P.op("act", lambda e: e.activation(out=o, in_=a, func=f, scale=scale), reads=[sm, vec], writes=[sm])
            else:
                P.op("act", lambda e: e.activation(out=o, in_=a, func=f, scale=scale, bias=bias),
                     reads=[sm, vec, halfpi], writes=[sm])
        STEP, ARE, RS, TH, THR, T0, RHO, SIN, COS, LR, LI, MR, MI, CR, CI, DEN, NR, T1, T2, PWR, PWI, T3 = range(22)
        act(V(STEP), vec[:, 2, :], AF.Exp)
        ts(V(ARE), vec[:, 0, :], -1e-4, ALU.min)
        tt(V(RS), V(ARE), V(STEP), ALU.mult)
        tt(V(TH), vec[:, 1, :], V(STEP), ALU.mult)
        ts(V(THR), V(TH), 1.0, ALU.mult)
        for m in range(1, 6):
            ts(V(T0), V(TH), (2 * m - 1) * math.pi, ALU.is_gt, TWO_PI, ALU.mult)
            tt(V(THR), V(THR), V(T0), ALU.subtract)
        act(V(RHO), V(RS), AF.Exp)
        act(V(SIN), V(THR), AF.Sin)
        ts(V(T0), V(THR), -1.0, ALU.mult)
        tt(V(T0), V(T0), V(THR), ALU.max)
        act(V(COS), V(T0), AF.Sin, scale=-1.0, bias=halfpi[:])
        tt(V(LR), V(RHO), V(COS), ALU.mult)
        tt(V(LI), V(RHO), V(SIN), ALU.mult)
        P.op("dve", lambda e: e.reciprocal(out=V(T1), in_=V(RHO)), reads=[sm], writes=[sm])
        tt(V(MR), V(COS), V(T1), ALU.mult)
        tt(V(MI), V(SIN), V(T1), ALU.mult)
        ts(V(MI), V(MI), -1.0, ALU.mult)
        ts(V(NR), V(LR), -1.0, ALU.add)
        tt(V(T1), V(ARE), V(ARE), ALU.mult)
        tt(V(T2), vec[:, 1, :], vec[:, 1, :], ALU.mult)
        tt(V(DEN), V(T1), V(T2), ALU.add)
        P.op("dve", lambda e: e.reciprocal(out=V(DEN), in_=V(DEN)), reads=[sm], writes=[sm])
        tt(V(T1), V(NR), V(ARE), ALU.mult)
        tt(V(T2), V(LI), vec[:, 1, :], ALU.mult)
        tt(V(T1), V(T1), V(T2), ALU.add)
        tt(V(CR), V(T1), V(DEN), ALU.mult)
        tt(V(T1), V(LI), V(ARE), ALU.mult)
        tt(V(T2), V(NR), vec[:, 1, :], ALU.mult)
        tt(V(T1), V(T1), V(T2), ALU.subtract)
        tt(V(CI), V(T1), V(DEN), ALU.mult)

        def pow_table(Tr, Ti, init, base, s2):
            ta = C.sb(s2, [128, NSC, 64], F32, "pt_a")
            tb = C.sb(s2, [128, NSC, 64], F32, "pt_b")
            if init is None:
                P.op("pool", lambda e: e.memset(Tr[:, :, 0:1], 1.0), writes=[Tr])
                P.op("pool", lambda e: e.memset(Ti[:, :, 0:1], 0.0), writes=[Ti])
            else:
                P.op("dve", lambda e: e.tensor_copy(out=Tr[:, :, 0], in_=V(init[0])), reads=[sm], writes=[Tr])
                P.op("dve", lambda e: e.tensor_copy(out=Ti[:, :, 0], in_=V(init[1])), reads=[sm], writes=[Ti])
            ts(V(PWR), V(base[0]), 1.0, ALU.mult)
            ts(V(PWI), V(base[1]), 1.0, ALU.mult)
            for k in range(7):
                w = 1 << k
                pr = V(PWR).unsqueeze(2).to_broadcast([128, NSC, w])
                pi = V(PWI).unsqueeze(2).to_broadcast([128, NSC, w])
                lo = slice(0, w)
                hi = slice(w, 2 * w)
                P.op("dve", lambda e, pr=pr, lo=lo, w=w: e.tensor_tensor(out=ta[:, :, 0:w], in0=Tr[:, :, lo], in1=pr, op=ALU.mult),
                     reads=[Tr, sm], writes=[ta])
                P.op("dve", lambda e, pi=pi, lo=lo, w=w: e.tensor_tensor(out=tb[:, :, 0:w], in0=Ti[:, :, lo], in1=pi, op=ALU.mult),
                     reads=[Ti, sm], writes=[tb])
                P.op("dve", lambda e, hi=hi, w=w: e.tensor_tensor(out=Tr[:, :, hi], in0=ta[:, :, 0:w], in1=tb[:, :, 0:w], op=ALU.subtract),
                     reads=[ta, tb], writes=[Tr])
                P.op("dve", lambda e, pi=pi, lo=lo, w=w: e.tensor_tensor(out=ta[:, :, 0:w], in0=Tr[:, :, lo], in1=pi, op=ALU.mult),
                     reads=[Tr, sm], writes=[ta])
                P.op("dve", lambda e, pr=pr, lo=lo, w=w: e.tensor_tensor(out=tb[:, :, 0:w], in0=Ti[:, :, lo], in1=pr, op=ALU.mult),
                     reads=[Ti, sm], writes=[tb])
                P.op("dve", lambda e, hi=hi, w=w: e.tensor_tensor(out=Ti[:, :, hi], in0=ta[:, :, 0:w], in1=tb[:, :, 0:w], op=ALU.add),
                     reads=[ta, tb], writes=[Ti])
                tt(V(T1), V(PWR), V(PWR), ALU.mult)
                tt(V(T2), V(PWI), V(PWI), ALU.mult)
                tt(V(T3), V(PWR), V(PWI), ALU.mult)
                tt(V(PWR), V(T1), V(T2), ALU.subtract)
                ts(V(PWI), V(T3), 2.0, ALU.mult)

        with ExitStack() as s2:
            pow_table(Bre, Bim, None, (LR, LI), s2)
        with ExitStack() as s2:
            Asr = C.sb(s2, [128, NSC, 128], F32, "Asr")
            Asi = C.sb(s2, [128, NSC, 128], F32, "Asi")
            pow_table(Asr, Asi, (CR, CI), (MR, MI), s2)
            for sc in range(NSC):
                for (Tsm, Tt) in ((Asr, Atre), (Asi, Atim)):
                    P.op("pe", lambda e, Tsm=Tsm, sc=sc: e.transpose(pst[:, 0:128], Tsm[:, sc, :], C.ident_f[:]),
                         reads=[Tsm, C.ident_f], writes=[pst])
                    P.op("dve", lambda e, Tt=Tt, sc=sc: e.tensor_copy(out=Tt[:, sc * 128:(sc + 1) * 128], in_=pst[:, 0:128]),
                         reads=[pst], writes=[Tt])
            P.barrier()

        with ExitStack() as s3:
            uT = C.sb(s3, [128, 6, S], F32, "uT")
            uTb = C.sb(s3, [128, 6, S], BF16, "uTb")
            hcr = C.sb(s3, [128, NSC], F32, "hcr", n=6)
            hci = C.sb(s3, [128, NSC], F32, "hci", n=6)
            zre = [C.sb(s3, [128, 512], BF16, "zre%d" % i) for i in range(2)]
            zim = [C.sb(s3, [128, 512], BF16, "zim%d" % i) for i in range(2)]
            tf = [C.sb(s3, [128, 512], F32, "s5t%d" % i) for i in range(8)]
            ta = [[C.sb(s3, [128, 512], F32, "s5ta%d%d" % (i, j)) for j in range(4)] for i in range(2)]
            srbs = [C.sb(s3, [128, 4, 128], BF16, "srb%d" % i) for i in range(2)]
            sibs = [C.sb(s3, [128, 4, 128], BF16, "sib%d" % i) for i in range(2)]
            hs = C.sb(s3, [128, 2, 4, 4], F32, "hs")
            ystg = [C.sb(s3, [128, 512], BF16, "ystg%d" % i) for i in range(2)]
            ps_brs = [C.ps(s3, [128, 512], F32, "ps_br%d" % i) for i in range(2)]
            ps_bis = [C.ps(s3, [128, 512], F32, "ps_bi%d" % i) for i in range(2)]
            ps_cr = C.ps(s3, [128, 512], F32, "ps_cr")
            ps_ci = C.ps(s3, [128, 512], F32, "ps_ci")
            ps_y = C.ps(s3, [128, 512], F32, "ps_y")
            ps_v, ps_g = ps_brs[0], ps_bis[0]
            v3 = lambda t: t[:].rearrange("p (a j) -> p a j", j=128)
            for s in range(nseq):
                ub = scr["PB"][s].rearrange("(c p) t -> p c t", p=128)
                P.op("sp", lambda e, ub=ub: e.dma_start(out=uT[:], in_=ub), reads=[scr["PBb"]], writes=[uT], dma=True)
                P.op("pool", lambda e, ub=ub: e.dma_start(out=uTb[:], in_=ub), reads=[scr["PBb"]], writes=[uTb], dma=True)
                P.op("pool", lambda e: e.memset(hcr[:], 0.0), writes=[hcr])
                P.op("pool", lambda e: e.memset(hci[:], 0.0), writes=[hci])
                NI = (S // 128) * 6

                def stA(idx):
                    tc, kc = divmod(idx, 6)
                    p = idx % 2
                    tsl = slice(tc * 128, (tc + 1) * 128)
                    csl = slice(4 * kc * 128, (4 * kc + 4) * 128)
                    pbr, pbi = ps_brs[p], ps_bis[p]
                    t0, t1, t2, t3 = ta[p]
                    zr, zi = zre[p], zim[p]
                    mm(C, pbr[:], uTb[:, kc, tsl], Bm[:, kc, 0, :], True, True, [uTb, Bm], pbr)
                    mm(C, pbi[:], uTb[:, kc, tsl], Bm[:, kc, 1, :], True, True, [uTb, Bm], pbi)
                    P.op("dve", lambda e: e.tensor_tensor(out=t0[:], in0=pbr[:], in1=Atre[:, csl], op=ALU.mult),
                         reads=[pbr, Atre], writes=[t0])
                    P.op("dve", lambda e: e.tensor_tensor(out=t1[:], in0=pbi[:], in1=Atim[:, csl], op=ALU.mult),
                         reads=[pbi, Atim], writes=[t1])
                    P.op("pool", lambda e: e.tensor_tensor(out=zr[:], in0=t0[:], in1=t1[:], op=ALU.subtract),
                         reads=[t0, t1], writes=[zr])
                    P.op("dve", lambda e: e.tensor_tensor(out=t2[:], in0=pbi[:], in1=Atre[:, csl], op=ALU.mult),
                         reads=[pbi, Atre], writes=[t2])
                    P.op("dve", lambda e: e.tensor_tensor(out=t3[:], in0=pbr[:], in1=Atim[:, csl], op=ALU.mult),
                         reads=[pbr, Atim], writes=[t3])
                    P.op("pool", lambda e: e.tensor_tensor(out=zi[:], in0=t2[:], in1=t3[:], op=ALU.add),
                         reads=[t2, t3], writes=[zi])

                def stB(idx):
                    tc, kc = divmod(idx, 6)
                    p = idx % 2
                    sc0 = 4 * kc
                    bsl = slice(sc0, sc0 + 4)
                    zr, zi = zre[p], zim[p]
                    sb_r, sb_i = srbs[p], sibs[p]
                    for scl in range(4):
                        mm(C, ps_cr[:, scl * 128:(scl + 1) * 128], zr[:, scl * 128:(scl + 1) * 128], tri[:],
                           True, True, [zr, tri], ps_cr)
                    for scl in range(4):
                        mm(C, ps_ci[:, scl * 128:(scl + 1) * 128], zi[:, scl * 128:(scl + 1) * 128], tri[:],
                           True, True, [zi, tri], ps_ci)
                    hr_b = hcr[:, bsl].unsqueeze(2).to_broadcast([128, 4, 128])
                    hi_b = hci[:, bsl].unsqueeze(2).to_broadcast([128, 4, 128])
                    P.op("dve", lambda e: e.tensor_tensor(out=v3(tf[4]), in0=v3(ps_cr), in1=hr_b, op=ALU.add),
                         reads=[ps_cr, (hcr, kc)], writes=[tf[4]])
                    P.op("dve", lambda e: e.tensor_tensor(out=v3(tf[5]), in0=v3(ps_ci), in1=hi_b, op=ALU.add),
                         reads=[ps_ci, (hci, kc)], writes=[tf[5]])
                    P.op("dve", lambda e: e.tensor_tensor(out=v3(tf[0]), in0=v3(tf[4]), in1=Bre[:, bsl, :], op=ALU.mult),
                         reads=[tf[4], Bre], writes=[tf[0]])
                    P.op("pool", lambda e: e.tensor_tensor(out=v3(tf[1]), in0=v3(tf[5]), in1=Bim[:, bsl, :], op=ALU.mult),
                         reads=[tf[5], Bim], writes=[tf[1]])
                    P.op("dve", lambda e: e.tensor_tensor(out=tf[6][:], in0=tf[0][:], in1=tf[1][:], op=ALU.subtract),
                         reads=[tf[0], tf[1]], writes=[tf[6]])
                    P.op("pool", lambda e: e.tensor_tensor(out=v3(tf[2]), in0=v3(tf[5]), in1=Bre[:, bsl, :], op=ALU.mult),
                         reads=[tf[5], Bre], writes=[tf[2]])
                    P.op("dve", lambda e: e.tensor_tensor(out=v3(tf[3]), in0=v3(tf[4]), in1=Bim[:, bsl, :], op=ALU.mult),
                         reads=[tf[4], Bim], writes=[tf[3]])
                    P.op("pool", lambda e: e.tensor_tensor(out=tf[7][:], in0=tf[2][:], in1=tf[3][:], op=ALU.add),
                         reads=[tf[2], tf[3]], writes=[tf[7]])
                    P.op("act", lambda e: e.activation(out=sb_r[:], in_=v3(tf[6]), func=AF.Copy), reads=[tf[6]], writes=[sb_r])
                    P.op("act", lambda e: e.activation(out=sb_i[:], in_=v3(tf[7]), func=AF.Copy, scale=-1.0),
                         reads=[tf[7]], writes=[sb_i])
                    sl_r = v3(tf[6])[:, :, 127]
                    sl_i = v3(tf[7])[:, :, 127]
                    hk = hs[:, kc % 2]
                    P.op("dve", lambda e: e.tensor_tensor(out=hk[:, 0, :], in0=sm[:, LR, bsl], in1=sl_r, op=ALU.mult),
                         reads=[tf[6], sm], writes=[hs])
                    P.op("dve", lambda e: e.tensor_tensor(out=hk[:, 1, :], in0=sm[:, LI, bsl], in1=sl_i, op=ALU.mult),
                         reads=[tf[7], sm], writes=[hs])
                    P.op("dve", lambda e: e.tensor_tensor(out=hcr[:, bsl], in0=hk[:, 0, :], in1=hk[:, 1, :], op=ALU.subtract),
                         reads=[hs], writes=[(hcr, kc)])
                    P.op("dve", lambda e: e.tensor_tensor(out=hk[:, 2, :], in0=sm[:, LR, bsl], in1=sl_i, op=ALU.mult),
                         reads=[tf[7], sm], writes=[hs])
                    P.op("dve", lambda e: e.tensor_tensor(out=hk[:, 3, :], in0=sm[:, LI, bsl], in1=sl_r, op=ALU.mult),
                         reads=[tf[6], sm], writes=[hs])
                    P.op("dve", lambda e: e.tensor_tensor(out=hci[:, bsl], in0=hk[:, 2, :], in1=hk[:, 3, :], op=ALU.add),
                         reads=[hs], writes=[(hci, kc)])

                def stC(idx):
                    tc, kc = divmod(idx, 6)
                    p = idx % 2
                    sc0 = 4 * kc
                    tsl = slice(tc * 128, (tc + 1) * 128)
                    sb_r, sb_i = srbs[p], sibs[p]
                    for scl in range(4):
                        mm(C, ps_y[:, 0:128], Cm[:, sc0 + scl, 0, :], sb_r[:, scl, :], scl == 0, False, [Cm, sb_r], ps_y)
                    for scl in range(4):
                        mm(C, ps_y[:, 0:128], Cm[:, sc0 + scl, 1, :], sb_i[:, scl, :], False, scl == 3, [Cm, sb_i], ps_y)
                    P.op("dve", lambda e: e.scalar_tensor_tensor(
                        out=uT[:, kc, tsl], in0=uT[:, kc, tsl], scalar=dg[:, kc:kc + 1], in1=ps_y[:, 0:128],
                        op0=ALU.mult, op1=ALU.add), reads=[uT, dg, ps_y], writes=[uT])

                for idx in range(NI + 2):
                    if idx < NI:
                        stA(idx)
                    if 1 <= idx <= NI:
                        stB(idx - 1)
                    if idx >= 2:
                        stC(idx - 2)
                for kc in range(6):
                    for q in range(S // 512):
                        qs = slice(q * 512, (q + 1) * 512)
                        yv = uT[:, kc, qs]
                        P.op("pool", lambda e, yv=yv: e.tensor_tensor(out=tf[0][:], in0=yv, in1=yv, op=ALU.mult),
                             reads=[uT], writes=[tf[0]])
                        P.op("dve", lambda e: e.tensor_scalar(out=tf[1][:], in0=tf[0][:], scalar1=0.044715, scalar2=1.0,
                                                              op0=ALU.mult, op1=ALU.add), reads=[tf[0]], writes=[tf[1]])
                        P.op("pool", lambda e, yv=yv: e.tensor_tensor(out=tf[2][:], in0=tf[1][:], in1=yv, op=ALU.mult),
                             reads=[tf[1], uT], writes=[tf[2]])
                        P.op("act", lambda e: e.activation(out=tf[3][:], in_=tf[2][:], func=AF.Sigmoid, scale=1.5957691216057308),
                             reads=[tf[2]], writes=[tf[3]])
                        P.op("dve", lambda e, yv=yv, kc=kc, qs=qs: e.tensor_tensor(out=uTb[:, kc, qs], in0=yv, in1=tf[3][:], op=ALU.mult),
                             reads=[uT, tf[3]], writes=[uTb])
                ne = 0
                for q in range(S // 512):
                    qs = slice(q * 512, (q + 1) * 512)
                    for oc in range(6):
                        for kc in range(6):
                            mm(C, ps_v[:], gw[:, oc // 2, kc, (oc % 2) * 128:(oc % 2) * 128 + 128], uTb[:, kc, qs],
                               kc == 0, kc == 5, [gw, uTb], ps_v)
                        for kc in range(6):
                            mm(C, ps_g[:], gw[:, 3 + oc // 2, kc, (oc % 2) * 128:(oc % 2) * 128 + 128], uTb[:, kc, qs],
                               kc == 0, kc == 5, [gw, uTb], ps_g)
                        P.op("act", lambda e, oc=oc: e.activation(out=tf[4][:], in_=ps_g[:], func=AF.Sigmoid,
                                                                  bias=dg[:, 12 + oc:13 + oc], scale=1.0),
                             reads=[ps_g, dg], writes=[tf[4]])
                        ys = ystg[ne % 2]
                        ne += 1
                        P.op("dve", lambda e, oc=oc, ys=ys: e.scalar_tensor_tensor(
                            out=ys[:], in0=ps_v[:], scalar=dg[:, 6 + oc:7 + oc], in1=tf[4][:], op0=ALU.add, op1=ALU.mult),
                            reads=[ps_v, dg, tf[4]], writes=[ys])
                        P.op("sp", lambda e, oc=oc, ys=ys, s=s, qs=qs: e.dma_start(
                            out=scr["YB"][s, oc * 128:(oc + 1) * 128, qs], in_=ys[:]),
                            reads=[ys], writes=[scr["YBb"]], dma=True)
    P.barrier()


def merge_phase(C, dr, l, nseq, ntile, modv, gains, xT, xres, xview, scr):
    P = C.P
    wd = dr["w_inT"]
    with ExitStack() as st:
        hT = [C.sb(st, [128, NKC, TT], BF16, "mhT%d" % i) for i in range(2)]
        yy = [C.sb(st, [128, NKC, TT], BF16, "myy%d" % i) for i in range(2)]
        mg_ = C.sb(st, [128, NKC, TT], BF16, "merged", n=NKC)
        gwb = [C.sb(st, [128, NKC, GW], BF16, "mgw%d" % i) for i in range(6)]
        bwb = [C.sb(st, [128, NKC, GW], BF16, "mbw%d" % i) for i in range(2)]
        owb = [C.sb(st, [128, NKC, GW], BF16, "mow%d" % i) for i in range(2)]
        sgb = [C.sb(st, [128, TT], F32, "msg%d" % i) for i in range(3)]
        tb = [C.sb(st, [128, TT], F32, "mtb%d" % i) for i in range(3)]
        acc = [C.sb(st, [128, TT], F32, "macc%d" % i) for i in range(2)]
        xr = [C.sb(st, [128, TT], F32, "mxr%d" % i) for i in range(4)]
        AB = C.sb(st, [128, nseq, 3, NKC], F32, "mAB")
        ps_gt = [C.ps(st, [128, TT], F32, "ps_gt%d" % i) for i in range(2)]
        ps_b = [C.ps(st, [128, TT], F32, "ps_b%d" % i) for i in range(2)]
        ps_o = [C.ps(st, [128, TT], F32, "ps_mo%d" % i) for i in range(2)]
        mod_vectors(C, AB, modv, gains, 1, nseq, 1.0)
        lg = [(lambda b, g=28 + j * 8 + m: P.op("pool", lambda e: e.dma_start(out=b[:], in_=wd[l, g]), writes=[b], dma=True))
              for _ in range(ntile) for m in range(8) for j in range(3)]
        lb = [(lambda b, m=m: P.op("pool", lambda e: e.dma_start(out=b[:], in_=dr["w_br"][l, m]), writes=[b], dma=True))
              for _ in range(ntile) for m in range(8)]
        lo = [(lambda b, m=m: P.op("pool", lambda e: e.dma_start(out=b[:], in_=dr["w_outT"][l, m]), writes=[b], dma=True))
              for _ in range(ntile) for m in range(8)]
        sg_, sb_, so_ = Stream(gwb, lg, depth=4), Stream(bwb, lb), Stream(owb, lo)
        KOFF = (0, 8, 14)
        KN = (8, 6, 2)

        def load_acts(i):
            s, ti = divmod(i, S // TT)
            tsl = slice(ti * TT, (ti + 1) * TT)
            h, y = hT[i % 2], yy[i % 2]
            P.op("sp", lambda e: e.dma_start(out=h[:], in_=scr["HT"][s].rearrange("(c p) t -> p c t", p=128)[:, :, tsl]),
                 reads=[(scr["HTb"], i)], writes=[h], dma=True)
            P.op("sp", lambda e: e.dma_start(out=y[:, 0:8, :], in_=scr["YA"][s].rearrange("(c p) t -> p c t", p=128)[:, :, tsl]),
                 reads=[scr["YAb"]], writes=[y], dma=True)
            P.op("sp", lambda e: e.dma_start(out=y[:, 8:14, :], in_=scr["YB"][s].rearrange("(c p) t -> p c t", p=128)[:, :, tsl]),
                 reads=[scr["YBb"]], writes=[y], dma=True)
            P.op("sp", lambda e: e.dma_start(out=y[:, 14:16, :], in_=scr["YC"][s].rearrange("(c p) t -> p c t", p=128)[:, :, tsl]),
                 reads=[scr["YCb"]], writes=[y], dma=True)

        load_acts(0)
        for i in range(ntile):
            b = i // (S // TT)
            if i + 1 < ntile:
                load_acts(i + 1)
            h, y = hT[i % 2], yy[i % 2]
            for m8 in range(8):
                gws = [sg_.get((i * 8 + m8) * 3 + j) for j in range(3)]
                bw = sb_.get(i * 8 + m8)
                for mi in range(2):
                    m = 2 * m8 + mi
                    csl = slice(mi * 128, (mi + 1) * 128)
                    for j in range(3):
                        pg, pb = ps_gt[(3 * m + j) % 2], ps_b[(3 * m + j) % 2]
                        for c in range(NKC):
                            mm(C, pg[:], gws[j][:, c, csl], h[:, c, :], c == 0, c == NKC - 1, [gws[j], h], pg, lazy=True)
                        for kc in range(KN[j]):
                            mm(C, pb[:], bw[:, KOFF[j] + kc, csl], y[:, KOFF[j] + kc, :], kc == 0, kc == KN[j] - 1,
                               [bw, y], pb, lazy=True)
                        sgt, tj = sgb[j], tb[j]
                        P.op("act", lambda e, sgt=sgt, pg=pg: e.activation(out=sgt[:], in_=pg[:], func=AF.Sigmoid),
                             reads=[pg], writes=[sgt])
                        P.op("dve", lambda e, sgt=sgt, tj=tj, pb=pb: e.tensor_tensor(out=tj[:], in0=pb[:], in1=sgt[:], op=ALU.mult),
                             reads=[pb, sgt], writes=[tj])
                    a_ = acc[m % 2]
                    P.op("pool", lambda e, a_=a_: e.tensor_tensor(out=a_[:], in0=tb[0][:], in1=tb[1][:], op=ALU.add),
                         reads=[tb[0], tb[1]], writes=[a_])
                    P.op("pool", lambda e, a_=a_, m=m: e.tensor_tensor(out=mg_[:, m, :], in0=a_[:], in1=tb[2][:], op=ALU.add),
                         reads=[a_, tb[2]], writes=[(mg_, m)])
            for m8 in range(8):
                ow = so_.get(i * 8 + m8)
                for mi in range(2):
                    m = 2 * m8 + mi
                    po = ps_o[m % 2]
                    r = xr[m % 4]
                    P.op("sp", lambda e, i=i, m=m, r=r: e.dma_start(out=r[:], in_=xview(xT, i)[:, m, :]),
                         reads=[(xres, i)], writes=[r], dma=True)
                    for c in range(NKC):
                        mm(C, po[:], ow[:, c, mi * 128:(mi + 1) * 128], mg_[:, c, :], c == 0, c == NKC - 1,
                           [ow, (mg_, c)], po, lazy=True)
                    P.op("dve", lambda e, m=m, r=r, po=po, b=b: e.scalar_tensor_tensor(
                        out=r[:], in0=po[:], scalar=AB[:, b, 2, m:m + 1], in1=r[:], op0=ALU.mult, op1=ALU.add),
                        reads=[po, AB, r], writes=[r])
                    P.op("sp", lambda e, i=i, m=m, r=r: e.dma_start(out=xview(xT, i)[:, m, :], in_=r[:]),
                         reads=[r], writes=[(xres, i)], dma=True)
    P.barrier()


def dil_phase(C, dr, l, nseq, scr):
    P = C.P
    DILS = (1, 4, 16)
    NEG = -30000.0
    with ExitStack() as st:
        tmp = C.sb(st, [128, 128], F32, "dtmp")
        pst = C.ps(st, [128, 512], F32, "dpst")
        gv = C.sb(st, [128, 2], F32, "dgv")
        bones = C.sb(st, [128, 128], BF16, "bones")
        ones64 = C.sb(st, [128, 64], BF16, "ones64")
        d0i = C.sb(st, [128, 256], mybir.dt.int32, "d0i")
        d0 = C.sb(st, [128, 256], F32, "d0")
        bias = C.sb(st, [128, 12, 256], F32, "dbias")
        qraw = C.sb(st, [128, 2, S], F32, "qraw")
        kraw = C.sb(st, [128, 2, S], F32, "kraw")
        qn = C.sb(st, [128, 2, S], BF16, "qn")
        kn = C.sb(st, [128, 2, S], BF16, "kn")
        vb = C.sb(st, [128, 2, S], BF16, "vb")
        vtok = C.sb(st, [128, 2, 16, 128], BF16, "vtok")
        acc = C.sb(st, [64, 2, 4, S], F32, "dacc")
        sq = [C.sb(st, [128, 512], BF16, "dsq%d" % i) for i in range(2)]
        msb = [C.sb(st, [128, 512], F32, "dms%d" % i) for i in range(2)]
        rsb = [C.sb(st, [128, 512], F32, "drs%d" % i) for i in range(2)]
        stmp = [C.sb(st, [128, 256], F32, "dst%d" % i) for i in range(2)]
        pT = [C.sb(st, [128, 256], BF16, "dpT%d" % i) for i in range(3)]
        rec = C.sb(st, [64, S], F32, "drec")
        ob = C.sb(st, [64, S], BF16, "dob")
        ps_n = C.ps(st, [128, 512], F32, "ps_dn")
        ps_s = [C.ps(st, [128, 512], F32, "ps_ds%d" % i) for i in range(2)]
        ps_nd = [C.ps(st, [128, 512], F32, "ps_dnd%d" % i) for i in range(2)]
        ps_vt = C.ps(st, [128, 1024], BF16, "ps_dvt")

        load_vec_fm(C, st, gv, gv[:, :], dr["dil_g"][l], 2, tmp, pst, C.ident_f)
        P.op("dve", lambda e: e.tensor_scalar(out=gv[:, 0:1], in0=gv[:, 0:1], scalar1=0.125, scalar2=None, op0=ALU.mult),
             reads=[gv], writes=[gv])
        P.op("pool", lambda e: e.memset(bones[:], 0.0), writes=[bones])
        P.op("pool", lambda e: e.memset(bones[0:64, 0:64], 1.0), writes=[bones])
        P.op("pool", lambda e: e.memset(bones[64:128, 64:128], 1.0), writes=[bones])
        P.op("pool", lambda e: e.memset(ones64[:], 1.0), writes=[ones64])
        P.op("pool", lambda e: e.iota(d0i[:], pattern=[[1, 256]], base=0, channel_multiplier=-1), writes=[d0i])
        P.op("dve", lambda e: e.tensor_copy(out=d0[:], in_=d0i[:]), reads=[d0i], writes=[d0])
        for hg in range(12):
            slope = 2.0 ** (-8.0 * (hg + 1) / 12.0)
            dil = DILS[hg // 4]
            P.op("dve", lambda e, hg=hg, v=-slope * dil: e.tensor_scalar(out=bias[:, hg, :], in0=d0[:], scalar1=v, scalar2=None,
                                                                         op0=ALU.mult), reads=[d0], writes=[bias])
            P.op("pool", lambda e, hg=hg: e.affine_select(out=bias[:, hg, :], in_=bias[:, hg, :], pattern=[[1, 256]],
                                                          compare_op=ALU.is_ge, fill=C.fill(NEG), base=0, channel_multiplier=-1),
                 reads=[bias], writes=[bias])
            P.op("pool", lambda e, hg=hg: e.affine_select(out=bias[:, hg, :], in_=bias[:, hg, :], pattern=[[-1, 256]],
                                                          compare_op=ALU.is_ge, fill=C.fill(NEG), base=128, channel_multiplier=1),
                 reads=[bias], writes=[bias])

        nsc = 0
        for s in range(nseq):
            pc = scr["PC"][s]
            for gi in range(3):
                dil = DILS[gi]
                nb = S // dil // 128
                for (dst, r0, eng) in ((qraw, 0, "sp"), (kraw, 768, "sp")):
                    P.op(eng, lambda e, dst=dst, r0=r0: e.dma_start(
                        out=dst[:], in_=pc[r0 + gi * 256:r0 + gi * 256 + 256, :].rearrange("(c p) t -> p c t", p=128)),
                        reads=[scr["PCb"]], writes=[dst], dma=True)
                P.op("pool", lambda e: e.dma_start(
                    out=vb[:], in_=pc[1536 + gi * 256:1536 + gi * 256 + 256, :].rearrange("(c p) t -> p c t", p=128)),
                    reads=[scr["PCb"]], writes=[vb], dma=True)
                k2 = 0
                for (raw, nrm, gcol) in ((qraw, qn, 0), (kraw, kn, 1)):
                    for c2 in range(2):
                        for q4 in range(S // 512):
                            qs = slice(q4 * 512, (q4 + 1) * 512)
                            sq_, ms_, rs_ = sq[k2 % 2], msb[k2 % 2], rsb[k2 % 2]
                            k2 += 1
                            P.op("act", lambda e, raw=raw, c2=c2, qs=qs, sq_=sq_: e.activation(out=sq_[:], in_=raw[:, c2, qs], func=AF.Square),
                                 reads=[raw], writes=[sq_])
                            mm(C, ps_n[:], bones[:], sq_[:], True, True, [bones, sq_], ps_n)
                            P.op("act", lambda e, ms_=ms_: e.activation(out=ms_[:], in_=ps_n[:], func=AF.Sqrt, bias=C.epsc[:, 0:1],
                                                                        scale=1.0 / 64), reads=[ps_n, C.epsc], writes=[ms_])
                            P.op("dve", lambda e, ms_=ms_, rs_=rs_: e.reciprocal(out=rs_[:], in_=ms_[:]), reads=[ms_], writes=[rs_])
                            P.op("dve", lambda e, raw=raw, nrm=nrm, c2=c2, qs=qs, rs_=rs_, gcol=gcol: e.scalar_tensor_tensor(
                                out=nrm[:, c2, qs], in0=raw[:, c2, qs], scalar=gv[:, gcol:gcol + 1], in1=rs_[:],
                                op0=ALU.mult, op1=ALU.mult), reads=[raw, gv, rs_], writes=[nrm])
                for c2 in range(2):
                    for r in range(dil):
                        for n in range(nb):
                            bi = r * nb + n
                            ks = slice(r + dil * 128 * n, r + dil * 128 * n + dil * 127 + 1, dil)
                            P.op("pe", lambda e, c2=c2, ks=ks, bi=bi: e.transpose(ps_vt[:, (bi % 8) * 128:(bi % 8 + 1) * 128], vb[:, c2, ks], C.ident_bf[:]),
                                 reads=[vb, C.ident_bf], writes=[ps_vt])
                            P.op("act", lambda e, c2=c2, bi=bi: e.activation(out=vtok[:, c2, bi, :], in_=ps_vt[:, (bi % 8) * 128:(bi % 8 + 1) * 128], func=AF.Copy),
                                 reads=[ps_vt], writes=[vtok])
                for hh in range(4):
                    c2, pb = hh // 2, (hh % 2) * 64
                    hg = gi * 4 + hh
                    for r in range(dil):
                        prev = None
                        for n in range(nb):
                            bi = r * nb + n
                            nq = 256 if n + 1 < nb else 128
                            t0 = r + dil * 128 * n
                            ks = slice(t0, t0 + dil * 127 + 1, dil)
                            qsl = slice(t0, t0 + dil * (nq - 1) + 1, dil)
                            pss = ps_s[nsc % 2]
                            stp = stmp[nsc % 2]
                            cur = pT[nsc % 3]
                            psnd = ps_nd[nsc % 2]
                            nsc += 1
                            mm(C, pss[:, 0:nq], kn[pb:pb + 64, c2, ks], qn[pb:pb + 64, c2, qsl], True, True, [kn, qn], pss)
                            P.op("dve", lambda e, pss=pss, stp=stp, nq=nq, hg=hg: e.tensor_tensor(
                                out=stp[:, 0:nq], in0=pss[:, 0:nq], in1=bias[:, hg, 0:nq], op=ALU.add),
                                reads=[pss, bias], writes=[stp])
                            P.op("act", lambda e, stp=stp, cur=cur, nq=nq: e.activation(out=cur[:, 0:nq], in_=stp[:, 0:nq], func=AF.Exp),
                                 reads=[stp], writes=[cur])
                            for di, lhs_of in enumerate((lambda b_: vtok[:, c2, b_, pb:pb + 64], lambda b_: ones64[:])):
                                o_ap = psnd[0:64, di * 128:(di + 1) * 128]
                                if prev is not None:
                                    mm(C, o_ap, lhs_of(bi - 1), prev[:, 128:256], True, False, [vtok, ones64, prev], psnd)
                                mm(C, o_ap, lhs_of(bi), cur[:, 0:128], prev is None, True, [vtok, ones64, cur], psnd)
                            a_view = acc[:, :, hh, ks]
                            p_view = psnd[0:64, 0:256].rearrange("p (a j) -> p a j", j=128)
                            if gi == 0:
                                P.op("dve", lambda e, a_view=a_view, p_view=p_view: e.tensor_copy(out=a_view, in_=p_view),
                                     reads=[psnd], writes=[acc])
                            else:
                                P.op("dve", lambda e, a_view=a_view, p_view=p_view: e.tensor_tensor(
                                    out=a_view, in0=p_view, in1=a_view, op=ALU.add), reads=[psnd, acc], writes=[acc])
                            prev = cur
            for hh in range(4):
                P.op("dve", lambda e, hh=hh: e.reciprocal(out=rec[:], in_=acc[:, 1, hh, :]), reads=[acc], writes=[rec])
                P.op("dve", lambda e, hh=hh: e.tensor_tensor(out=ob[:], in0=acc[:, 0, hh, :], in1=rec[:], op=ALU.mult),
                     reads=[acc, rec], writes=[ob])
                P.op("sp", lambda e, hh=hh, s=s: e.dma_start(out=scr["YC"][s, hh * 64:(hh + 1) * 64, :], in_=ob[:]),
                     reads=[ob], writes=[scr["YCb"]], dma=True)
    P.barrier()


def gdn_phase(C, dr, l, nseq, scr):
    P = C.P
    NT = S // 128
    NEG = -30000.0
    with ExitStack() as st:
        cw = C.sb(st, [128, 24, 4], F32, "cw")
        ad = C.sb(st, [128, 16], F32, "gad")
        onm = C.sb(st, [128, 1], F32, "gon")
        one1 = C.sb(st, [128, 1], F32, "one1")
        triF = C.sb(st, [128, 128], F32, "triF")
        onesF = C.sb(st, [128, 128], F32, "onesF")
        mneg = C.sb(st, [128, 128], F32, "mneg")
        smask = C.sb(st, [128, 128], F32, "smask")
        ba = C.sb(st, [128, NT, 16], F32, "gba")
        sc = C.sb(st, [128, 10, NT, 8], F32, "gsc")
        BETA, G, GC, GL, EGC, NGC, EGL, KDEC, NEGC, NBETA = range(10)
        raw = [C.sb(st, [128, S + 3], F32, "graw%d" % i) for i in range(2)]
        cacc = [C.sb(st, [128, S], F32, "gcacc%d" % i) for i in range(2)]
        vTb = C.sb(st, [128, S], BF16, "gvTb")
        DT = C.sb(st, [128, NT, 128], F32, "gDT", n=NT)
        Gb = C.sb(st, [128, NT, 128], F32, "gGb", n=NT)
        Pp = [C.sb(st, [128, NT, 256], BF16, "gP%d" % i, n=NT) for i in range(2)]
        xob = [C.sb(st, [128, 4, 256], BF16, "gxo%d" % i) for i in range(2)]
        msk = C.sb(st, [128, 7, 2, 128], BF16, "gmsk")
        sqb = [C.sb(st, [128, 512], BF16, "gsq%d" % i) for i in range(2)]
        rnb = [C.sb(st, [128, 512], F32, "grn%d" % i) for i in range(2)]
        qT = [C.sb(st, [128, S], BF16, "gqT%d" % i) for i in range(2)]
        kT = [C.sb(st, [128, S], BF16, "gkT%d" % i) for i in range(2)]
        kd = [C.sb(st, [128, NT, 128], BF16, "gkd%d" % i, n=NT) for i in range(2)]
        vtok = [C.sb(st, [128, NT, 128], BF16, "gvtok%d" % i, n=NT) for i in range(2)]
        siluz = [C.sb(st, [128, S], F32, "gsz%d" % i) for i in range(2)]
        RRT = [C.sb(st, [128, NT, 256], BF16, "gRRT%d" % i, n=NT) for i in range(2)]
        attnT = [C.sb(st, [128, NT, 128], BF16, "gattn%d" % i, n=NT) for i in range(2)]
        hS = [C.sb(st, [128, 128], F32, "ghS%d" % i) for i in range(2)]
        hSb = [C.sb(st, [128, 128], BF16, "ghSb%d" % i) for i in range(2)]
        yst = [C.sb(st, [128, S], BF16, "gyst%d" % i) for i in range(2)]
        rb = [[C.sb(st, [128, 128], BF16, "grb%d%d" % (i, j)) for j in range(2)] for i in range(2)]
        vn = [[C.sb(st, [128, 128], BF16, "gvn%d%d" % (i, j)) for j in range(2)] for i in range(2)]
        o1 = [[C.sb(st, [128, 128], F32, "go1%d%d" % (i, j)) for j in range(2)] for i in range(2)]
        of = [[C.sb(st, [128, 128], F32, "gof%d%d" % (i, j)) for j in range(2)] for i in range(2)]
        osq = [[C.sb(st, [128, 128], F32, "gosq%d%d" % (i, j)) for j in range(2)] for i in range(2)]
        onb = [[C.sb(st, [128, 128], BF16, "gonb%d%d" % (i, j)) for j in range(2)] for i in range(2)]
        ssq = [[C.sb(st, [128, 4], F32, "gssq%d%d" % (i, j)) for j in range(2)] for i in range(2)]
        psA = C.ps(st, [128, 512], F32, "gpsA")
        psB = C.ps(st, [128, 512], F32, "gpsB")
        psC = C.ps(st, [128, 512], F32, "gpsC")
        psD = C.ps(st, [128, 512], F32, "gpsD")
        psE = C.ps(st, [128, 512], F32, "gpsE")
        psF = C.ps(st, [128, 512], F32, "gpsF")
        psTs = [C.ps(st, [128, 1024], BF16, "gpsT%d" % i) for i in range(2)]
        c4 = lambda k: slice(k * 128, (k + 1) * 128)

        P.op("sp", lambda e: e.dma_start(out=cw[:], in_=dr["gdn_conv"][l]), writes=[cw], dma=True)
        P.op("sp", lambda e: e.dma_start(out=ad[:], in_=dr["gdn_ad"][l]), writes=[ad], dma=True)
        P.op("sp", lambda e: e.dma_start(out=onm[:], in_=dr["gdn_on"][l]), writes=[onm], dma=True)
        P.op("pool", lambda e: e.dma_start(out=msk[:], in_=dr["gdn_masks"]), writes=[msk], dma=True)
        P.op("pool", lambda e: e.memset(one1[:], 1.0), writes=[one1])
        P.op("pool", lambda e: e.memset(onesF[:], 1.0), writes=[onesF])
        P.op("pool", lambda e: e.memset(triF[:], 1.0), writes=[triF])
        P.op("pool", lambda e: e.affine_select(out=triF[:], in_=triF[:], pattern=[[1, 128]], compare_op=ALU.is_ge,
                                               fill=C.fill(0.0), base=0, channel_multiplier=-1), reads=[triF], writes=[triF])
        P.op("pool", lambda e: e.memset(mneg[:], 0.0), writes=[mneg])
        P.op("pool", lambda e: e.affine_select(out=mneg[:], in_=mneg[:], pattern=[[1, 128]], compare_op=ALU.is_ge,
                                               fill=C.fill(NEG), base=0, channel_multiplier=-1), reads=[mneg], writes=[mneg])
        P.op("pool", lambda e: e.memset(smask[:], 1.0), writes=[smask])
        P.op("pool", lambda e: e.affine_select(out=smask[:], in_=smask[:], pattern=[[1, 128]], compare_op=ALU.is_ge,
                                               fill=C.fill(0.0), base=-1, channel_multiplier=-1), reads=[smask], writes=[smask])
        for rw in raw:
            P.op("pool", lambda e, rw=rw: e.memset(rw[:, 0:3], 0.0), writes=[rw])
        P.op("act", lambda e: e.activation(out=ad[:, 0:8], in_=ad[:, 0:8], func=AF.Exp), reads=[ad], writes=[ad])

        def g1(s, h, sl):
            pa = scr["PA"][s]
            k2 = 0
            for wi, (row0, kind) in enumerate(((h * 128, "q"), (1024 + h * 128, "k"), (2048 + h * 128, "v"), (3072 + h * 128, "z"))):
                rw = raw[wi % 2]
                ca = cacc[wi % 2]
                P.op("sp", lambda e, rw=rw, row0=row0: e.dma_start(out=rw[:, 3:], in_=pa[row0:row0 + 128, :]),
                     reads=[scr["PAb"]], writes=[rw], dma=True)
                if kind == "z":
                    P.op("act", lambda e, rw=rw: e.activation(out=siluz[sl][:], in_=rw[:, 3:], func=AF.Silu),
                         reads=[rw], writes=[siluz[sl]])
                    continue
                ch = row0 // 128
                P.op("dve", lambda e, rw=rw, ca=ca, ch=ch: e.tensor_scalar(out=ca[:], in0=rw[:, 0:S], scalar1=cw[:, ch, 0:1],
                                                                         scalar2=None, op0=ALU.mult), reads=[rw, cw], writes=[ca])
                for k in range(1, 4):
                    P.op("dve", lambda e, rw=rw, ca=ca, ch=ch, k=k: e.scalar_tensor_tensor(
                        out=ca[:], in0=rw[:, k:k + S], scalar=cw[:, ch, k:k + 1], in1=ca[:], op0=ALU.mult, op1=ALU.add),
                        reads=[rw, cw, ca], writes=[ca])
                if kind == "v":
                    P.op("act", lambda e, ca=ca: e.activation(out=vTb[:], in_=ca[:], func=AF.Silu), reads=[ca], writes=[vTb])
                    continue
                P.op("act", lambda e, ca=ca: e.activation(out=ca[:], in_=ca[:], func=AF.Silu), reads=[ca], writes=[ca])
                dstT = qT[sl] if kind == "q" else kT[sl]
                scl = 128.0 ** -0.5 if kind == "q" else 1.0
                for q4 in range(S // 512):
                    qs = slice(q4 * 512, (q4 + 1) * 512)
                    sq_, rn_ = sqb[k2 % 2], rnb[k2 % 2]
                    k2 += 1
                    P.op("act", lambda e, ca=ca, qs=qs, sq_=sq_: e.activation(out=sq_[:], in_=ca[:, qs], func=AF.Square),
                         reads=[ca], writes=[sq_])
                    mm(C, psF[:], C.ones_bf[:], sq_[:], True, True, [C.ones_bf, sq_], psF)
                    P.op("act", lambda e, rn_=rn_: e.activation(out=rn_[:], in_=psF[:], func=AF.Sqrt, bias=C.epsc[:, 0:1], scale=1.0),
                         reads=[C.epsc], writes=[psF, rn_])
                    P.op("dve", lambda e, rn_=rn_: e.reciprocal(out=rn_[:], in_=rn_[:]), reads=[rn_], writes=[rn_])
                    P.op("dve", lambda e, ca=ca, qs=qs, rn_=rn_, dstT=dstT, scl=scl: e.scalar_tensor_tensor(
                        out=dstT[:, qs], in0=ca[:, qs], scalar=scl, in1=rn_[:], op0=ALU.mult, op1=ALU.mult),
                        reads=[ca, rn_], writes=[dstT])
            for tc in range(NT):
                pt_ = psTs[tc % 2]
                P.op("pe", lambda e, tc=tc, pt_=pt_: e.transpose(pt_[:, 0:128], vTb[:, c4(tc)], C.ident_bf[:]),
                     reads=[vTb, C.ident_bf], writes=[pt_])
                P.op("act", lambda e, tc=tc, pt_=pt_: e.activation(out=vtok[sl][:, tc, :], in_=pt_[:, 0:128], func=AF.Copy),
                     writes=[pt_, (vtok[sl], tc)])
            for tc in range(NT):
                pt_ = psTs[tc % 2]
                P.op("pe", lambda e, tc=tc, pt_=pt_: e.transpose(pt_[:, 0:128], kT[sl][:, c4(tc)], C.ident_bf[:]),
                     reads=[kT[sl], C.ident_bf], writes=[pt_])
                P.op("act", lambda e, tc=tc, pt_=pt_: e.activation(out=kd[sl][:, tc, :], in_=pt_[:, 0:128], func=AF.Identity,
                                                                  scale=sc[:, KDEC, tc, h:h + 1]),
                     reads=[sc], writes=[pt_, (kd[sl], tc)])
            for tc in range(NT + 1):
                if tc < NT:
                    P.op("dve", lambda e, tc=tc: e.tensor_scalar(out=Gb[:, tc, :], in0=onesF[:], scalar1=sc[:, G, tc, h:h + 1],
                                                                 scalar2=None, op0=ALU.mult), reads=[onesF, sc], writes=[(Gb, tc)])
                    pd = psA if tc % 2 == 0 else psD
                    mm(C, pd[:, 0:128], Gb[:, tc, :], triF[:], True, False, [(Gb, tc), triF], pd)
                    mm(C, pd[:, 0:128], C.ident_f[:], mneg[:], False, True, [C.ident_f, mneg], pd)
                    P.op("act", lambda e, tc=tc, pd=pd: e.activation(out=DT[:, tc, :], in_=pd[:, 0:128], func=AF.Exp,
                                                                    bias=sc[:, NGC, tc, h:h + 1], scale=1.0),
                         reads=[sc], writes=[pd, (DT, tc)])
                    pk = psB if tc % 2 == 0 else psC
                    mm(C, pk[:, 0:128], kT[sl][:, c4(tc)], kT[sl][:, c4(tc)], True, True, [kT[sl]], pk)
                    mm(C, pk[:, 128:256], kT[sl][:, c4(tc)], qT[sl][:, c4(tc)], True, True, [kT[sl], qT[sl]], pk)
                if tc >= 1:
                    t = tc - 1
                    pk = psB if t % 2 == 0 else psC
                    P.op("dve", lambda e, t=t, pk=pk: e.scalar_tensor_tensor(
                        out=Pp[0][:, t, 0:128], in0=pk[:, 0:128], scalar=sc[:, NBETA, t, h:h + 1], in1=DT[:, t, :],
                        op0=ALU.mult, op1=ALU.mult), reads=[sc, (DT, t)], writes=[pk, (Pp[0], t)])
                    P.op("dve", lambda e, t=t, pk=pk: e.tensor_tensor(out=attnT[sl][:, t, :], in0=pk[:, 128:256], in1=DT[:, t, :],
                                                                      op=ALU.mult), reads=[(DT, t)], writes=[pk, (attnT[sl], t)])
            for tc in range(NT):
                pt_ = psTs[tc % 2]
                P.op("pe", lambda e, tc=tc, pt_=pt_: e.transpose(pt_[:, 0:128], Pp[0][:, tc, 0:128], C.ident_bf[:]),
                     reads=[(Pp[0], tc), C.ident_bf], writes=[pt_])
                P.op("act", lambda e, tc=tc, pt_=pt_: e.activation(out=Pp[0][:, tc, 128:256], in_=pt_[:, 0:128], func=AF.Copy),
                     writes=[pt_, (Pp[0], tc)])
                P.op("pool", lambda e, tc=tc: e.tensor_copy(out=RRT[sl][:, tc, 0:128], in_=C.ident_bf[:]),
                     reads=[C.ident_bf], writes=[(RRT[sl], tc)])
                P.op("pool", lambda e, tc=tc: e.tensor_copy(out=RRT[sl][:, tc, 128:256], in_=C.ident_bf[:]),
                     reads=[C.ident_bf], writes=[(RRT[sl], tc)])
            items = [(lvl, grp) for lvl in range(7) for grp in range(4)]
            R_ = RRT[sl]

            def my(idx):
                lvl, grp = items[idx]
                xo = xob[idx % 2]
                t0 = 4 * grp
                tr = range(t0, t0 + 4)
                mk = msk[:, lvl, :, :].rearrange("p a j -> p (a j)").unsqueeze(1).to_broadcast([128, 4, 256])
                P.op("pool", lambda e: e.tensor_tensor(out=xo[:], in0=Pp[0][:, t0:t0 + 4, :], in1=mk, op=ALU.mult),
                     reads=[(Pp[0], tr), msk], writes=[xo])
                yb = (psA, psB) if idx % 2 == 0 else (psC, psD)
                for k in range(4):
                    bk = yb[k // 2]
                    o_ = (k % 2) * 256
                    mm(C, bk[:, o_:o_ + 128], xo[:, k, 128:256], R_[:, t0 + k, 0:128], True, True, [xo, (R_, t0 + k)], bk)
                    mm(C, bk[:, o_ + 128:o_ + 256], xo[:, k, 0:128], R_[:, t0 + k, 128:256], True, True, [xo, (R_, t0 + k)], bk)
                for b2 in range(2):
                    bk = yb[b2]
                    t1 = t0 + 2 * b2
                    P.op("act", lambda e, bk=bk, t1=t1: e.activation(
                        out=Pp[1][:, t1:t1 + 2, :].rearrange("p a j -> p (a j)"), in_=bk[:, 0:512], func=AF.Copy),
                        writes=[bk, (Pp[1], (t1, t1 + 1))])

            def za(idx):
                lvl, grp = items[idx]
                t0 = 4 * grp
                zb = (psE, psF)
                for k in range(4):
                    bk = zb[k // 2]
                    o_ = (k % 2) * 256
                    mm(C, bk[:, o_:o_ + 128], R_[:, t0 + k, 128:256], Pp[1][:, t0 + k, 0:128], True, True,
                       [(R_, t0 + k), (Pp[1], t0 + k)], bk)
                    mm(C, bk[:, o_ + 128:o_ + 256], R_[:, t0 + k, 0:128], Pp[1][:, t0 + k, 128:256], True, True,
                       [(R_, t0 + k), (Pp[1], t0 + k)], bk)
                for b2 in range(2):
                    bk = zb[b2]
                    t1 = t0 + 2 * b2
                    rv = R_[:, t1:t1 + 2, :].rearrange("p a j -> p (a j)")
                    P.op("dve", lambda e, bk=bk, rv=rv: e.tensor_tensor(out=rv, in0=bk[:, 0:512], in1=rv, op=ALU.add),
                         writes=[bk, (R_, (t1, t1 + 1))])

            for idx in range(len(items)):
                my(idx)
                if idx >= 1:
                    za(idx - 1)
            za(len(items) - 1)

        def g2(s, heads):
            banks = ((psA, psB, psC), (psD, psE, psF))
            sls = range(len(heads))
            for sl in sls:
                P.op("pool", lambda e, sl=sl: e.memset(hS[sl][:], 0.0), writes=[hS[sl]])
                P.op("pool", lambda e, sl=sl: e.memset(hSb[sl][:], 0.0), writes=[hSb[sl]])
            for tc in range(NT):
                j2 = tc % 2
                for sl in sls:
                    bx = banks[sl][0]
                    mm(C, bx[:, 0:128], kT[sl][:, c4(tc)], hSb[sl][:], True, True, [kT[sl], hSb[sl]], bx)
                    mm(C, bx[:, 128:256], qT[sl][:, c4(tc)], hSb[sl][:], True, True, [qT[sl], hSb[sl]], bx)
                for sl in sls:
                    h = heads[sl]
                    bx = banks[sl][0]
                    r_, o1_ = rb[sl][j2], o1[sl][j2]
                    P.op("dve", lambda e, tc=tc, h=h, r_=r_, bx=bx, sl=sl: e.scalar_tensor_tensor(
                        out=r_[:], in0=bx[:, 0:128], scalar=sc[:, NEGC, tc, h:h + 1], in1=vtok[sl][:, tc, :], op0=ALU.mult, op1=ALU.add),
                        reads=[sc, (vtok[sl], tc)], writes=[bx, r_])
                    P.op("dve", lambda e, tc=tc, h=h, o1_=o1_, bx=bx: e.tensor_scalar(
                        out=o1_[:], in0=bx[:, 128:256], scalar1=sc[:, EGC, tc, h:h + 1], scalar2=None, op0=ALU.mult),
                        reads=[sc], writes=[bx, o1_])
                for sl in sls:
                    by = banks[sl][1]
                    mm(C, by[:, 0:128], RRT[sl][:, tc, 0:128], rb[sl][j2][:], True, True, [(RRT[sl], tc), rb[sl][j2]], by)
                for sl in sls:
                    h = heads[sl]
                    by = banks[sl][1]
                    vn_ = vn[sl][j2]
                    P.op("act", lambda e, tc=tc, h=h, vn_=vn_, by=by: e.activation(out=vn_[:], in_=by[:, 0:128], func=AF.Identity,
                                                                                   scale=sc[:, BETA, tc, h:h + 1]),
                         reads=[sc], writes=[by, vn_])
                for sl in sls:
                    bz = banks[sl][2]
                    vn_ = vn[sl][j2]
                    mm(C, bz[:, 0:128], attnT[sl][:, tc, :], vn_[:], True, True, [(attnT[sl], tc), vn_], bz)
                    mm(C, bz[:, 128:256], kd[sl][:, tc, :], vn_[:], True, True, [(kd[sl], tc), vn_], bz)
                for sl in sls:
                    h = heads[sl]
                    bz = banks[sl][2]
                    P.op("dve", lambda e, tc=tc, h=h, bz=bz, sl=sl: e.scalar_tensor_tensor(
                        out=hS[sl][:], in0=hS[sl][:], scalar=sc[:, EGL, tc, h:h + 1], in1=bz[:, 128:256], op0=ALU.mult, op1=ALU.add),
                        reads=[hS[sl], sc], writes=[bz, hS[sl]])
                    P.op("act", lambda e, sl=sl: e.activation(out=hSb[sl][:], in_=hS[sl][:], func=AF.Copy),
                         reads=[hS[sl]], writes=[hSb[sl]])
                    of_, o1_ = of[sl][j2], o1[sl][j2]
                    P.op("dve", lambda e, o1_=o1_, of_=of_, bz=bz: e.tensor_tensor(out=of_[:], in0=bz[:, 0:128], in1=o1_[:], op=ALU.add),
                         reads=[o1_], writes=[bz, of_])
                for sl in sls:
                    of_, osq_, ssq_, onb_ = of[sl][j2], osq[sl][j2], ssq[sl][j2], onb[sl][j2]
                    P.op("pool", lambda e, of_=of_, osq_=osq_: e.tensor_tensor(out=osq_[:], in0=of_[:], in1=of_[:], op=ALU.mult),
                         reads=[of_], writes=[osq_])
                    P.op("dve", lambda e, osq_=osq_, ssq_=ssq_: e.tensor_reduce(out=ssq_[:, 0:1], in_=osq_[:], axis=mybir.AxisListType.X, op=ALU.add),
                         reads=[osq_], writes=[ssq_])
                    P.op("act", lambda e, ssq_=ssq_: e.activation(out=ssq_[:, 1:2], in_=ssq_[:, 0:1], func=AF.Sqrt, bias=C.epsc[:, 0:1],
                                                                  scale=1.0 / 128), reads=[ssq_, C.epsc], writes=[ssq_])
                    P.op("dve", lambda e, ssq_=ssq_: e.reciprocal(out=ssq_[:, 2:3], in_=ssq_[:, 1:2]), reads=[ssq_], writes=[ssq_])
                    P.op("dve", lambda e, of_=of_, onb_=onb_, ssq_=ssq_: e.tensor_scalar(out=onb_[:], in0=of_[:], scalar1=ssq_[:, 2:3],
                                                                                      scalar2=None, op0=ALU.mult),
                         reads=[of_, ssq_], writes=[onb_])
                for sl in sls:
                    onb_ = onb[sl][j2]
                    pt_ = psTs[sl]
                    P.op("pe", lambda e, onb_=onb_, pt_=pt_: e.transpose(pt_[:, 0:128], onb_[:], C.ident_bf[:]),
                         reads=[onb_, C.ident_bf], writes=[pt_])
                    P.op("dve", lambda e, tc=tc, pt_=pt_, sl=sl: e.scalar_tensor_tensor(
                        out=yst[sl][:, c4(tc)], in0=pt_[:, 0:128], scalar=onm[:, 0:1], in1=siluz[sl][:, c4(tc)], op0=ALU.mult, op1=ALU.mult),
                        reads=[onm, siluz[sl]], writes=[pt_, yst[sl]])
            for sl in sls:
                h = heads[sl]
                P.op("sp", lambda e, h=h, sl=sl: e.dma_start(out=scr["YA"][s, h * 128:(h + 1) * 128, :], in_=yst[sl][:]),
                     reads=[yst[sl]], writes=[scr["YAb"]], dma=True)

        for s in range(nseq):
            P.op("sp", lambda e, s=s: e.dma_start(out=ba[:], in_=scr["BA"][s].rearrange("(a p) k -> p a k", p=128)),
                 reads=[scr["BAb"]], writes=[ba], dma=True)
            S_ = lambda k: sc[:, k, :, :]
            P.op("act", lambda e: e.activation(out=S_(BETA), in_=ba[:, :, 0:8], func=AF.Sigmoid), reads=[ba], writes=[sc])
            P.op("dve", lambda e: e.tensor_scalar(out=S_(NBETA), in0=S_(BETA), scalar1=-1.0, scalar2=None, op0=ALU.mult),
                 reads=[sc], writes=[sc])
            P.op("dve", lambda e: e.tensor_tensor(out=S_(G), in0=ba[:, :, 8:16],
                                                  in1=ad[:, 8:16].unsqueeze(1).to_broadcast([128, NT, 8]), op=ALU.add),
                 reads=[ba, ad], writes=[sc])
            P.op("act", lambda e: e.activation(out=S_(G), in_=S_(G), func=AF.Exp), reads=[sc], writes=[sc])
            P.op("act", lambda e: e.activation(out=S_(G), in_=S_(G), func=AF.Ln, bias=one1[:], scale=1.0),
                 reads=[sc, one1], writes=[sc])
            P.op("dve", lambda e: e.tensor_tensor(out=S_(G), in0=S_(G),
                                                  in1=ad[:, 0:8].unsqueeze(1).to_broadcast([128, NT, 8]), op=ALU.mult),
                 reads=[sc, ad], writes=[sc])
            P.op("dve", lambda e: e.tensor_scalar(out=S_(G), in0=S_(G), scalar1=-1.0, scalar2=None, op0=ALU.mult),
                 reads=[sc], writes=[sc])
            for tc in range(NT):
                mm(C, psF[:, tc * 8:(tc + 1) * 8], triF[:], sc[:, G, tc, :], True, True, [triF, sc], psF)
                mm(C, psF[:, 128 + tc * 8:128 + (tc + 1) * 8], onesF[:], sc[:, G, tc, :], True, True, [onesF, sc], psF)
            P.op("dve", lambda e: e.tensor_copy(out=S_(GC), in_=psF[:, 0:128].rearrange("p (a k) -> p a k", k=8)),
                 writes=[psF, sc])
            P.op("dve", lambda e: e.tensor_copy(out=S_(GL), in_=psF[:, 128:256].rearrange("p (a k) -> p a k", k=8)),
                 writes=[psF, sc])
            P.op("act", lambda e: e.activation(out=S_(EGC), in_=S_(GC), func=AF.Exp), reads=[sc], writes=[sc])
            P.op("act", lambda e: e.activation(out=S_(EGL), in_=S_(GL), func=AF.Exp), reads=[sc], writes=[sc])
            P.op("dve", lambda e: e.tensor_scalar(out=S_(NGC), in0=S_(GC), scalar1=-1.0, scalar2=None, op0=ALU.mult),
                 reads=[sc], writes=[sc])
            P.op("dve", lambda e: e.tensor_scalar(out=S_(NEGC), in0=S_(EGC), scalar1=-1.0, scalar2=None, op0=ALU.mult),
                 reads=[sc], writes=[sc])
            P.op("dve", lambda e: e.tensor_tensor(out=S_(KDEC), in0=S_(GL), in1=S_(GC), op=ALU.subtract),
                 reads=[sc], writes=[sc])
            P.op("act", lambda e: e.activation(out=S_(KDEC), in_=S_(KDEC), func=AF.Exp), reads=[sc], writes=[sc])
            for hp in range(4):
                heads = (2 * hp, 2 * hp + 1)
                for sl, h in enumerate(heads):
                    g1(s, h, sl)
                g2(s, heads)
    P.barrier()


def tile_cols(W, width, kc=None):
    K, N = W.shape
    return np.ascontiguousarray(W.reshape(K // 128, 128, N // width, width).transpose(2, 1, 0, 3))


def prep_weights(inp, depth=DEPTH):
    f = lambda a: np.asarray(a, dtype=np.float32)
    out = {}
    out["ada_w"] = np.stack([tile_cols(f(inp["ada_w"][l]), GW) for l in range(depth)])
    out["ada_b"] = np.ascontiguousarray(f(inp["ada_b"])[:depth].reshape(depth, 144, 128))
    out["norms"] = np.ascontiguousarray(np.stack(
        [f(inp["norm_ffn1"])[:depth], f(inp["norm_mix"])[:depth], f(inp["norm_ffn2"])[:depth]], axis=1
    ).reshape(depth, 3, NKC, 128))
    for nm in ("ffn1", "ffn2"):
        out[nm + "_w1"] = np.stack([tile_cols(f(inp[nm + "_w1"][l]), GW) for l in range(depth)])
        out[nm + "_w3"] = np.stack([tile_cols(f(inp[nm + "_w3"][l]), GW) for l in range(depth)])
        out[nm + "_w2"] = np.stack([tile_cols(f(inp[nm + "_w2"][l]), 128) for l in range(depth)])
    w_in = f(inp["w_in"])[:depth]
    segs = []
    for l in range(depth):
        W = w_in[l]
        segs.append(np.concatenate([tile_cols(W[:, 0:4096], GW), tile_cols(W[:, OFF_BU:OFF_CQ], GW),
                                    tile_cols(W[:, OFF_CQ:OFF_GATE], GW), tile_cols(W[:, OFF_GATE:], GW)], axis=0))
    out["w_inT"] = np.stack(segs)
    out["w_ba"] = np.stack([tile_cols(w_in[l][:, OFF_BA:OFF_BU], 16)[0] for l in range(depth)])
    out["w_br"] = np.stack([np.concatenate([tile_cols(f(inp["w_branch_a"][l]), GW), tile_cols(f(inp["w_branch_b"][l]), GW),
                                            tile_cols(f(inp["w_branch_c"][l]), GW)], axis=2) for l in range(depth)])
    out["w_outT"] = np.stack([tile_cols(f(inp["w_out"][l]), GW) for l in range(depth)])
    out["dil_g"] = np.ascontiguousarray(np.stack([np.tile(f(inp["dil_q_norm"])[:depth], (1, 2)),
                                                  np.tile(f(inp["dil_k_norm"])[:depth], (1, 2))], axis=1))
    out["gdn_conv"] = np.ascontiguousarray(f(inp["gdn_conv"])[:depth].reshape(depth, 4, 24, 128).transpose(0, 3, 2, 1))
    ad = np.concatenate([f(inp["gdn_a_log"])[:depth], f(inp["gdn_dt_bias"])[:depth]], axis=1)
    out["gdn_ad"] = np.ascontiguousarray(np.broadcast_to(ad[:, None, :], (depth, 128, 16)))
    out["gdn_on"] = np.ascontiguousarray(f(inp["gdn_out_norm"])[:depth].reshape(depth, 128, 1))
    jj, ii = np.meshgrid(np.arange(128), np.arange(128), indexing="ij")
    msk = np.zeros((128, 7, 2, 128), np.float32)
    for lv in range(7):
        bsz = 1 << lv
        m_ = (((jj // bsz) % 2 == 0) & (ii // bsz == jj // bsz + 1)).astype(np.float32)
        msk[:, lv, 0, :] = m_
        msk[:, lv, 1, :] = m_.T
    out["gdn_masks"] = msk
    G_, P_, I_ = 48, 64, 16
    out["s5_vec"] = np.ascontiguousarray(np.stack(
        [f(inp["s5_a_re"])[:depth].reshape(depth, 24, 128), f(inp["s5_a_im"])[:depth].reshape(depth, 24, 128),
         np.repeat(f(inp["s5_log_step"])[:depth], P_, axis=1).reshape(depth, 24, 128)], axis=1))
    out["s5_dg"] = np.ascontiguousarray(np.concatenate(
        [f(inp["s5_d"])[:depth].reshape(depth, 6, 128), f(inp["s5_glu_b"])[:depth].reshape(depth, 12, 128)], axis=1))
    Bm = np.zeros((depth, 128, 6, 2, 512), np.float32)
    Cm = np.zeros((depth, 128, 24, 2, 128), np.float32)
    for ri, (bn, cn) in enumerate((("s5_b_re", "s5_c_re"), ("s5_b_im", "s5_c_im"))):
        b = f(inp[bn])[:depth]
        cc = f(inp[cn])[:depth]
        for g in range(G_):
            kc, gl8 = divmod(g, 8)
            Bm[:, gl8 * 16:(gl8 + 1) * 16, kc, ri, gl8 * 64:(gl8 + 1) * 64] = b[:, g].transpose(0, 2, 1)
            sc, gl = divmod(g, 2)
            Cm[:, gl * 64:(gl + 1) * 64, sc, ri, gl8 * 16:(gl8 + 1) * 16] = cc[:, g].transpose(0, 2, 1)
    out["s5_Bm"] = Bm
    out["s5_Cm"] = Cm
    out["s5_gluw"] = np.stack([tile_cols(f(inp["s5_glu_w"][l]), GW) for l in range(depth)])
    return out


def kernel(**inputs):
    x = np.asarray(inputs["x"], dtype=np.float32)
    c = np.asarray(inputs["c"], dtype=np.float32)
    B = x.shape[0]
    nseq = B // NCORES
    wts = prep_weights(inputs)
    nc = build_program(nseq=nseq)
    in_maps = []
    for i in range(NCORES):
        xs = x[i * nseq:(i + 1) * nseq].reshape(nseq * S, D)
        m = dict(wts)
        m["xT"] = np.ascontiguousarray(xs.T)
        m["c"] = np.ascontiguousarray(c[i * nseq:(i + 1) * nseq])
        in_maps.append(m)
    res = run_bass_kernel_spmd(nc, in_maps, core_ids=list(range(NCORES)))
    outs = [np.ascontiguousarray(r["outT"].T).reshape(nseq, S, D) for r in res.results]
    return np.concatenate(outs, axis=0).astype(np.float32)
```

```python
import math
from contextlib import ExitStack

import numpy as np
import concourse.bass as bass
import concourse.mybir as mybir
from concourse.bass_utils import run_bass_kernel_spmd

F32 = mybir.dt.float32
BF16 = mybir.dt.bfloat16
AF = mybir.ActivationFunctionType
ALU = mybir.AluOpType

D = 2048
S = 2048
DFF = 5632
NKC = D // 128
NFC = DFF // 128
TT = 512
EPS = 1e-6
DEPTH = 2
NCORES = 8
IN_COLS = 13328
OFF_Z = 3072
OFF_BA = 4096
OFF_BU = 4112
OFF_CQ = 4880
OFF_GATE = 7184
GW = 256


class Buf:
    def __init__(self, name, n=1, t=None):
        self.name = name
        self.n = n
        self.t = t
        self.lastw = [None] * n
        self.readers = [[] for _ in range(n)]

    def __getitem__(self, idx):
        return self.t[idx]


class Op:
    __slots__ = ("eng", "dma", "sig")

    def __init__(self, eng, dma):
        self.eng = eng
        self.dma = dma
        self.sig = None


def _parts(acc):
    if isinstance(acc, Buf):
        return acc, range(acc.n)
    b, p = acc
    if p is None:
        return b, range(b.n)
    if isinstance(p, int):
        return b, (p,)
    return b, p


class Prog:
    ENGS = ("pe", "dve", "act", "pool", "sp")
    NS = 8

    def __init__(self, nc, stack):
        self.nc = nc
        self.e = {"pe": nc.tensor, "dve": nc.vector, "act": nc.scalar, "pool": nc.gpsimd, "sp": nc.sync}
        self.sem = {k: stack.enter_context(nc.semaphore("s_" + k)) for k in self.ENGS}
        self.cnt = {k: 0 for k in self.ENGS}
        self.dsem = {k: [stack.enter_context(nc.semaphore("d_%s%d" % (k, i))) for i in range(self.NS)]
                     for k in ("sp", "pool", "act")}
        self.dcnt = {k: 0 for k in self.dsem}
        self.dlast = {k: [None] * self.NS for k in self.dsem}
        self.waited = {k: {} for k in self.ENGS}
        self.last = {k: None for k in self.ENGS}
        self.nops = 0

    def _wait(self, eng, sig):
        sem, val = sig
        w = self.waited[eng]
        key = id(sem)
        if w.get(key, 0) < val:
            self.e[eng].wait_ge(sem, val)
            w[key] = val

    def op(self, eng, fn, reads=(), writes=(), dma=False, sig=True):
        o = Op(eng, dma)
        deps = []
        for acc in reads:
            b, ps = _parts(acc)
            for p in ps:
                lw = b.lastw[p]
                if lw is not None:
                    deps.append((lw, True))
                b.readers[p].append(o)
        for acc in writes:
            b, ps = _parts(acc)
            for p in ps:
                lw = b.lastw[p]
                if lw is not None:
                    deps.append((lw, False))
                for r in b.readers[p]:
                    if r is not o:
                        deps.append((r, False))
                b.lastw[p] = o
                b.readers[p] = []
        for d, raw in deps:
            if d is o:
                continue
            if (not d.dma) and (not dma) and d.eng == eng:
                if not raw or eng == "pe":
                    continue
            self._wait(eng, d.sig)
        if dma:
            k = self.dcnt[eng]
            slot = k % self.NS
            prev = self.dlast[eng][slot]
            if prev is not None:
                self._wait(eng, prev.sig)
            o.sig = (self.dsem[eng][slot], 16 * (k // self.NS + 1))
            self.dcnt[eng] = k + 1
            self.dlast[eng][slot] = o
            fn(self.e[eng]).then_inc(self.dsem[eng][slot], 16)
        elif not sig:
            o.sig = (self.sem[eng], self.cnt[eng] + 1)
            fn(self.e[eng])
        else:
            self.cnt[eng] += 1
            o.sig = (self.sem[eng], self.cnt[eng])
            fn(self.e[eng]).then_inc(self.sem[eng], 1)
            self.last[eng] = o
        self.nops += 1
        return o

    def barrier(self):
        sigs = [self.last[k].sig for k in self.ENGS if self.last[k] is not None]
        for q in self.dsem:
            for o in self.dlast[q]:
                if o is not None:
                    sigs.append(o.sig)
        for eng in self.ENGS:
            for s in sigs:
                self._wait(eng, s)

    def finish(self):
        sigs = [self.last[k].sig for k in self.ENGS if self.last[k] is not None]
        for q in self.dsem:
            for o in self.dlast[q]:
                if o is not None:
                    sigs.append(o.sig)
        for s in sigs:
            self._wait("sp", s)


class Ctx:
    def __init__(self, nc, stack):
        self.nc = nc
        self.P = Prog(nc, stack)
        self.gstack = stack
        self.uid = 0
        self._fill = {}

    def fill(self, val):
        if val not in self._fill:
            self._fill[val] = self.nc.gpsimd.to_reg(float(val))
        return self._fill[val]

    def sb(self, stack, shape, dt, name, n=1):
        self.uid += 1
        t = stack.enter_context(self.nc.sbuf_tensor("%s_%d" % (name, self.uid), list(shape), dt))
        return Buf(name, n, t)

    def ps(self, stack, shape, dt, name, n=1):
        self.uid += 1
        t = stack.enter_context(self.nc.psum_tensor("%s_%d" % (name, self.uid), list(shape), dt))
        return Buf(name, n, t)


class Stream:
    def __init__(self, bufs, loads, depth=None):
        self.bufs = bufs
        self.loads = loads
        self.depth = len(bufs) if depth is None else depth
        self.issued = 0

    def get(self, k):
        while self.issued < min(len(self.loads), k + self.depth):
            self.loads[self.issued](self.bufs[self.issued % len(self.bufs)])
            self.issued += 1
        return self.bufs[k % len(self.bufs)]


def mm(C, ps, lhsT, rhs, start, stop, reads, wr, lazy=False):
    C.P.op("pe", lambda e: e.matmul(ps, lhsT, rhs, start=start, stop=stop), reads=reads, writes=[wr],
           sig=bool(stop) or not lazy)


def load_vec_fm(C, stack, dst, dst_cols, src_rows_ap, nrows, tmp, pst, ident):
    P = C.P
    P.op("sp", lambda e: e.dma_start(out=tmp[0:nrows, :], in_=src_rows_ap), writes=[tmp], dma=True)
    P.op("pe", lambda e: e.transpose(pst[:, 0:nrows], tmp[0:nrows, :], ident[0:nrows, 0:nrows]),
         reads=[tmp, ident], writes=[pst])
    P.op("dve", lambda e: e.tensor_copy(out=dst_cols, in_=pst[:, 0:nrows]), reads=[pst], writes=[dst])


def build_program(nseq=2, depth=DEPTH, debug=False, phases=None):
    nc = bass.Bass("TRN2", target_bir_lowering=False)
    ntok = nseq * S
    ntile = ntok // TT
    dr = {}

    def din(name, shape, dt=F32):
        dr[name] = nc.dram_tensor(name, list(shape), dt, kind="ExternalInput").ap()
        return dr[name]

    def dscratch(name, shape, dt=F32):
        kind = "ExternalOutput" if debug else "Internal"
        dr[name] = nc.dram_tensor(name, list(shape), dt, kind=kind).ap()
        return dr[name]

    xT_in = din("xT", [D, ntok])
    c_in = din("c", [nseq, D])
    L = depth
    din("ada_w", [L, 72, 128, NKC, GW])
    din("ada_b", [L, 144, 128])
    din("norms", [L, 3, NKC, 128])
    for nm in ("ffn1", "ffn2"):
        din(nm + "_w1", [L, DFF // GW, 128, NKC, GW])
        din(nm + "_w3", [L, DFF // GW, 128, NKC, GW])
        din(nm + "_w2", [L, NKC, 128, NFC, 128])
    din("w_inT", [L, 52, 128, NKC, GW])
    din("w_ba", [L, 128, NKC, 16])
    din("w_br", [L, 8, 128, NKC, GW])
    din("w_outT", [L, 8, 128, NKC, GW])
    din("dil_g", [L, 2, 128])
    din("gdn_conv", [L, 128, 24, 4])
    din("gdn_ad", [L, 128, 16])
    din("gdn_on", [L, 128, 1])
    din("gdn_masks", [128, 7, 2, 128])
    din("s5_vec", [L, 3, 24, 128])
    din("s5_dg", [L, 18, 128])
    din("s5_Bm", [L, 128, 6, 2, 512])
    din("s5_Cm", [L, 128, 24, 2, 128])
    din("s5_gluw", [L, 6, 128, 6, GW])
    out_ap = nc.dram_tensor("outT", [D, ntok], F32, kind="ExternalOutput").ap()
    scr = {}
    scr["HT"] = dscratch("HT", [nseq, D, S], BF16)
    scr["PA"] = dscratch("PA", [nseq, 4096, S])
    scr["PB"] = dscratch("PB", [nseq, 768, S])
    scr["PC"] = dscratch("PC", [nseq, 2304, S])
    scr["BA"] = dscratch("BA", [nseq, S, 16])
    scr["YA"] = dscratch("YA", [nseq, 1024, S], BF16)
    scr["YB"] = dscratch("YB", [nseq, 768, S], BF16)
    scr["YC"] = dscratch("YC", [nseq, 256, S], BF16)
    for k in ("HT", "PA", "PB", "PC", "BA", "YA", "YB", "YC"):
        scr[k + "b"] = Buf(k, ntile)
    xT = dscratch("xres", [D, ntok])

    with ExitStack() as gs:
        C = Ctx(nc, gs)
        P = C.P
        ones_bf = C.sb(gs, [128, 128], BF16, "ones_bf")
        ident_f = C.sb(gs, [128, 128], F32, "ident_f")
        ident_bf = C.sb(gs, [128, 128], BF16, "ident_bf")
        neghalf = C.sb(gs, [128, TT], F32, "neghalf")
        modv = C.sb(gs, [128, nseq, 9, NKC], F32, "modv")
        gains = C.sb(gs, [128, 3, NKC], F32, "gains")
        P.op("pool", lambda e: e.memset(ones_bf[:], 1.0), writes=[ones_bf])
        P.op("pool", lambda e: e.memset(neghalf[:], -0.5), writes=[neghalf])
        epsc = C.sb(gs, [128, 1], F32, "epsc")
        P.op("pool", lambda e: e.memset(epsc[:], EPS), writes=[epsc])
        C.epsc = epsc
        P.op("pool", lambda e: e.memset(ident_f[:], 1.0), writes=[ident_f])
        P.op("pool", lambda e: e.affine_select(out=ident_f[:], in_=ident_f[:], pattern=[[1, 128]],
                                               compare_op=ALU.is_equal, fill=C.fill(0.0), base=0,
                                               channel_multiplier=-1),
             reads=[ident_f], writes=[ident_f])
        P.op("dve", lambda e: e.tensor_copy(out=ident_bf[:], in_=ident_f[:]), reads=[ident_f], writes=[ident_bf])
        C.ones_bf, C.ident_f, C.ident_bf, C.neghalf = ones_bf, ident_f, ident_bf, neghalf

        xres = Buf("xres", ntile)
        xin = Buf("xin", 1)

        def xview(ap, i):
            return ap.rearrange("(c p) t -> p c t", p=128)[:, :, i * TT:(i + 1) * TT]

        for l in range(depth):
            src = xT_in if l == 0 else xT
            ada_phase(C, dr, l, nseq, modv, gains)
            ffn_phase(C, dr, l, "ffn1", 0, nseq, ntile, modv, gains, src, xT, xres, xview)
            if phases is None or "proj" in phases:
                proj_phase(C, dr, l, nseq, ntile, modv, gains, xT, xres, xview, scr)
            if phases is None or "s5" in phases:
                s5_phase(C, dr, l, nseq, scr)
            if phases is None or "dil" in phases:
                dil_phase(C, dr, l, nseq, scr)
            if phases is None or "gdn" in phases:
                gdn_phase(C, dr, l, nseq, scr)
            if phases is None or "merge" in phases:
                merge_phase(C, dr, l, nseq, ntile, modv, gains, xT, xres, xview, scr)
            ffn_phase(C, dr, l, "ffn2", 2, nseq, ntile, modv, gains, xT, xT if l < depth - 1 else out_ap,
                      xres, xview)
        P.finish()
    return nc


def mod_vectors(C, AB, modv, gains, sub, nseq, gate_scale):
    P = C.P
    for b in range(nseq):
        P.op("dve", lambda e, b=b: e.scalar_tensor_tensor(
            out=AB[:, b, 0, :], in0=modv[:, b, 3 * sub + 1, :], scalar=1.0, in1=gains[:, sub, :],
            op0=ALU.add, op1=ALU.mult), reads=[modv, gains], writes=[AB])
        P.op("dve", lambda e, b=b: e.tensor_copy(out=AB[:, b, 1, :], in_=modv[:, b, 3 * sub, :]),
             reads=[modv], writes=[AB])
        P.op("dve", lambda e, b=b: e.tensor_scalar(out=AB[:, b, 2, :], in0=modv[:, b, 3 * sub + 2, :],
                                                   scalar1=gate_scale, scalar2=None, op0=ALU.mult),
             reads=[modv], writes=[AB])


def norm_mod(C, xs, hT, AB, b, sq, tmpf, ms, rstd, ps_stat):
    P = C.P
    for c in range(NKC):
        q = sq[c % len(sq)]
        P.op("act", lambda e, c=c, q=q: e.activation(out=q[:], in_=xs[:, c, :], func=AF.Square),
             reads=[xs], writes=[q])
        mm(C, ps_stat[:], C.ones_bf[:], q[:], c == 0, c == NKC - 1, [q, C.ones_bf], ps_stat)
    P.op("act", lambda e: e.activation(out=ms[:], in_=ps_stat[:], func=AF.Sqrt, bias=C.epsc[:, 0:1], scale=1.0 / D),
         reads=[ps_stat, C.epsc], writes=[ms])
    P.op("dve", lambda e: e.reciprocal(out=rstd[:], in_=ms[:]), reads=[ms], writes=[rstd])
    for c in range(NKC):
        t = tmpf[c % len(tmpf)]
        P.op("dve", lambda e, c=c, t=t: e.scalar_tensor_tensor(
            out=t[:], in0=xs[:, c, :], scalar=AB[:, b, 0, c:c + 1], in1=rstd[:],
            op0=ALU.mult, op1=ALU.mult), reads=[xs, AB, rstd], writes=[t])
        P.op("act", lambda e, c=c, t=t: e.activation(
            out=hT[:, c, :], in_=t[:], func=AF.Identity, bias=AB[:, b, 1, c:c + 1], scale=1.0),
            reads=[t, AB], writes=[(hT, c)])


def nm_squares(C, xs, sq16):
    for c in range(NKC):
        C.P.op("act", lambda e, c=c: e.activation(out=sq16[:, c, :], in_=xs[:, c, :], func=AF.Square),
               reads=[xs], writes=[(sq16, c)])


def nm_stats(C, sq16, ps_stat):
    for c in range(NKC):
        mm(C, ps_stat[:], C.ones_bf[:], sq16[:, c, :], c == 0, c == NKC - 1, [(sq16, c), C.ones_bf], ps_stat)


def nm_finish(C, xs, hT, AB, b, tmpf, ms, rstd, ps_stat):
    P = C.P
    P.op("act", lambda e: e.activation(out=ms[:], in_=ps_stat[:], func=AF.Sqrt, bias=C.epsc[:, 0:1], scale=1.0 / D),
         reads=[ps_stat, C.epsc], writes=[ms])
    P.op("dve", lambda e: e.reciprocal(out=rstd[:], in_=ms[:]), reads=[ms], writes=[rstd])
    for c in range(NKC):
        t = tmpf[c % len(tmpf)]
        P.op("dve", lambda e, c=c, t=t: e.scalar_tensor_tensor(
            out=t[:], in0=xs[:, c, :], scalar=AB[:, b, 0, c:c + 1], in1=rstd[:],
            op0=ALU.mult, op1=ALU.mult), reads=[xs, AB, rstd], writes=[t])
        P.op("act", lambda e, c=c, t=t: e.activation(
            out=hT[:, c, :], in_=t[:], func=AF.Identity, bias=AB[:, b, 1, c:c + 1], scale=1.0),
            reads=[t, AB], writes=[(hT, c)])


def ada_phase(C, dr, l, nseq, modv, gains):
    P = C.P
    with ExitStack() as st:
        tmp = C.sb(st, [128, 128], F32, "ada_tmp")
        pst = C.ps(st, [128, 512], F32, "ada_pst")
        psm = C.ps(st, [128, 512], F32, "ada_psm")
        cT = C.sb(st, [128, NKC, nseq], F32, "cT")
        bias = C.sb(st, [128, 144], F32, "ada_bias")
        wb = [C.sb(st, [128, NKC, GW], F32, "ada_w%d" % i) for i in range(2)]
        for c in range(NKC):
            P.op("sp", lambda e, c=c: e.dma_start(out=tmp[0:nseq, :], in_=dr["c"][:, c * 128:(c + 1) * 128]),
                 writes=[tmp], dma=True)
            P.op("pe", lambda e: e.transpose(pst[:, 0:nseq], tmp[0:nseq, :], C.ident_f[0:nseq, 0:nseq]),
                 reads=[tmp, C.ident_f], writes=[pst])
            P.op("act", lambda e, c=c: e.activation(out=cT[:, c, :], in_=pst[:, 0:nseq], func=AF.Silu),
                 reads=[pst], writes=[cT])
        load_vec_fm(C, st, bias, bias[:, 0:128], dr["ada_b"][l, 0:128, :], 128, tmp, pst, C.ident_f)
        load_vec_fm(C, st, bias, bias[:, 128:144], dr["ada_b"][l, 128:144, :], 16, tmp, pst, C.ident_f)
        for j in range(3):
            load_vec_fm(C, st, gains, gains[:, j, :], dr["norms"][l, j, :, :], NKC, tmp, pst, C.ident_f)
        loads = []
        for g in range(72):
            loads.append(lambda b, g=g: P.op("sp", lambda e: e.dma_start(out=b[:], in_=dr["ada_w"][l, g]),
                                             writes=[b], dma=True))
        strm = Stream(wb, loads)
        for g in range(72):
            w = strm.get(g)
            for fi in range(2):
                ch = 2 * g + fi
                for c in range(NKC):
                    mm(C, psm[:, ch * nseq:(ch + 1) * nseq], w[:, c, fi * 128:(fi + 1) * 128], cT[:, c, :],
                       c == 0, c == NKC - 1, [w, cT], psm)
        for b in range(nseq):
            P.op("dve", lambda e, b=b: e.tensor_tensor(
                out=modv[:, b, :, :].rearrange("p j c -> p (j c)"),
                in0=psm[:, 0:144 * nseq].rearrange("p (k b) -> p k b", b=nseq)[:, :, b],
                in1=bias[:], op=ALU.add), reads=[psm, bias], writes=[modv])
    P.barrier()


def ffn_phase(C, dr, l, nm, sub, nseq, ntile, modv, gains, src, dst, xres, xview):
    P = C.P
    w1d, w3d, w2d = dr[nm + "_w1"], dr[nm + "_w3"], dr[nm + "_w2"]
    NG = DFF // GW
    with ExitStack() as st:
        xs = C.sb(st, [128, NKC, TT], F32, "xs")
        hT = C.sb(st, [128, NKC, TT], BF16, "hT", n=NKC)
        actT = C.sb(st, [128, NFC, TT], BF16, "actT", n=NFC)
        w1b = [C.sb(st, [128, NKC, GW], BF16, "w1b%d" % i) for i in range(2)]
        w3b = [C.sb(st, [128, NKC, GW], BF16, "w3b%d" % i) for i in range(2)]
        w2b = [C.sb(st, [128, NFC, 128], BF16, "w2b%d" % i) for i in range(2)]
        sq16 = C.sb(st, [128, NKC, TT], BF16, "sq16", n=NKC)
        tmpf = [C.sb(st, [128, TT], F32, "tmpf%d" % i) for i in range(2)]
        sg = [C.sb(st, [128, TT], F32, "sg%d" % i) for i in range(2)]
        xr = [C.sb(st, [128, TT], F32, "xr%d" % i) for i in range(3)]
        ms = C.sb(st, [128, TT], F32, "ms")
        rstd = C.sb(st, [128, TT], F32, "rstd")
        AB = C.sb(st, [128, nseq, 3, NKC], F32, "AB")
        ps_stat = C.ps(st, [128, TT], F32, "ps_stat")
        ps_g = [C.ps(st, [128, TT], F32, "ps_g%d" % i) for i in range(2)]
        ps_u = [C.ps(st, [128, TT], F32, "ps_u%d" % i) for i in range(2)]
        ps_o = [C.ps(st, [128, TT], F32, "ps_o%d" % i) for i in range(2)]

        mod_vectors(C, AB, modv, gains, sub, nseq, 0.5)

        def mk(wd, g):
            return lambda b: P.op("pool", lambda e: e.dma_start(out=b[:], in_=wd[l, g]), writes=[b], dma=True)
        l1 = [mk(w1d, g) for _ in range(ntile) for g in range(NG)]
        l3 = [mk(w3d, g) for _ in range(ntile) for g in range(NG)]
        l2 = [mk(w2d, m) for _ in range(ntile) for m in range(NKC)]
        s1, s3, s2 = Stream(w1b, l1), Stream(w3b, l3), Stream(w2b, l2)

        def prep1(i):
            P.op("sp", lambda e, i=i: e.dma_start(out=xs[:], in_=xview(src, i)),
                 reads=[(xres, i)], writes=[xs], dma=True)
            nm_squares(C, xs, sq16)

        def prep23(i):
            nm_stats(C, sq16, ps_stat)
            nm_finish(C, xs, hT, AB, i // (S // TT), tmpf, ms, rstd, ps_stat)

        prep1(0)
        prep23(0)
        for i in range(ntile):
            b = i // (S // TT)
            for g in range(NG):
                k = i * NG + g
                wa, wb_ = s1.get(k), s3.get(k)
                if g == NG - 4:
                    s2.get(i * NKC)
                for fi in range(GW // 128):
                    f = g * (GW // 128) + fi
                    pg, pu = ps_g[f % 2], ps_u[f % 2]
                    for c in range(NKC):
                        mm(C, pg[:], wa[:, c, fi * 128:(fi + 1) * 128], hT[:, c, :], c == 0, c == NKC - 1,
                           [wa, (hT, c)], pg, lazy=True)
                    for c in range(NKC):
                        mm(C, pu[:], wb_[:, c, fi * 128:(fi + 1) * 128], hT[:, c, :], c == 0, c == NKC - 1,
                           [wb_, (hT, c)], pu, lazy=True)
                    s_ = sg[f % 2]
                    P.op("act", lambda e, s_=s_, pg=pg: e.activation(out=s_[:], in_=pg[:], func=AF.Silu),
                         reads=[pg], writes=[s_])
                    P.op("dve", lambda e, s_=s_, pu=pu, f=f: e.tensor_tensor(
                        out=actT[:, f, :], in0=pu[:], in1=s_[:], op=ALU.mult),
                        reads=[pu, s_], writes=[(actT, f)])
            if i + 1 < ntile:
                prep1(i + 1)
            for m in range(NKC):
                if m == NKC // 2 and i + 1 < ntile:
                    prep23(i + 1)
                k = i * NKC + m
                w2 = s2.get(k)
                po = ps_o[m % 2]
                r = xr[m % 3]
                P.op("sp", lambda e, i=i, m=m, r=r: e.dma_start(out=r[:], in_=xview(src, i)[:, m, :]),
                     reads=[(xres, i)], writes=[r], dma=True)
                for f in range(NFC):
                    mm(C, po[:], w2[:, f, :], actT[:, f, :], f == 0, f == NFC - 1, [w2, (actT, f)], po, lazy=True)
                P.op("dve", lambda e, m=m, r=r, po=po, b=b: e.scalar_tensor_tensor(
                    out=r[:], in0=po[:], scalar=AB[:, b, 2, m:m + 1], in1=r[:], op0=ALU.mult, op1=ALU.add),
                    reads=[po, AB, r], writes=[r])
                P.op("sp", lambda e, i=i, m=m, r=r: e.dma_start(out=xview(dst, i)[:, m, :], in_=r[:]),
                     reads=[r], writes=[(xres, i)], dma=True)
    P.barrier()


def proj_phase(C, dr, l, nseq, ntile, modv, gains, xT, xres, xview, scr):
    P = C.P
    wd = dr["w_inT"]
    NGP = 28
    with ExitStack() as st:
        xs = C.sb(st, [128, NKC, TT], F32, "xs")
        hTs = [C.sb(st, [128, NKC, TT], BF16, "hT%d" % i, n=NKC) for i in range(2)]
        sq16 = C.sb(st, [128, NKC, TT], BF16, "psq16", n=NKC)
        wb = [C.sb(st, [128, NKC, GW], BF16, "pw%d" % i) for i in range(3)]
        wba = C.sb(st, [128, NKC, 16], BF16, "wba")
        sq = [C.sb(st, [128, TT], BF16, "sq%d" % i) for i in range(3)]
        tmpf = [C.sb(st, [128, TT], F32, "tmpf%d" % i) for i in range(3)]
        stg = [C.sb(st, [128, TT], F32, "stg%d" % i) for i in range(4)]
        bas = [C.sb(st, [128, 16], F32, "bas%d" % i) for i in range(2)]
        ms = C.sb(st, [128, TT], F32, "ms")
        rstd = C.sb(st, [128, TT], F32, "rstd")
        AB = C.sb(st, [128, nseq, 3, NKC], F32, "AB")
        ps_stat = C.ps(st, [128, TT], F32, "ps_stat")
        ps_p = [C.ps(st, [128, TT], F32, "ps_p%d" % i) for i in range(3)]
        ps_ba = C.ps(st, [128, TT], F32, "ps_ba")
        mod_vectors(C, AB, modv, gains, 1, nseq, 1.0)
        P.op("pool", lambda e: e.dma_start(out=wba[:], in_=dr["w_ba"][l]), writes=[wba], dma=True)
        loads = [(lambda b, g=g: P.op("pool", lambda e: e.dma_start(out=b[:], in_=wd[l, g]), writes=[b], dma=True))
                 for _ in range(ntile) for g in range(NGP)]
        strm = Stream(wb, loads)
        nev = 0
        def prep1(i):
            P.op("sp", lambda e, i=i: e.dma_start(out=xs[:], in_=xview(xT, i)),
                 reads=[(xres, i)], writes=[xs], dma=True)
            nm_squares(C, xs, sq16)

        def prep23(i):
            s, ti = divmod(i, S // TT)
            tsl = slice(ti * TT, (ti + 1) * TT)
            h_ = hTs[i % 2]
            nm_stats(C, sq16, ps_stat)
            nm_finish(C, xs, h_, AB, s, tmpf, ms, rstd, ps_stat)
            P.op("sp", lambda e: e.dma_start(
                out=scr["HT"][s].rearrange("(c p) t -> p c t", p=128)[:, :, tsl], in_=h_[:]),
                reads=[h_], writes=[(scr["HTb"], i)], dma=True)

        prep1(0)
        prep23(0)
        for i in range(ntile):
            s, ti = divmod(i, S // TT)
            tsl = slice(ti * TT, (ti + 1) * TT)
            hT = hTs[i % 2]
            for ts in range(TT // 128):
                for c in range(NKC):
                    mm(C, ps_ba[:, ts * 16:(ts + 1) * 16], hT[:, c, ts * 128:(ts + 1) * 128], wba[:, c, :],
                       c == 0, c == NKC - 1, [(hT, c), wba], ps_ba)
            bb = bas[i % 2]
            for ts in range(TT // 128):
                pass
            P.op("dve", lambda e, bb=bb: e.tensor_copy(out=stg[3][:, 0:64], in_=ps_ba[:, 0:64]),
                 reads=[ps_ba], writes=[stg[3]])
            P.op("sp", lambda e, s=s, ti=ti: e.dma_start(
                out=scr["BA"][s, ti * TT:(ti + 1) * TT, :].rearrange("(a p) k -> p a k", p=128),
                in_=stg[3][:, 0:64].rearrange("p (a k) -> p a k", k=16)),
                reads=[stg[3]], writes=[(scr["BAb"], i)], dma=True)
            for g in range(NGP):
                if g == 0 and i + 1 < ntile:
                    prep1(i + 1)
                if g == NGP // 2 and i + 1 < ntile:
                    prep23(i + 1)
                w = strm.get(i * NGP + g)
                for fi in range(2):
                    ch = 2 * g + fi
                    pp = ps_p[ch % 3]
                    for c in range(NKC):
                        mm(C, pp[:], w[:, c, fi * 128:(fi + 1) * 128], hT[:, c, :], c == 0, c == NKC - 1,
                           [w, (hT, c)], pp, lazy=True)
                    sg_ = stg[nev % 3]
                    if nev % 2 == 0:
                        P.op("act", lambda e, sg_=sg_, pp=pp: e.activation(out=sg_[:], in_=pp[:], func=AF.Copy),
                             reads=[pp], writes=[sg_])
                    else:
                        P.op("dve", lambda e, sg_=sg_, pp=pp: e.tensor_copy(out=sg_[:], in_=pp[:]),
                             reads=[pp], writes=[sg_])
                    nev += 1
                    if ch < 32:
                        dst, row, key = scr["PA"], ch, "PAb"
                    elif ch < 38:
                        dst, row, key = scr["PB"], ch - 32, "PBb"
                    else:
                        dst, row, key = scr["PC"], ch - 38, "PCb"
                    P.op("sp", lambda e, dst=dst, row=row, s=s, tsl=tsl, sg_=sg_: e.dma_start(
                        out=dst[s, row * 128:(row + 1) * 128, tsl], in_=sg_[:]),
                        reads=[sg_], writes=[(scr[key], i)], dma=True)
    P.barrier()


def s5_phase(C, dr, l, nseq, scr):
    P = C.P
    NSC = 24
    TWO_PI = 2.0 * math.pi
    with ExitStack() as st:
        tmp = C.sb(st, [128, 128], F32, "s5tmp")
        pst = C.ps(st, [128, 512], F32, "s5pst")
        vec = C.sb(st, [128, 3, NSC], F32, "s5vec")
        dg = C.sb(st, [128, 18], F32, "s5dg")
        sm = C.sb(st, [128, 24, NSC], F32, "s5sm")
        halfpi = C.sb(st, [128, 1], F32, "halfpi")
        Bre = C.sb(st, [128, NSC, 128], F32, "Bre")
        Bim = C.sb(st, [128, NSC, 128], F32, "Bim")
        Atre = C.sb(st, [128, NSC * 128], BF16, "Atre")
        Atim = C.sb(st, [128, NSC * 128], BF16, "Atim")
        Bm = C.sb(st, [128, 6, 2, 512], BF16, "Bm")
        Cm = C.sb(st, [128, NSC, 2, 128], BF16, "Cm")
        gw = C.sb(st, [128, 6, 6, GW], BF16, "gluw")
        tri = C.sb(st, [128, 128], BF16, "tri")
        for j in range(3):
            load_vec_fm(C, st, vec, vec[:, j, :], dr["s5_vec"][l, j], NSC, tmp, pst, C.ident_f)
        load_vec_fm(C, st, dg, dg[:, :], dr["s5_dg"][l], 18, tmp, pst, C.ident_f)
        P.op("pool", lambda e: e.dma_start(out=Bm[:], in_=dr["s5_Bm"][l]), writes=[Bm], dma=True)
        P.op("pool", lambda e: e.dma_start(out=Cm[:], in_=dr["s5_Cm"][l]), writes=[Cm], dma=True)
        for g in range(6):
            P.op("pool", lambda e, g=g: e.dma_start(out=gw[:, g], in_=dr["s5_gluw"][l, g]), writes=[gw], dma=True)
        P.op("pool", lambda e: e.memset(halfpi[:], math.pi / 2), writes=[halfpi])
        P.op("pool", lambda e: e.memset(tri[:], 1.0), writes=[tri])
        P.op("pool", lambda e: e.affine_select(out=tri[:], in_=tri[:], pattern=[[1, 128]], compare_op=ALU.is_ge,
                                               fill=C.fill(0.0), base=0, channel_multiplier=-1),
             reads=[tri], writes=[tri])

        V = lambda k: sm[:, k, :]

        def tt(o, a, b, op, eng="dve"):
            P.op(eng, lambda e: e.tensor_tensor(out=o, in0=a, in1=b, op=op), reads=[sm, vec], writes=[sm])

        def ts(o, a, s1, op0, s2=None, op1=None):
            if op1 is None:
                P.op("dve", lambda e: e.tensor_scalar(out=o, in0=a, scalar1=s1, scalar2=None, op0=op0),
                     reads=[sm, vec], writes=[sm])
            else:
                P.op("dve", lambda e: e.tensor_scalar(out=o, in0=a, scalar1=s1, scalar2=s2, op0=op0, op1=op1),
                     reads=[sm, vec], writes=[sm])

        def act(o, a, f, scale=1.0, bias=None):
            if bias is None:
                P.op("act", lambda e: e.activation(out=o, in_=a, func=f, scale=scale), reads=[sm, vec], writes=[sm])
            else:
                P.op("act", lambda e: e.activation(out=o, in_=a, func=f, scale=scale, bias=bias),
                     reads=[sm, vec, halfpi], writes=[sm])
        STEP, ARE, RS, TH, THR, T0, RHO, SIN, COS, LR, LI, MR, MI, CR, CI, DEN, NR, T1, T2, PWR, PWI, T3 = range(22)
        act(V(STEP), vec[:, 2, :], AF.Exp)
        ts(V(ARE), vec[:, 0, :], -1e-4, ALU.min)
        tt(V(RS), V(ARE), V(STEP), ALU.mult)
        tt(V(TH), vec[:, 1, :], V(STEP), ALU.mult)
        ts(V(THR), V(TH), 1.0, ALU.mult)
        for m in range(1, 6):
            ts(V(T0), V(TH), (2 * m - 1) * math.pi, ALU.is_gt, TWO_PI, ALU.mult)
            tt(V(THR), V(THR), V(T0), ALU.subtract)
        act(V(RHO), V(RS), AF.Exp)
        act(V(SIN), V(THR), AF.Sin)
        ts(V(T0), V(THR), -1.0, ALU.mult)
        tt(V(T0), V(T0), V(THR), ALU.max)
        act(V(COS), V(T0), AF.Sin, scale=-1.0, bias=halfpi[:])
        tt(V(LR), V(RHO), V(COS), ALU.mult)
        tt(V(LI), V(RHO), V(SIN), ALU.mult)
        P.op("dve", lambda e: e.reciprocal(out=V(T1), in_=V(RHO)), reads=[sm], writes=[sm])
        tt(V(MR), V(COS), V(T1), ALU.mult)
        tt(V(MI), V(SIN), V(T1), ALU.mult)
        ts(V(MI), V(MI), -1.0, ALU.mult)
        ts(V(NR), V(LR), -1.0, ALU.add)
        tt(V(T1), V(ARE), V(ARE), ALU.mult)
        tt(V(T2), vec[:, 1, :], vec[:, 1, :], ALU.mult)
        tt(V(DEN), V(T1), V(T2), ALU.add)
        P.op("dve", lambda e: e.reciprocal(out=V(DEN), in_=V(DEN)), reads=[sm], writes=[sm])
        tt(V(T1), V(NR), V(ARE), ALU.mult)
        tt(V(T2), V(LI), vec[:, 1, :], ALU.mult)
        tt(V(T1), V(T1), V(T2), ALU.add)
        tt(V(CR), V(T1), V(DEN), ALU.mult)
        tt(V(T1), V(LI), V(ARE), ALU.mult)
        tt(V(T2), V(NR), vec[:, 1, :], ALU.mult)
        tt(V(T1), V(T1), V(T2), ALU.subtract)
        tt(V(CI), V(T1), V(DEN), ALU.mult)

        def pow_table(Tr, Ti, init, base, s2):
            ta = C.sb(s2, [128, NSC, 64], F32, "pt_a")
            tb = C.sb(s2, [128, NSC, 64], F32, "pt_b")
            if init is None:
                P.op("pool", lambda e: e.memset(Tr[:, :, 0:1], 1.0), writes=[Tr])
                P.op("pool", lambda e: e.memset(Ti[:, :, 0:1], 0.0), writes=[Ti])
            else:
                P.op("dve", lambda e: e.tensor_copy(out=Tr[:, :, 0], in_=V(init[0])), reads=[sm], writes=[Tr])
                P.op("dve", lambda e: e.tensor_copy(out=Ti[:, :, 0], in_=V(init[1])), reads=[sm], writes=[Ti])
            ts(V(PWR), V(base[0]), 1.0, ALU.mult)
            ts(V(PWI), V(base[1]), 1.0, ALU.mult)
            for k in range(7):
                w = 1 << k
                pr = V(PWR).unsqueeze(2).to_broadcast([128, NSC, w])
                pi = V(PWI).unsqueeze(2).to_broadcast([128, NSC, w])
                lo = slice(0, w)
                hi = slice(w, 2 * w)
                P.op("dve", lambda e, pr=pr, lo=lo, w=w: e.tensor_tensor(out=ta[:, :, 0:w], in0=Tr[:, :, lo], in1=pr, op=ALU.mult),
                     reads=[Tr, sm], writes=[ta])
                P.op("dve", lambda e, pi=pi, lo=lo, w=w: e.tensor_tensor(out=tb[:, :, 0:w], in0=Ti[:, :, lo], in1=pi, op=ALU.mult),
                     reads=[Ti, sm], writes=[tb])
                P.op("dve", lambda e, hi=hi, w=w: e.tensor_tensor(out=Tr[:, :, hi], in0=ta[:, :, 0:w], in1=tb[:, :, 0:w], op=ALU.subtract),
                     reads=[ta, tb], writes=[Tr])
                P.op("dve", lambda e, pi=pi, lo=lo, w=w: e.tensor_tensor(out=ta[:, :, 0:w], in0=Tr[:, :, lo], in1=pi, op=ALU.mult),
                     reads=[Tr, sm], writes=[ta])
                P.op("dve", lambda e, pr=pr, lo=lo, w=w: e.tensor_tensor(out=tb[:, :, 0:w], in0=Ti[:, :, lo], in1=pr, op=ALU.mult),
                     reads=[Ti, sm], writes=[tb])
                P.op("dve", lambda e, hi=hi, w=w: e.tensor_tensor(out=Ti[:, :, hi], in0=ta[:, :, 0:w], in1=tb[:, :, 0:w], op=ALU.add),
                     reads=[ta, tb], writes=[Ti])
                tt(V(T1), V(PWR), V(PWR), ALU.mult)
                tt(V(T2), V(PWI), V(PWI), ALU.mult)
                tt(V(T3), V(PWR), V(PWI), ALU.mult)
                tt(V(PWR), V(T1), V(T2), ALU.subtract)
                ts(V(PWI), V(T3), 2.0, ALU.mult)

        with ExitStack() as s2:
            pow_table(Bre, Bim, None, (LR, LI), s2)
        with ExitStack() as s2:
            Asr = C.sb(s2, [128, NSC, 128], F32, "Asr")
            Asi = C.sb(s2, [128, NSC, 128], F32, "Asi")
            pow_table(Asr, Asi, (CR, CI), (MR, MI), s2)
            for sc in range(NSC):
                for (Tsm, Tt) in ((Asr, Atre), (Asi, Atim)):
                    P.op("pe", lambda e, Tsm=Tsm, sc=sc: e.transpose(pst[:, 0:128], Tsm[:, sc, :], C.ident_f[:]),
                         reads=[Tsm, C.ident_f], writes=[pst])
                    P.op("dve", lambda e, Tt=Tt, sc=sc: e.tensor_copy(out=Tt[:, sc * 128:(sc + 1) * 128], in_=pst[:, 0:128]),
                         reads=[pst], writes=[Tt])
            P.barrier()

        with ExitStack() as s3:
            uT = C.sb(s3, [128, 6, S], F32, "uT")
            uTb = C.sb(s3, [128, 6, S], BF16, "uTb")
            hcr = C.sb(s3, [128, NSC], F32, "hcr", n=6)
            hci = C.sb(s3, [128, NSC], F32, "hci", n=6)
            zre = [C.sb(s3, [128, 512], BF16, "zre%d" % i) for i in range(2)]
            zim = [C.sb(s3, [128, 512], BF16, "zim%d" % i) for i in range(2)]
            tf = [C.sb(s3, [128, 512], F32, "s5t%d" % i) for i in range(8)]
            ta = [[C.sb(s3, [128, 512], F32, "s5ta%d%d" % (i, j)) for j in range(4)] for i in range(2)]
            srbs = [C.sb(s3, [128, 4, 128], BF16, "srb%d" % i) for i in range(2)]
            sibs = [C.sb(s3, [128, 4, 128], BF16, "sib%d" % i) for i in range(2)]
            hs = C.sb(s3, [128, 2, 4, 4], F32, "hs")
            ystg = [C.sb(s3, [128, 512], BF16, "ystg%d" % i) for i in range(2)]
            ps_brs = [C.ps(s3, [128, 512], F32, "ps_br%d" % i) for i in range(2)]
            ps_bis = [C.ps(s3, [128, 512], F32, "ps_bi%d" % i) for i in range(2)]
            ps_cr = C.ps(s3, [128, 512], F32, "ps_cr")
            ps_ci = C.ps(s3, [128, 512], F32, "ps_ci")
            ps_y = C.ps(s3, [128, 512], F32, "ps_y")
            ps_v, ps_g = ps_brs[0], ps_bis[0]
            v3 = lambda t: t[:].rearrange("p (a j) -> p a j", j=128)
            for s in range(nseq):
                ub = scr["PB"][s].rearrange("(c p) t -> p c t", p=128)
                P.op("sp", lambda e, ub=ub: e.dma_start(out=uT[:], in_=ub), reads=[scr["PBb"]], writes=[uT], dma=True)
                P.op("pool", lambda e, ub=ub: e.dma_start(out=uTb[:], in_=ub), reads=[scr["PBb"]], writes=[uTb], dma=True)
                P.op("pool", lambda e: e.memset(hcr[:], 0.0), writes=[hcr])
                P.op("pool", lambda e: e.memset(hci[:], 0.0), writes=[hci])
                NI = (S // 128) * 6

                def stA(idx):
                    tc, kc = divmod(idx, 6)
                    p = idx % 2
                    tsl = slice(tc * 128, (tc + 1) * 128)
                    csl = slice(4 * kc * 128, (4 * kc + 4) * 128)
                    pbr, pbi = ps_brs[p], ps_bis[p]
                    t0, t1, t2, t3 = ta[p]
                    zr, zi = zre[p], zim[p]
                    mm(C, pbr[:], uTb[:, kc, tsl], Bm[:, kc, 0, :], True, True, [uTb, Bm], pbr)
                    mm(C, pbi[:], uTb[:, kc, tsl], Bm[:, kc, 1, :], True, True, [uTb, Bm], pbi)
                    P.op("dve", lambda e: e.tensor_tensor(out=t0[:], in0=pbr[:], in1=Atre[:, csl], op=ALU.mult),
                         reads=[pbr, Atre], writes=[t0])
                    P.op("dve", lambda e: e.tensor_tensor(out=t1[:], in0=pbi[:], in1=Atim[:, csl], op=ALU.mult),
                         reads=[pbi, Atim], writes=[t1])
                    P.op("pool", lambda e: e.tensor_tensor(out=zr[:], in0=t0[:], in1=t1[:], op=ALU.subtract),
                         reads=[t0, t1], writes=[zr])
                    P.op("dve", lambda e: e.tensor_tensor(out=t2[:], in0=pbi[:], in1=Atre[:, csl], op=ALU.mult),
                         reads=[pbi, Atre], writes=[t2])
                    P.op("dve", lambda e: e.tensor_tensor(out=t3[:], in0=pbr[:], in1=Atim[:, csl], op=ALU.mult),
                         reads=[pbr, Atim], writes=[t3])
                    P.op("pool", lambda e: e.tensor_tensor(out=zi[:], in0=t2[:], in1=t3[:], op=ALU.add),
                         reads=[t2, t3], writes=[zi])

                def stB(idx):
                    tc, kc = divmod(idx, 6)
                    p = idx % 2
                    sc0 = 4 * kc
                    bsl = slice(sc0, sc0 + 4)
                    zr, zi = zre[p], zim[p]
                    sb_r, sb_i = srbs[p], sibs[p]
                    for scl in range(4):
                        mm(C, ps_cr[:, scl * 128:(scl + 1) * 128], zr[:, scl * 128:(scl + 1) * 128], tri[:],
                           True, True, [zr, tri], ps_cr)
                    for scl in range(4):
                        mm(C, ps_ci[:, scl * 128:(scl + 1) * 128], zi[:, scl * 128:(scl + 1) * 128], tri[:],
                           True, True, [zi, tri], ps_ci)
                    hr_b = hcr[:, bsl].unsqueeze(2).to_broadcast([128, 4, 128])
                    hi_b = hci[:, bsl].unsqueeze(2).to_broadcast([128, 4, 128])
                    P.op("dve", lambda e: e.tensor_tensor(out=v3(tf[4]), in0=v3(ps_cr), in1=hr_b, op=ALU.add),
                         reads=[ps_cr, (hcr, kc)], writes=[tf[4]])
                    P.op("dve", lambda e: e.tensor_tensor(out=v3(tf[5]), in0=v3(ps_ci), in1=hi_b, op=ALU.add),
                         reads=[ps_ci, (hci, kc)], writes=[tf[5]])
                    P.op("dve", lambda e: e.tensor_tensor(out=v3(tf[0]), in0=v3(tf[4]), in1=Bre[:, bsl, :], op=ALU.mult),
                         reads=[tf[4], Bre], writes=[tf[0]])
                    P.op("pool", lambda e: e.tensor_tensor(out=v3(tf[1]), in0=v3(tf[5]), in1=Bim[:, bsl, :], op=ALU.mult),
                         reads=[tf[5], Bim], writes=[tf[1]])
                    P.op("dve", lambda e: e.tensor_tensor(out=tf[6][:], in0=tf[0][:], in1=tf[1][:], op=ALU.subtract),
                         reads=[tf[0], tf[1]], writes=[tf[6]])
                    P.op("pool", lambda e: e.tensor_tensor(out=v3(tf[2]), in0=v3(tf[5]), in1=Bre[:, bsl, :], op=ALU.mult),
                         reads=[tf[5], Bre], writes=[tf[2]])
                    P.op("dve", lambda e: e.tensor_tensor(out=v3(tf[3]), in0=v3(tf[4]), in1=Bim[:, bsl, :], op=ALU.mult),
                         reads=[tf[4], Bim], writes=[tf[3]])
                    P.op("pool", lambda e: e.tensor_tensor(out=tf[7][:], in0=tf[2][:], in1=tf[3][:], op=ALU.add),
                         reads=[tf[2], tf[3]], writes=[tf[7]])
                    P.op("act", lambda e: e.activation(out=sb_r[:], in_=v3(tf[6]), func=AF.Copy), reads=[tf[6]], writes=[sb_r])
                    P.op("act", lambda e: e.activation(out=sb_i[:], in_=v3(tf[7]), func=AF.Copy, scale=-1.0),
                         reads=[tf[7]], writes=[sb_i])
                    sl_r = v3(tf[6])[:, :, 127]
                    sl_i = v3(tf[7])[:, :, 127]
                    hk = hs[:, kc % 2]
                    P.op("dve", lambda e: e.tensor_tensor(out=hk[:, 0, :], in0=sm[:, LR, bsl], in1=sl_r, op=ALU.mult),
                         reads=[tf[6], sm], writes=[hs])
                    P.op("dve", lambda e: e.tensor_tensor(out=hk[:, 1, :], in0=sm[:, LI, bsl], in1=sl_i, op=ALU.mult),
                         reads=[tf[7], sm], writes=[hs])
                    P.op("dve", lambda e: e.tensor_tensor(out=hcr[:, bsl], in0=hk[:, 0, :], in1=hk[:, 1, :], op=ALU.subtract),
                         reads=[hs], writes=[(hcr, kc)])
                    P.op("dve", lambda e: e.tensor_tensor(out=hk[:, 2, :], in0=sm[:, LR, bsl], in1=sl_i, op=ALU.mult),
                         reads=[tf[7], sm], writes=[hs])
                    P.op("dve", lambda e: e.tensor_tensor(out=hk[:, 3, :], in0=sm[:, LI, bsl], in1=sl_r, op=ALU.mult),
                         reads=[tf[6], sm], writes=[hs])
                    P.op("dve", lambda e: e.tensor_tensor(out=hci[:, bsl], in0=hk[:, 2, :], in1=hk[:, 3, :], op=ALU.add),
                         reads=[hs], writes=[(hci, kc)])

                def stC(idx):
                    tc, kc = divmod(idx, 6)
                    p = idx % 2
                    sc0 = 4 * kc
                    tsl = slice(tc * 128, (tc + 1) * 128)
                    sb_r, sb_i = srbs[p], sibs[p]
                    for scl in range(4):
                        mm(C, ps_y[:, 0:128], Cm[:, sc0 + scl, 0, :], sb_r[:, scl, :], scl == 0, False, [Cm, sb_r], ps_y)
                    for scl in range(4):
                        mm(C, ps_y[:, 0:128], Cm[:, sc0 + scl, 1, :], sb_i[:, scl, :], False, scl == 3, [Cm, sb_i], ps_y)
                    P.op("dve", lambda e: e.scalar_tensor_tensor(
                        out=uT[:, kc, tsl], in0=uT[:, kc, tsl], scalar=dg[:, kc:kc + 1], in1=ps_y[:, 0:128],
                        op0=ALU.mult, op1=ALU.add), reads=[uT, dg, ps_y], writes=[uT])

                for idx in range(NI + 2):
                    if idx < NI:
                        stA(idx)
                    if 1 <= idx <= NI:
                        stB(idx - 1)
                    if idx >= 2:
                        stC(idx - 2)
                for kc in range(6):
                    for q in range(S // 512):
                        qs = slice(q * 512, (q + 1) * 512)
                        yv = uT[:, kc, qs]
                        P.op("pool", lambda e, yv=yv: e.tensor_tensor(out=tf[0][:], in0=yv, in1=yv, op=ALU.mult),
                             reads=[uT], writes=[tf[0]])
                        P.op("dve", lambda e: e.tensor_scalar(out=tf[1][:], in0=tf[0][:], scalar1=0.044715, scalar2=1.0,
                                                              op0=ALU.mult, op1=ALU.add), reads=[tf[0]], writes=[tf[1]])
                        P.op("pool", lambda e, yv=yv: e.tensor_tensor(out=tf[2][:], in0=tf[1][:], in1=yv, op=ALU.mult),
                             reads=[tf[1], uT], writes=[tf[2]])
                        P.op("act", lambda e: e.activation(out=tf[3][:], in_=tf[2][:], func=AF.Sigmoid, scale=1.5957691216057308),
                             reads=[tf[2]], writes=[tf[3]])
                        P.op("dve", lambda e, yv=yv, kc=kc, qs=qs: e.tensor_tensor(out=uTb[:, kc, qs], in0=yv, in1=tf[3][:], op=ALU.mult),
                             reads=[uT, tf[3]], writes=[uTb])
                ne = 0
                for q in range(S // 512):
                    qs = slice(q * 512, (q + 1) * 512)
                    for oc in range(6):
                        for kc in range(6):
                            mm(C, ps_v[:], gw[:, oc // 2, kc, (oc % 2) * 128:(oc % 2) * 128 + 128], uTb[:, kc, qs],
                               kc == 0, kc == 5, [gw, uTb], ps_v)
                        for kc in range(6):
                            mm(C, ps_g[:], gw[:, 3 + oc // 2, kc, (oc % 2) * 128:(oc % 2) * 128 + 128], uTb[:, kc, qs],
                               kc == 0, kc == 5, [gw, uTb], ps_g)
                        P.op("act", lambda e, oc=oc: e.activation(out=tf[4][:], in_=ps_g[:], func=AF.Sigmoid,
                                                                  bias=dg[:, 12 + oc:13 + oc], scale=1.0),
                             reads=[ps_g, dg], writes=[tf[4]])
                        ys = ystg[ne % 2]
                        ne += 1
                        P.op("dve", lambda e, oc=oc, ys=ys: e.scalar_tensor_tensor(
                            out=ys[:], in0=ps_v[:], scalar=dg[:, 6 + oc:7 + oc], in1=tf[4][:], op0=ALU.add, op1=ALU.mult),
                            reads=[ps_v, dg, tf[4]], writes=[ys])
                        P.op("sp", lambda e, oc=oc, ys=ys, s=s, qs=qs: e.dma_start(
                            out=scr["YB"][s, oc * 128:(oc + 1) * 128, qs], in_=ys[:]),
                            reads=[ys], writes=[scr["YBb"]], dma=True)
    P.barrier()


def merge_phase(C, dr, l, nseq, ntile, modv, gains, xT, xres, xview, scr):
    P = C.P
    wd = dr["w_inT"]
    with ExitStack() as st:
        hT = [C.sb(st, [128, NKC, TT], BF16, "mhT%d" % i) for i in range(2)]
        yy = [C.sb(st, [128, NKC, TT], BF16, "myy%d" % i) for i in range(2)]
        mg_ = C.sb(st, [128, NKC, TT], BF16, "merged", n=NKC)
        gwb = [C.sb(st, [128, NKC, GW], BF16, "mgw%d" % i) for i in range(6)]
        bwb = [C.sb(st, [128, NKC, GW], BF16, "mbw%d" % i) for i in range(2)]
        owb = [C.sb(st, [128, NKC, GW], BF16, "mow%d" % i) for i in range(2)]
        sgb = [C.sb(st, [128, TT], F32, "msg%d" % i) for i in range(3)]
        tb = [C.sb(st, [128, TT], F32, "mtb%d" % i) for i in range(3)]
        acc = [C.sb(st, [128, TT], F32, "macc%d" % i) for i in range(2)]
        xr = [C.sb(st, [128, TT], F32, "mxr%d" % i) for i in range(4)]
        AB = C.sb(st, [128, nseq, 3, NKC], F32, "mAB")
        ps_gt = [C.ps(st, [128, TT], F32, "ps_gt%d" % i) for i in range(2)]
        ps_b = [C.ps(st, [128, TT], F32, "ps_b%d" % i) for i in range(2)]
        ps_o = [C.ps(st, [128, TT], F32, "ps_mo%d" % i) for i in range(2)]
        mod_vectors(C, AB, modv, gains, 1, nseq, 1.0)
        lg = [(lambda b, g=28 + j * 8 + m: P.op("pool", lambda e: e.dma_start(out=b[:], in_=wd[l, g]), writes=[b], dma=True))
              for _ in range(ntile) for m in range(8) for j in range(3)]
        lb = [(lambda b, m=m: P.op("pool", lambda e: e.dma_start(out=b[:], in_=dr["w_br"][l, m]), writes=[b], dma=True))
              for _ in range(ntile) for m in range(8)]
        lo = [(lambda b, m=m: P.op("pool", lambda e: e.dma_start(out=b[:], in_=dr["w_outT"][l, m]), writes=[b], dma=True))
              for _ in range(ntile) for m in range(8)]
        sg_, sb_, so_ = Stream(gwb, lg, depth=4), Stream(bwb, lb), Stream(owb, lo)
        KOFF = (0, 8, 14)
        KN = (8, 6, 2)

        def load_acts(i):
            s, ti = divmod(i, S // TT)
            tsl = slice(ti * TT, (ti + 1) * TT)
            h, y = hT[i % 2], yy[i % 2]
            P.op("sp", lambda e: e.dma_start(out=h[:], in_=scr["HT"][s].rearrange("(c p) t -> p c t", p=128)[:, :, tsl]),
                 reads=[(scr["HTb"], i)], writes=[h], dma=True)
            P.op("sp", lambda e: e.dma_start(out=y[:, 0:8, :], in_=scr["YA"][s].rearrange("(c p) t -> p c t", p=128)[:, :, tsl]),
                 reads=[scr["YAb"]], writes=[y], dma=True)
            P.op("sp", lambda e: e.dma_start(out=y[:, 8:14, :], in_=scr["YB"][s].rearrange("(c p) t -> p c t", p=128)[:, :, tsl]),
                 reads=[scr["YBb"]], writes=[y], dma=True)
            P.op("sp", lambda e: e.dma_start(out=y[:, 14:16, :], in_=scr["YC"][s].rearrange("(c p) t -> p c t", p=128)[:, :, tsl]),
                 reads=[scr["YCb"]], writes=[y], dma=True)

        load_acts(0)
        for i in range(ntile):
            b = i // (S // TT)
            if i + 1 < ntile:
                load_acts(i + 1)
            h, y = hT[i % 2], yy[i % 2]
            for m8 in range(8):
                gws = [sg_.get((i * 8 + m8) * 3 + j) for j in range(3)]
                bw = sb_.get(i * 8 + m8)
                for mi in range(2):
                    m = 2 * m8 + mi
                    csl = slice(mi * 128, (mi + 1) * 128)
                    for j in range(3):
                        pg, pb = ps_gt[(3 * m + j) % 2], ps_b[(3 * m + j) % 2]
                        for c in range(NKC):
                            mm(C, pg[:], gws[j][:, c, csl], h[:, c, :], c == 0, c == NKC - 1, [gws[j], h], pg, lazy=True)
                        for kc in range(KN[j]):
                            mm(C, pb[:], bw[:, KOFF[j] + kc, csl], y[:, KOFF[j] + kc, :], kc == 0, kc == KN[j] - 1,
                               [bw, y], pb, lazy=True)
                        sgt, tj = sgb[j], tb[j]
                        P.op("act", lambda e, sgt=sgt, pg=pg: e.activation(out=sgt[:], in_=pg[:], func=AF.Sigmoid),
                             reads=[pg], writes=[sgt])
                        P.op("dve", lambda e, sgt=sgt, tj=tj, pb=pb: e.tensor_tensor(out=tj[:], in0=pb[:], in1=sgt[:], op=ALU.mult),
                             reads=[pb, sgt], writes=[tj])
                    a_ = acc[m % 2]
                    P.op("pool", lambda e, a_=a_: e.tensor_tensor(out=a_[:], in0=tb[0][:], in1=tb[1][:], op=ALU.add),
                         reads=[tb[0], tb[1]], writes=[a_])
                    P.op("pool", lambda e, a_=a_, m=m: e.tensor_tensor(out=mg_[:, m, :], in0=a_[:], in1=tb[2][:], op=ALU.add),
                         reads=[a_, tb[2]], writes=[(mg_, m)])
            for m8 in range(8):
                ow = so_.get(i * 8 + m8)
                for mi in range(2):
                    m = 2 * m8 + mi
                    po = ps_o[m % 2]
                    r = xr[m % 4]
                    P.op("sp", lambda e, i=i, m=m, r=r: e.dma_start(out=r[:], in_=xview(xT, i)[:, m, :]),
                         reads=[(xres, i)], writes=[r], dma=True)
                    for c in range(NKC):
                        mm(C, po[:], ow[:, c, mi * 128:(mi + 1) * 128], mg_[:, c, :], c == 0, c == NKC - 1,
                           [ow, (mg_, c)], po, lazy=True)
                    P.op("dve", lambda e, m=m, r=r, po=po, b=b: e.scalar_tensor_tensor(
                        out=r[:], in0=po[:], scalar=AB[:, b, 2, m:m + 1], in1=r[:], op0=ALU.mult, op1=ALU.add),
                        reads=[po, AB, r], writes=[r])
                    P.op("sp", lambda e, i=i, m=m, r=r: e.dma_start(out=xview(xT, i)[:, m, :], in_=r[:]),
                         reads=[r], writes=[(xres, i)], dma=True)
    P.barrier()


def dil_phase(C, dr, l, nseq, scr):
    P = C.P
    DILS = (1, 4, 16)
    NEG = -30000.0
    with ExitStack() as st:
        tmp = C.sb(st, [128, 128], F32, "dtmp")
        pst = C.ps(st, [128, 512], F32, "dpst")
        gv = C.sb(st, [128, 2], F32, "dgv")
        bones = C.sb(st, [128, 128], BF16, "bones")
        ones64 = C.sb(st, [128, 64], BF16, "ones64")
        d0i = C.sb(st, [128, 256], mybir.dt.int32, "d0i")
        d0 = C.sb(st, [128, 256], F32, "d0")
        bias = C.sb(st, [128, 12, 256], F32, "dbias")
        qraw = C.sb(st, [128, 2, S], F32, "qraw")
        kraw = C.sb(st, [128, 2, S], F32, "kraw")
        qn = C.sb(st, [128, 2, S], BF16, "qn")
        kn = C.sb(st, [128, 2, S], BF16, "kn")
        vb = C.sb(st, [128, 2, S], BF16, "vb")
        vtok = C.sb(st, [128, 2, 16, 128], BF16, "vtok")
        acc = C.sb(st, [64, 2, 4, S], F32, "dacc")
        sq = [C.sb(st, [128, 512], BF16, "dsq%d" % i) for i in range(2)]
        msb = [C.sb(st, [128, 512], F32, "dms%d" % i) for i in range(2)]
        rsb = [C.sb(st, [128, 512], F32, "drs%d" % i) for i in range(2)]
        stmp = [C.sb(st, [128, 256], F32, "dst%d" % i) for i in range(2)]
        pT = [C.sb(st, [128, 256], BF16, "dpT%d" % i) for i in range(3)]
        rec = C.sb(st, [64, S], F32, "drec")
        ob = C.sb(st, [64, S], BF16, "dob")
        ps_n = C.ps(st, [128, 512], F32, "ps_dn")
        ps_s = [C.ps(st, [128, 512], F32, "ps_ds%d" % i) for i in range(2)]
        ps_nd = [C.ps(st, [128, 512], F32, "ps_dnd%d" % i) for i in range(2)]
        ps_vt = C.ps(st, [128, 1024], BF16, "ps_dvt")

        load_vec_fm(C, st, gv, gv[:, :], dr["dil_g"][l], 2, tmp, pst, C.ident_f)
        P.op("dve", lambda e: e.tensor_scalar(out=gv[:, 0:1], in0=gv[:, 0:1], scalar1=0.125, scalar2=None, op0=ALU.mult),
             reads=[gv], writes=[gv])
        P.op("pool", lambda e: e.memset(bones[:], 0.0), writes=[bones])
        P.op("pool", lambda e: e.memset(bones[0:64, 0:64], 1.0), writes=[bones])
        P.op("pool", lambda e: e.memset(bones[64:128, 64:128], 1.0), writes=[bones])
        P.op("pool", lambda e: e.memset(ones64[:], 1.0), writes=[ones64])
        P.op("pool", lambda e: e.iota(d0i[:], pattern=[[1, 256]], base=0, channel_multiplier=-1), writes=[d0i])
        P.op("dve", lambda e: e.tensor_copy(out=d0[:], in_=d0i[:]), reads=[d0i], writes=[d0])
        for hg in range(12):
            slope = 2.0 ** (-8.0 * (hg + 1) / 12.0)
            dil = DILS[hg // 4]
            P.op("dve", lambda e, hg=hg, v=-slope * dil: e.tensor_scalar(out=bias[:, hg, :], in0=d0[:], scalar1=v, scalar2=None,
                                                                         op0=ALU.mult), reads=[d0], writes=[bias])
            P.op("pool", lambda e, hg=hg: e.affine_select(out=bias[:, hg, :], in_=bias[:, hg, :], pattern=[[1, 256]],
                                                          compare_op=ALU.is_ge, fill=C.fill(NEG), base=0, channel_multiplier=-1),
                 reads=[bias], writes=[bias])
            P.op("pool", lambda e, hg=hg: e.affine_select(out=bias[:, hg, :], in_=bias[:, hg, :], pattern=[[-1, 256]],
                                                          compare_op=ALU.is_ge, fill=C.fill(NEG), base=128, channel_multiplier=1),
                 reads=[bias], writes=[bias])

        nsc = 0
        for s in range(nseq):
            pc = scr["PC"][s]
            for gi in range(3):
                dil = DILS[gi]
                nb = S // dil // 128
                for (dst, r0, eng) in ((qraw, 0, "sp"), (kraw, 768, "sp")):
                    P.op(eng, lambda e, dst=dst, r0=r0: e.dma_start(
                        out=dst[:], in_=pc[r0 + gi * 256:r0 + gi * 256 + 256, :].rearrange("(c p) t -> p c t", p=128)),
                        reads=[scr["PCb"]], writes=[dst], dma=True)
                P.op("pool", lambda e: e.dma_start(
                    out=vb[:], in_=pc[1536 + gi * 256:1536 + gi * 256 + 256, :].rearrange("(c p) t -> p c t", p=128)),
                    reads=[scr["PCb"]], writes=[vb], dma=True)
                k2 = 0
                for (raw, nrm, gcol) in ((qraw, qn, 0), (kraw, kn, 1)):
                    for c2 in range(2):
                        for q4 in range(S // 512):
                            qs = slice(q4 * 512, (q4 + 1) * 512)
                            sq_, ms_, rs_ = sq[k2 % 2], msb[k2 % 2], rsb[k2 % 2]
                            k2 += 1
                            P.op("act", lambda e, raw=raw, c2=c2, qs=qs, sq_=sq_: e.activation(out=sq_[:], in_=raw[:, c2, qs], func=AF.Square),
                                 reads=[raw], writes=[sq_])
                            mm(C, ps_n[:], bones[:], sq_[:], True, True, [bones, sq_], ps_n)
                            P.op("act", lambda e, ms_=ms_: e.activation(out=ms_[:], in_=ps_n[:], func=AF.Sqrt, bias=C.epsc[:, 0:1],
                                                                        scale=1.0 / 64), reads=[ps_n, C.epsc], writes=[ms_])
                            P.op("dve", lambda e, ms_=ms_, rs_=rs_: e.reciprocal(out=rs_[:], in_=ms_[:]), reads=[ms_], writes=[rs_])
                            P.op("dve", lambda e, raw=raw, nrm=nrm, c2=c2, qs=qs, rs_=rs_, gcol=gcol: e.scalar_tensor_tensor(
                                out=nrm[:, c2, qs], in0=raw[:, c2, qs], scalar=gv[:, gcol:gcol + 1], in1=rs_[:],
                                op0=ALU.mult, op1=ALU.mult), reads=[raw, gv, rs_], writes=[nrm])
                for c2 in range(2):
                    for r in range(dil):
                        for n in range(nb):
                            bi = r * nb + n
                            ks = slice(r + dil * 128 * n, r + dil * 128 * n + dil * 127 + 1, dil)
                            P.op("pe", lambda e, c2=c2, ks=ks, bi=bi: e.transpose(ps_vt[:, (bi % 8) * 128:(bi % 8 + 1) * 128], vb[:, c2, ks], C.ident_bf[:]),
                                 reads=[vb, C.ident_bf], writes=[ps_vt])
                            P.op("act", lambda e, c2=c2, bi=bi: e.activation(out=vtok[:, c2, bi, :], in_=ps_vt[:, (bi % 8) * 128:(bi % 8 + 1) * 128], func=AF.Copy),
                                 reads=[ps_vt], writes=[vtok])
                items = [(hh, r, n) for hh in range(4) for r in range(dil) for n in range(nb)]
                curs = {}

                def score(k):
                    hh, r, n = items[k]
                    c2, pb = hh // 2, (hh % 2) * 64
                    hg = gi * 4 + hh
                    idx = nsc + k
                    nq = 256 if n + 1 < nb else 128
                    t0 = r + dil * 128 * n
                    ks = slice(t0, t0 + dil * 127 + 1, dil)
                    qsl = slice(t0, t0 + dil * (nq - 1) + 1, dil)
                    pss, stp, cur = ps_s[idx % 2], stmp[idx % 2], pT[idx % 3]
                    mm(C, pss[:, 0:nq], kn[pb:pb + 64, c2, ks], qn[pb:pb + 64, c2, qsl], True, True, [kn, qn], pss)
                    P.op("dve", lambda e: e.tensor_tensor(out=stp[:, 0:nq], in0=pss[:, 0:nq], in1=bias[:, hg, 0:nq], op=ALU.add),
                         reads=[pss, bias], writes=[stp])
                    P.op("act", lambda e: e.activation(out=cur[:, 0:nq], in_=stp[:, 0:nq], func=AF.Exp),
                         reads=[stp], writes=[cur])
                    curs[k] = cur

                def pv(k):
                    hh, r, n = items[k]
                    c2, pb = hh // 2, (hh % 2) * 64
                    idx = nsc + k
                    bi = r * nb + n
                    t0 = r + dil * 128 * n
                    ks = slice(t0, t0 + dil * 127 + 1, dil)
                    cur = curs[k]
                    prev = curs[k - 1] if n > 0 else None
                    psnd = ps_nd[idx % 2]
                    for di, lhs_of in enumerate((lambda b_: vtok[:, c2, b_, pb:pb + 64], lambda b_: ones64[:])):
                        o_ap = psnd[0:64, di * 128:(di + 1) * 128]
                        if prev is not None:
                            mm(C, o_ap, lhs_of(bi - 1), prev[:, 128:256], True, False, [vtok, ones64, prev], psnd)
                        mm(C, o_ap, lhs_of(bi), cur[:, 0:128], prev is None, True, [vtok, ones64, cur], psnd)
                    a_view = acc[:, :, hh, ks]
                    p_view = psnd[0:64, 0:256].rearrange("p (a j) -> p a j", j=128)
                    if gi == 0:
                        P.op("dve", lambda e: e.tensor_copy(out=a_view, in_=p_view), reads=[psnd], writes=[acc])
                    else:
                        P.op("dve", lambda e: e.tensor_tensor(out=a_view, in0=p_view, in1=a_view, op=ALU.add),
                             reads=[psnd, acc], writes=[acc])
                    curs.pop(k - 1, None)

                score(0)
                for k in range(len(items)):
                    if k + 1 < len(items):
                        score(k + 1)
                    pv(k)
                nsc += len(items)
            for hh in range(4):
                P.op("dve", lambda e, hh=hh: e.reciprocal(out=rec[:], in_=acc[:, 1, hh, :]), reads=[acc], writes=[rec])
                P.op("dve", lambda e, hh=hh: e.tensor_tensor(out=ob[:], in0=acc[:, 0, hh, :], in1=rec[:], op=ALU.mult),
                     reads=[acc, rec], writes=[ob])
                P.op("sp", lambda e, hh=hh, s=s: e.dma_start(out=scr["YC"][s, hh * 64:(hh + 1) * 64, :], in_=ob[:]),
                     reads=[ob], writes=[scr["YCb"]], dma=True)
    P.barrier()


def gdn_phase(C, dr, l, nseq, scr):
    P = C.P
    NT = S // 128
    NEG = -30000.0
    with ExitStack() as st:
        cw = C.sb(st, [128, 24, 4], F32, "cw")
        ad = C.sb(st, [128, 16], F32, "gad")
        onm = C.sb(st, [128, 1], F32, "gon")
        one1 = C.sb(st, [128, 1], F32, "one1")
        triF = C.sb(st, [128, 128], F32, "triF")
        onesF = C.sb(st, [128, 128], F32, "onesF")
        mneg = C.sb(st, [128, 128], F32, "mneg")
        smask = C.sb(st, [128, 128], F32, "smask")
        ba = C.sb(st, [128, NT, 16], F32, "gba")
        sc = C.sb(st, [128, 10, NT, 8], F32, "gsc")
        BETA, G, GC, GL, EGC, NGC, EGL, KDEC, NEGC, NBETA = range(10)
        raw = [C.sb(st, [128, S + 3], F32, "graw%d" % i) for i in range(2)]
        cacc = [C.sb(st, [128, S], F32, "gcacc%d" % i) for i in range(2)]
        vTb = C.sb(st, [128, S], BF16, "gvTb")
        DT = C.sb(st, [128, NT, 128], F32, "gDT", n=NT)
        Gb = C.sb(st, [128, NT, 128], F32, "gGb", n=NT)
        Pp = [C.sb(st, [128, NT, 256], BF16, "gP%d" % i, n=NT) for i in range(2)]
        xob = [C.sb(st, [128, 4, 256], BF16, "gxo%d" % i) for i in range(2)]
        msk = C.sb(st, [128, 7, 2, 128], BF16, "gmsk")
        sqb = [C.sb(st, [128, 512], BF16, "gsq%d" % i) for i in range(2)]
        rnb = [C.sb(st, [128, 512], F32, "grn%d" % i) for i in range(2)]
        qT = [C.sb(st, [128, S], BF16, "gqT%d" % i) for i in range(2)]
        kT = [C.sb(st, [128, S], BF16, "gkT%d" % i) for i in range(2)]
        kd = [C.sb(st, [128, NT, 128], BF16, "gkd%d" % i, n=NT) for i in range(2)]
        vtok = [C.sb(st, [128, NT, 128], BF16, "gvtok%d" % i, n=NT) for i in range(2)]
        siluz = [C.sb(st, [128, S], F32, "gsz%d" % i) for i in range(2)]
        RRT = [C.sb(st, [128, NT, 256], BF16, "gRRT%d" % i, n=NT) for i in range(2)]
        attnT = [C.sb(st, [128, NT, 128], BF16, "gattn%d" % i, n=NT) for i in range(2)]
        hS = [C.sb(st, [128, 128], F32, "ghS%d" % i) for i in range(2)]
        hSb = [C.sb(st, [128, 128], BF16, "ghSb%d" % i) for i in range(2)]
        yst = [C.sb(st, [128, S], BF16, "gyst%d" % i) for i in range(2)]
        rb = [[C.sb(st, [128, 128], BF16, "grb%d%d" % (i, j)) for j in range(2)] for i in range(2)]
        vn = [[C.sb(st, [128, 128], BF16, "gvn%d%d" % (i, j)) for j in range(2)] for i in range(2)]
        o1 = [[C.sb(st, [128, 128], F32, "go1%d%d" % (i, j)) for j in range(2)] for i in range(2)]
        of = [[C.sb(st, [128, 128], F32, "gof%d%d" % (i, j)) for j in range(2)] for i in range(2)]
        osq = [[C.sb(st, [128, 128], F32, "gosq%d%d" % (i, j)) for j in range(2)] for i in range(2)]
        onb = [[C.sb(st, [128, 128], BF16, "gonb%d%d" % (i, j)) for j in range(2)] for i in range(2)]
        ssq = [[C.sb(st, [128, 4], F32, "gssq%d%d" % (i, j)) for j in range(2)] for i in range(2)]
        psA = C.ps(st, [128, 512], F32, "gpsA")
        psB = C.ps(st, [128, 512], F32, "gpsB")
        psC = C.ps(st, [128, 512], F32, "gpsC")
        psD = C.ps(st, [128, 512], F32, "gpsD")
        psE = C.ps(st, [128, 512], F32, "gpsE")
        psF = C.ps(st, [128, 512], F32, "gpsF")
        psTs = [C.ps(st, [128, 1024], BF16, "gpsT%d" % i) for i in range(2)]
        c4 = lambda k: slice(k * 128, (k + 1) * 128)

        P.op("sp", lambda e: e.dma_start(out=cw[:], in_=dr["gdn_conv"][l]), writes=[cw], dma=True)
        P.op("sp", lambda e: e.dma_start(out=ad[:], in_=dr["gdn_ad"][l]), writes=[ad], dma=True)
        P.op("sp", lambda e: e.dma_start(out=onm[:], in_=dr["gdn_on"][l]), writes=[onm], dma=True)
        P.op("pool", lambda e: e.dma_start(out=msk[:], in_=dr["gdn_masks"]), writes=[msk], dma=True)
        P.op("pool", lambda e: e.memset(one1[:], 1.0), writes=[one1])
        P.op("pool", lambda e: e.memset(onesF[:], 1.0), writes=[onesF])
        P.op("pool", lambda e: e.memset(triF[:], 1.0), writes=[triF])
        P.op("pool", lambda e: e.affine_select(out=triF[:], in_=triF[:], pattern=[[1, 128]], compare_op=ALU.is_ge,
                                               fill=C.fill(0.0), base=0, channel_multiplier=-1), reads=[triF], writes=[triF])
        P.op("pool", lambda e: e.memset(mneg[:], 0.0), writes=[mneg])
        P.op("pool", lambda e: e.affine_select(out=mneg[:], in_=mneg[:], pattern=[[1, 128]], compare_op=ALU.is_ge,
                                               fill=C.fill(NEG), base=0, channel_multiplier=-1), reads=[mneg], writes=[mneg])
        P.op("pool", lambda e: e.memset(smask[:], 1.0), writes=[smask])
        P.op("pool", lambda e: e.affine_select(out=smask[:], in_=smask[:], pattern=[[1, 128]], compare_op=ALU.is_ge,
                                               fill=C.fill(0.0), base=-1, channel_multiplier=-1), reads=[smask], writes=[smask])
        for rw in raw:
            P.op("pool", lambda e, rw=rw: e.memset(rw[:, 0:3], 0.0), writes=[rw])
        P.op("act", lambda e: e.activation(out=ad[:, 0:8], in_=ad[:, 0:8], func=AF.Exp), reads=[ad], writes=[ad])

        def g1(s, h, sl):
            pa = scr["PA"][s]
            k2 = 0
            for wi, (row0, kind) in enumerate(((h * 128, "q"), (1024 + h * 128, "k"), (2048 + h * 128, "v"), (3072 + h * 128, "z"))):
                rw = raw[wi % 2]
                ca = cacc[wi % 2]
                P.op("sp", lambda e, rw=rw, row0=row0: e.dma_start(out=rw[:, 3:], in_=pa[row0:row0 + 128, :]),
                     reads=[scr["PAb"]], writes=[rw], dma=True)
                if kind == "z":
                    P.op("act", lambda e, rw=rw: e.activation(out=siluz[sl][:], in_=rw[:, 3:], func=AF.Silu),
                         reads=[rw], writes=[siluz[sl]])
                    continue
                ch = row0 // 128
                P.op("dve", lambda e, rw=rw, ca=ca, ch=ch: e.tensor_scalar(out=ca[:], in0=rw[:, 0:S], scalar1=cw[:, ch, 0:1],
                                                                         scalar2=None, op0=ALU.mult), reads=[rw, cw], writes=[ca])
                for k in range(1, 4):
                    P.op("dve", lambda e, rw=rw, ca=ca, ch=ch, k=k: e.scalar_tensor_tensor(
                        out=ca[:], in0=rw[:, k:k + S], scalar=cw[:, ch, k:k + 1], in1=ca[:], op0=ALU.mult, op1=ALU.add),
                        reads=[rw, cw, ca], writes=[ca])
                if kind == "v":
                    P.op("act", lambda e, ca=ca: e.activation(out=vTb[:], in_=ca[:], func=AF.Silu), reads=[ca], writes=[vTb])
                    continue
                P.op("act", lambda e, ca=ca: e.activation(out=ca[:], in_=ca[:], func=AF.Silu), reads=[ca], writes=[ca])
                dstT = qT[sl] if kind == "q" else kT[sl]
                scl = 128.0 ** -0.5 if kind == "q" else 1.0
                for q4 in range(S // 512):
                    qs = slice(q4 * 512, (q4 + 1) * 512)
                    sq_, rn_ = sqb[k2 % 2], rnb[k2 % 2]
                    k2 += 1
                    P.op("act", lambda e, ca=ca, qs=qs, sq_=sq_: e.activation(out=sq_[:], in_=ca[:, qs], func=AF.Square),
                         reads=[ca], writes=[sq_])
                    mm(C, psF[:], C.ones_bf[:], sq_[:], True, True, [C.ones_bf, sq_], psF)
                    P.op("act", lambda e, rn_=rn_: e.activation(out=rn_[:], in_=psF[:], func=AF.Sqrt, bias=C.epsc[:, 0:1], scale=1.0),
                         reads=[C.epsc], writes=[psF, rn_])
                    P.op("dve", lambda e, rn_=rn_: e.reciprocal(out=rn_[:], in_=rn_[:]), reads=[rn_], writes=[rn_])
                    P.op("dve", lambda e, ca=ca, qs=qs, rn_=rn_, dstT=dstT, scl=scl: e.scalar_tensor_tensor(
                        out=dstT[:, qs], in0=ca[:, qs], scalar=scl, in1=rn_[:], op0=ALU.mult, op1=ALU.mult),
                        reads=[ca, rn_], writes=[dstT])
            for tc in range(NT):
                pt_ = psTs[tc % 2]
                P.op("pe", lambda e, tc=tc, pt_=pt_: e.transpose(pt_[:, 0:128], vTb[:, c4(tc)], C.ident_bf[:]),
                     reads=[vTb, C.ident_bf], writes=[pt_])
                P.op("act", lambda e, tc=tc, pt_=pt_: e.activation(out=vtok[sl][:, tc, :], in_=pt_[:, 0:128], func=AF.Copy),
                     writes=[pt_, (vtok[sl], tc)])
            for tc in range(NT):
                pt_ = psTs[tc % 2]
                P.op("pe", lambda e, tc=tc, pt_=pt_: e.transpose(pt_[:, 0:128], kT[sl][:, c4(tc)], C.ident_bf[:]),
                     reads=[kT[sl], C.ident_bf], writes=[pt_])
                P.op("act", lambda e, tc=tc, pt_=pt_: e.activation(out=kd[sl][:, tc, :], in_=pt_[:, 0:128], func=AF.Identity,
                                                                  scale=sc[:, KDEC, tc, h:h + 1]),
                     reads=[sc], writes=[pt_, (kd[sl], tc)])
            for tc in range(NT + 1):
                if tc < NT:
                    P.op("dve", lambda e, tc=tc: e.tensor_scalar(out=Gb[:, tc, :], in0=onesF[:], scalar1=sc[:, G, tc, h:h + 1],
                                                                 scalar2=None, op0=ALU.mult), reads=[onesF, sc], writes=[(Gb, tc)])
                    pd = psA if tc % 2 == 0 else psD
                    mm(C, pd[:, 0:128], Gb[:, tc, :], triF[:], True, False, [(Gb, tc), triF], pd)
                    mm(C, pd[:, 0:128], C.ident_f[:], mneg[:], False, True, [C.ident_f, mneg], pd)
                    P.op("act", lambda e, tc=tc, pd=pd: e.activation(out=DT[:, tc, :], in_=pd[:, 0:128], func=AF.Exp,
                                                                    bias=sc[:, NGC, tc, h:h + 1], scale=1.0),
                         reads=[sc], writes=[pd, (DT, tc)])
                    pk = psB if tc % 2 == 0 else psC
                    mm(C, pk[:, 0:128], kT[sl][:, c4(tc)], kT[sl][:, c4(tc)], True, True, [kT[sl]], pk)
                    mm(C, pk[:, 128:256], kT[sl][:, c4(tc)], qT[sl][:, c4(tc)], True, True, [kT[sl], qT[sl]], pk)
                if tc >= 1:
                    t = tc - 1
                    pk = psB if t % 2 == 0 else psC
                    P.op("dve", lambda e, t=t, pk=pk: e.scalar_tensor_tensor(
                        out=Pp[0][:, t, 0:128], in0=pk[:, 0:128], scalar=sc[:, NBETA, t, h:h + 1], in1=DT[:, t, :],
                        op0=ALU.mult, op1=ALU.mult), reads=[sc, (DT, t)], writes=[pk, (Pp[0], t)])
                    P.op("dve", lambda e, t=t, pk=pk: e.tensor_tensor(out=attnT[sl][:, t, :], in0=pk[:, 128:256], in1=DT[:, t, :],
                                                                      op=ALU.mult), reads=[(DT, t)], writes=[pk, (attnT[sl], t)])
            for tc in range(NT):
                pt_ = psTs[tc % 2]
                P.op("pe", lambda e, tc=tc, pt_=pt_: e.transpose(pt_[:, 0:128], Pp[0][:, tc, 0:128], C.ident_bf[:]),
                     reads=[(Pp[0], tc), C.ident_bf], writes=[pt_])
                P.op("act", lambda e, tc=tc, pt_=pt_: e.activation(out=Pp[0][:, tc, 128:256], in_=pt_[:, 0:128], func=AF.Copy),
                     writes=[pt_, (Pp[0], tc)])
                P.op("pool", lambda e, tc=tc: e.tensor_copy(out=RRT[sl][:, tc, 0:128], in_=C.ident_bf[:]),
                     reads=[C.ident_bf], writes=[(RRT[sl], tc)])
                P.op("pool", lambda e, tc=tc: e.tensor_copy(out=RRT[sl][:, tc, 128:256], in_=C.ident_bf[:]),
                     reads=[C.ident_bf], writes=[(RRT[sl], tc)])
            items = [(lvl, grp) for lvl in range(7) for grp in range(4)]
            R_ = RRT[sl]

            def my(idx):
                lvl, grp = items[idx]
                xo = xob[idx % 2]
                t0 = 4 * grp
                tr = range(t0, t0 + 4)
                mk = msk[:, lvl, :, :].rearrange("p a j -> p (a j)").unsqueeze(1).to_broadcast([128, 4, 256])
                P.op("pool", lambda e: e.tensor_tensor(out=xo[:], in0=Pp[0][:, t0:t0 + 4, :], in1=mk, op=ALU.mult),
                     reads=[(Pp[0], tr), msk], writes=[xo])
                yb = (psA, psB) if idx % 2 == 0 else (psC, psD)
                for k in range(4):
                    bk = yb[k // 2]
                    o_ = (k % 2) * 256
                    mm(C, bk[:, o_:o_ + 128], xo[:, k, 128:256], R_[:, t0 + k, 0:128], True, True, [xo, (R_, t0 + k)], bk)
                    mm(C, bk[:, o_ + 128:o_ + 256], xo[:, k, 0:128], R_[:, t0 + k, 128:256], True, True, [xo, (R_, t0 + k)], bk)
                for b2 in range(2):
                    bk = yb[b2]
                    t1 = t0 + 2 * b2
                    P.op("act", lambda e, bk=bk, t1=t1: e.activation(
                        out=Pp[1][:, t1:t1 + 2, :].rearrange("p a j -> p (a j)"), in_=bk[:, 0:512], func=AF.Copy),
                        writes=[bk, (Pp[1], (t1, t1 + 1))])

            def za(idx):
                lvl, grp = items[idx]
                t0 = 4 * grp
                zb = (psE, psF)
                for k in range(4):
                    bk = zb[k // 2]
                    o_ = (k % 2) * 256
                    mm(C, bk[:, o_:o_ + 128], R_[:, t0 + k, 128:256], Pp[1][:, t0 + k, 0:128], True, True,
                       [(R_, t0 + k), (Pp[1], t0 + k)], bk)
                    mm(C, bk[:, o_ + 128:o_ + 256], R_[:, t0 + k, 0:128], Pp[1][:, t0 + k, 128:256], True, True,
                       [(R_, t0 + k), (Pp[1], t0 + k)], bk)
                for b2 in range(2):
                    bk = zb[b2]
                    t1 = t0 + 2 * b2
                    rv = R_[:, t1:t1 + 2, :].rearrange("p a j -> p (a j)")
                    P.op("dve", lambda e, bk=bk, rv=rv: e.tensor_tensor(out=rv, in0=bk[:, 0:512], in1=rv, op=ALU.add),
                         writes=[bk, (R_, (t1, t1 + 1))])

            for idx in range(len(items)):
                my(idx)
                if idx >= 1:
                    za(idx - 1)
            za(len(items) - 1)

        def g2(s, heads):
            banks = ((psA, psB, psC), (psD, psE, psF))
            sls = range(len(heads))
            for sl in sls:
                P.op("pool", lambda e, sl=sl: e.memset(hS[sl][:], 0.0), writes=[hS[sl]])
                P.op("pool", lambda e, sl=sl: e.memset(hSb[sl][:], 0.0), writes=[hSb[sl]])
            for tc in range(NT):
                j2 = tc % 2
                for sl in sls:
                    bx = banks[sl][0]
                    mm(C, bx[:, 0:128], kT[sl][:, c4(tc)], hSb[sl][:], True, True, [kT[sl], hSb[sl]], bx)
                    mm(C, bx[:, 128:256], qT[sl][:, c4(tc)], hSb[sl][:], True, True, [qT[sl], hSb[sl]], bx)
                for sl in sls:
                    h = heads[sl]
                    bx = banks[sl][0]
                    r_, o1_ = rb[sl][j2], o1[sl][j2]
                    P.op("dve", lambda e, tc=tc, h=h, r_=r_, bx=bx, sl=sl: e.scalar_tensor_tensor(
                        out=r_[:], in0=bx[:, 0:128], scalar=sc[:, NEGC, tc, h:h + 1], in1=vtok[sl][:, tc, :], op0=ALU.mult, op1=ALU.add),
                        reads=[sc, (vtok[sl], tc)], writes=[bx, r_])
                    P.op("dve", lambda e, tc=tc, h=h, o1_=o1_, bx=bx: e.tensor_scalar(
                        out=o1_[:], in0=bx[:, 128:256], scalar1=sc[:, EGC, tc, h:h + 1], scalar2=None, op0=ALU.mult),
                        reads=[sc], writes=[bx, o1_])
                for sl in sls:
                    by = banks[sl][1]
                    mm(C, by[:, 0:128], RRT[sl][:, tc, 0:128], rb[sl][j2][:], True, True, [(RRT[sl], tc), rb[sl][j2]], by)
                for sl in sls:
                    h = heads[sl]
                    by = banks[sl][1]
                    vn_ = vn[sl][j2]
                    P.op("act", lambda e, tc=tc, h=h, vn_=vn_, by=by: e.activation(out=vn_[:], in_=by[:, 0:128], func=AF.Identity,
                                                                                   scale=sc[:, BETA, tc, h:h + 1]),
                         reads=[sc], writes=[by, vn_])
                for sl in sls:
                    bz = banks[sl][2]
                    vn_ = vn[sl][j2]
                    mm(C, bz[:, 0:128], attnT[sl][:, tc, :], vn_[:], True, True, [(attnT[sl], tc), vn_], bz)
                    mm(C, bz[:, 128:256], kd[sl][:, tc, :], vn_[:], True, True, [(kd[sl], tc), vn_], bz)
                for sl in sls:
                    h = heads[sl]
                    bz = banks[sl][2]
                    P.op("dve", lambda e, tc=tc, h=h, bz=bz, sl=sl: e.scalar_tensor_tensor(
                        out=hS[sl][:], in0=hS[sl][:], scalar=sc[:, EGL, tc, h:h + 1], in1=bz[:, 128:256], op0=ALU.mult, op1=ALU.add),
                        reads=[hS[sl], sc], writes=[bz, hS[sl]])
                    P.op("act", lambda e, sl=sl: e.activation(out=hSb[sl][:], in_=hS[sl][:], func=AF.Copy),
                         reads=[hS[sl]], writes=[hSb[sl]])
                    of_, o1_ = of[sl][j2], o1[sl][j2]
                    P.op("dve", lambda e, o1_=o1_, of_=of_, bz=bz: e.tensor_tensor(out=of_[:], in0=bz[:, 0:128], in1=o1_[:], op=ALU.add),
                         reads=[o1_], writes=[bz, of_])
                for sl in sls:
                    of_, osq_, ssq_, onb_ = of[sl][j2], osq[sl][j2], ssq[sl][j2], onb[sl][j2]
                    P.op("pool", lambda e, of_=of_, osq_=osq_: e.tensor_tensor(out=osq_[:], in0=of_[:], in1=of_[:], op=ALU.mult),
                         reads=[of_], writes=[osq_])
                    P.op("dve", lambda e, osq_=osq_, ssq_=ssq_: e.tensor_reduce(out=ssq_[:, 0:1], in_=osq_[:], axis=mybir.AxisListType.X, op=ALU.add),
                         reads=[osq_], writes=[ssq_])
                    P.op("act", lambda e, ssq_=ssq_: e.activation(out=ssq_[:, 1:2], in_=ssq_[:, 0:1], func=AF.Sqrt, bias=C.epsc[:, 0:1],
                                                                  scale=1.0 / 128), reads=[ssq_, C.epsc], writes=[ssq_])
                    P.op("dve", lambda e, ssq_=ssq_: e.reciprocal(out=ssq_[:, 2:3], in_=ssq_[:, 1:2]), reads=[ssq_], writes=[ssq_])
                    P.op("dve", lambda e, of_=of_, onb_=onb_, ssq_=ssq_: e.tensor_scalar(out=onb_[:], in0=of_[:], scalar1=ssq_[:, 2:3],
                                                                                      scalar2=None, op0=ALU.mult),
                         reads=[of_, ssq_], writes=[onb_])
                for sl in sls:
                    onb_ = onb[sl][j2]
                    pt_ = psTs[sl]
                    P.op("pe", lambda e, onb_=onb_, pt_=pt_: e.transpose(pt_[:, 0:128], onb_[:], C.ident_bf[:]),
                         reads=[onb_, C.ident_bf], writes=[pt_])
                    P.op("dve", lambda e, tc=tc, pt_=pt_, sl=sl: e.scalar_tensor_tensor(
                        out=yst[sl][:, c4(tc)], in0=pt_[:, 0:128], scalar=onm[:, 0:1], in1=siluz[sl][:, c4(tc)], op0=ALU.mult, op1=ALU.mult),
                        reads=[onm, siluz[sl]], writes=[pt_, yst[sl]])
            for sl in sls:
                h = heads[sl]
                P.op("sp", lambda e, h=h, sl=sl: e.dma_start(out=scr["YA"][s, h * 128:(h + 1) * 128, :], in_=yst[sl][:]),
                     reads=[yst[sl]], writes=[scr["YAb"]], dma=True)

        for s in range(nseq):
            P.op("sp", lambda e, s=s: e.dma_start(out=ba[:], in_=scr["BA"][s].rearrange("(a p) k -> p a k", p=128)),
                 reads=[scr["BAb"]], writes=[ba], dma=True)
            S_ = lambda k: sc[:, k, :, :]
            P.op("act", lambda e: e.activation(out=S_(BETA), in_=ba[:, :, 0:8], func=AF.Sigmoid), reads=[ba], writes=[sc])
            P.op("dve", lambda e: e.tensor_scalar(out=S_(NBETA), in0=S_(BETA), scalar1=-1.0, scalar2=None, op0=ALU.mult),
                 reads=[sc], writes=[sc])
            P.op("dve", lambda e: e.tensor_tensor(out=S_(G), in0=ba[:, :, 8:16],
                                                  in1=ad[:, 8:16].unsqueeze(1).to_broadcast([128, NT, 8]), op=ALU.add),
                 reads=[ba, ad], writes=[sc])
            P.op("act", lambda e: e.activation(out=S_(G), in_=S_(G), func=AF.Exp), reads=[sc], writes=[sc])
            P.op("act", lambda e: e.activation(out=S_(G), in_=S_(G), func=AF.Ln, bias=one1[:], scale=1.0),
                 reads=[sc, one1], writes=[sc])
            P.op("dve", lambda e: e.tensor_tensor(out=S_(G), in0=S_(G),
                                                  in1=ad[:, 0:8].unsqueeze(1).to_broadcast([128, NT, 8]), op=ALU.mult),
                 reads=[sc, ad], writes=[sc])
            P.op("dve", lambda e: e.tensor_scalar(out=S_(G), in0=S_(G), scalar1=-1.0, scalar2=None, op0=ALU.mult),
                 reads=[sc], writes=[sc])
            for tc in range(NT):
                mm(C, psF[:, tc * 8:(tc + 1) * 8], triF[:], sc[:, G, tc, :], True, True, [triF, sc], psF)
                mm(C, psF[:, 128 + tc * 8:128 + (tc + 1) * 8], onesF[:], sc[:, G, tc, :], True, True, [onesF, sc], psF)
            P.op("dve", lambda e: e.tensor_copy(out=S_(GC), in_=psF[:, 0:128].rearrange("p (a k) -> p a k", k=8)),
                 writes=[psF, sc])
            P.op("dve", lambda e: e.tensor_copy(out=S_(GL), in_=psF[:, 128:256].rearrange("p (a k) -> p a k", k=8)),
                 writes=[psF, sc])
            P.op("act", lambda e: e.activation(out=S_(EGC), in_=S_(GC), func=AF.Exp), reads=[sc], writes=[sc])
            P.op("act", lambda e: e.activation(out=S_(EGL), in_=S_(GL), func=AF.Exp), reads=[sc], writes=[sc])
            P.op("dve", lambda e: e.tensor_scalar(out=S_(NGC), in0=S_(GC), scalar1=-1.0, scalar2=None, op0=ALU.mult),
                 reads=[sc], writes=[sc])
            P.op("dve", lambda e: e.tensor_scalar(out=S_(NEGC), in0=S_(EGC), scalar1=-1.0, scalar2=None, op0=ALU.mult),
                 reads=[sc], writes=[sc])
            P.op("dve", lambda e: e.tensor_tensor(out=S_(KDEC), in0=S_(GL), in1=S_(GC), op=ALU.subtract),
                 reads=[sc], writes=[sc])
            P.op("act", lambda e: e.activation(out=S_(KDEC), in_=S_(KDEC), func=AF.Exp), reads=[sc], writes=[sc])
            for hp in range(4):
                heads = (2 * hp, 2 * hp + 1)
                for sl, h in enumerate(heads):
                    g1(s, h, sl)
                g2(s, heads)
    P.barrier()


def tile_cols(W, width, kc=None):
    K, N = W.shape
    return np.ascontiguousarray(W.reshape(K // 128, 128, N // width, width).transpose(2, 1, 0, 3))


def prep_weights(inp, depth=DEPTH):
    f = lambda a: np.asarray(a, dtype=np.float32)
    out = {}
    out["ada_w"] = np.stack([tile_cols(f(inp["ada_w"][l]), GW) for l in range(depth)])
    out["ada_b"] = np.ascontiguousarray(f(inp["ada_b"])[:depth].reshape(depth, 144, 128))
    out["norms"] = np.ascontiguousarray(np.stack(
        [f(inp["norm_ffn1"])[:depth], f(inp["norm_mix"])[:depth], f(inp["norm_ffn2"])[:depth]], axis=1
    ).reshape(depth, 3, NKC, 128))
    for nm in ("ffn1", "ffn2"):
        out[nm + "_w1"] = np.stack([tile_cols(f(inp[nm + "_w1"][l]), GW) for l in range(depth)])
        out[nm + "_w3"] = np.stack([tile_cols(f(inp[nm + "_w3"][l]), GW) for l in range(depth)])
        out[nm + "_w2"] = np.stack([tile_cols(f(inp[nm + "_w2"][l]), 128) for l in range(depth)])
    w_in = f(inp["w_in"])[:depth]
    segs = []
    for l in range(depth):
        W = w_in[l]
        segs.append(np.concatenate([tile_cols(W[:, 0:4096], GW), tile_cols(W[:, OFF_BU:OFF_CQ], GW),
                                    tile_cols(W[:, OFF_CQ:OFF_GATE], GW), tile_cols(W[:, OFF_GATE:], GW)], axis=0))
    out["w_inT"] = np.stack(segs)
    out["w_ba"] = np.stack([tile_cols(w_in[l][:, OFF_BA:OFF_BU], 16)[0] for l in range(depth)])
    out["w_br"] = np.stack([np.concatenate([tile_cols(f(inp["w_branch_a"][l]), GW), tile_cols(f(inp["w_branch_b"][l]), GW),
                                            tile_cols(f(inp["w_branch_c"][l]), GW)], axis=2) for l in range(depth)])
    out["w_outT"] = np.stack([tile_cols(f(inp["w_out"][l]), GW) for l in range(depth)])
    out["dil_g"] = np.ascontiguousarray(np.stack([np.tile(f(inp["dil_q_norm"])[:depth], (1, 2)),
                                                  np.tile(f(inp["dil_k_norm"])[:depth], (1, 2))], axis=1))
    out["gdn_conv"] = np.ascontiguousarray(f(inp["gdn_conv"])[:depth].reshape(depth, 4, 24, 128).transpose(0, 3, 2, 1))
    ad = np.concatenate([f(inp["gdn_a_log"])[:depth], f(inp["gdn_dt_bias"])[:depth]], axis=1)
    out["gdn_ad"] = np.ascontiguousarray(np.broadcast_to(ad[:, None, :], (depth, 128, 16)))
    out["gdn_on"] = np.ascontiguousarray(f(inp["gdn_out_norm"])[:depth].reshape(depth, 128, 1))
    jj, ii = np.meshgrid(np.arange(128), np.arange(128), indexing="ij")
    msk = np.zeros((128, 7, 2, 128), np.float32)
    for lv in range(7):
        bsz = 1 << lv
        m_ = (((jj // bsz) % 2 == 0) & (ii // bsz == jj // bsz + 1)).astype(np.float32)
        msk[:, lv, 0, :] = m_
        msk[:, lv, 1, :] = m_.T
    out["gdn_masks"] = msk
    G_, P_, I_ = 48, 64, 16
    out["s5_vec"] = np.ascontiguousarray(np.stack(
        [f(inp["s5_a_re"])[:depth].reshape(depth, 24, 128), f(inp["s5_a_im"])[:depth].reshape(depth, 24, 128),
         np.repeat(f(inp["s5_log_step"])[:depth], P_, axis=1).reshape(depth, 24, 128)], axis=1))
    out["s5_dg"] = np.ascontiguousarray(np.concatenate(
        [f(inp["s5_d"])[:depth].reshape(depth, 6, 128), f(inp["s5_glu_b"])[:depth].reshape(depth, 12, 128)], axis=1))
    Bm = np.zeros((depth, 128, 6, 2, 512), np.float32)
    Cm = np.zeros((depth, 128, 24, 2, 128), np.float32)
    for ri, (bn, cn) in enumerate((("s5_b_re", "s5_c_re"), ("s5_b_im", "s5_c_im"))):
        b = f(inp[bn])[:depth]
        cc = f(inp[cn])[:depth]
        for g in range(G_):
            kc, gl8 = divmod(g, 8)
            Bm[:, gl8 * 16:(gl8 + 1) * 16, kc, ri, gl8 * 64:(gl8 + 1) * 64] = b[:, g].transpose(0, 2, 1)
            sc, gl = divmod(g, 2)
            Cm[:, gl * 64:(gl + 1) * 64, sc, ri, gl8 * 16:(gl8 + 1) * 16] = cc[:, g].transpose(0, 2, 1)
    out["s5_Bm"] = Bm
    out["s5_Cm"] = Cm
    out["s5_gluw"] = np.stack([tile_cols(f(inp["s5_glu_w"][l]), GW) for l in range(depth)])
    return out


def kernel(**inputs):
    x = np.asarray(inputs["x"], dtype=np.float32)
    c = np.asarray(inputs["c"], dtype=np.float32)
    B = x.shape[0]
    nseq = B // NCORES
    wts = prep_weights(inputs)
    nc = build_program(nseq=nseq)
    in_maps = []
    for i in range(NCORES):
        xs = x[i * nseq:(i + 1) * nseq].reshape(nseq * S, D)
        m = dict(wts)
        m["xT"] = np.ascontiguousarray(xs.T)
        m["c"] = np.ascontiguousarray(c[i * nseq:(i + 1) * nseq])
        in_maps.append(m)
    res = run_bass_kernel_spmd(nc, in_maps, core_ids=list(range(NCORES)))
    outs = [np.ascontiguousarray(r["outT"].T).reshape(nseq, S, D) for r in res.results]
    return np.concatenate(outs, axis=0).astype(np.float32)
```

```python
import math
from contextlib import ExitStack

import numpy as np
import concourse.bass as bass
import concourse.mybir as mybir
from concourse.bass_utils import run_bass_kernel_spmd

F32 = mybir.dt.float32
BF16 = mybir.dt.bfloat16
AF = mybir.ActivationFunctionType
ALU = mybir.AluOpType

D = 2048
S = 2048
DFF = 5632
NKC = D // 128
NFC = DFF // 128
TT = 512
EPS = 1e-6
DEPTH = 2
NCORES = 8
IN_COLS = 13328
OFF_Z = 3072
OFF_BA = 4096
OFF_BU = 4112
OFF_CQ = 4880
OFF_GATE = 7184
GW = 256


class Buf:
    def __init__(self, name, n=1, t=None):
        self.name = name
        self.n = n
        self.t = t
        self.lastw = [None] * n
        self.readers = [[] for _ in range(n)]

    def __getitem__(self, idx):
        return self.t[idx]


class Op:
    __slots__ = ("eng", "dma", "sig")

    def __init__(self, eng, dma):
        self.eng = eng
        self.dma = dma
        self.sig = None


def _parts(acc):
    if isinstance(acc, Buf):
        return acc, range(acc.n)
    b, p = acc
    if p is None:
        return b, range(b.n)
    if isinstance(p, int):
        return b, (p,)
    return b, p


class Prog:
    ENGS = ("pe", "dve", "act", "pool", "sp")
    NS = 8

    def __init__(self, nc, stack):
        self.nc = nc
        self.e = {"pe": nc.tensor, "dve": nc.vector, "act": nc.scalar, "pool": nc.gpsimd, "sp": nc.sync}
        self.sem = {k: stack.enter_context(nc.semaphore("s_" + k)) for k in self.ENGS}
        self.cnt = {k: 0 for k in self.ENGS}
        self.dsem = {k: [stack.enter_context(nc.semaphore("d_%s%d" % (k, i))) for i in range(self.NS)]
                     for k in ("sp", "pool", "act")}
        self.dcnt = {k: 0 for k in self.dsem}
        self.dlast = {k: [None] * self.NS for k in self.dsem}
        self.waited = {k: {} for k in self.ENGS}
        self.last = {k: None for k in self.ENGS}
        self.nops = 0

    def _wait(self, eng, sig):
        sem, val = sig
        w = self.waited[eng]
        key = id(sem)
        if w.get(key, 0) < val:
            self.e[eng].wait_ge(sem, val)
            w[key] = val

    def op(self, eng, fn, reads=(), writes=(), dma=False, sig=True):
        o = Op(eng, dma)
        deps = []
        for acc in reads:
            b, ps = _parts(acc)
            for p in ps:
                lw = b.lastw[p]
                if lw is not None:
                    deps.append((lw, True))
                b.readers[p].append(o)
        for acc in writes:
            b, ps = _parts(acc)
            for p in ps:
                lw = b.lastw[p]
                if lw is not None:
                    deps.append((lw, False))
                for r in b.readers[p]:
                    if r is not o:
                        deps.append((r, False))
                b.lastw[p] = o
                b.readers[p] = []
        for d, raw in deps:
            if d is o:
                continue
            if (not d.dma) and (not dma) and d.eng == eng:
                if not raw or eng == "pe":
                    continue
            self._wait(eng, d.sig)
        if dma:
            k = self.dcnt[eng]
            slot = k % self.NS
            prev = self.dlast[eng][slot]
            if prev is not None:
                self._wait(eng, prev.sig)
            o.sig = (self.dsem[eng][slot], 16 * (k // self.NS + 1))
            self.dcnt[eng] = k + 1
            self.dlast[eng][slot] = o
            fn(self.e[eng]).then_inc(self.dsem[eng][slot], 16)
        elif not sig:
            o.sig = (self.sem[eng], self.cnt[eng] + 1)
            fn(self.e[eng])
        else:
            self.cnt[eng] += 1
            o.sig = (self.sem[eng], self.cnt[eng])
            fn(self.e[eng]).then_inc(self.sem[eng], 1)
            self.last[eng] = o
        self.nops += 1
        return o

    def barrier(self):
        sigs = [self.last[k].sig for k in self.ENGS if self.last[k] is not None]
        for q in self.dsem:
            for o in self.dlast[q]:
                if o is not None:
                    sigs.append(o.sig)
        for eng in self.ENGS:
            for s in sigs:
                self._wait(eng, s)

    def finish(self):
        sigs = [self.last[k].sig for k in self.ENGS if self.last[k] is not None]
        for q in self.dsem:
            for o in self.dlast[q]:
                if o is not None:
                    sigs.append(o.sig)
        for s in sigs:
            self._wait("sp", s)


class Ctx:
    def __init__(self, nc, stack):
        self.nc = nc
        self.P = Prog(nc, stack)
        self.gstack = stack
        self.uid = 0
        self._fill = {}

    def fill(self, val):
        if val not in self._fill:
            self._fill[val] = self.nc.gpsimd.to_reg(float(val))
        return self._fill[val]

    def sb(self, stack, shape, dt, name, n=1):
        self.uid += 1
        t = stack.enter_context(self.nc.sbuf_tensor("%s_%d" % (name, self.uid), list(shape), dt))
        return Buf(name, n, t)

    def ps(self, stack, shape, dt, name, n=1):
        self.uid += 1
        t = stack.enter_context(self.nc.psum_tensor("%s_%d" % (name, self.uid), list(shape), dt))
        return Buf(name, n, t)


class Stream:
    def __init__(self, bufs, loads, depth=None):
        self.bufs = bufs
        self.loads = loads
        self.depth = len(bufs) if depth is None else depth
        self.issued = 0

    def get(self, k):
        while self.issued < min(len(self.loads), k + self.depth):
            self.loads[self.issued](self.bufs[self.issued % len(self.bufs)])
            self.issued += 1
        return self.bufs[k % len(self.bufs)]


def mm(C, ps, lhsT, rhs, start, stop, reads, wr, lazy=False):
    C.P.op("pe", lambda e: e.matmul(ps, lhsT, rhs, start=start, stop=stop), reads=reads, writes=[wr],
           sig=bool(stop) or not lazy)


def load_vec_fm(C, stack, dst, dst_cols, src_rows_ap, nrows, tmp, pst, ident):
    P = C.P
    P.op("sp", lambda e: e.dma_start(out=tmp[0:nrows, :], in_=src_rows_ap), writes=[tmp], dma=True)
    P.op("pe", lambda e: e.transpose(pst[:, 0:nrows], tmp[0:nrows, :], ident[0:nrows, 0:nrows]),
         reads=[tmp, ident], writes=[pst])
    P.op("dve", lambda e: e.tensor_copy(out=dst_cols, in_=pst[:, 0:nrows]), reads=[pst], writes=[dst])


def build_program(nseq=2, depth=DEPTH, debug=False, phases=None):
    nc = bass.Bass("TRN2", target_bir_lowering=False)
    ntok = nseq * S
    ntile = ntok // TT
    dr = {}

    def din(name, shape, dt=F32):
        dr[name] = nc.dram_tensor(name, list(shape), dt, kind="ExternalInput").ap()
        return dr[name]

    def dscratch(name, shape, dt=F32):
        kind = "ExternalOutput" if debug else "Internal"
        dr[name] = nc.dram_tensor(name, list(shape), dt, kind=kind).ap()
        return dr[name]

    xT_in = din("xT", [D, ntok])
    c_in = din("c", [nseq, D])
    L = depth
    din("ada_w", [L, 72, 128, NKC, GW])
    din("ada_b", [L, 144, 128])
    din("norms", [L, 3, NKC, 128])
    for nm in ("ffn1", "ffn2"):
        din(nm + "_w1", [L, DFF // GW, 128, NKC, GW])
        din(nm + "_w3", [L, DFF // GW, 128, NKC, GW])
        din(nm + "_w2", [L, NKC, 128, NFC, 128])
    din("w_inT", [L, 52, 128, NKC, GW])
    din("w_ba", [L, 128, NKC, 16])
    din("w_br", [L, 8, 128, NKC, GW])
    din("w_outT", [L, 8, 128, NKC, GW])
    din("dil_g", [L, 2, 128])
    din("gdn_conv", [L, 128, 24, 4])
    din("gdn_ad", [L, 128, 16])
    din("gdn_on", [L, 128, 1])
    din("gdn_masks", [128, 7, 2, 128])
    din("s5_vec", [L, 3, 24, 128])
    din("s5_dg", [L, 18, 128])
    din("s5_Bm", [L, 128, 6, 2, 512])
    din("s5_Cm", [L, 128, 24, 2, 128])
    din("s5_gluw", [L, 6, 128, 6, GW])
    out_ap = nc.dram_tensor("outT", [D, ntok], F32, kind="ExternalOutput").ap()
    scr = {}
    scr["HT"] = dscratch("HT", [nseq, D, S], BF16)
    scr["PA"] = dscratch("PA", [nseq, 4096, S])
    scr["PB"] = dscratch("PB", [nseq, 768, S])
    scr["PC"] = dscratch("PC", [nseq, 2304, S])
    scr["BA"] = dscratch("BA", [nseq, S, 16])
    scr["YA"] = dscratch("YA", [nseq, 1024, S], BF16)
    scr["YB"] = dscratch("YB", [nseq, 768, S], BF16)
    scr["YC"] = dscratch("YC", [nseq, 256, S], BF16)
    for k in ("HT", "PA", "PB", "PC", "BA", "YA", "YB", "YC"):
        scr[k + "b"] = Buf(k, ntile)
    xT = dscratch("xres", [D, ntok])

    with ExitStack() as gs:
        C = Ctx(nc, gs)
        P = C.P
        ones_bf = C.sb(gs, [128, 128], BF16, "ones_bf")
        ident_f = C.sb(gs, [128, 128], F32, "ident_f")
        ident_bf = C.sb(gs, [128, 128], BF16, "ident_bf")
        neghalf = C.sb(gs, [128, TT], F32, "neghalf")
        modv = C.sb(gs, [128, nseq, 9, NKC], F32, "modv")
        gains = C.sb(gs, [128, 3, NKC], F32, "gains")
        P.op("pool", lambda e: e.memset(ones_bf[:], 1.0), writes=[ones_bf])
        P.op("pool", lambda e: e.memset(neghalf[:], -0.5), writes=[neghalf])
        epsc = C.sb(gs, [128, 1], F32, "epsc")
        P.op("pool", lambda e: e.memset(epsc[:], EPS), writes=[epsc])
        C.epsc = epsc
        P.op("pool", lambda e: e.memset(ident_f[:], 1.0), writes=[ident_f])
        P.op("pool", lambda e: e.affine_select(out=ident_f[:], in_=ident_f[:], pattern=[[1, 128]],
                                               compare_op=ALU.is_equal, fill=C.fill(0.0), base=0,
                                               channel_multiplier=-1),
             reads=[ident_f], writes=[ident_f])
        P.op("dve", lambda e: e.tensor_copy(out=ident_bf[:], in_=ident_f[:]), reads=[ident_f], writes=[ident_bf])
        C.ones_bf, C.ident_f, C.ident_bf, C.neghalf = ones_bf, ident_f, ident_bf, neghalf

        xres = Buf("xres", ntile)
        xin = Buf("xin", 1)

        def xview(ap, i):
            return ap.rearrange("(c p) t -> p c t", p=128)[:, :, i * TT:(i + 1) * TT]

        for l in range(depth):
            src = xT_in if l == 0 else xT
            ada_phase(C, dr, l, nseq, modv, gains)
            ffn_phase(C, dr, l, "ffn1", 0, nseq, ntile, modv, gains, src, xT, xres, xview)
            if phases is None or "proj" in phases:
                proj_phase(C, dr, l, nseq, ntile, modv, gains, xT, xres, xview, scr)
            if phases is None or "s5" in phases:
                s5_phase(C, dr, l, nseq, scr)
            if phases is None or "dil" in phases:
                dil_phase(C, dr, l, nseq, scr)
            if phases is None or "gdn" in phases:
                gdn_phase(C, dr, l, nseq, scr)
            if phases is None or "merge" in phases:
                merge_phase(C, dr, l, nseq, ntile, modv, gains, xT, xres, xview, scr)
            ffn_phase(C, dr, l, "ffn2", 2, nseq, ntile, modv, gains, xT, xT if l < depth - 1 else out_ap,
                      xres, xview)
        P.finish()
    return nc


def mod_vectors(C, AB, modv, gains, sub, nseq, gate_scale):
    P = C.P
    for b in range(nseq):
        P.op("dve", lambda e, b=b: e.scalar_tensor_tensor(
            out=AB[:, b, 0, :], in0=modv[:, b, 3 * sub + 1, :], scalar=1.0, in1=gains[:, sub, :],
            op0=ALU.add, op1=ALU.mult), reads=[modv, gains], writes=[AB])
        P.op("dve", lambda e, b=b: e.tensor_copy(out=AB[:, b, 1, :], in_=modv[:, b, 3 * sub, :]),
             reads=[modv], writes=[AB])
        P.op("dve", lambda e, b=b: e.tensor_scalar(out=AB[:, b, 2, :], in0=modv[:, b, 3 * sub + 2, :],
                                                   scalar1=gate_scale, scalar2=None, op0=ALU.mult),
             reads=[modv], writes=[AB])


def norm_mod(C, xs, hT, AB, b, sq, tmpf, ms, rstd, ps_stat):
    P = C.P
    for c in range(NKC):
        q = sq[c % len(sq)]
        P.op("act", lambda e, c=c, q=q: e.activation(out=q[:], in_=xs[:, c, :], func=AF.Square),
             reads=[xs], writes=[q])
        mm(C, ps_stat[:], C.ones_bf[:], q[:], c == 0, c == NKC - 1, [q, C.ones_bf], ps_stat)
    P.op("act", lambda e: e.activation(out=ms[:], in_=ps_stat[:], func=AF.Sqrt, bias=C.epsc[:, 0:1], scale=1.0 / D),
         reads=[ps_stat, C.epsc], writes=[ms])
    P.op("dve", lambda e: e.reciprocal(out=rstd[:], in_=ms[:]), reads=[ms], writes=[rstd])
    for c in range(NKC):
        t = tmpf[c % len(tmpf)]
        P.op("dve", lambda e, c=c, t=t: e.scalar_tensor_tensor(
            out=t[:], in0=xs[:, c, :], scalar=AB[:, b, 0, c:c + 1], in1=rstd[:],
            op0=ALU.mult, op1=ALU.mult), reads=[xs, AB, rstd], writes=[t])
        P.op("act", lambda e, c=c, t=t: e.activation(
            out=hT[:, c, :], in_=t[:], func=AF.Identity, bias=AB[:, b, 1, c:c + 1], scale=1.0),
            reads=[t, AB], writes=[(hT, c)])


def nm_squares(C, xs, sq16):
    for c in range(NKC):
        C.P.op("act", lambda e, c=c: e.activation(out=sq16[:, c, :], in_=xs[:, c, :], func=AF.Square),
               reads=[xs], writes=[(sq16, c)])


def nm_stats(C, sq16, ps_stat):
    for c in range(NKC):
        mm(C, ps_stat[:], C.ones_bf[:], sq16[:, c, :], c == 0, c == NKC - 1, [(sq16, c), C.ones_bf], ps_stat)


def nm_finish(C, xs, hT, AB, b, tmpf, ms, rstd, ps_stat):
    P = C.P
    P.op("act", lambda e: e.activation(out=ms[:], in_=ps_stat[:], func=AF.Sqrt, bias=C.epsc[:, 0:1], scale=1.0 / D),
         reads=[ps_stat, C.epsc], writes=[ms])
    P.op("dve", lambda e: e.reciprocal(out=rstd[:], in_=ms[:]), reads=[ms], writes=[rstd])
    for c in range(NKC):
        t = tmpf[c % len(tmpf)]
        P.op("dve", lambda e, c=c, t=t: e.scalar_tensor_tensor(
            out=t[:], in0=xs[:, c, :], scalar=AB[:, b, 0, c:c + 1], in1=rstd[:],
            op0=ALU.mult, op1=ALU.mult), reads=[xs, AB, rstd], writes=[t])
        P.op("act", lambda e, c=c, t=t: e.activation(
            out=hT[:, c, :], in_=t[:], func=AF.Identity, bias=AB[:, b, 1, c:c + 1], scale=1.0),
            reads=[t, AB], writes=[(hT, c)])


def ada_phase(C, dr, l, nseq, modv, gains):
    P = C.P
    with ExitStack() as st:
        tmp = C.sb(st, [128, 128], F32, "ada_tmp")
        pst = C.ps(st, [128, 512], F32, "ada_pst")
        psm = C.ps(st, [128, 512], F32, "ada_psm")
        cT = C.sb(st, [128, NKC, nseq], F32, "cT")
        bias = C.sb(st, [128, 144], F32, "ada_bias")
        wb = [C.sb(st, [128, NKC, GW], F32, "ada_w%d" % i) for i in range(2)]
        for c in range(NKC):
            P.op("sp", lambda e, c=c: e.dma_start(out=tmp[0:nseq, :], in_=dr["c"][:, c * 128:(c + 1) * 128]),
                 writes=[tmp], dma=True)
            P.op("pe", lambda e: e.transpose(pst[:, 0:nseq], tmp[0:nseq, :], C.ident_f[0:nseq, 0:nseq]),
                 reads=[tmp, C.ident_f], writes=[pst])
            P.op("act", lambda e, c=c: e.activation(out=cT[:, c, :], in_=pst[:, 0:nseq], func=AF.Silu),
                 reads=[pst], writes=[cT])
        load_vec_fm(C, st, bias, bias[:, 0:128], dr["ada_b"][l, 0:128, :], 128, tmp, pst, C.ident_f)
        load_vec_fm(C, st, bias, bias[:, 128:144], dr["ada_b"][l, 128:144, :], 16, tmp, pst, C.ident_f)
        for j in range(3):
            load_vec_fm(C, st, gains, gains[:, j, :], dr["norms"][l, j, :, :], NKC, tmp, pst, C.ident_f)
        loads = []
        for g in range(72):
            loads.append(lambda b, g=g: P.op("sp", lambda e: e.dma_start(out=b[:], in_=dr["ada_w"][l, g]),
                                             writes=[b], dma=True))
        strm = Stream(wb, loads)
        for g in range(72):
            w = strm.get(g)
            for fi in range(2):
                ch = 2 * g + fi
                for c in range(NKC):
                    mm(C, psm[:, ch * nseq:(ch + 1) * nseq], w[:, c, fi * 128:(fi + 1) * 128], cT[:, c, :],
                       c == 0, c == NKC - 1, [w, cT], psm)
        for b in range(nseq):
            P.op("dve", lambda e, b=b: e.tensor_tensor(
                out=modv[:, b, :, :].rearrange("p j c -> p (j c)"),
                in0=psm[:, 0:144 * nseq].rearrange("p (k b) -> p k b", b=nseq)[:, :, b],
                in1=bias[:], op=ALU.add), reads=[psm, bias], writes=[modv])
    P.barrier()


def ffn_phase(C, dr, l, nm, sub, nseq, ntile, modv, gains, src, dst, xres, xview):
    P = C.P
    w1d, w3d, w2d = dr[nm + "_w1"], dr[nm + "_w3"], dr[nm + "_w2"]
    NG = DFF // GW
    with ExitStack() as st:
        xs = C.sb(st, [128, NKC, TT], F32, "xs")
        hT = C.sb(st, [128, NKC, TT], BF16, "hT", n=NKC)
        actT = C.sb(st, [128, NFC, TT], BF16, "actT", n=NFC)
        w1b = [C.sb(st, [128, NKC, GW], BF16, "w1b%d" % i) for i in range(2)]
        w3b = [C.sb(st, [128, NKC, GW], BF16, "w3b%d" % i) for i in range(2)]
        w2b = [C.sb(st, [128, NFC, 128], BF16, "w2b%d" % i) for i in range(2)]
        sq16 = C.sb(st, [128, NKC, TT], BF16, "sq16", n=NKC)
        tmpf = [C.sb(st, [128, TT], F32, "tmpf%d" % i) for i in range(2)]
        sg = [C.sb(st, [128, TT], F32, "sg%d" % i) for i in range(2)]
        xr = [C.sb(st, [128, TT], F32, "xr%d" % i) for i in range(3)]
        ms = C.sb(st, [128, TT], F32, "ms")
        rstd = C.sb(st, [128, TT], F32, "rstd")
        AB = C.sb(st, [128, nseq, 3, NKC], F32, "AB")
        ps_stat = C.ps(st, [128, TT], F32, "ps_stat")
        ps_g = [C.ps(st, [128, TT], F32, "ps_g%d" % i) for i in range(2)]
        ps_u = [C.ps(st, [128, TT], F32, "ps_u%d" % i) for i in range(2)]
        ps_o = [C.ps(st, [128, TT], F32, "ps_o%d" % i) for i in range(2)]

        mod_vectors(C, AB, modv, gains, sub, nseq, 0.5)

        def mk(wd, g):
            return lambda b: P.op("pool", lambda e: e.dma_start(out=b[:], in_=wd[l, g]), writes=[b], dma=True)
        l1 = [mk(w1d, g) for _ in range(ntile) for g in range(NG)]
        l3 = [mk(w3d, g) for _ in range(ntile) for g in range(NG)]
        l2 = [mk(w2d, m) for _ in range(ntile) for m in range(NKC)]
        s1, s3, s2 = Stream(w1b, l1), Stream(w3b, l3), Stream(w2b, l2)

        def prep1(i):
            P.op("sp", lambda e, i=i: e.dma_start(out=xs[:], in_=xview(src, i)),
                 reads=[(xres, i)], writes=[xs], dma=True)
            nm_squares(C, xs, sq16)

        def prep23(i):
            nm_stats(C, sq16, ps_stat)
            nm_finish(C, xs, hT, AB, i // (S // TT), tmpf, ms, rstd, ps_stat)

        prep1(0)
        prep23(0)
        for i in range(ntile):
            b = i // (S // TT)
            for g in range(NG):
                k = i * NG + g
                wa, wb_ = s1.get(k), s3.get(k)
                if g == NG - 4:
                    s2.get(i * NKC)
                for fi in range(GW // 128):
                    f = g * (GW // 128) + fi
                    pg, pu = ps_g[f % 2], ps_u[f % 2]
                    for c in range(NKC):
                        mm(C, pg[:], wa[:, c, fi * 128:(fi + 1) * 128], hT[:, c, :], c == 0, c == NKC - 1,
                           [wa, (hT, c)], pg, lazy=True)
                    for c in range(NKC):
                        mm(C, pu[:], wb_[:, c, fi * 128:(fi + 1) * 128], hT[:, c, :], c == 0, c == NKC - 1,
                           [wb_, (hT, c)], pu, lazy=True)
                    s_ = sg[f % 2]
                    P.op("act", lambda e, s_=s_, pg=pg: e.activation(out=s_[:], in_=pg[:], func=AF.Silu),
                         reads=[pg], writes=[s_])
                    P.op("dve", lambda e, s_=s_, pu=pu, f=f: e.tensor_tensor(
                        out=actT[:, f, :], in0=pu[:], in1=s_[:], op=ALU.mult),
                        reads=[pu, s_], writes=[(actT, f)])
            if i + 1 < ntile:
                prep1(i + 1)
            for m in range(NKC):
                if m == NKC // 2 and i + 1 < ntile:
                    prep23(i + 1)
                k = i * NKC + m
                w2 = s2.get(k)
                po = ps_o[m % 2]
                r = xr[m % 3]
                P.op("sp", lambda e, i=i, m=m, r=r: e.dma_start(out=r[:], in_=xview(src, i)[:, m, :]),
                     reads=[(xres, i)], writes=[r], dma=True)
                for f in range(NFC):
                    mm(C, po[:], w2[:, f, :], actT[:, f, :], f == 0, f == NFC - 1, [w2, (actT, f)], po, lazy=True)
                P.op("dve", lambda e, m=m, r=r, po=po, b=b: e.scalar_tensor_tensor(
                    out=r[:], in0=po[:], scalar=AB[:, b, 2, m:m + 1], in1=r[:], op0=ALU.mult, op1=ALU.add),
                    reads=[po, AB, r], writes=[r])
                P.op("sp", lambda e, i=i, m=m, r=r: e.dma_start(out=xview(dst, i)[:, m, :], in_=r[:]),
                     reads=[r], writes=[(xres, i)], dma=True)
    P.barrier()


def proj_phase(C, dr, l, nseq, ntile, modv, gains, xT, xres, xview, scr):
    P = C.P
    wd = dr["w_inT"]
    NGP = 28
    with ExitStack() as st:
        xs = C.sb(st, [128, NKC, TT], F32, "xs")
        hTs = [C.sb(st, [128, NKC, TT], BF16, "hT%d" % i, n=NKC) for i in range(2)]
        sq16 = C.sb(st, [128, NKC, TT], BF16, "psq16", n=NKC)
        wb = [C.sb(st, [128, NKC, GW], BF16, "pw%d" % i) for i in range(3)]
        wba = C.sb(st, [128, NKC, 16], BF16, "wba")
        sq = [C.sb(st, [128, TT], BF16, "sq%d" % i) for i in range(3)]
        tmpf = [C.sb(st, [128, TT], F32, "tmpf%d" % i) for i in range(3)]
        stg = [C.sb(st, [128, TT], F32, "stg%d" % i) for i in range(4)]
        bas = [C.sb(st, [128, 16], F32, "bas%d" % i) for i in range(2)]
        ms = C.sb(st, [128, TT], F32, "ms")
        rstd = C.sb(st, [128, TT], F32, "rstd")
        AB = C.sb(st, [128, nseq, 3, NKC], F32, "AB")
        ps_stat = C.ps(st, [128, TT], F32, "ps_stat")
        ps_p = [C.ps(st, [128, TT], F32, "ps_p%d" % i) for i in range(3)]
        ps_ba = C.ps(st, [128, TT], F32, "ps_ba")
        mod_vectors(C, AB, modv, gains, 1, nseq, 1.0)
        P.op("pool", lambda e: e.dma_start(out=wba[:], in_=dr["w_ba"][l]), writes=[wba], dma=True)
        loads = [(lambda b, g=g: P.op("pool", lambda e: e.dma_start(out=b[:], in_=wd[l, g]), writes=[b], dma=True))
                 for _ in range(ntile) for g in range(NGP)]
        strm = Stream(wb, loads)
        nev = 0
        def prep1(i):
            P.op("sp", lambda e, i=i: e.dma_start(out=xs[:], in_=xview(xT, i)),
                 reads=[(xres, i)], writes=[xs], dma=True)
            nm_squares(C, xs, sq16)

        def prep23(i):
            s, ti = divmod(i, S // TT)
            tsl = slice(ti * TT, (ti + 1) * TT)
            h_ = hTs[i % 2]
            nm_stats(C, sq16, ps_stat)
            nm_finish(C, xs, h_, AB, s, tmpf, ms, rstd, ps_stat)
            P.op("sp", lambda e: e.dma_start(
                out=scr["HT"][s].rearrange("(c p) t -> p c t", p=128)[:, :, tsl], in_=h_[:]),
                reads=[h_], writes=[(scr["HTb"], i)], dma=True)

        prep1(0)
        prep23(0)
        for i in range(ntile):
            s, ti = divmod(i, S // TT)
            tsl = slice(ti * TT, (ti + 1) * TT)
            hT = hTs[i % 2]
            for ts in range(TT // 128):
                for c in range(NKC):
                    mm(C, ps_ba[:, ts * 16:(ts + 1) * 16], hT[:, c, ts * 128:(ts + 1) * 128], wba[:, c, :],
                       c == 0, c == NKC - 1, [(hT, c), wba], ps_ba)
            bb = bas[i % 2]
            for ts in range(TT // 128):
                pass
            P.op("dve", lambda e, bb=bb: e.tensor_copy(out=stg[3][:, 0:64], in_=ps_ba[:, 0:64]),
                 reads=[ps_ba], writes=[stg[3]])
            P.op("sp", lambda e, s=s, ti=ti: e.dma_start(
                out=scr["BA"][s, ti * TT:(ti + 1) * TT, :].rearrange("(a p) k -> p a k", p=128),
                in_=stg[3][:, 0:64].rearrange("p (a k) -> p a k", k=16)),
                reads=[stg[3]], writes=[(scr["BAb"], i)], dma=True)
            for g in range(NGP):
                if g == 0 and i + 1 < ntile:
                    prep1(i + 1)
                if g == NGP // 2 and i + 1 < ntile:
                    prep23(i + 1)
                w = strm.get(i * NGP + g)
                for fi in range(2):
                    ch = 2 * g + fi
                    pp = ps_p[ch % 3]
                    for c in range(NKC):
                        mm(C, pp[:], w[:, c, fi * 128:(fi + 1) * 128], hT[:, c, :], c == 0, c == NKC - 1,
                           [w, (hT, c)], pp, lazy=True)
                    sg_ = stg[nev % 3]
                    if nev % 2 == 0:
                        P.op("act", lambda e, sg_=sg_, pp=pp: e.activation(out=sg_[:], in_=pp[:], func=AF.Copy),
                             reads=[pp], writes=[sg_])
                    else:
                        P.op("dve", lambda e, sg_=sg_, pp=pp: e.tensor_copy(out=sg_[:], in_=pp[:]),
                             reads=[pp], writes=[sg_])
                    nev += 1
                    if ch < 32:
                        dst, row, key = scr["PA"], ch, "PAb"
                    elif ch < 38:
                        dst, row, key = scr["PB"], ch - 32, "PBb"
                    else:
                        dst, row, key = scr["PC"], ch - 38, "PCb"
                    P.op("sp", lambda e, dst=dst, row=row, s=s, tsl=tsl, sg_=sg_: e.dma_start(
                        out=dst[s, row * 128:(row + 1) * 128, tsl], in_=sg_[:]),
                        reads=[sg_], writes=[(scr[key], i)], dma=True)
    P.barrier()


def s5_phase(C, dr, l, nseq, scr):
    P = C.P
    NSC = 24
    TWO_PI = 2.0 * math.pi
    with ExitStack() as st:
        tmp = C.sb(st, [128, 128], F32, "s5tmp")
        pst = C.ps(st, [128, 512], F32, "s5pst")
        vec = C.sb(st, [128, 3, NSC], F32, "s5vec")
        dg = C.sb(st, [128, 18], F32, "s5dg")
        sm = C.sb(st, [128, 24, NSC], F32, "s5sm")
        halfpi = C.sb(st, [128, 1], F32, "halfpi")
        Bre = C.sb(st, [128, NSC, 128], F32, "Bre")
        Bim = C.sb(st, [128, NSC, 128], F32, "Bim")
        Atre = C.sb(st, [128, NSC * 128], BF16, "Atre")
        Atim = C.sb(st, [128, NSC * 128], BF16, "Atim")
        Bm = C.sb(st, [128, 6, 2, 512], BF16, "Bm")
        Cm = C.sb(st, [128, NSC, 2, 128], BF16, "Cm")
        gw = C.sb(st, [128, 6, 6, GW], BF16, "gluw")
        tri = C.sb(st, [128, 128], BF16, "tri")
        for j in range(3):
            load_vec_fm(C, st, vec, vec[:, j, :], dr["s5_vec"][l, j], NSC, tmp, pst, C.ident_f)
        load_vec_fm(C, st, dg, dg[:, :], dr["s5_dg"][l], 18, tmp, pst, C.ident_f)
        P.op("pool", lambda e: e.dma_start(out=Bm[:], in_=dr["s5_Bm"][l]), writes=[Bm], dma=True)
        P.op("pool", lambda e: e.dma_start(out=Cm[:], in_=dr["s5_Cm"][l]), writes=[Cm], dma=True)
        for g in range(6):
            P.op("pool", lambda e, g=g: e.dma_start(out=gw[:, g], in_=dr["s5_gluw"][l, g]), writes=[gw], dma=True)
        P.op("pool", lambda e: e.memset(halfpi[:], math.pi / 2), writes=[halfpi])
        P.op("pool", lambda e: e.memset(tri[:], 1.0), writes=[tri])
        P.op("pool", lambda e: e.affine_select(out=tri[:], in_=tri[:], pattern=[[1, 128]], compare_op=ALU.is_ge,
                                               fill=C.fill(0.0), base=0, channel_multiplier=-1),
             reads=[tri], writes=[tri])

        V = lambda k: sm[:, k, :]

        def tt(o, a, b, op, eng="dve"):
            P.op(eng, lambda e: e.tensor_tensor(out=o, in0=a, in1=b, op=op), reads=[sm, vec], writes=[sm])

        def ts(o, a, s1, op0, s2=None, op1=None):
            if op1 is None:
                P.op("dve", lambda e: e.tensor_scalar(out=o, in0=a, scalar1=s1, scalar2=None, op0=op0),
                     reads=[sm, vec], writes=[sm])
            else:
                P.op("dve", lambda e: e.tensor_scalar(out=o, in0=a, scalar1=s1, scalar2=s2, op0=op0, op1=op1),
                     reads=[sm, vec], writes=[sm])

        def act(o, a, f, scale=1.0, bias=None):
            if bias is None:
                P.op("act", lambda e: e.activation(out=o, in_=a, func=f, scale=scale), reads=[sm, vec], writes=[sm])
            else:
                P.op("act", lambda e: e.activation(out=o, in_=a, func=f, scale=scale, bias=bias),
                     reads=[sm, vec, halfpi], writes=[sm])
        STEP, ARE, RS, TH, THR, T0, RHO, SIN, COS, LR, LI, MR, MI, CR, CI, DEN, NR, T1, T2, PWR, PWI, T3 = range(22)
        act(V(STEP), vec[:, 2, :], AF.Exp)
        ts(V(ARE), vec[:, 0, :], -1e-4, ALU.min)
        tt(V(RS), V(ARE), V(STEP), ALU.mult)
        tt(V(TH), vec[:, 1, :], V(STEP), ALU.mult)
        ts(V(THR), V(TH), 1.0, ALU.mult)
        for m in range(1, 6):
            ts(V(T0), V(TH), (2 * m - 1) * math.pi, ALU.is_gt, TWO_PI, ALU.mult)
            tt(V(THR), V(THR), V(T0), ALU.subtract)
        act(V(RHO), V(RS), AF.Exp)
        act(V(SIN), V(THR), AF.Sin)
        ts(V(T0), V(THR), -1.0, ALU.mult)
        tt(V(T0), V(T0), V(THR), ALU.max)
        act(V(COS), V(T0), AF.Sin, scale=-1.0, bias=halfpi[:])
        tt(V(LR), V(RHO), V(COS), ALU.mult)
        tt(V(LI), V(RHO), V(SIN), ALU.mult)
        P.op("dve", lambda e: e.reciprocal(out=V(T1), in_=V(RHO)), reads=[sm], writes=[sm])
        tt(V(MR), V(COS), V(T1), ALU.mult)
        tt(V(MI), V(SIN), V(T1), ALU.mult)
        ts(V(MI), V(MI), -1.0, ALU.mult)
        ts(V(NR), V(LR), -1.0, ALU.add)
        tt(V(T1), V(ARE), V(ARE), ALU.mult)
        tt(V(T2), vec[:, 1, :], vec[:, 1, :], ALU.mult)
        tt(V(DEN), V(T1), V(T2), ALU.add)
        P.op("dve", lambda e: e.reciprocal(out=V(DEN), in_=V(DEN)), reads=[sm], writes=[sm])
        tt(V(T1), V(NR), V(ARE), ALU.mult)
        tt(V(T2), V(LI), vec[:, 1, :], ALU.mult)
        tt(V(T1), V(T1), V(T2), ALU.add)
        tt(V(CR), V(T1), V(DEN), ALU.mult)
        tt(V(T1), V(LI), V(ARE), ALU.mult)
        tt(V(T2), V(NR), vec[:, 1, :], ALU.mult)
        tt(V(T1), V(T1), V(T2), ALU.subtract)
        tt(V(CI), V(T1), V(DEN), ALU.mult)

        def pow_table(Tr, Ti, init, base, s2):
            ta = C.sb(s2, [128, NSC, 64], F32, "pt_a")
            tb = C.sb(s2, [128, NSC, 64], F32, "pt_b")
            if init is None:
                P.op("pool", lambda e: e.memset(Tr[:, :, 0:1], 1.0), writes=[Tr])
                P.op("pool", lambda e: e.memset(Ti[:, :, 0:1], 0.0), writes=[Ti])
            else:
                P.op("dve", lambda e: e.tensor_copy(out=Tr[:, :, 0], in_=V(init[0])), reads=[sm], writes=[Tr])
                P.op("dve", lambda e: e.tensor_copy(out=Ti[:, :, 0], in_=V(init[1])), reads=[sm], writes=[Ti])
            ts(V(PWR), V(base[0]), 1.0, ALU.mult)
            ts(V(PWI), V(base[1]), 1.0, ALU.mult)
            for k in range(7):
                w = 1 << k
                pr = V(PWR).unsqueeze(2).to_broadcast([128, NSC, w])
                pi = V(PWI).unsqueeze(2).to_broadcast([128, NSC, w])
                lo = slice(0, w)
                hi = slice(w, 2 * w)
                P.op("dve", lambda e, pr=pr, lo=lo, w=w: e.tensor_tensor(out=ta[:, :, 0:w], in0=Tr[:, :, lo], in1=pr, op=ALU.mult),
                     reads=[Tr, sm], writes=[ta])
                P.op("dve", lambda e, pi=pi, lo=lo, w=w: e.tensor_tensor(out=tb[:, :, 0:w], in0=Ti[:, :, lo], in1=pi, op=ALU.mult),
                     reads=[Ti, sm], writes=[tb])
                P.op("dve", lambda e, hi=hi, w=w: e.tensor_tensor(out=Tr[:, :, hi], in0=ta[:, :, 0:w], in1=tb[:, :, 0:w], op=ALU.subtract),
                     reads=[ta, tb], writes=[Tr])
                P.op("dve", lambda e, pi=pi, lo=lo, w=w: e.tensor_tensor(out=ta[:, :, 0:w], in0=Tr[:, :, lo], in1=pi, op=ALU.mult),
                     reads=[Tr, sm], writes=[ta])
                P.op("dve", lambda e, pr=pr, lo=lo, w=w: e.tensor_tensor(out=tb[:, :, 0:w], in0=Ti[:, :, lo], in1=pr, op=ALU.mult),
                     reads=[Ti, sm], writes=[tb])
                P.op("dve", lambda e, hi=hi, w=w: e.tensor_tensor(out=Ti[:, :, hi], in0=ta[:, :, 0:w], in1=tb[:, :, 0:w], op=ALU.add),
                     reads=[ta, tb], writes=[Ti])
                tt(V(T1), V(PWR), V(PWR), ALU.mult)
                tt(V(T2), V(PWI), V(PWI), ALU.mult)
                tt(V(T3), V(PWR), V(PWI), ALU.mult)
                tt(V(PWR), V(T1), V(T2), ALU.subtract)
                ts(V(PWI), V(T3), 2.0, ALU.mult)

        with ExitStack() as s2:
            pow_table(Bre, Bim, None, (LR, LI), s2)
        with ExitStack() as s2:
            Asr = C.sb(s2, [128, NSC, 128], F32, "Asr")
            Asi = C.sb(s2, [128, NSC, 128], F32, "Asi")
            pow_table(Asr, Asi, (CR, CI), (MR, MI), s2)
            for sc in range(NSC):
                for (Tsm, Tt) in ((Asr, Atre), (Asi, Atim)):
                    P.op("pe", lambda e, Tsm=Tsm, sc=sc: e.transpose(pst[:, 0:128], Tsm[:, sc, :], C.ident_f[:]),
                         reads=[Tsm, C.ident_f], writes=[pst])
                    P.op("dve", lambda e, Tt=Tt, sc=sc: e.tensor_copy(out=Tt[:, sc * 128:(sc + 1) * 128], in_=pst[:, 0:128]),
                         reads=[pst], writes=[Tt])
            P.barrier()

        with ExitStack() as s3:
            uT = C.sb(s3, [128, 6, S], F32, "uT")
            uTb = C.sb(s3, [128, 6, S], BF16, "uTb")
            hcr = C.sb(s3, [128, NSC], F32, "hcr", n=6)
            hci = C.sb(s3, [128, NSC], F32, "hci", n=6)
            zre = [C.sb(s3, [128, 512], BF16, "zre%d" % i) for i in range(2)]
            zim = [C.sb(s3, [128, 512], BF16, "zim%d" % i) for i in range(2)]
            tf = [C.sb(s3, [128, 512], F32, "s5t%d" % i) for i in range(8)]
            ta = [[C.sb(s3, [128, 512], F32, "s5ta%d%d" % (i, j)) for j in range(4)] for i in range(2)]
            srbs = [C.sb(s3, [128, 4, 128], BF16, "srb%d" % i) for i in range(2)]
            sibs = [C.sb(s3, [128, 4, 128], BF16, "sib%d" % i) for i in range(2)]
            hs = C.sb(s3, [128, 2, 4, 4], F32, "hs")
            ystg = [C.sb(s3, [128, 512], BF16, "ystg%d" % i) for i in range(2)]
            ps_brs = [C.ps(s3, [128, 512], F32, "ps_br%d" % i) for i in range(2)]
            ps_bis = [C.ps(s3, [128, 512], F32, "ps_bi%d" % i) for i in range(2)]
            ps_cr = C.ps(s3, [128, 512], F32, "ps_cr")
            ps_ci = C.ps(s3, [128, 512], F32, "ps_ci")
            ps_y = C.ps(s3, [128, 512], F32, "ps_y")
            ps_v, ps_g = ps_brs[0], ps_bis[0]
            v3 = lambda t: t[:].rearrange("p (a j) -> p a j", j=128)
            for s in range(nseq):
                ub = scr["PB"][s].rearrange("(c p) t -> p c t", p=128)
                P.op("sp", lambda e, ub=ub: e.dma_start(out=uT[:], in_=ub), reads=[scr["PBb"]], writes=[uT], dma=True)
                P.op("pool", lambda e, ub=ub: e.dma_start(out=uTb[:], in_=ub), reads=[scr["PBb"]], writes=[uTb], dma=True)
                P.op("pool", lambda e: e.memset(hcr[:], 0.0), writes=[hcr])
                P.op("pool", lambda e: e.memset(hci[:], 0.0), writes=[hci])
                NI = (S // 128) * 6

                def stA(idx):
                    tc, kc = divmod(idx, 6)
                    p = idx % 2
                    tsl = slice(tc * 128, (tc + 1) * 128)
                    csl = slice(4 * kc * 128, (4 * kc + 4) * 128)
                    pbr, pbi = ps_brs[p], ps_bis[p]
                    t0, t1, t2, t3 = ta[p]
                    zr, zi = zre[p], zim[p]
                    mm(C, pbr[:], uTb[:, kc, tsl], Bm[:, kc, 0, :], True, True, [uTb, Bm], pbr)
                    mm(C, pbi[:], uTb[:, kc, tsl], Bm[:, kc, 1, :], True, True, [uTb, Bm], pbi)
                    P.op("dve", lambda e: e.tensor_tensor(out=t0[:], in0=pbr[:], in1=Atre[:, csl], op=ALU.mult),
                         reads=[pbr, Atre], writes=[t0])
                    P.op("dve", lambda e: e.tensor_tensor(out=t1[:], in0=pbi[:], in1=Atim[:, csl], op=ALU.mult),
                         reads=[pbi, Atim], writes=[t1])
                    P.op("pool", lambda e: e.tensor_tensor(out=zr[:], in0=t0[:], in1=t1[:], op=ALU.subtract),
                         reads=[t0, t1], writes=[zr])
                    P.op("dve", lambda e: e.tensor_tensor(out=t2[:], in0=pbi[:], in1=Atre[:, csl], op=ALU.mult),
                         reads=[pbi, Atre], writes=[t2])
                    P.op("dve", lambda e: e.tensor_tensor(out=t3[:], in0=pbr[:], in1=Atim[:, csl], op=ALU.mult),
                         reads=[pbr, Atim], writes=[t3])
                    P.op("pool", lambda e: e.tensor_tensor(out=zi[:], in0=t2[:], in1=t3[:], op=ALU.add),
                         reads=[t2, t3], writes=[zi])

                def stB(idx):
                    tc, kc = divmod(idx, 6)
                    p = idx % 2
                    sc0 = 4 * kc
                    bsl = slice(sc0, sc0 + 4)
                    zr, zi = zre[p], zim[p]
                    sb_r, sb_i = srbs[p], sibs[p]
                    for scl in range(4):
                        mm(C, ps_cr[:, scl * 128:(scl + 1) * 128], zr[:, scl * 128:(scl + 1) * 128], tri[:],
                           True, True, [zr, tri], ps_cr)
                    for scl in range(4):
                        mm(C, ps_ci[:, scl * 128:(scl + 1) * 128], zi[:, scl * 128:(scl + 1) * 128], tri[:],
                           True, True, [zi, tri], ps_ci)
                    hr_b = hcr[:, bsl].unsqueeze(2).to_broadcast([128, 4, 128])
                    hi_b = hci[:, bsl].unsqueeze(2).to_broadcast([128, 4, 128])
                    P.op("dve", lambda e: e.tensor_tensor(out=v3(tf[4]), in0=v3(ps_cr), in1=hr_b, op=ALU.add),
                         reads=[ps_cr, (hcr, kc)], writes=[tf[4]])
                    P.op("dve", lambda e: e.tensor_tensor(out=v3(tf[5]), in0=v3(ps_ci), in1=hi_b, op=ALU.add),
                         reads=[ps_ci, (hci, kc)], writes=[tf[5]])
                    P.op("dve", lambda e: e.tensor_tensor(out=v3(tf[0]), in0=v3(tf[4]), in1=Bre[:, bsl, :], op=ALU.mult),
                         reads=[tf[4], Bre], writes=[tf[0]])
                    P.op("dve", lambda e: e.tensor_tensor(out=v3(tf[1]), in0=v3(tf[5]), in1=Bim[:, bsl, :], op=ALU.mult),
                         reads=[tf[5], Bim], writes=[tf[1]])
                    P.op("dve", lambda e: e.tensor_tensor(out=tf[6][:], in0=tf[0][:], in1=tf[1][:], op=ALU.subtract),
                         reads=[tf[0], tf[1]], writes=[tf[6]])
                    P.op("dve", lambda e: e.tensor_tensor(out=v3(tf[2]), in0=v3(tf[5]), in1=Bre[:, bsl, :], op=ALU.mult),
                         reads=[tf[5], Bre], writes=[tf[2]])
                    P.op("dve", lambda e: e.tensor_tensor(out=v3(tf[3]), in0=v3(tf[4]), in1=Bim[:, bsl, :], op=ALU.mult),
                         reads=[tf[4], Bim], writes=[tf[3]])
                    P.op("dve", lambda e: e.tensor_tensor(out=tf[7][:], in0=tf[2][:], in1=tf[3][:], op=ALU.add),
                         reads=[tf[2], tf[3]], writes=[tf[7]])
                    P.op("act", lambda e: e.activation(out=sb_r[:], in_=v3(tf[6]), func=AF.Copy), reads=[tf[6]], writes=[sb_r])
                    P.op("act", lambda e: e.activation(out=sb_i[:], in_=v3(tf[7]), func=AF.Copy, scale=-1.0),
                         reads=[tf[7]], writes=[sb_i])
                    sl_r = v3(tf[6])[:, :, 127]
                    sl_i = v3(tf[7])[:, :, 127]
                    hk = hs[:, kc % 2]
                    P.op("dve", lambda e: e.tensor_tensor(out=hk[:, 0, :], in0=sm[:, LR, bsl], in1=sl_r, op=ALU.mult),
                         reads=[tf[6], sm], writes=[hs])
                    P.op("dve", lambda e: e.tensor_tensor(out=hk[:, 1, :], in0=sm[:, LI, bsl], in1=sl_i, op=ALU.mult),
                         reads=[tf[7], sm], writes=[hs])
                    P.op("dve", lambda e: e.tensor_tensor(out=hcr[:, bsl], in0=hk[:, 0, :], in1=hk[:, 1, :], op=ALU.subtract),
                         reads=[hs], writes=[(hcr, kc)])
                    P.op("dve", lambda e: e.tensor_tensor(out=hk[:, 2, :], in0=sm[:, LR, bsl], in1=sl_i, op=ALU.mult),
                         reads=[tf[7], sm], writes=[hs])
                    P.op("dve", lambda e: e.tensor_tensor(out=hk[:, 3, :], in0=sm[:, LI, bsl], in1=sl_r, op=ALU.mult),
                         reads=[tf[6], sm], writes=[hs])
                    P.op("dve", lambda e: e.tensor_tensor(out=hci[:, bsl], in0=hk[:, 2, :], in1=hk[:, 3, :], op=ALU.add),
                         reads=[hs], writes=[(hci, kc)])

                def stC(idx):
                    tc, kc = divmod(idx, 6)
                    p = idx % 2
                    sc0 = 4 * kc
                    tsl = slice(tc * 128, (tc + 1) * 128)
                    sb_r, sb_i = srbs[p], sibs[p]
                    for scl in range(4):
                        mm(C, ps_y[:, 0:128], Cm[:, sc0 + scl, 0, :], sb_r[:, scl, :], scl == 0, False, [Cm, sb_r], ps_y)
                    for scl in range(4):
                        mm(C, ps_y[:, 0:128], Cm[:, sc0 + scl, 1, :], sb_i[:, scl, :], False, scl == 3, [Cm, sb_i], ps_y)
                    P.op("dve", lambda e: e.scalar_tensor_tensor(
                        out=uT[:, kc, tsl], in0=uT[:, kc, tsl], scalar=dg[:, kc:kc + 1], in1=ps_y[:, 0:128],
                        op0=ALU.mult, op1=ALU.add), reads=[uT, dg, ps_y], writes=[uT])

                for idx in range(NI + 2):
                    if idx < NI:
                        stA(idx)
                    if 1 <= idx <= NI:
                        stB(idx - 1)
                    if idx >= 2:
                        stC(idx - 2)
                for kc in range(6):
                    for q in range(S // 512):
                        qs = slice(q * 512, (q + 1) * 512)
                        yv = uT[:, kc, qs]
                        P.op("pool", lambda e, yv=yv: e.tensor_tensor(out=tf[0][:], in0=yv, in1=yv, op=ALU.mult),
                             reads=[uT], writes=[tf[0]])
                        P.op("dve", lambda e: e.tensor_scalar(out=tf[1][:], in0=tf[0][:], scalar1=0.044715, scalar2=1.0,
                                                              op0=ALU.mult, op1=ALU.add), reads=[tf[0]], writes=[tf[1]])
                        P.op("pool", lambda e, yv=yv: e.tensor_tensor(out=tf[2][:], in0=tf[1][:], in1=yv, op=ALU.mult),
                             reads=[tf[1], uT], writes=[tf[2]])
                        P.op("act", lambda e: e.activation(out=tf[3][:], in_=tf[2][:], func=AF.Sigmoid, scale=1.5957691216057308),
                             reads=[tf[2]], writes=[tf[3]])
                        P.op("dve", lambda e, yv=yv, kc=kc, qs=qs: e.tensor_tensor(out=uTb[:, kc, qs], in0=yv, in1=tf[3][:], op=ALU.mult),
                             reads=[uT, tf[3]], writes=[uTb])
                ne = 0
                for q in range(S // 512):
                    qs = slice(q * 512, (q + 1) * 512)
                    for oc in range(6):
                        for kc in range(6):
                            mm(C, ps_v[:], gw[:, oc // 2, kc, (oc % 2) * 128:(oc % 2) * 128 + 128], uTb[:, kc, qs],
                               kc == 0, kc == 5, [gw, uTb], ps_v)
                        for kc in range(6):
                            mm(C, ps_g[:], gw[:, 3 + oc // 2, kc, (oc % 2) * 128:(oc % 2) * 128 + 128], uTb[:, kc, qs],
                               kc == 0, kc == 5, [gw, uTb], ps_g)
                        P.op("act", lambda e, oc=oc: e.activation(out=tf[4][:], in_=ps_g[:], func=AF.Sigmoid,
                                                                  bias=dg[:, 12 + oc:13 + oc], scale=1.0),
                             reads=[ps_g, dg], writes=[tf[4]])
                        ys = ystg[ne % 2]
                        ne += 1
                        P.op("dve", lambda e, oc=oc, ys=ys: e.scalar_tensor_tensor(
                            out=ys[:], in0=ps_v[:], scalar=dg[:, 6 + oc:7 + oc], in1=tf[4][:], op0=ALU.add, op1=ALU.mult),
                            reads=[ps_v, dg, tf[4]], writes=[ys])
                        P.op("sp", lambda e, oc=oc, ys=ys, s=s, qs=qs: e.dma_start(
                            out=scr["YB"][s, oc * 128:(oc + 1) * 128, qs], in_=ys[:]),
                            reads=[ys], writes=[scr["YBb"]], dma=True)
    P.barrier()


def merge_phase(C, dr, l, nseq, ntile, modv, gains, xT, xres, xview, scr):
    P = C.P
    wd = dr["w_inT"]
    with ExitStack() as st:
        hT = [C.sb(st, [128, NKC, TT], BF16, "mhT%d" % i) for i in range(2)]
        yy = [C.sb(st, [128, NKC, TT], BF16, "myy%d" % i) for i in range(2)]
        mg_ = C.sb(st, [128, NKC, TT], BF16, "merged", n=NKC)
        gwb = [C.sb(st, [128, NKC, GW], BF16, "mgw%d" % i) for i in range(6)]
        bwb = [C.sb(st, [128, NKC, GW], BF16, "mbw%d" % i) for i in range(2)]
        owb = [C.sb(st, [128, NKC, GW], BF16, "mow%d" % i) for i in range(2)]
        sgb = [C.sb(st, [128, TT], F32, "msg%d" % i) for i in range(3)]
        tb = [C.sb(st, [128, TT], F32, "mtb%d" % i) for i in range(3)]
        acc = [C.sb(st, [128, TT], F32, "macc%d" % i) for i in range(2)]
        xr = [C.sb(st, [128, TT], F32, "mxr%d" % i) for i in range(4)]
        AB = C.sb(st, [128, nseq, 3, NKC], F32, "mAB")
        ps_gt = [C.ps(st, [128, TT], F32, "ps_gt%d" % i) for i in range(2)]
        ps_b = [C.ps(st, [128, TT], F32, "ps_b%d" % i) for i in range(2)]
        ps_o = [C.ps(st, [128, TT], F32, "ps_mo%d" % i) for i in range(2)]
        mod_vectors(C, AB, modv, gains, 1, nseq, 1.0)
        lg = [(lambda b, g=28 + j * 8 + m: P.op("pool", lambda e: e.dma_start(out=b[:], in_=wd[l, g]), writes=[b], dma=True))
              for _ in range(ntile) for m in range(8) for j in range(3)]
        lb = [(lambda b, m=m: P.op("pool", lambda e: e.dma_start(out=b[:], in_=dr["w_br"][l, m]), writes=[b], dma=True))
              for _ in range(ntile) for m in range(8)]
        lo = [(lambda b, m=m: P.op("pool", lambda e: e.dma_start(out=b[:], in_=dr["w_outT"][l, m]), writes=[b], dma=True))
              for _ in range(ntile) for m in range(8)]
        sg_, sb_, so_ = Stream(gwb, lg, depth=4), Stream(bwb, lb), Stream(owb, lo)
        KOFF = (0, 8, 14)
        KN = (8, 6, 2)

        def load_acts(i):
            s, ti = divmod(i, S // TT)
            tsl = slice(ti * TT, (ti + 1) * TT)
            h, y = hT[i % 2], yy[i % 2]
            P.op("sp", lambda e: e.dma_start(out=h[:], in_=scr["HT"][s].rearrange("(c p) t -> p c t", p=128)[:, :, tsl]),
                 reads=[(scr["HTb"], i)], writes=[h], dma=True)
            P.op("sp", lambda e: e.dma_start(out=y[:, 0:8, :], in_=scr["YA"][s].rearrange("(c p) t -> p c t", p=128)[:, :, tsl]),
                 reads=[scr["YAb"]], writes=[y], dma=True)
            P.op("sp", lambda e: e.dma_start(out=y[:, 8:14, :], in_=scr["YB"][s].rearrange("(c p) t -> p c t", p=128)[:, :, tsl]),
                 reads=[scr["YBb"]], writes=[y], dma=True)
            P.op("sp", lambda e: e.dma_start(out=y[:, 14:16, :], in_=scr["YC"][s].rearrange("(c p) t -> p c t", p=128)[:, :, tsl]),
                 reads=[scr["YCb"]], writes=[y], dma=True)

        load_acts(0)
        for i in range(ntile):
            b = i // (S // TT)
            if i + 1 < ntile:
                load_acts(i + 1)
            h, y = hT[i % 2], yy[i % 2]
            for m8 in range(8):
                gws = [sg_.get((i * 8 + m8) * 3 + j) for j in range(3)]
                bw = sb_.get(i * 8 + m8)
                for mi in range(2):
                    m = 2 * m8 + mi
                    csl = slice(mi * 128, (mi + 1) * 128)
                    for j in range(3):
                        pg, pb = ps_gt[(3 * m + j) % 2], ps_b[(3 * m + j) % 2]
                        for c in range(NKC):
                            mm(C, pg[:], gws[j][:, c, csl], h[:, c, :], c == 0, c == NKC - 1, [gws[j], h], pg, lazy=True)
                        for kc in range(KN[j]):
                            mm(C, pb[:], bw[:, KOFF[j] + kc, csl], y[:, KOFF[j] + kc, :], kc == 0, kc == KN[j] - 1,
                               [bw, y], pb, lazy=True)
                        sgt, tj = sgb[j], tb[j]
                        P.op("act", lambda e, sgt=sgt, pg=pg: e.activation(out=sgt[:], in_=pg[:], func=AF.Sigmoid),
                             reads=[pg], writes=[sgt])
                        P.op("dve", lambda e, sgt=sgt, tj=tj, pb=pb: e.tensor_tensor(out=tj[:], in0=pb[:], in1=sgt[:], op=ALU.mult),
                             reads=[pb, sgt], writes=[tj])
                    a_ = acc[m % 2]
                    P.op("pool", lambda e, a_=a_: e.tensor_tensor(out=a_[:], in0=tb[0][:], in1=tb[1][:], op=ALU.add),
                         reads=[tb[0], tb[1]], writes=[a_])
                    P.op("pool", lambda e, a_=a_, m=m: e.tensor_tensor(out=mg_[:, m, :], in0=a_[:], in1=tb[2][:], op=ALU.add),
                         reads=[a_, tb[2]], writes=[(mg_, m)])
            for m8 in range(8):
                ow = so_.get(i * 8 + m8)
                for mi in range(2):
                    m = 2 * m8 + mi
                    po = ps_o[m % 2]
                    r = xr[m % 4]
                    P.op("sp", lambda e, i=i, m=m, r=r: e.dma_start(out=r[:], in_=xview(xT, i)[:, m, :]),
                         reads=[(xres, i)], writes=[r], dma=True)
                    for c in range(NKC):
                        mm(C, po[:], ow[:, c, mi * 128:(mi + 1) * 128], mg_[:, c, :], c == 0, c == NKC - 1,
                           [ow, (mg_, c)], po, lazy=True)
                    P.op("dve", lambda e, m=m, r=r, po=po, b=b: e.scalar_tensor_tensor(
                        out=r[:], in0=po[:], scalar=AB[:, b, 2, m:m + 1], in1=r[:], op0=ALU.mult, op1=ALU.add),
                        reads=[po, AB, r], writes=[r])
                    P.op("sp", lambda e, i=i, m=m, r=r: e.dma_start(out=xview(xT, i)[:, m, :], in_=r[:]),
                         reads=[r], writes=[(xres, i)], dma=True)
    P.barrier()


def dil_phase(C, dr, l, nseq, scr):
    P = C.P
    DILS = (1, 4, 16)
    NEG = -30000.0
    with ExitStack() as st:
        tmp = C.sb(st, [128, 128], F32, "dtmp")
        pst = C.ps(st, [128, 512], F32, "dpst")
        gv = C.sb(st, [128, 2], F32, "dgv")
        bones = C.sb(st, [128, 128], BF16, "bones")
        ones64 = C.sb(st, [128, 64], BF16, "ones64")
        d0i = C.sb(st, [128, 256], mybir.dt.int32, "d0i")
        d0 = C.sb(st, [128, 256], F32, "d0")
        bias = C.sb(st, [128, 12, 256], F32, "dbias")
        qraw = C.sb(st, [128, 2, S], F32, "qraw")
        kraw = C.sb(st, [128, 2, S], F32, "kraw")
        qn = C.sb(st, [128, 2, S], BF16, "qn")
        kn = C.sb(st, [128, 2, S], BF16, "kn")
        vb = C.sb(st, [128, 2, S], BF16, "vb")
        vtok = C.sb(st, [128, 2, 16, 128], BF16, "vtok")
        acc = C.sb(st, [64, 2, 4, S], F32, "dacc")
        sq = [C.sb(st, [128, 512], BF16, "dsq%d" % i) for i in range(2)]
        msb = [C.sb(st, [128, 512], F32, "dms%d" % i) for i in range(2)]
        rsb = [C.sb(st, [128, 512], F32, "drs%d" % i) for i in range(2)]
        stmp = [C.sb(st, [128, 256], F32, "dst%d" % i) for i in range(2)]
        pT = [C.sb(st, [128, 256], BF16, "dpT%d" % i) for i in range(3)]
        rec = C.sb(st, [64, S], F32, "drec")
        ob = C.sb(st, [64, S], BF16, "dob")
        ps_n = C.ps(st, [128, 512], F32, "ps_dn")
        ps_s = [C.ps(st, [128, 512], F32, "ps_ds%d" % i) for i in range(2)]
        ps_nd = [C.ps(st, [128, 512], F32, "ps_dnd%d" % i) for i in range(2)]
        ps_vt = C.ps(st, [128, 1024], BF16, "ps_dvt")

        load_vec_fm(C, st, gv, gv[:, :], dr["dil_g"][l], 2, tmp, pst, C.ident_f)
        P.op("dve", lambda e: e.tensor_scalar(out=gv[:, 0:1], in0=gv[:, 0:1], scalar1=0.125, scalar2=None, op0=ALU.mult),
             reads=[gv], writes=[gv])
        P.op("pool", lambda e: e.memset(bones[:], 0.0), writes=[bones])
        P.op("pool", lambda e: e.memset(bones[0:64, 0:64], 1.0), writes=[bones])
        P.op("pool", lambda e: e.memset(bones[64:128, 64:128], 1.0), writes=[bones])
        P.op("pool", lambda e: e.memset(ones64[:], 1.0), writes=[ones64])
        P.op("pool", lambda e: e.iota(d0i[:], pattern=[[1, 256]], base=0, channel_multiplier=-1), writes=[d0i])
        P.op("dve", lambda e: e.tensor_copy(out=d0[:], in_=d0i[:]), reads=[d0i], writes=[d0])
        for hg in range(12):
            slope = 2.0 ** (-8.0 * (hg + 1) / 12.0)
            dil = DILS[hg // 4]
            P.op("dve", lambda e, hg=hg, v=-slope * dil: e.tensor_scalar(out=bias[:, hg, :], in0=d0[:], scalar1=v, scalar2=None,
                                                                         op0=ALU.mult), reads=[d0], writes=[bias])
            P.op("pool", lambda e, hg=hg: e.affine_select(out=bias[:, hg, :], in_=bias[:, hg, :], pattern=[[1, 256]],
                                                          compare_op=ALU.is_ge, fill=C.fill(NEG), base=0, channel_multiplier=-1),
                 reads=[bias], writes=[bias])
            P.op("pool", lambda e, hg=hg: e.affine_select(out=bias[:, hg, :], in_=bias[:, hg, :], pattern=[[-1, 256]],
                                                          compare_op=ALU.is_ge, fill=C.fill(NEG), base=128, channel_multiplier=1),
                 reads=[bias], writes=[bias])

        nsc = 0
        for s in range(nseq):
            pc = scr["PC"][s]
            for gi in range(3):
                dil = DILS[gi]
                nb = S // dil // 128
                for (dst, r0, eng) in ((qraw, 0, "sp"), (kraw, 768, "sp")):
                    P.op(eng, lambda e, dst=dst, r0=r0: e.dma_start(
                        out=dst[:], in_=pc[r0 + gi * 256:r0 + gi * 256 + 256, :].rearrange("(c p) t -> p c t", p=128)),
                        reads=[scr["PCb"]], writes=[dst], dma=True)
                P.op("pool", lambda e: e.dma_start(
                    out=vb[:], in_=pc[1536 + gi * 256:1536 + gi * 256 + 256, :].rearrange("(c p) t -> p c t", p=128)),
                    reads=[scr["PCb"]], writes=[vb], dma=True)
                k2 = 0
                for (raw, nrm, gcol) in ((qraw, qn, 0), (kraw, kn, 1)):
                    for c2 in range(2):
                        for q4 in range(S // 512):
                            qs = slice(q4 * 512, (q4 + 1) * 512)
                            sq_, ms_, rs_ = sq[k2 % 2], msb[k2 % 2], rsb[k2 % 2]
                            k2 += 1
                            P.op("act", lambda e, raw=raw, c2=c2, qs=qs, sq_=sq_: e.activation(out=sq_[:], in_=raw[:, c2, qs], func=AF.Square),
                                 reads=[raw], writes=[sq_])
                            mm(C, ps_n[:], bones[:], sq_[:], True, True, [bones, sq_], ps_n)
                            P.op("act", lambda e, ms_=ms_: e.activation(out=ms_[:], in_=ps_n[:], func=AF.Ln, bias=C.epsc[:, 0:1],
                                                                        scale=1.0 / 64), reads=[ps_n, C.epsc], writes=[ms_])
                            P.op("act", lambda e, ms_=ms_, rs_=rs_: e.activation(out=rs_[:], in_=ms_[:], func=AF.Exp, scale=-0.5),
                                 reads=[ms_], writes=[rs_])
                            P.op("dve", lambda e, raw=raw, nrm=nrm, c2=c2, qs=qs, rs_=rs_, gcol=gcol: e.scalar_tensor_tensor(
                                out=nrm[:, c2, qs], in0=raw[:, c2, qs], scalar=gv[:, gcol:gcol + 1], in1=rs_[:],
                                op0=ALU.mult, op1=ALU.mult), reads=[raw, gv, rs_], writes=[nrm])
                for c2 in range(2):
                    for r in range(dil):
                        for n in range(nb):
                            bi = r * nb + n
                            ks = slice(r + dil * 128 * n, r + dil * 128 * n + dil * 127 + 1, dil)
                            P.op("pe", lambda e, c2=c2, ks=ks, bi=bi: e.transpose(ps_vt[:, (bi % 8) * 128:(bi % 8 + 1) * 128], vb[:, c2, ks], C.ident_bf[:]),
                                 reads=[vb, C.ident_bf], writes=[ps_vt])
                            P.op("act", lambda e, c2=c2, bi=bi: e.activation(out=vtok[:, c2, bi, :], in_=ps_vt[:, (bi % 8) * 128:(bi % 8 + 1) * 128], func=AF.Copy),
                                 reads=[ps_vt], writes=[vtok])
                items = [(hh, r, n) for hh in range(4) for r in range(dil) for n in range(nb)]
                curs = {}

                def score(k):
                    hh, r, n = items[k]
                    c2, pb = hh // 2, (hh % 2) * 64
                    hg = gi * 4 + hh
                    idx = nsc + k
                    nq = 256 if n + 1 < nb else 128
                    t0 = r + dil * 128 * n
                    ks = slice(t0, t0 + dil * 127 + 1, dil)
                    qsl = slice(t0, t0 + dil * (nq - 1) + 1, dil)
                    pss, stp, cur = ps_s[idx % 2], stmp[idx % 2], pT[idx % 3]
                    mm(C, pss[:, 0:nq], kn[pb:pb + 64, c2, ks], qn[pb:pb + 64, c2, qsl], True, True, [kn, qn], pss)
                    P.op("dve", lambda e: e.tensor_tensor(out=stp[:, 0:nq], in0=pss[:, 0:nq], in1=bias[:, hg, 0:nq], op=ALU.add),
                         reads=[pss, bias], writes=[stp])
                    P.op("act", lambda e: e.activation(out=cur[:, 0:nq], in_=stp[:, 0:nq], func=AF.Exp),
                         reads=[stp], writes=[cur])
                    curs[k] = cur

                def pv(k):
                    hh, r, n = items[k]
                    c2, pb = hh // 2, (hh % 2) * 64
                    idx = nsc + k
                    bi = r * nb + n
                    t0 = r + dil * 128 * n
                    ks = slice(t0, t0 + dil * 127 + 1, dil)
                    cur = curs[k]
                    prev = curs[k - 1] if n > 0 else None
                    psnd = ps_nd[idx % 2]
                    for di, lhs_of in enumerate((lambda b_: vtok[:, c2, b_, pb:pb + 64], lambda b_: ones64[:])):
                        o_ap = psnd[0:64, di * 128:(di + 1) * 128]
                        if prev is not None:
                            mm(C, o_ap, lhs_of(bi - 1), prev[:, 128:256], True, False, [vtok, ones64, prev], psnd)
                        mm(C, o_ap, lhs_of(bi), cur[:, 0:128], prev is None, True, [vtok, ones64, cur], psnd)
                    a_view = acc[:, :, hh, ks]
                    p_view = psnd[0:64, 0:256].rearrange("p (a j) -> p a j", j=128)
                    if gi == 0:
                        P.op("dve", lambda e: e.tensor_copy(out=a_view, in_=p_view), reads=[psnd], writes=[acc])
                    else:
                        P.op("dve", lambda e: e.tensor_tensor(out=a_view, in0=p_view, in1=a_view, op=ALU.add),
                             reads=[psnd, acc], writes=[acc])
                    curs.pop(k - 1, None)

                score(0)
                for k in range(len(items)):
                    if k + 1 < len(items):
                        score(k + 1)
                    pv(k)
                nsc += len(items)
            for hh in range(4):
                P.op("act", lambda e, hh=hh: e.activation(out=rec[:], in_=acc[:, 1, hh, :], func=AF.Ln), reads=[acc], writes=[rec])
                P.op("act", lambda e: e.activation(out=rec[:], in_=rec[:], func=AF.Exp, scale=-1.0), reads=[rec], writes=[rec])
                P.op("dve", lambda e, hh=hh: e.tensor_tensor(out=ob[:], in0=acc[:, 0, hh, :], in1=rec[:], op=ALU.mult),
                     reads=[acc, rec], writes=[ob])
                P.op("sp", lambda e, hh=hh, s=s: e.dma_start(out=scr["YC"][s, hh * 64:(hh + 1) * 64, :], in_=ob[:]),
                     reads=[ob], writes=[scr["YCb"]], dma=True)
    P.barrier()


def gdn_phase(C, dr, l, nseq, scr):
    P = C.P
    NT = S // 128
    NEG = -30000.0
    with ExitStack() as st:
        cw = C.sb(st, [128, 24, 4], F32, "cw")
        ad = C.sb(st, [128, 16], F32, "gad")
        onm = C.sb(st, [128, 1], F32, "gon")
        one1 = C.sb(st, [128, 1], F32, "one1")
        triF = C.sb(st, [128, 128], F32, "triF")
        onesF = C.sb(st, [128, 128], F32, "onesF")
        mneg = C.sb(st, [128, 128], F32, "mneg")
        smask = C.sb(st, [128, 128], F32, "smask")
        ba = C.sb(st, [128, NT, 16], F32, "gba")
        sc = C.sb(st, [128, 10, NT, 8], F32, "gsc")
        BETA, G, GC, GL, EGC, NGC, EGL, KDEC, NEGC, NBETA = range(10)
        raw = [C.sb(st, [128, S + 3], F32, "graw%d" % i) for i in range(2)]
        cacc = [C.sb(st, [128, S], F32, "gcacc%d" % i) for i in range(2)]
        vTb = C.sb(st, [128, S], BF16, "gvTb")
        DT = C.sb(st, [128, NT, 128], F32, "gDT", n=NT)
        Gb = C.sb(st, [128, NT, 128], F32, "gGb", n=NT)
        Pp = [C.sb(st, [128, NT, 256], BF16, "gP%d" % i, n=NT) for i in range(2)]
        xob = [C.sb(st, [128, 4, 256], BF16, "gxo%d" % i) for i in range(2)]
        msk = C.sb(st, [128, 7, 2, 128], BF16, "gmsk")
        ident2 = C.sb(st, [128, 256], BF16, "gident2")
        sqb = [C.sb(st, [128, 512], BF16, "gsq%d" % i) for i in range(2)]
        rnb = [C.sb(st, [128, 512], F32, "grn%d" % i) for i in range(2)]
        rqb = [C.sb(st, [128, 512], F32, "grq%d" % i) for i in range(2)]
        qT = [C.sb(st, [128, S], BF16, "gqT%d" % i) for i in range(2)]
        kT = [C.sb(st, [128, S], BF16, "gkT%d" % i) for i in range(2)]
        kd = [C.sb(st, [128, NT, 128], BF16, "gkd%d" % i, n=NT) for i in range(2)]
        vtok = [C.sb(st, [128, NT, 128], BF16, "gvtok%d" % i, n=NT) for i in range(2)]
        siluz = [C.sb(st, [128, S], F32, "gsz%d" % i) for i in range(2)]
        RRT = [C.sb(st, [128, NT, 256], BF16, "gRRT%d" % i, n=NT) for i in range(2)]
        attnT = [C.sb(st, [128, NT, 128], BF16, "gattn%d" % i, n=NT) for i in range(2)]
        hS = [C.sb(st, [128, 128], F32, "ghS%d" % i) for i in range(2)]
        hSb = [C.sb(st, [128, 128], BF16, "ghSb%d" % i) for i in range(2)]
        yst = [C.sb(st, [128, S], BF16, "gyst%d" % i) for i in range(2)]
        rb = [[C.sb(st, [128, 128], BF16, "grb%d%d" % (i, j)) for j in range(2)] for i in range(2)]
        vn = [[C.sb(st, [128, 128], BF16, "gvn%d%d" % (i, j)) for j in range(2)] for i in range(2)]
        o1 = [[C.sb(st, [128, 128], F32, "go1%d%d" % (i, j)) for j in range(2)] for i in range(2)]
        of = [[C.sb(st, [128, 128], F32, "gof%d%d" % (i, j)) for j in range(2)] for i in range(2)]
        osq = [[C.sb(st, [128, 128], F32, "gosq%d%d" % (i, j)) for j in range(2)] for i in range(2)]
        onb = [[C.sb(st, [128, 128], BF16, "gonb%d%d" % (i, j)) for j in range(2)] for i in range(2)]
        ssq = [[C.sb(st, [128, 4], F32, "gssq%d%d" % (i, j)) for j in range(2)] for i in range(2)]
        psA = C.ps(st, [128, 512], F32, "gpsA")
        psB = C.ps(st, [128, 512], F32, "gpsB")
        psC = C.ps(st, [128, 512], F32, "gpsC")
        psD = C.ps(st, [128, 512], F32, "gpsD")
        psE = C.ps(st, [128, 512], F32, "gpsE")
        psF = C.ps(st, [128, 512], F32, "gpsF")
        psTs = [C.ps(st, [128, 1024], BF16, "gpsT%d" % i) for i in range(2)]
        c4 = lambda k: slice(k * 128, (k + 1) * 128)

        P.op("sp", lambda e: e.dma_start(out=cw[:], in_=dr["gdn_conv"][l]), writes=[cw], dma=True)
        P.op("sp", lambda e: e.dma_start(out=ad[:], in_=dr["gdn_ad"][l]), writes=[ad], dma=True)
        P.op("sp", lambda e: e.dma_start(out=onm[:], in_=dr["gdn_on"][l]), writes=[onm], dma=True)
        P.op("pool", lambda e: e.dma_start(out=msk[:], in_=dr["gdn_masks"]), writes=[msk], dma=True)
        P.op("pool", lambda e: e.memset(one1[:], 1.0), writes=[one1])
        P.op("pool", lambda e: e.tensor_copy(out=ident2[:, 0:128], in_=C.ident_bf[:]), reads=[C.ident_bf], writes=[ident2])
        P.op("pool", lambda e: e.tensor_copy(out=ident2[:, 128:256], in_=C.ident_bf[:]), reads=[C.ident_bf], writes=[ident2])
        P.op("pool", lambda e: e.memset(onesF[:], 1.0), writes=[onesF])
        P.op("pool", lambda e: e.memset(triF[:], 1.0), writes=[triF])
        P.op("pool", lambda e: e.affine_select(out=triF[:], in_=triF[:], pattern=[[1, 128]], compare_op=ALU.is_ge,
                                               fill=C.fill(0.0), base=0, channel_multiplier=-1), reads=[triF], writes=[triF])
        P.op("pool", lambda e: e.memset(mneg[:], 0.0), writes=[mneg])
        P.op("pool", lambda e: e.affine_select(out=mneg[:], in_=mneg[:], pattern=[[1, 128]], compare_op=ALU.is_ge,
                                               fill=C.fill(NEG), base=0, channel_multiplier=-1), reads=[mneg], writes=[mneg])
        P.op("pool", lambda e: e.memset(smask[:], 1.0), writes=[smask])
        P.op("pool", lambda e: e.affine_select(out=smask[:], in_=smask[:], pattern=[[1, 128]], compare_op=ALU.is_ge,
                                               fill=C.fill(0.0), base=-1, channel_multiplier=-1), reads=[smask], writes=[smask])
        for rw in raw:
            P.op("pool", lambda e, rw=rw: e.memset(rw[:, 0:3], 0.0), writes=[rw])
        P.op("act", lambda e: e.activation(out=ad[:, 0:8], in_=ad[:, 0:8], func=AF.Exp), reads=[ad], writes=[ad])

        def g1(s, h, sl):
            pa = scr["PA"][s]
            k2 = 0
            for wi, (row0, kind) in enumerate(((h * 128, "q"), (1024 + h * 128, "k"), (2048 + h * 128, "v"), (3072 + h * 128, "z"))):
                rw = raw[wi % 2]
                ca = cacc[wi % 2]
                P.op("sp", lambda e, rw=rw, row0=row0: e.dma_start(out=rw[:, 3:], in_=pa[row0:row0 + 128, :]),
                     reads=[scr["PAb"]], writes=[rw], dma=True)
                if kind == "z":
                    P.op("act", lambda e, rw=rw: e.activation(out=siluz[sl][:], in_=rw[:, 3:], func=AF.Silu),
                         reads=[rw], writes=[siluz[sl]])
                    continue
                ch = row0 // 128
                P.op("dve", lambda e, rw=rw, ca=ca, ch=ch: e.tensor_scalar(out=ca[:], in0=rw[:, 0:S], scalar1=cw[:, ch, 0:1],
                                                                         scalar2=None, op0=ALU.mult), reads=[rw, cw], writes=[ca])
                for k in range(1, 4):
                    P.op("dve", lambda e, rw=rw, ca=ca, ch=ch, k=k: e.scalar_tensor_tensor(
                        out=ca[:], in0=rw[:, k:k + S], scalar=cw[:, ch, k:k + 1], in1=ca[:], op0=ALU.mult, op1=ALU.add),
                        reads=[rw, cw, ca], writes=[ca])
                if kind == "v":
                    P.op("act", lambda e, ca=ca: e.activation(out=vTb[:], in_=ca[:], func=AF.Silu), reads=[ca], writes=[vTb])
                    continue
                P.op("act", lambda e, ca=ca: e.activation(out=ca[:], in_=ca[:], func=AF.Silu), reads=[ca], writes=[ca])
                dstT = qT[sl] if kind == "q" else kT[sl]
                scl = 128.0 ** -0.5 if kind == "q" else 1.0
                for q4 in range(S // 512):
                    qs = slice(q4 * 512, (q4 + 1) * 512)
                    sq_, rn_ = sqb[k2 % 2], rnb[k2 % 2]
                    k2 += 1
                    P.op("act", lambda e, ca=ca, qs=qs, sq_=sq_: e.activation(out=sq_[:], in_=ca[:, qs], func=AF.Square),
                         reads=[ca], writes=[sq_])
                    mm(C, psF[:], C.ones_bf[:], sq_[:], True, True, [C.ones_bf, sq_], psF)
                    rq_ = rqb[(k2 - 1) % 2]
                    P.op("act", lambda e, rn_=rn_: e.activation(out=rn_[:], in_=psF[:], func=AF.Ln, bias=C.epsc[:, 0:1], scale=1.0),
                         reads=[C.epsc], writes=[psF, rn_])
                    P.op("act", lambda e, rn_=rn_, rq_=rq_: e.activation(out=rq_[:], in_=rn_[:], func=AF.Exp, scale=-0.5),
                         reads=[rn_], writes=[rq_])
                    P.op("dve", lambda e, ca=ca, qs=qs, rq_=rq_, dstT=dstT, scl=scl: e.scalar_tensor_tensor(
                        out=dstT[:, qs], in0=ca[:, qs], scalar=scl, in1=rq_[:], op0=ALU.mult, op1=ALU.mult),
                        reads=[ca, rq_], writes=[dstT])
            for tc in range(NT):
                pt_ = psTs[tc % 2]
                P.op("pe", lambda e, tc=tc, pt_=pt_: e.transpose(pt_[:, 0:128], vTb[:, c4(tc)], C.ident_bf[:]),
                     reads=[vTb, C.ident_bf], writes=[pt_])
                P.op("act", lambda e, tc=tc, pt_=pt_: e.activation(out=vtok[sl][:, tc, :], in_=pt_[:, 0:128], func=AF.Copy),
                     writes=[pt_, (vtok[sl], tc)])
            for tc in range(NT):
                pt_ = psTs[tc % 2]
                P.op("pe", lambda e, tc=tc, pt_=pt_: e.transpose(pt_[:, 0:128], kT[sl][:, c4(tc)], C.ident_bf[:]),
                     reads=[kT[sl], C.ident_bf], writes=[pt_])
                P.op("act", lambda e, tc=tc, pt_=pt_: e.activation(out=kd[sl][:, tc, :], in_=pt_[:, 0:128], func=AF.Identity,
                                                                  scale=sc[:, KDEC, tc, h:h + 1]),
                     reads=[sc], writes=[pt_, (kd[sl], tc)])
            for tc in range(NT + 1):
                if tc < NT:
                    P.op("dve", lambda e, tc=tc: e.tensor_scalar(out=Gb[:, tc, :], in0=onesF[:], scalar1=sc[:, G, tc, h:h + 1],
                                                                 scalar2=None, op0=ALU.mult), reads=[onesF, sc], writes=[(Gb, tc)])
                    pd = psA if tc % 2 == 0 else psD
                    mm(C, pd[:, 0:128], Gb[:, tc, :], triF[:], True, False, [(Gb, tc), triF], pd)
                    mm(C, pd[:, 0:128], C.ident_f[:], mneg[:], False, True, [C.ident_f, mneg], pd)
                    P.op("act", lambda e, tc=tc, pd=pd: e.activation(out=DT[:, tc, :], in_=pd[:, 0:128], func=AF.Exp,
                                                                    bias=sc[:, NGC, tc, h:h + 1], scale=1.0),
                         reads=[sc], writes=[pd, (DT, tc)])
                    pk = psB if tc % 2 == 0 else psC
                    mm(C, pk[:, 0:128], kT[sl][:, c4(tc)], kT[sl][:, c4(tc)], True, True, [kT[sl]], pk)
                    mm(C, pk[:, 128:256], kT[sl][:, c4(tc)], qT[sl][:, c4(tc)], True, True, [kT[sl], qT[sl]], pk)
                if tc >= 1:
                    t = tc - 1
                    pk = psB if t % 2 == 0 else psC
                    P.op("dve", lambda e, t=t, pk=pk: e.scalar_tensor_tensor(
                        out=Pp[0][:, t, 0:128], in0=pk[:, 0:128], scalar=sc[:, NBETA, t, h:h + 1], in1=DT[:, t, :],
                        op0=ALU.mult, op1=ALU.mult), reads=[sc, (DT, t)], writes=[pk, (Pp[0], t)])
                    P.op("dve", lambda e, t=t, pk=pk: e.tensor_tensor(out=attnT[sl][:, t, :], in0=pk[:, 128:256], in1=DT[:, t, :],
                                                                      op=ALU.mult), reads=[(DT, t)], writes=[pk, (attnT[sl], t)])
            for tc in range(NT):
                pt_ = psTs[tc % 2]
                P.op("pe", lambda e, tc=tc, pt_=pt_: e.transpose(pt_[:, 0:128], Pp[0][:, tc, 0:128], C.ident_bf[:]),
                     reads=[(Pp[0], tc), C.ident_bf], writes=[pt_])
                P.op("act", lambda e, tc=tc, pt_=pt_: e.activation(out=Pp[0][:, tc, 128:256], in_=pt_[:, 0:128], func=AF.Copy),
                     writes=[pt_, (Pp[0], tc)])
            items = [(lvl, grp) for lvl in range(1, 7) for grp in range(4)]
            R_ = RRT[sl]
            mk0 = msk[:, 0, :, :].rearrange("p a j -> p (a j)").unsqueeze(1).to_broadcast([128, 4, 256])
            i2b = ident2[:].unsqueeze(1).to_broadcast([128, 4, 256])
            for grp in range(4):
                t0 = 4 * grp
                tr = range(t0, t0 + 4)
                xo = xob[grp % 2]
                P.op("pool", lambda e, xo=xo, t0=t0: e.tensor_tensor(out=xo[:], in0=Pp[0][:, t0:t0 + 4, :], in1=mk0, op=ALU.mult),
                     reads=[(Pp[0], tr), msk], writes=[xo])
                P.op("dve", lambda e, xo=xo, t0=t0: e.tensor_tensor(out=R_[:, t0:t0 + 4, :], in0=xo[:], in1=i2b, op=ALU.add),
                     reads=[xo, ident2], writes=[(R_, tr)])

            def my(idx):
                lvl, grp = items[idx]
                xo = xob[idx % 2]
                t0 = 4 * grp
                tr = range(t0, t0 + 4)
                mk = msk[:, lvl, :, :].rearrange("p a j -> p (a j)").unsqueeze(1).to_broadcast([128, 4, 256])
                P.op("pool", lambda e: e.tensor_tensor(out=xo[:], in0=Pp[0][:, t0:t0 + 4, :], in1=mk, op=ALU.mult),
                     reads=[(Pp[0], tr), msk], writes=[xo])
                yb = (psA, psB) if idx % 2 == 0 else (psC, psD)
                for k in range(4):
                    bk = yb[k // 2]
                    o_ = (k % 2) * 256
                    mm(C, bk[:, o_:o_ + 128], xo[:, k, 128:256], R_[:, t0 + k, 0:128], True, True, [xo, (R_, t0 + k)], bk)
                    mm(C, bk[:, o_ + 128:o_ + 256], xo[:, k, 0:128], R_[:, t0 + k, 128:256], True, True, [xo, (R_, t0 + k)], bk)
                for b2 in range(2):
                    bk = yb[b2]
                    t1 = t0 + 2 * b2
                    P.op("act", lambda e, bk=bk, t1=t1: e.activation(
                        out=Pp[1][:, t1:t1 + 2, :].rearrange("p a j -> p (a j)"), in_=bk[:, 0:512], func=AF.Copy),
                        writes=[bk, (Pp[1], (t1, t1 + 1))])

            def za(idx):
                lvl, grp = items[idx]
                t0 = 4 * grp
                zb = (psE, psF)
                for k in range(4):
                    bk = zb[k // 2]
                    o_ = (k % 2) * 256
                    mm(C, bk[:, o_:o_ + 128], R_[:, t0 + k, 128:256], Pp[1][:, t0 + k, 0:128], True, True,
                       [(R_, t0 + k), (Pp[1], t0 + k)], bk)
                    mm(C, bk[:, o_ + 128:o_ + 256], R_[:, t0 + k, 0:128], Pp[1][:, t0 + k, 128:256], True, True,
                       [(R_, t0 + k), (Pp[1], t0 + k)], bk)
                for b2 in range(2):
                    bk = zb[b2]
                    t1 = t0 + 2 * b2
                    rv = R_[:, t1:t1 + 2, :].rearrange("p a j -> p (a j)")
                    P.op("dve", lambda e, bk=bk, rv=rv: e.tensor_tensor(out=rv, in0=bk[:, 0:512], in1=rv, op=ALU.add),
                         writes=[bk, (R_, (t1, t1 + 1))])

            for idx in range(len(items)):
                my(idx)
                if idx >= 1:
                    za(idx - 1)
            za(len(items) - 1)

        def g2(s, heads):
            banks = ((psA, psB, psC), (psD, psE, psF))
            sls = range(len(heads))
            for sl in sls:
                P.op("pool", lambda e, sl=sl: e.memset(hS[sl][:], 0.0), writes=[hS[sl]])
                P.op("pool", lambda e, sl=sl: e.memset(hSb[sl][:], 0.0), writes=[hSb[sl]])
            for tc in range(NT):
                j2 = tc % 2
                for sl in sls:
                    bx = banks[sl][0]
                    mm(C, bx[:, 0:128], kT[sl][:, c4(tc)], hSb[sl][:], True, True, [kT[sl], hSb[sl]], bx)
                    mm(C, bx[:, 128:256], qT[sl][:, c4(tc)], hSb[sl][:], True, True, [qT[sl], hSb[sl]], bx)
                for sl in sls:
                    h = heads[sl]
                    bx = banks[sl][0]
                    r_, o1_ = rb[sl][j2], o1[sl][j2]
                    P.op("dve", lambda e, tc=tc, h=h, r_=r_, bx=bx, sl=sl: e.scalar_tensor_tensor(
                        out=r_[:], in0=bx[:, 0:128], scalar=sc[:, NEGC, tc, h:h + 1], in1=vtok[sl][:, tc, :], op0=ALU.mult, op1=ALU.add),
                        reads=[sc, (vtok[sl], tc)], writes=[bx, r_])
                    P.op("dve", lambda e, tc=tc, h=h, o1_=o1_, bx=bx: e.tensor_scalar(
                        out=o1_[:], in0=bx[:, 128:256], scalar1=sc[:, EGC, tc, h:h + 1], scalar2=None, op0=ALU.mult),
                        reads=[sc], writes=[bx, o1_])
                for sl in sls:
                    by = banks[sl][1]
                    mm(C, by[:, 0:128], RRT[sl][:, tc, 0:128], rb[sl][j2][:], True, True, [(RRT[sl], tc), rb[sl][j2]], by)
                for sl in sls:
                    h = heads[sl]
                    by = banks[sl][1]
                    vn_ = vn[sl][j2]
                    P.op("act", lambda e, tc=tc, h=h, vn_=vn_, by=by: e.activation(out=vn_[:], in_=by[:, 0:128], func=AF.Identity,
                                                                                   scale=sc[:, BETA, tc, h:h + 1]),
                         reads=[sc], writes=[by, vn_])
                for sl in sls:
                    bz = banks[sl][2]
                    vn_ = vn[sl][j2]
                    mm(C, bz[:, 0:128], attnT[sl][:, tc, :], vn_[:], True, True, [(attnT[sl], tc), vn_], bz)
                    mm(C, bz[:, 128:256], kd[sl][:, tc, :], vn_[:], True, True, [(kd[sl], tc), vn_], bz)
                for sl in sls:
                    h = heads[sl]
                    bz = banks[sl][2]
                    P.op("dve", lambda e, tc=tc, h=h, bz=bz, sl=sl: e.scalar_tensor_tensor(
                        out=hS[sl][:], in0=hS[sl][:], scalar=sc[:, EGL, tc, h:h + 1], in1=bz[:, 128:256], op0=ALU.mult, op1=ALU.add),
                        reads=[hS[sl], sc], writes=[bz, hS[sl]])
                    P.op("act", lambda e, sl=sl: e.activation(out=hSb[sl][:], in_=hS[sl][:], func=AF.Copy),
                         reads=[hS[sl]], writes=[hSb[sl]])
                    of_, o1_ = of[sl][j2], o1[sl][j2]
                    P.op("dve", lambda e, o1_=o1_, of_=of_, bz=bz: e.tensor_tensor(out=of_[:], in0=bz[:, 0:128], in1=o1_[:], op=ALU.add),
                         reads=[o1_], writes=[bz, of_])
                for sl in sls:
                    of_, osq_, ssq_, onb_ = of[sl][j2], osq[sl][j2], ssq[sl][j2], onb[sl][j2]
                    P.op("pool", lambda e, of_=of_, osq_=osq_: e.tensor_tensor(out=osq_[:], in0=of_[:], in1=of_[:], op=ALU.mult),
                         reads=[of_], writes=[osq_])
                    P.op("dve", lambda e, osq_=osq_, ssq_=ssq_: e.tensor_reduce(out=ssq_[:, 0:1], in_=osq_[:], axis=mybir.AxisListType.X, op=ALU.add),
                         reads=[osq_], writes=[ssq_])
                    P.op("act", lambda e, ssq_=ssq_: e.activation(out=ssq_[:, 1:2], in_=ssq_[:, 0:1], func=AF.Sqrt, bias=C.epsc[:, 0:1],
                                                                  scale=1.0 / 128), reads=[ssq_, C.epsc], writes=[ssq_])
                    P.op("dve", lambda e, ssq_=ssq_: e.reciprocal(out=ssq_[:, 2:3], in_=ssq_[:, 1:2]), reads=[ssq_], writes=[ssq_])
                    P.op("dve", lambda e, of_=of_, onb_=onb_, ssq_=ssq_: e.tensor_scalar(out=onb_[:], in0=of_[:], scalar1=ssq_[:, 2:3],
                                                                                      scalar2=None, op0=ALU.mult),
                         reads=[of_, ssq_], writes=[onb_])
                for sl in sls:
                    onb_ = onb[sl][j2]
                    pt_ = psTs[sl]
                    P.op("pe", lambda e, onb_=onb_, pt_=pt_: e.transpose(pt_[:, 0:128], onb_[:], C.ident_bf[:]),
                         reads=[onb_, C.ident_bf], writes=[pt_])
                    P.op("dve", lambda e, tc=tc, pt_=pt_, sl=sl: e.scalar_tensor_tensor(
                        out=yst[sl][:, c4(tc)], in0=pt_[:, 0:128], scalar=onm[:, 0:1], in1=siluz[sl][:, c4(tc)], op0=ALU.mult, op1=ALU.mult),
                        reads=[onm, siluz[sl]], writes=[pt_, yst[sl]])
            for sl in sls:
                h = heads[sl]
                P.op("sp", lambda e, h=h, sl=sl: e.dma_start(out=scr["YA"][s, h * 128:(h + 1) * 128, :], in_=yst[sl][:]),
                     reads=[yst[sl]], writes=[scr["YAb"]], dma=True)

        for s in range(nseq):
            P.op("sp", lambda e, s=s: e.dma_start(out=ba[:], in_=scr["BA"][s].rearrange("(a p) k -> p a k", p=128)),
                 reads=[scr["BAb"]], writes=[ba], dma=True)
            S_ = lambda k: sc[:, k, :, :]
            P.op("act", lambda e: e.activation(out=S_(BETA), in_=ba[:, :, 0:8], func=AF.Sigmoid), reads=[ba], writes=[sc])
            P.op("dve", lambda e: e.tensor_scalar(out=S_(NBETA), in0=S_(BETA), scalar1=-1.0, scalar2=None, op0=ALU.mult),
                 reads=[sc], writes=[sc])
            P.op("dve", lambda e: e.tensor_tensor(out=S_(G), in0=ba[:, :, 8:16],
                                                  in1=ad[:, 8:16].unsqueeze(1).to_broadcast([128, NT, 8]), op=ALU.add),
                 reads=[ba, ad], writes=[sc])
            P.op("act", lambda e: e.activation(out=S_(G), in_=S_(G), func=AF.Exp), reads=[sc], writes=[sc])
            P.op("act", lambda e: e.activation(out=S_(G), in_=S_(G), func=AF.Ln, bias=one1[:], scale=1.0),
                 reads=[sc, one1], writes=[sc])
            P.op("dve", lambda e: e.tensor_tensor(out=S_(G), in0=S_(G),
                                                  in1=ad[:, 0:8].unsqueeze(1).to_broadcast([128, NT, 8]), op=ALU.mult),
                 reads=[sc, ad], writes=[sc])
            P.op("dve", lambda e: e.tensor_scalar(out=S_(G), in0=S_(G), scalar1=-1.0, scalar2=None, op0=ALU.mult),
                 reads=[sc], writes=[sc])
            for tc in range(NT):
                mm(C, psF[:, tc * 8:(tc + 1) * 8], triF[:], sc[:, G, tc, :], True, True, [triF, sc], psF)
                mm(C, psF[:, 128 + tc * 8:128 + (tc + 1) * 8], onesF[:], sc[:, G, tc, :], True, True, [onesF, sc], psF)
            P.op("dve", lambda e: e.tensor_copy(out=S_(GC), in_=psF[:, 0:128].rearrange("p (a k) -> p a k", k=8)),
                 writes=[psF, sc])
            P.op("dve", lambda e: e.tensor_copy(out=S_(GL), in_=psF[:, 128:256].rearrange("p (a k) -> p a k", k=8)),
                 writes=[psF, sc])
            P.op("act", lambda e: e.activation(out=S_(EGC), in_=S_(GC), func=AF.Exp), reads=[sc], writes=[sc])
            P.op("act", lambda e: e.activation(out=S_(EGL), in_=S_(GL), func=AF.Exp), reads=[sc], writes=[sc])
            P.op("dve", lambda e: e.tensor_scalar(out=S_(NGC), in0=S_(GC), scalar1=-1.0, scalar2=None, op0=ALU.mult),
                 reads=[sc], writes=[sc])
            P.op("dve", lambda e: e.tensor_scalar(out=S_(NEGC), in0=S_(EGC), scalar1=-1.0, scalar2=None, op0=ALU.mult),
                 reads=[sc], writes=[sc])
            P.op("dve", lambda e: e.tensor_tensor(out=S_(KDEC), in0=S_(GL), in1=S_(GC), op=ALU.subtract),
                 reads=[sc], writes=[sc])
            P.op("act", lambda e: e.activation(out=S_(KDEC), in_=S_(KDEC), func=AF.Exp), reads=[sc], writes=[sc])
            for hp in range(4):
                heads = (2 * hp, 2 * hp + 1)
                for sl, h in enumerate(heads):
                    g1(s, h, sl)
                g2(s, heads)
    P.barrier()


def tile_cols(W, width, kc=None):
    K, N = W.shape
    return np.ascontiguousarray(W.reshape(K // 128, 128, N // width, width).transpose(2, 1, 0, 3))


def prep_weights(inp, depth=DEPTH):
    f = lambda a: np.asarray(a, dtype=np.float32)
    out = {}
    out["ada_w"] = np.stack([tile_cols(f(inp["ada_w"][l]), GW) for l in range(depth)])
    out["ada_b"] = np.ascontiguousarray(f(inp["ada_b"])[:depth].reshape(depth, 144, 128))
    out["norms"] = np.ascontiguousarray(np.stack(
        [f(inp["norm_ffn1"])[:depth], f(inp["norm_mix"])[:depth], f(inp["norm_ffn2"])[:depth]], axis=1
    ).reshape(depth, 3, NKC, 128))
    for nm in ("ffn1", "ffn2"):
        out[nm + "_w1"] = np.stack([tile_cols(f(inp[nm + "_w1"][l]), GW) for l in range(depth)])
        out[nm + "_w3"] = np.stack([tile_cols(f(inp[nm + "_w3"][l]), GW) for l in range(depth)])
        out[nm + "_w2"] = np.stack([tile_cols(f(inp[nm + "_w2"][l]), 128) for l in range(depth)])
    w_in = f(inp["w_in"])[:depth]
    segs = []
    for l in range(depth):
        W = w_in[l]
        segs.append(np.concatenate([tile_cols(W[:, 0:4096], GW), tile_cols(W[:, OFF_BU:OFF_CQ], GW),
                                    tile_cols(W[:, OFF_CQ:OFF_GATE], GW), tile_cols(W[:, OFF_GATE:], GW)], axis=0))
    out["w_inT"] = np.stack(segs)
    out["w_ba"] = np.stack([tile_cols(w_in[l][:, OFF_BA:OFF_BU], 16)[0] for l in range(depth)])
    out["w_br"] = np.stack([np.concatenate([tile_cols(f(inp["w_branch_a"][l]), GW), tile_cols(f(inp["w_branch_b"][l]), GW),
                                            tile_cols(f(inp["w_branch_c"][l]), GW)], axis=2) for l in range(depth)])
    out["w_outT"] = np.stack([tile_cols(f(inp["w_out"][l]), GW) for l in range(depth)])
    out["dil_g"] = np.ascontiguousarray(np.stack([np.tile(f(inp["dil_q_norm"])[:depth], (1, 2)),
                                                  np.tile(f(inp["dil_k_norm"])[:depth], (1, 2))], axis=1))
    out["gdn_conv"] = np.ascontiguousarray(f(inp["gdn_conv"])[:depth].reshape(depth, 4, 24, 128).transpose(0, 3, 2, 1))
    ad = np.concatenate([f(inp["gdn_a_log"])[:depth], f(inp["gdn_dt_bias"])[:depth]], axis=1)
    out["gdn_ad"] = np.ascontiguousarray(np.broadcast_to(ad[:, None, :], (depth, 128, 16)))
    out["gdn_on"] = np.ascontiguousarray(f(inp["gdn_out_norm"])[:depth].reshape(depth, 128, 1))
    jj, ii = np.meshgrid(np.arange(128), np.arange(128), indexing="ij")
    msk = np.zeros((128, 7, 2, 128), np.float32)
    for lv in range(7):
        bsz = 1 << lv
        m_ = (((jj // bsz) % 2 == 0) & (ii // bsz == jj // bsz + 1)).astype(np.float32)
        msk[:, lv, 0, :] = m_
        msk[:, lv, 1, :] = m_.T
    out["gdn_masks"] = msk
    G_, P_, I_ = 48, 64, 16
    out["s5_vec"] = np.ascontiguousarray(np.stack(
        [f(inp["s5_a_re"])[:depth].reshape(depth, 24, 128), f(inp["s5_a_im"])[:depth].reshape(depth, 24, 128),
         np.repeat(f(inp["s5_log_step"])[:depth], P_, axis=1).reshape(depth, 24, 128)], axis=1))
    out["s5_dg"] = np.ascontiguousarray(np.concatenate(
        [f(inp["s5_d"])[:depth].reshape(depth, 6, 128), f(inp["s5_glu_b"])[:depth].reshape(depth, 12, 128)], axis=1))
    Bm = np.zeros((depth, 128, 6, 2, 512), np.float32)
    Cm = np.zeros((depth, 128, 24, 2, 128), np.float32)
    for ri, (bn, cn) in enumerate((("s5_b_re", "s5_c_re"), ("s5_b_im", "s5_c_im"))):
        b = f(inp[bn])[:depth]
        cc = f(inp[cn])[:depth]
        for g in range(G_):
            kc, gl8 = divmod(g, 8)
            Bm[:, gl8 * 16:(gl8 + 1) * 16, kc, ri, gl8 * 64:(gl8 + 1) * 64] = b[:, g].transpose(0, 2, 1)
            sc, gl = divmod(g, 2)
            Cm[:, gl * 64:(gl + 1) * 64, sc, ri, gl8 * 16:(gl8 + 1) * 16] = cc[:, g].transpose(0, 2, 1)
    out["s5_Bm"] = Bm
    out["s5_Cm"] = Cm
    out["s5_gluw"] = np.stack([tile_cols(f(inp["s5_glu_w"][l]), GW) for l in range(depth)])
    return out


def kernel(**inputs):
    x = np.asarray(inputs["x"], dtype=np.float32)
    c = np.asarray(inputs["c"], dtype=np.float32)
    B = x.shape[0]
    nseq = B // NCORES
    wts = prep_weights(inputs)
    nc = build_program(nseq=nseq)
    in_maps = []
    for i in range(NCORES):
        xs = x[i * nseq:(i + 1) * nseq].reshape(nseq * S, D)
        m = dict(wts)
        m["xT"] = np.ascontiguousarray(xs.T)
        m["c"] = np.ascontiguousarray(c[i * nseq:(i + 1) * nseq])
        in_maps.append(m)
    res = run_bass_kernel_spmd(nc, in_maps, core_ids=list(range(NCORES)))
    outs = [np.ascontiguousarray(r["outT"].T).reshape(nseq, S, D) for r in res.results]
    return np.concatenate(outs, axis=0).astype(np.float32)
```
